# Optimizing a Trainium2 kernel written in Bass

```python
import math
import jax, jax.numpy as jnp
from jax import lax
import numpy as np

D_MODEL = 1024
BATCH = 4
SEQ = 8192
DEPTH = 2

N_A_LAYERS = DEPTH // 2
N_B_LAYERS = DEPTH - N_A_LAYERS
D_FF = 2816
RWKV_HEAD = 64
RWKV_HEADS = D_MODEL // RWKV_HEAD
LORA_DECAY = 64
LORA_AAA = 64
LORA_GATE = 160
RWKV_LN_EPS = 64e-5
N_SHIFT_MIX = 6
DIFF_HEAD = 64
DIFF_HEADS = D_MODEL // (2 * DIFF_HEAD)
DIFF_V_HEAD = 2 * DIFF_HEAD
Q_BLOCK = 128
NORM_EPS = 1e-6
SUBLN_EPS = 1e-5

kernel_name = "rwkv7_diffattn_yoco_macaron"


def rms_norm(x, g, eps=NORM_EPS):
    xf = x.astype(jnp.float32)
    y = xf * lax.rsqrt(jnp.mean(xf * xf, axis=-1, keepdims=True) + eps)
    return (y * g.astype(jnp.float32)).astype(x.dtype)


def swiglu(h, w_in, w_out):
    gate, up = jnp.split(h @ w_in, 2, axis=-1)
    return (jax.nn.silu(gate) * up) @ w_out


def lambda_init(layer_idx):
    return 0.8 - 0.6 * math.exp(-0.3 * layer_idx)


def alibi_slopes(n_heads):
    return 2.0 ** (-8.0 * jnp.arange(1, n_heads + 1, dtype=jnp.float32) / n_heads)


def wkv7_scan(r, decay, k, v, a, b):
    B, T, H, N = r.shape

    def step(S, inp):
        r_t, w_t, k_t, v_t, a_t, b_t = inp
        sa = jnp.einsum('bhij,bhj->bhi', S, a_t)
        S = S * w_t[:, :, None, :] + sa[..., None] * b_t[:, :, None, :] + v_t[..., None] * k_t[:, :, None, :]
        return S, jnp.einsum('bhij,bhj->bhi', S, r_t)

    xs = [jnp.moveaxis(t.astype(jnp.float32), 1, 0) for t in (r, decay, k, v, a, b)]
    S0 = jnp.zeros((B, H, N, N), jnp.float32)
    _, y = lax.scan(step, S0, xs)
    return jnp.moveaxis(y, 0, 1)


def rwkv7_time_mix(h, mu, w_rkv, w0, w1, w2, a0, a1, a2, g1, g2, k_k, k_a, r_k, ln_g, ln_b, w_o):
    B, T, C = h.shape
    H, N = RWKV_HEADS, RWKV_HEAD
    h_prev = jnp.pad(h, ((0, 0), (1, 0), (0, 0)))[:, :-1]
    dx = h_prev - h
    r, k, v = jnp.einsum('sbtc,scd->sbtd', h[None] + dx[None] * mu[:3, None, None, :], w_rkv)
    xw = h + dx * mu[3]
    xa = h + dx * mu[4]
    xg = h + dx * mu[5]
    w_log = -jax.nn.softplus(-(w0 + jnp.tanh(xw @ w1) @ w2)) - 0.5
    decay = jnp.exp(-jnp.exp(w_log.astype(jnp.float32)))
    a = jax.nn.sigmoid(a0 + (xa @ a1) @ a2)
    g = jax.nn.sigmoid(xg @ g1) @ g2
    kk = (k * k_k).reshape(B, T, H, N).astype(jnp.float32)
    kk = kk * lax.rsqrt(jnp.maximum(jnp.sum(kk * kk, axis=-1, keepdims=True), 1e-24))
    k = k * (1 + (a - 1) * k_a)
    heads = lambda t: t.reshape(B, T, H, N)
    a_h = heads(a).astype(jnp.float32)
    y = wkv7_scan(heads(r), heads(decay), heads(k), heads(v), -kk, kk * a_h)
    mean = jnp.mean(y, axis=-1, keepdims=True)
    var = jnp.mean(jnp.square(y - mean), axis=-1, keepdims=True)
    y = ((y - mean) * lax.rsqrt(var + RWKV_LN_EPS)).reshape(B, T, C) * ln_g + ln_b
    bonus = jnp.sum(heads(r).astype(jnp.float32) * heads(k).astype(jnp.float32) * r_k, axis=-1, keepdims=True)
    y = y + (bonus * heads(v).astype(jnp.float32)).reshape(B, T, C)
    return (y.astype(h.dtype) * g) @ w_o


def shared_kv(x, kv_norm, w_kv, k_norm):
    B, T, _ = x.shape
    kv = rms_norm(x, kv_norm) @ w_kv
    k = rms_norm(kv[..., :D_MODEL].reshape(B, T, DIFF_HEADS, 2, DIFF_HEAD), k_norm).transpose(0, 2, 3, 1, 4)
    v = kv[..., D_MODEL:].reshape(B, T, DIFF_HEADS, DIFF_V_HEAD).transpose(0, 2, 1, 3)
    return k, v


def diff_attention(h, k_sh, v_sh, w_q, q_norm, lam, subln, w_o, lam_init):
    B, T, _ = h.shape
    H, d = DIFF_HEADS, DIFF_HEAD
    n_blk = T // Q_BLOCK
    q = rms_norm((h @ w_q).reshape(B, T, H, 2, d), q_norm)
    q = q.reshape(B, n_blk, Q_BLOCK, H, 2, d).transpose(1, 0, 3, 4, 2, 5)
    lamf = lam.astype(jnp.float32)
    lam_full = jnp.exp(jnp.sum(lamf[0] * lamf[1])) - jnp.exp(jnp.sum(lamf[2] * lamf[3])) + lam_init
    slopes = alibi_slopes(H)
    k_pos = jnp.arange(T, dtype=jnp.int32)
    scale = d ** -0.5

    def block(args):
        qb, start = args
        s = jnp.einsum('bhcqd,bhckd->bhcqk', qb, k_sh).astype(jnp.float32) * scale
        dist = (start + jnp.arange(Q_BLOCK, dtype=jnp.int32))[:, None] - k_pos[None, :]
        bias = -slopes[:, None, None] * dist.astype(jnp.float32)
        s = jnp.where(dist >= 0, s + bias[None, :, None], -jnp.inf)
        p = jax.nn.softmax(s, axis=-1)
        attn = p[:, :, 0] - lam_full * p[:, :, 1]
        return jnp.einsum('bhqk,bhkd->bhqd', attn.astype(v_sh.dtype), v_sh)

    starts = jnp.arange(n_blk, dtype=jnp.int32) * Q_BLOCK
    o = lax.map(block, (q, starts))
    o = rms_norm(o, subln, SUBLN_EPS) * (1.0 - lam_init)
    o = o.transpose(1, 0, 3, 2, 4).reshape(B, T, H * DIFF_V_HEAD)
    return o @ w_o


def setup_inputs(seed: int = 0) -> dict:
    key = jax.random.key(seed)
    ks = jax.random.split(key, 29)
    C, F = D_MODEL, D_FF
    nA, nB = N_A_LAYERS, N_B_LAYERS
    nrm = lambda k, shape, s: jax.random.normal(k, shape, jnp.float32) * s
    gain = lambda k, shape: 1.0 + nrm(k, shape, 0.02)
    return {
        "x": nrm(ks[0], (BATCH, SEQ, C), 1.0),
        "ffn_norm": gain(ks[1], (DEPTH, 2, C)),
        "ffn_w_in": nrm(ks[2], (DEPTH, 2, C, 2 * F), C ** -0.5),
        "ffn_w_out": nrm(ks[3], (DEPTH, 2, F, C), F ** -0.5),
        "mix_norm": gain(ks[4], (DEPTH, C)),
        "rwkv_mu": jax.random.uniform(ks[5], (nA, N_SHIFT_MIX, C), jnp.float32),
        "rwkv_w_rkv": nrm(ks[6], (nA, 3, C, C), C ** -0.5),
        "rwkv_w0": jax.random.uniform(ks[7], (nA, C), jnp.float32, -6.0, -1.0),
        "rwkv_w1": nrm(ks[8], (nA, C, LORA_DECAY), C ** -0.5),
        "rwkv_w2": nrm(ks[9], (nA, LORA_DECAY, C), 0.5 * LORA_DECAY ** -0.5),
        "rwkv_a0": nrm(ks[10], (nA, C), 0.1),
        "rwkv_a1": nrm(ks[11], (nA, C, LORA_AAA), C ** -0.5),
        "rwkv_a2": nrm(ks[12], (nA, LORA_AAA, C), 0.5 * LORA_AAA ** -0.5),
        "rwkv_g1": nrm(ks[13], (nA, C, LORA_GATE), C ** -0.5),
        "rwkv_g2": nrm(ks[14], (nA, LORA_GATE, C), LORA_GATE ** -0.5),
        "rwkv_k_k": 0.85 + nrm(ks[15], (nA, C), 0.02),
        "rwkv_k_a": gain(ks[16], (nA, C)),
        "rwkv_r_k": nrm(ks[17], (nA, RWKV_HEADS, RWKV_HEAD), 0.1),
        "rwkv_ln_g": gain(ks[18], (nA, C)),
        "rwkv_ln_b": nrm(ks[19], (nA, C), 0.02),
        "rwkv_w_o": nrm(ks[20], (nA, C, C), 0.5 * C ** -0.5),
        "kv_norm": gain(ks[21], (C,)),
        "w_kv": nrm(ks[22], (C, 2 * C), C ** -0.5),
        "k_norm": gain(ks[23], (DIFF_HEAD,)),
        "diff_w_q": nrm(ks[24], (nB, C, C), C ** -0.5),
        "diff_q_norm": gain(ks[25], (nB, DIFF_HEAD)),
        "diff_lambda": nrm(ks[26], (nB, 4, DIFF_HEAD), 0.1),
        "diff_subln": gain(ks[27], (nB, DIFF_V_HEAD)),
        "diff_w_o": nrm(ks[28], (nB, C, C), 0.5 * C ** -0.5),
    }


def reference(x, ffn_norm, ffn_w_in, ffn_w_out, mix_norm, rwkv_mu, rwkv_w_rkv, rwkv_w0, rwkv_w1, rwkv_w2,
              rwkv_a0, rwkv_a1, rwkv_a2, rwkv_g1, rwkv_g2, rwkv_k_k, rwkv_k_a, rwkv_r_k, rwkv_ln_g, rwkv_ln_b,
              rwkv_w_o, kv_norm, w_kv, k_norm, diff_w_q, diff_q_norm, diff_lambda, diff_subln, diff_w_o):
    k_sh = None
    v_sh = None
    for l in range(DEPTH):
        x = x + 0.5 * swiglu(rms_norm(x, ffn_norm[l, 0]), ffn_w_in[l, 0], ffn_w_out[l, 0])
        h = rms_norm(x, mix_norm[l])
        if l < N_A_LAYERS:
            i = l
            x = x + rwkv7_time_mix(h, rwkv_mu[i], rwkv_w_rkv[i], rwkv_w0[i], rwkv_w1[i], rwkv_w2[i],
                                   rwkv_a0[i], rwkv_a1[i], rwkv_a2[i], rwkv_g1[i], rwkv_g2[i],
                                   rwkv_k_k[i], rwkv_k_a[i], rwkv_r_k[i], rwkv_ln_g[i], rwkv_ln_b[i], rwkv_w_o[i])
        else:
            j = l - N_A_LAYERS
            x = x + diff_attention(h, k_sh, v_sh, diff_w_q[j], diff_q_norm[j], diff_lambda[j],
                                   diff_subln[j], diff_w_o[j], lambda_init(l))
        x = x + 0.5 * swiglu(rms_norm(x, ffn_norm[l, 1]), ffn_w_in[l, 1], ffn_w_out[l, 1])
        if l == N_A_LAYERS - 1:
            k_sh, v_sh = shared_kv(x, kv_norm, w_kv, k_norm)
    return x
```

```python
import numpy as np
from contextlib import ExitStack
import concourse.bass as bass
import concourse.mybir as mybir
from concourse.bass_utils import run_bass_kernel_spmd

F32 = mybir.dt.float32
BF16 = mybir.dt.bfloat16
ALU = mybir.AluOpType
AF = mybir.ActivationFunctionType
AX = mybir.AxisListType

NCORES = 8
SEM_CAP = 8192


class _Op:
    __slots__ = ("eng", "fn", "deps", "dma", "signal", "sem", "val", "idx")


class Prog:
    ENGS = ("pe", "act", "dve", "pool", "sp")

    def __init__(self, nc, es, n_dma_sems=12):
        self.nc = nc
        self.sem_es = es
        self.es = es
        self.ops = []
        self.lastw = {}
        self.readers = {}
        self.n_dma_sems = n_dma_sems
        self.uid = 0
        self.psum_keys = set()
        self.emitted = 0
        self.cnt = {e: 0 for e in self.ENGS}
        self.dma_cnt = [0] * (2 * n_dma_sems)
        self.dma_last = [None] * (2 * n_dma_sems)
        self.n_dma = {"sp": 0, "pool": 0}
        self.n_cc = 0
        self.sems = {}
        self.waited = {e: {} for e in self.ENGS}
        self.nstage = 0

    def sb(self, name, shape, dt):
        return self.es.enter_context(self.nc.sbuf_tensor("g%d_%s" % (self.nstage, name), list(shape), dt))

    def ps(self, name, shape, dt=F32):
        return self.es.enter_context(self.nc.psum_tensor("g%d_%s" % (self.nstage, name), list(shape), dt))

    def _sem(self, key):
        if key not in self.sems:
            self.sems[key] = self.sem_es.enter_context(self.nc.semaphore("s_%s_%s" % key))
        return self.sems[key]

    class _Stage:
        def __init__(self, P):
            self.P = P

        def __enter__(self):
            self.P.es = ExitStack()
            self.P.es.__enter__()
            return self.P

        def __exit__(self, *a):
            if a[0] is None:
                self.P.emit_stage()
            self.P.es.__exit__(*a)
            self.P.es = self.P.sem_es
            return False

    def stage(self):
        return Prog._Stage(self)

    def add(self, eng, fn, r=(), w=(), dma=False):
        op = _Op()
        op.eng, op.fn, op.dma = eng, fn, dma
        op.idx = len(self.ops)
        op.signal = False
        op.sem = op.val = None
        deps = {}
        for k in r:
            d = self.lastw.get(k)
            if d is not None:
                deps[d] = True
        for k in r:
            if k in self.psum_keys:
                for rd in self.readers.get(k, ()):
                    if self.ops[rd].eng != eng:
                        deps[rd] = True
        for k in w:
            d = self.lastw.get(k)
            if d is not None:
                deps[d] = True
            for rd in self.readers.get(k, ()):
                if rd not in deps:
                    deps[rd] = False
        for k in r:
            lst = self.readers.setdefault(k, [])
            if not dma:
                lst[:] = [x for x in lst if self.ops[x].dma or self.ops[x].eng != eng]
            lst.append(op.idx)
        for k in w:
            self.lastw[k] = op.idx
            self.readers[k] = []
        op.deps = deps
        self.ops.append(op)
        return op

    def pe(self, fn, r=(), w=()):
        return self.add("pe", fn, r, w)

    def act(self, fn, r=(), w=()):
        return self.add("act", fn, r, w)

    def dve(self, fn, r=(), w=()):
        return self.add("dve", fn, r, w)

    def pool(self, fn, r=(), w=()):
        return self.add("pool", fn, r, w)

    def dma(self, q, out, in_, r=(), w=()):
        return self.add(q, lambda e: e.dma_start(out=out, in_=in_), r, w, dma=True)

    def finalize(self):
        self.emit_stage()

    def emit_stage(self):
        nc, ops = self.nc, self.ops
        s0 = self.emitted
        stage_ops = ops[s0:]
        self.emitted = len(ops)
        self.nstage += 1
        self.lastw, self.readers = {}, {}
        if not stage_ops:
            return
        for op in stage_ops:
            op.deps = {d: st for d, st in op.deps.items() if d >= s0}
            for d, strict in op.deps.items():
                p = ops[d]
                if p.dma:
                    continue
                if p.eng != op.eng or op.dma:
                    p.signal = True
                elif strict and p.eng != "pe":
                    p.signal = True
        NS2 = 2 * self.n_dma_sems
        for op in stage_ops:
            if op.dma == "cc":
                self.n_cc += 1
                op.sem, op.val = ("cc", 0), self.n_cc
            elif op.dma:
                j = self.n_dma[op.eng] % self.n_dma_sems + (self.n_dma_sems if op.eng == "pool" else 0)
                self.n_dma[op.eng] += 1
                self.dma_cnt[j] += 1
                op.sem, op.val = ("dma", j), 16 * self.dma_cnt[j]
                if self.dma_last[j] is not None and self.dma_last[j] >= s0:
                    op.deps[self.dma_last[j]] = True
                self.dma_last[j] = op.idx
            elif op.signal:
                t = self.cnt[op.eng]
                self.cnt[op.eng] += 1
                op.sem, op.val = (op.eng, t // SEM_CAP), t % SEM_CAP + 1
        per_eng = {e: [] for e in self.ENGS}
        for op in stage_ops:
            per_eng[op.eng].append(op)
        final = {}
        for op in stage_ops:
            if op.dma:
                final[op.sem] = max(final.get(op.sem, 0), op.val)
        sems = self._sem

        def emit(e, eng):
            waited = self.waited[eng]
            for op in per_eng[eng]:
                need = {}
                for d, strict in op.deps.items():
                    p = ops[d]
                    if (not p.dma) and p.eng == eng and not op.dma:
                        if not strict or eng == "pe":
                            continue
                    if p.sem is None:
                        continue
                    if need.get(p.sem, 0) < p.val:
                        need[p.sem] = p.val
                for sk, v in need.items():
                    if waited.get(sk, 0) < v:
                        e.wait_ge(sems(sk), v)
                        waited[sk] = v
                ins = op.fn(e)
                if op.dma == "cc":
                    ins.then_inc(sems(op.sem), 1)
                elif op.dma:
                    ins.then_inc(sems(op.sem), 16)
                elif op.signal:
                    ins.then_inc(sems(op.sem), 1)
            if eng == "sp":
                for sk, v in final.items():
                    if waited.get(sk, 0) < v:
                        e.wait_ge(sems(sk), v)
                        waited[sk] = v

        with nc.Block() as block:
            @block.tensor
            def _(e):
                emit(e, "pe")

            @block.scalar
            def _(e):
                emit(e, "act")

            @block.vector
            def _(e):
                emit(e, "dve")

            @block.gpsimd
            def _(e):
                emit(e, "pool")

            @block.sync
            def _(e):
                emit(e, "sp")


D = 1024
KC = 8
FF = 2816
FC = 22
EPS = 1e-6


def _rmsnorm(P, nc, x_sb, xkey, h_sb, hkey, g_sb, gcol0, sq_sb, ones_sb, ps_ss, rstd_sb, t0, tn, tag, o0=None):
    kq = "sq" + tag
    if o0 is None:
        o0 = t0
    for kc in range(KC):
        P.act(lambda e, kc=kc: e.activation(out=sq_sb[:, kc, 0:tn], in_=x_sb[:, kc, t0:t0 + tn], func=AF.Square),
              r=[xkey], w=[kq])
    for kc in range(KC):
        P.pe(lambda e, kc=kc: e.matmul(ps_ss[:, 0:tn], lhsT=ones_sb[:, :], rhs=sq_sb[:, kc, 0:tn],
                                       start=(kc == 0), stop=(kc == KC - 1)),
             r=[kq, "ones"], w=["ps_ss"])
    P.act(lambda e: e.activation(out=rstd_sb[:, 0:tn], in_=ps_ss[:, 0:tn], func=AF.Sqrt, bias=EPS, scale=1.0),
          r=["ps_ss"], w=["rstd"])
    P.dve(lambda e: e.reciprocal(out=rstd_sb[:, 0:tn], in_=rstd_sb[:, 0:tn]), r=["rstd"], w=["rstd"])
    for kc in range(KC):
        P.dve(lambda e, kc=kc: e.scalar_tensor_tensor(out=h_sb[:, kc, o0:o0 + tn], in0=x_sb[:, kc, t0:t0 + tn],
                                                      scalar=g_sb[:, gcol0 + kc:gcol0 + kc + 1],
                                                      in1=rstd_sb[:, 0:tn], op0=ALU.mult, op1=ALU.mult),
              r=[xkey, "rstd", "gains"], w=[hkey])


def emit_LA(nc, P, io, NT, has_add, TB=1024, TT=512):
    xT = io("xT", [D, NT], "in")
    gains = io("gains", [128, 2 * KC], "in")
    w_in = io("w_in", [FC, 128, KC, 256], "in")
    w_out = io("w_out", [KC, 128, FC, 128], "in")
    if has_add:
        aT = io("aT", [D, NT], "in")
        w_add = io("w_add", [KC, 128, KC, 128], "in")
    yT = io("yT", [D, NT], "out")
    hT = io("hT", [D, NT], "out")
    xT_v = xT.rearrange("(kc p) t -> p kc t", p=128)
    yT_v = yT.rearrange("(kc p) t -> p kc t", p=128)
    hT_v = hT.rearrange("(kc p) t -> p kc t", p=128)
    NB = NT // TB
    NS = TB // TT
    with P.stage():
        x_sb = P.sb("x_sb", [128, KC, TB], F32)
        h_sb = P.sb("h_sb", [128, KC, TB], BF16)
        act_sb = P.sb("act_sb", [128, FC, TB], BF16)
        sq_sb = P.sb("sq_sb", [128, KC, TT], BF16)
        ho_sb = P.sb("ho_sb", [128, KC, TT], F32)
        rstd_sb = P.sb("rstd_sb", [128, TT], F32)
        silu_sb = [P.sb("silu_sb%d" % i, [128, TT], F32) for i in range(2)]
        g_sb = P.sb("g_sb", [128, 2 * KC], F32)
        ones_sb = P.sb("ones_sb", [128, 128], BF16)
        win_sb = [P.sb("win_sb%d" % i, [128, KC, 256], BF16) for i in range(2)]
        wout_sb = [P.sb("wout_sb%d" % i, [128, FC, 128], BF16) for i in range(2)]
        if has_add:
            a_sb = P.sb("a_sb", [128, KC, TB], BF16)
            wadd_sb = [P.sb("wadd_sb%d" % i, [128, KC, 128], BF16) for i in range(2)]
        ps_g = [P.ps("ps_g%d" % i, [128, TT]) for i in range(2)]
        ps_u = [P.ps("ps_u%d" % i, [128, TT]) for i in range(2)]
        ps_o = [P.ps("ps_o%d" % i, [128, TT]) for i in range(2)]
        ps_ss = P.ps("ps_ss", [128, TT])
        P.psum_keys.update(["ps_g0", "ps_g1", "ps_u0", "ps_u1", "ps_o0", "ps_o1", "ps_ss"])

        P.dma("sp", g_sb[:, :], gains[:, :], w=["gains"])
        P.pool(lambda e: e.memset(ones_sb[:, :], 1.0 / D), w=["ones"])
        nw = [0, 0, 0]
        for b in range(NB):
            tb0 = b * TB
            for kc in range(KC):
                P.dma("sp", x_sb[:, kc, :], xT_v[:, kc, tb0:tb0 + TB], w=["x"])
            if has_add:
                for kc in range(KC):
                    P.dma("pool", a_sb[:, kc, :], aT.rearrange("(kc p) t -> p kc t", p=128)[:, kc, tb0:tb0 + TB],
                          w=["a"])
                for oc in range(KC):
                    s = nw[2] % 2
                    nw[2] += 1
                    P.dma("pool", wadd_sb[s][:, :, :], w_add[oc], w=["wadd%d" % s])
                    for st in range(NS):
                        t0 = st * TT
                        pb = ps_o[(oc * NS + st) % 2]
                        pk = "ps_o%d" % ((oc * NS + st) % 2)
                        for kc in range(KC):
                            P.pe(lambda e, kc=kc, s=s, pb=pb, t0=t0: e.matmul(
                                pb[:, :], lhsT=wadd_sb[s][:, kc, :], rhs=a_sb[:, kc, t0:t0 + TT],
                                start=(kc == 0), stop=(kc == KC - 1)), r=["wadd%d" % s, "a"], w=[pk])
                        P.dve(lambda e, oc=oc, pb=pb, t0=t0: e.tensor_tensor(
                            out=x_sb[:, oc, t0:t0 + TT], in0=pb[:, :], in1=x_sb[:, oc, t0:t0 + TT], op=ALU.add),
                            r=[pk, "x"], w=["x"])
            for st in range(NS):
                _rmsnorm(P, nc, x_sb, "x", h_sb, "h", g_sb, 0, sq_sb, ones_sb, ps_ss, rstd_sb, st * TT, TT, "")
            for j in range(FC):
                s = nw[0] % 2
                nw[0] += 1
                P.dma("pool", win_sb[s][:, :, :], w_in[j], w=["win%d" % s])
                for st in range(NS):
                    t0 = st * TT
                    q = (j * NS + st) % 2
                    for kc in range(KC):
                        P.pe(lambda e, kc=kc, s=s, q=q, t0=t0: e.matmul(
                            ps_g[q][:, :], lhsT=win_sb[s][:, kc, 0:128], rhs=h_sb[:, kc, t0:t0 + TT],
                            start=(kc == 0), stop=(kc == KC - 1)), r=["win%d" % s, "h"], w=["ps_g%d" % q])
                    for kc in range(KC):
                        P.pe(lambda e, kc=kc, s=s, q=q, t0=t0: e.matmul(
                            ps_u[q][:, :], lhsT=win_sb[s][:, kc, 128:256], rhs=h_sb[:, kc, t0:t0 + TT],
                            start=(kc == 0), stop=(kc == KC - 1)), r=["win%d" % s, "h"], w=["ps_u%d" % q])
                    P.act(lambda e, q=q: e.activation(out=silu_sb[q][:, :], in_=ps_g[q][:, :], func=AF.Silu),
                          r=["ps_g%d" % q], w=["silu%d" % q])
                    P.dve(lambda e, q=q, j=j, t0=t0: e.tensor_tensor(
                        out=act_sb[:, j, t0:t0 + TT], in0=ps_u[q][:, :], in1=silu_sb[q][:, :], op=ALU.mult),
                        r=["ps_u%d" % q, "silu%d" % q], w=["act"])
            for oc in range(KC):
                s = nw[1] % 2
                nw[1] += 1
                P.dma("pool", wout_sb[s][:, :, :], w_out[oc], w=["wout%d" % s])
                for st in range(NS):
                    t0 = st * TT
                    q = (oc * NS + st) % 2
                    for j in range(FC):
                        P.pe(lambda e, j=j, s=s, q=q, t0=t0: e.matmul(
                            ps_o[q][:, :], lhsT=wout_sb[s][:, j, :], rhs=act_sb[:, j, t0:t0 + TT],
                            start=(j == 0), stop=(j == FC - 1)), r=["wout%d" % s, "act"], w=["ps_o%d" % q])
                    P.dve(lambda e, oc=oc, q=q, t0=t0: e.scalar_tensor_tensor(
                        out=x_sb[:, oc, t0:t0 + TT], in0=ps_o[q][:, :], scalar=0.5, in1=x_sb[:, oc, t0:t0 + TT],
                        op0=ALU.mult, op1=ALU.add), r=["ps_o%d" % q, "x"], w=["x"])
            for kc in range(KC):
                P.dma("sp", yT_v[:, kc, tb0:tb0 + TB], x_sb[:, kc, :], r=["x"], w=["yT"])
            for st in range(NS):
                _rmsnorm(P, nc, x_sb, "x", ho_sb, "ho", g_sb, KC, sq_sb, ones_sb, ps_ss, rstd_sb, st * TT, TT, "", o0=0)
                for kc in range(KC):
                    P.dma("sp", hT_v[:, kc, tb0 + st * TT:tb0 + (st + 1) * TT], ho_sb[:, kc, :], r=["ho"], w=["hT"])
        pass


def _blockones(P, t, val, key):
    P.pool(lambda e: e.memset(t[:, :], 0.0), w=[key])
    P.pool(lambda e: e.memset(t[0:64, 0:64], val), w=[key])
    P.pool(lambda e: e.memset(t[64:128, 64:128], val), w=[key])


def emit_LP(nc, P, io, NT, has_v, TB=1024, TT=512):
    hT = io("hT", [D, NT], "in")
    wn = io("wn", [KC, 128, KC, 128], "in")
    gn = io("gn", [128, 1], "in")
    nT = io("nT", [D, NT], "out")
    if has_v:
        wv = io("wv", [128, KC, D], "in")
        v_tok = io("v_tok", [NT, D], "out")
    hT_v = hT.rearrange("(kc p) t -> p kc t", p=128)
    nT_v = nT.rearrange("(kc p) t -> p kc t", p=128)
    NB, NS = NT // TB, TB // TT
    with P.stage():
        h_sb = P.sb("h_sb", [128, KC, TB], BF16)
        wn_sb = [P.sb("wn_sb%d" % i, [128, KC, 128], BF16) for i in range(2)]
        sq_sb = [P.sb("sq_sb%d" % i, [128, TT], BF16) for i in range(2)]
        rstd_sb = [P.sb("rstd_sb%d" % i, [128, TT], F32) for i in range(2)]
        o_sb = [P.sb("o_sb%d" % i, [128, TT], F32) for i in range(2)]
        g_sb = P.sb("g_sb", [128, 1], F32)
        bo_sb = P.sb("bo_sb", [128, 128], BF16)
        ps_p = [P.ps("ps_p%d" % i, [128, TT]) for i in range(2)]
        ps_s = [P.ps("ps_s%d" % i, [128, TT]) for i in range(2)]
        P.psum_keys.update(["ps_p0", "ps_p1", "ps_s0", "ps_s1", "ps_v0", "ps_v1"])
        if has_v:
            wv_sb = P.sb("wv_sb", [128, KC, D], BF16)
            v_sb = [P.sb("v_sb%d" % i, [128, 512], F32) for i in range(2)]
            ps_v = [P.ps("ps_v%d" % i, [128, 512]) for i in range(2)]
            for kc in range(KC):
                P.dma("pool", wv_sb[:, kc, :], wv[:, kc, :], w=["wv"])
        P.dma("sp", g_sb[:, :], gn[:, :], w=["gn"])
        _blockones(P, bo_sb, 1.0 / 64, "bo")
        it = 0
        nv = 0
        for b in range(NB):
            tb0 = b * TB
            for kc in range(KC):
                P.dma("pool", h_sb[:, kc, :], hT_v[:, kc, tb0:tb0 + TB], w=["h"])
            for oc in range(KC):
                s = oc % 2
                P.dma("pool", wn_sb[s][:, :, :], wn[oc], w=["wn%d" % s])
                for st in range(NS):
                    t0 = st * TT
                    q = it % 2
                    it += 1
                    for kc in range(KC):
                        P.pe(lambda e, kc=kc, s=s, q=q, t0=t0: e.matmul(
                            ps_p[q][:, :], lhsT=wn_sb[s][:, kc, :], rhs=h_sb[:, kc, t0:t0 + TT],
                            start=(kc == 0), stop=(kc == KC - 1)), r=["wn%d" % s, "h"], w=["ps_p%d" % q])
                    P.act(lambda e, q=q: e.activation(out=sq_sb[q][:, :], in_=ps_p[q][:, :], func=AF.Square),
                          r=["ps_p%d" % q], w=["sq%d" % q])
                    P.pe(lambda e, q=q: e.matmul(ps_s[q][:, :], lhsT=bo_sb[:, :], rhs=sq_sb[q][:, :],
                                                 start=True, stop=True), r=["sq%d" % q, "bo"], w=["ps_s%d" % q])
                    P.act(lambda e, q=q: e.activation(out=rstd_sb[q][:, :], in_=ps_s[q][:, :], func=AF.Sqrt,
                                                      bias=EPS, scale=1.0), r=["ps_s%d" % q], w=["rstd%d" % q])
                    P.dve(lambda e, q=q: e.reciprocal(out=rstd_sb[q][:, :], in_=rstd_sb[q][:, :]),
                          r=["rstd%d" % q], w=["rstd%d" % q])
                    P.dve(lambda e, q=q: e.scalar_tensor_tensor(
                        out=o_sb[q][:, :], in0=ps_p[q][:, :], scalar=g_sb[:, 0:1], in1=rstd_sb[q][:, :],
                        op0=ALU.mult, op1=ALU.mult), r=["ps_p%d" % q, "rstd%d" % q, "gn"], w=["o%d" % q])
                    P.dma("sp", nT_v[:, oc, tb0 + t0:tb0 + t0 + TT], o_sb[q][:, :], r=["o%d" % q], w=["nT"])
            if has_v:
                for tbk in range(TB // 128):
                    for half in range(2):
                        q = nv % 2
                        nv += 1
                        for kc in range(KC):
                            P.pe(lambda e, kc=kc, q=q, tbk=tbk, half=half: e.matmul(
                                ps_v[q][:, :], lhsT=h_sb[:, kc, tbk * 128:(tbk + 1) * 128],
                                rhs=wv_sb[:, kc, half * 512:(half + 1) * 512],
                                start=(kc == 0), stop=(kc == KC - 1)), r=["wv", "h"], w=["ps_v%d" % q])
                        P.act(lambda e, q=q: e.copy(out=v_sb[q][:, :], in_=ps_v[q][:, :]),
                              r=["ps_v%d" % q], w=["v%d" % q])
                        P.dma("sp", v_tok[tb0 + tbk * 128:tb0 + (tbk + 1) * 128, half * 512:(half + 1) * 512],
                              v_sb[q][:, :], r=["v%d" % q], w=["v_tok"])
        pass


LAM_INIT1 = 0.8 - 0.6 * float(np.exp(-0.3))
SUBLN_EPS = 1e-5
NDD = 67


def emit_ATT(nc, P, io, T, HL=4):
    qT = io("qT", [HL * 128, T], "in")
    kT = io("kT", [HL * 128, T], "in")
    v_tok = io("v_tok", [T, HL * 128], "in")
    qaug = io("qaug", [2, T], "in")
    kaug = io("kaug", [HL, 2, T], "in")
    btab = io("btab", [128, HL * NDD], "in")
    tri = io("tri", [128, 128], "in")
    lam = io("lam", [1, 256], "in")
    subg = io("subg", [128, 128], "in")
    o_tok = io("o_tok", [T, HL * 128], "out")
    NQB = T // 512
    NKB = T // 128
    with P.stage():
        q_sb = P.sb("q_sb", [66, 2, T], BF16)
        k_sb = P.sb("k_sb", [66, 2, T], BF16)
        v_sb = P.sb("v_sb", [128, NKB, 130], BF16)
        bt_sb = P.sb("bt_sb", [128, HL * NDD], F32)
        tri_sb = P.sb("tri_sb", [128, 128], BF16)
        z_sb = P.sb("z_sb", [128, 512], BF16)
        pt_sb = [P.sb("pt_sb%d" % i, [128, 512], BF16) for i in range(4)]
        lam_sb = P.sb("lam_sb", [1, 256], F32)
        lt_sb = P.sb("lt_sb", [1, 128], F32)
        ls_sb = P.sb("ls_sb", [1, 4], F32)
        one_row = P.sb("one_row", [1, 128], F32)
        nl_sb = P.sb("nl_sb", [128, 1], F32)
        sg_sb = P.sb("sg_sb", [128, 128], F32)
        rec_sb = [P.sb("rec_sb%d" % i, [128, 4], F32) for i in range(2)]
        o0_sb = [P.sb("o0_sb%d" % i, [128, 128], F32) for i in range(2)]
        od_sb = [P.sb("od_sb%d" % i, [128, 128], F32) for i in range(2)]
        junk_sb = [P.sb("junk_sb%d" % i, [128, 128], F32) for i in range(2)]
        out_sb = [P.sb("out_sb%d" % i, [128, 128], F32) for i in range(2)]
        ps_S = [P.ps("ps_S%d" % i, [128, 512]) for i in range(4)]
        ps_O = [P.ps("ps_O%d" % i, [128, 512]) for i in range(3)]
        ps_m = P.ps("ps_m", [128, 512])
        P.psum_keys.update(["S0", "S1", "S2", "S3", "O0", "O1", "O2", "ps_m"])

        P.dma("sp", bt_sb[:, :], btab[:, :], w=["bt"])
        P.dma("pool", tri_sb[:, :], tri[:, :], w=["tri"])
        P.dma("sp", lam_sb[:, :], lam[:, :], w=["lam"])
        P.dma("sp", sg_sb[:, :], subg[:, :], w=["sg"])
        P.pool(lambda e: e.memset(z_sb[:, :], 0.0), w=["z"])
        P.pool(lambda e: e.memset(one_row[:, :], 1.0), w=["one_row"])
        P.pool(lambda e: e.memset(v_sb[:, :, 128:130], 1.0), w=["vones"])
        P.dve(lambda e: e.tensor_tensor(out=lt_sb[:, 0:64], in0=lam_sb[:, 0:64], in1=lam_sb[:, 64:128], op=ALU.mult),
              r=["lam"], w=["lt"])
        P.dve(lambda e: e.tensor_tensor(out=lt_sb[:, 64:128], in0=lam_sb[:, 128:192], in1=lam_sb[:, 192:256],
                                        op=ALU.mult), r=["lam"], w=["lt"])
        P.dve(lambda e: e.tensor_reduce(out=ls_sb[:, 0:1], in_=lt_sb[:, 0:64], axis=AX.X, op=ALU.add),
              r=["lt"], w=["ls"])
        P.dve(lambda e: e.tensor_reduce(out=ls_sb[:, 1:2], in_=lt_sb[:, 64:128], axis=AX.X, op=ALU.add),
              r=["lt"], w=["ls"])
        P.act(lambda e: e.activation(out=ls_sb[:, 0:2], in_=ls_sb[:, 0:2], func=AF.Exp), r=["ls"], w=["ls"])
        P.dve(lambda e: e.scalar_tensor_tensor(out=ls_sb[:, 2:3], in0=ls_sb[:, 1:2], scalar=-LAM_INIT1,
                                               in1=ls_sb[:, 0:1], op0=ALU.add, op1=ALU.subtract),
              r=["ls"], w=["ls"])
        P.pe(lambda e: e.matmul(ps_m[:, 0:1], lhsT=one_row[:, :], rhs=ls_sb[:, 2:3], start=True, stop=True),
             r=["ls", "one_row"], w=["ps_m"])
        P.dve(lambda e: e.tensor_copy(out=nl_sb[:, :], in_=ps_m[:, 0:1]), r=["ps_m"], w=["nl"])
        P.dve(lambda e: e.tensor_scalar(out=sg_sb[:, :], in0=sg_sb[:, :], scalar1=1.0 - LAM_INIT1, scalar2=None,
                                        op0=ALU.mult), r=["sg"], w=["sg"])

        def acc(a):
            return ps_O[a // 3], (a % 3) * 132

        sl = 0
        fin = 0
        for hl in range(HL):
            for c in range(2):
                r0 = hl * 128 + c * 64
                P.dma("pool", q_sb[0:64, c, :], qT[r0:r0 + 64, :], w=["q"])
                P.dma("pool", k_sb[0:64, c, :], kT[r0:r0 + 64, :], w=["k"])
                P.dma("pool", q_sb[64:66, c, :], qaug[:, :], w=["q"])
                P.dma("pool", k_sb[64:66, c, :], kaug[hl], w=["k"])
            P.dma("pool", v_sb[:, :, 0:128],
                  v_tok.rearrange("(blk p) v -> p blk v", p=128)[:, :, hl * 128:(hl + 1) * 128], w=["v"])
            for qb in range(NQB):
                for bk in range(3):
                    P.pe(lambda e, bk=bk: e.matmul(ps_O[bk][:, :], lhsT=z_sb[:, 0:128], rhs=z_sb[:, :],
                                                   start=True, stop=False), r=["z"], w=["O%d" % bk])
                nkb = 4 * qb + 4
                for kb in range(nkb):
                    rr = kb - 4 * qb
                    j0 = max(0, 128 * rr)
                    for c in range(2):
                        s = sl % 4
                        sl += 1
                        P.pe(lambda e, s=s, c=c, kb=kb, qb=qb, j0=j0: e.matmul(
                            ps_S[s][:, j0:512], lhsT=k_sb[0:66, c, kb * 128:(kb + 1) * 128],
                            rhs=q_sb[0:66, c, qb * 512 + j0:(qb + 1) * 512], start=True, stop=True),
                            r=["q", "k"], w=["S%d" % s])
                        col = hl * NDD + rr + 63
                        P.act(lambda e, s=s, j0=j0, col=col: e.activation(
                            out=pt_sb[s][:, j0:512], in_=ps_S[s][:, j0:512], func=AF.Exp,
                            bias=bt_sb[:, col:col + 1], scale=0.125), r=["S%d" % s, "bt"], w=["pt%d" % s])
                        if rr >= 0:
                            P.dve(lambda e, s=s, j0=j0: e.tensor_tensor(
                                out=pt_sb[s][:, j0:j0 + 128], in0=pt_sb[s][:, j0:j0 + 128], in1=tri_sb[:, :],
                                op=ALU.mult), r=["pt%d" % s, "tri"], w=["pt%d" % s])
                        for jj in range(j0 // 128, 4):
                            a = jj * 2 + c
                            pb, off = acc(a)
                            P.pe(lambda e, s=s, jj=jj, kb=kb, pb=pb, off=off, last=(kb == 4 * qb + jj): e.matmul(
                                pb[:, off:off + 129], lhsT=pt_sb[s][:, jj * 128:(jj + 1) * 128],
                                rhs=v_sb[:, kb, 0:129], start=False, stop=last),
                                r=["pt%d" % s, "v", "vones"], w=["O%d" % (a // 3)])
                for jj in range(4):
                    f = fin % 2
                    fin += 1
                    pb0, off0 = acc(jj * 2)
                    pb1, off1 = acc(jj * 2 + 1)
                    k0, k1 = "O%d" % ((jj * 2) // 3), "O%d" % ((jj * 2 + 1) // 3)
                    P.dve(lambda e, f=f, pb0=pb0, off0=off0: e.reciprocal(
                        out=rec_sb[f][:, 0:1], in_=pb0[:, off0 + 128:off0 + 129]), r=[k0], w=["rec%d" % f])
                    P.dve(lambda e, f=f, pb1=pb1, off1=off1: e.reciprocal(
                        out=rec_sb[f][:, 1:2], in_=pb1[:, off1 + 128:off1 + 129]), r=[k1], w=["rec%d" % f])
                    P.dve(lambda e, f=f: e.tensor_tensor(out=rec_sb[f][:, 2:3], in0=rec_sb[f][:, 1:2],
                                                         in1=nl_sb[:, 0:1], op=ALU.mult),
                          r=["rec%d" % f, "nl"], w=["rec%d" % f])
                    P.act(lambda e, f=f, pb0=pb0, off0=off0: e.activation(
                        out=o0_sb[f][:, :], in_=pb0[:, off0:off0 + 128], func=AF.Copy, scale=rec_sb[f][:, 0:1]),
                        r=[k0, "rec%d" % f], w=["o0%d" % f])
                    P.dve(lambda e, f=f, pb1=pb1, off1=off1: e.scalar_tensor_tensor(
                        out=od_sb[f][:, :], in0=pb1[:, off1:off1 + 128], scalar=rec_sb[f][:, 2:3],
                        in1=o0_sb[f][:, :], op0=ALU.mult, op1=ALU.add),
                        r=[k1, "rec%d" % f, "o0%d" % f], w=["od%d" % f])
                    P.act(lambda e, f=f: e.activation(out=junk_sb[f][:, :], in_=od_sb[f][:, :], func=AF.Square,
                                                      accum_out=rec_sb[f][:, 3:4]),
                          r=["od%d" % f], w=["junk%d" % f, "ss%d" % f])
                    P.act(lambda e, f=f: e.activation(out=rec_sb[f][:, 3:4], in_=rec_sb[f][:, 3:4], func=AF.Sqrt,
                                                      bias=SUBLN_EPS, scale=1.0 / 128),
                          r=["ss%d" % f], w=["ss%d" % f])
                    P.dve(lambda e, f=f: e.reciprocal(out=rec_sb[f][:, 3:4], in_=rec_sb[f][:, 3:4]),
                          r=["ss%d" % f], w=["ss%d" % f])
                    P.dve(lambda e, f=f: e.scalar_tensor_tensor(
                        out=out_sb[f][:, :], in0=od_sb[f][:, :], scalar=rec_sb[f][:, 3:4], in1=sg_sb[:, :],
                        op0=ALU.mult, op1=ALU.mult), r=["od%d" % f, "ss%d" % f, "sg"], w=["out%d" % f])
                    P.dma("sp", o_tok[qb * 512 + jj * 128:qb * 512 + (jj + 1) * 128, hl * 128:(hl + 1) * 128],
                          out_sb[f][:, :], r=["out%d" % f], w=["o_tok"])
        pass


def att_consts(T, heads):
    t = np.arange(T)
    qaug = np.stack([(t % 512) // 16, t % 16]).astype(np.float32)
    kaug = np.zeros((len(heads), 2, T), np.float32)
    btab = np.zeros((128, len(heads) * NDD), np.float32)
    p = np.arange(128, dtype=np.float64)
    for i, h in enumerate(heads):
        slope = 2.0 ** (-(h + 1))
        kaug[i, 0, :] = -16.0 * slope / 0.125
        kaug[i, 1, :] = -slope / 0.125
        for dd in range(NDD):
            btab[:, i * NDD + dd] = slope * (p + 128.0 * (dd - 63))
    tri = (np.arange(128)[:, None] <= np.arange(128)[None, :]).astype(np.float32)
    return qaug, kaug, btab, tri


C0 = float(np.exp(-0.5))
RW_LN_EPS = 64e-5
CL = 128


def emit_RW(nc, P, io, T, TT=256, LEVEL=9):
    dt_in = lambda n, s: io(n, s, "in")
    hTp = dt_in("hTp", [D, T + 1])
    wr_d, wk_d, wv_d = dt_in("wr", [D, 512]), dt_in("wk", [D, 512]), dt_in("wv", [D, 512])
    l1_d = dt_in("l1", [D, 288])
    w2_d, a2_d, g2_d = dt_in("w2c", [64, 512]), dt_in("a2c", [64, 512]), dt_in("g2c", [160, 512])
    mu_d = dt_in("mu6", [128, 48])
    cv_d = dt_in("cv", [128, 20])
    lng_d, lnb_d = dt_in("lng", [128, 512]), dt_in("lnb", [128, 512])
    ident_d, mask2_d, maskL_d = dt_in("ident", [128, 128]), dt_in("mask2", [128, 256]), dt_in("maskL", [128, 128])
    rmask_d, ind_d = dt_in("rmask", [128, TT]), dt_in("ind", [128, 2])
    yg = io("yg_tok", [T, 512], "out")
    hTp_v = hTp.rearrange("(kc p) t -> p kc t", p=128)
    NTI = T // TT
    NCL = TT // CL
    with P.stage():
        sb = P.sb
        W1 = {n: sb("W1" + n, [128, KC, 512], BF16) for n in "rkv"}
        W2 = {n: sb("W2" + n, [128, KC, 512], BF16) for n in "rkv"}
        L1a = sb("L1a", [128, KC, 288], BF16)
        L1b = sb("L1b", [128, KC, 288], BF16)
        stg = [sb("stg%d" % i, [128, 512], F32) for i in range(2)]
        stg1 = sb("stgL", [128, KC, 288], F32)
        w2_sb, a2_sb = sb("w2_sb", [64, 512], BF16), sb("a2_sb", [64, 512], BF16)
        g2a_sb, g2b_sb = sb("g2a_sb", [128, 512], BF16), sb("g2b_sb", [32, 512], BF16)
        mu_sb, omu_sb, cv_sb = sb("mu_sb", [128, 48], F32), sb("omu_sb", [128, 48], F32), sb("cv_sb", [128, 24], F32)
        lng_sb, lnb_sb = sb("lng_sb", [128, 512], F32), sb("lnb_sb", [128, 512], F32)
        ident_bf, identf = sb("ident_bf", [128, 128], BF16), sb("identf", [128, 128], F32)
        mask2_sb, maskL_sb = sb("mask2_sb", [128, 256], F32), sb("maskL_sb", [128, 128], F32)
        rmask_sb, ind_sb, bo_sb = sb("rmask_sb", [128, TT], F32), sb("ind_sb", [128, 2], BF16), sb("bo_sb", [128, 128], BF16)
        hp = [sb("hp%d" % i, [128, KC, TT], BF16) for i in range(2)]
        hq = [sb("hq%d" % i, [128, KC, TT], BF16) for i in range(2)]
        r32, k32 = sb("r32", [128, 4, TT], F32), sb("k32", [128, 4, TT], F32)
        vtok, v32 = sb("vtok", [128, NCL, 512], BF16), sb("v32", [128, NCL, 512], F32)
        gtok = sb("gtok", [128, NCL, 512], F32)
        lw_sb, la_sb = sb("lw_sb", [64, TT], BF16), sb("la_sb", [64, TT], BF16)
        lga_sb, lgb_sb = sb("lga_sb", [128, TT], BF16), sb("lgb_sb", [32, TT], BF16)
        sc = {n: sb("sc_" + n, [128, TT], F32) for n in
              ("sg", "a", "kk", "rn", "kkn", "tmp", "kmod", "bvec", "cs", "tmp2", "Em", "Epv")}
        kk2_sb = sb("kk2_sb", [128, TT], BF16)
        Ep = sb("Ep", [128, 4, TT], F32)
        AR = sb("AR", [128, 4, NCL, 256], BF16)
        bt, kt, rk = sb("bt", [128, 4, TT], BF16), sb("kt", [128, 4, TT], BF16), sb("rk", [128, 4, TT], BF16)
        NTb_sb, NTk_sb = sb("NTb_sb", [128, 4, 512], BF16), sb("NTk_sb", [128, 4, 512], BF16)
        at2, rt2 = sb("at2", [128, 4, 2, TT], BF16), sb("rt2", [128, 4, 2, TT], BF16)
        bt2, kt2 = sb("bt2", [128, 4, 2, TT], BF16), sb("kt2", [128, 4, 2, TT], BF16)
        Xs = [sb("Xs%d" % i, [128, 256], BF16) for i in range(2)]
        XTs = [sb("XTs%d" % i, [128, 256], BF16) for i in range(2)]
        Ps = [sb("Ps%d" % i, [128, 256], BF16) for i in range(2)]
        Pfin = sb("Pfin", [128, 4, 256], BF16)
        btok, ktok = sb("btok", [128, 4, 128], BF16), sb("ktok", [128, 4, 128], BF16)
        Zs, Us = sb("Zs", [128, 4, 128], BF16), sb("Us", [128, 4, 128], BF16)
        ytok = sb("ytok", [128, NCL, 512], F32)
        H32, Hbf, Htmp = sb("H32", [128, 4, 128], F32), sb("Hbf", [128, 4, 128], BF16), sb("Htmp", [128, 4, 128], F32)
        ysq, st_sb = sb("ysq", [128, 512], F32), sb("st_sb", [128, 40], F32)
        out_sb = [sb("out_sb%d" % i, [128, 512], F32) for i in range(2)]
        bk = [P.ps("bk%d" % i, [128, 512]) for i in range(5)] + [None] + [P.ps("bk%d" % i, [128, 512]) for i in (6, 7)]
        bk5 = P.ps("bk5", [128, 1024], BF16)
        BK = lambda i: "bk%d" % i
        P.psum_keys.update([BK(i) for i in range(8)])

        for dst, src, q in ((mu_sb, mu_d, "sp"), (lng_sb, lng_d, "sp"), (lnb_sb, lnb_d, "sp"),
                            (identf, ident_d, "sp"), (mask2_sb, mask2_d, "sp"), (maskL_sb, maskL_d, "sp"),
                            (rmask_sb, rmask_d, "sp"), (ident_bf, ident_d, "pool"), (ind_sb, ind_d, "pool"),
                            (w2_sb, w2_d, "pool"), (a2_sb, a2_d, "pool")):
            P.dma(q, dst[:, :], src[:, :], w=["const"])
        P.dma("sp", cv_sb[:, 0:20], cv_d[:, :], w=["const"])
        P.dma("pool", g2a_sb[:, :], g2_d[0:128, :], w=["const"])
        P.dma("pool", g2b_sb[:, :], g2_d[128:160, :], w=["const"])
        _blockones(P, bo_sb, 1.0, "bo")
        P.dve(lambda e: e.tensor_scalar(out=omu_sb[:, :], in0=mu_sb[:, :], scalar1=-1.0, scalar2=1.0,
                                        op0=ALU.mult, op1=ALU.add), r=["const"], w=["omu"])
        P.dve(lambda e: e.tensor_scalar(out=cv_sb[:, 20:24], in0=cv_sb[:, 12:16], scalar1=-1.0, scalar2=1.0,
                                        op0=ALU.mult, op1=ALU.add), r=["const"], w=["cv2"])
        P.pool(lambda e: e.memset(H32[:, :, :], 0.0), w=["H32"])
        for tz, kz in ((at2, "at2"), (rt2, "rt2"), (bt2, "bt2"), (kt2, "kt2")):
            P.pool(lambda e, tz=tz: e.memset(tz[:, :, :, :], 0.0), w=[kz])
        P.pool(lambda e: e.memset(Hbf[:, :, :], 0.0), w=["Hbf"])
        ns = 0
        for wi, (n, src) in enumerate((("r", wr_d), ("k", wk_d), ("v", wv_d))):
            for kc in range(KC):
                s = ns % 2
                ns += 1
                P.dma("sp", stg[s][:, :], src[kc * 128:(kc + 1) * 128, :], w=["stg%d" % s])
                P.dve(lambda e, s=s, n=n, kc=kc, wi=wi: e.tensor_scalar(
                    out=W1[n][:, kc, :], in0=stg[s][:, :], scalar1=omu_sb[:, wi * 8 + kc:wi * 8 + kc + 1],
                    scalar2=None, op0=ALU.mult), r=["stg%d" % s, "omu"], w=["W"])
                P.dve(lambda e, s=s, n=n, kc=kc, wi=wi: e.tensor_scalar(
                    out=W2[n][:, kc, :], in0=stg[s][:, :], scalar1=mu_sb[:, wi * 8 + kc:wi * 8 + kc + 1],
                    scalar2=None, op0=ALU.mult), r=["stg%d" % s, "const"], w=["W"])
        P.dma("sp", stg1[:, :, :], l1_d.rearrange("(kc p) l -> p kc l", p=128), w=["stg1"])
        for li, (c0, c1) in enumerate(((0, 64), (64, 128), (128, 288))):
            wi = 3 + li
            for kc in range(KC):
                P.dve(lambda e, kc=kc, wi=wi, c0=c0, c1=c1: e.tensor_scalar(
                    out=L1a[:, kc, c0:c1], in0=stg1[:, kc, c0:c1], scalar1=omu_sb[:, wi * 8 + kc:wi * 8 + kc + 1],
                    scalar2=None, op0=ALU.mult), r=["stg1", "omu"], w=["W"])
                P.dve(lambda e, kc=kc, wi=wi, c0=c0, c1=c1: e.tensor_scalar(
                    out=L1b[:, kc, c0:c1], in0=stg1[:, kc, c0:c1], scalar1=mu_sb[:, wi * 8 + kc:wi * 8 + kc + 1],
                    scalar2=None, op0=ALU.mult), r=["stg1", "const"], w=["W"])

        nb = [0]

        def nbk():
            nb[0] += 1
            return nb[0] % 2

        def proj_fm(pb, key, wa, wb, c0, c1, hpt, hk, n, hqt=None):
            m = c1 - c0
            hpt, hqt = hpt
            for kc in range(KC):
                P.pe(lambda e, kc=kc: e.matmul(pb[0:m, 0:n], lhsT=wa[:, kc, c0:c1], rhs=hpt[:, kc, 0:n],
                                               start=(kc == 0), stop=False), r=["W", hk], w=[key])
            for kc in range(KC):
                P.pe(lambda e, kc=kc: e.matmul(pb[0:m, 0:n], lhsT=wb[:, kc, c0:c1], rhs=hqt[:, kc, 0:n],
                                               start=False, stop=(kc == KC - 1)), r=["W", hk], w=[key])

        fo = 0
        for ti in range(NTI if LEVEL >= 1 else 0):
            t0 = ti * TT
            hs = ti % 2
            hk = "hp%d" % hs
            hp_, hq_ = hp[hs], hq[hs]
            hpt = (hp_, hq_)
            P.dma("pool", hp_[:, :, :], hTp_v[:, :, t0 + 1:t0 + TT + 1], w=[hk])
            P.dma("pool", hq_[:, :, :], hTp_v[:, :, t0:t0 + TT], w=[hk])
            for pc in range(4):
                for n, dst in (("r", r32), ("k", k32)):
                    b = nbk()
                    proj_fm(bk[b], BK(b), W1[n], W2[n], pc * 128, (pc + 1) * 128, hpt, hk, TT)
                    P.act(lambda e, b=b, dst=dst, pc=pc: e.copy(out=dst[:, pc, :], in_=bk[b][:, 0:TT]),
                          r=[BK(b)], w=[n + "32"])
            for cl in range(NCL):
                b = nbk()
                for kc in range(KC):
                    P.pe(lambda e, kc=kc, b=b, cl=cl, hpt=hp_: e.matmul(
                        bk[b][:, :], lhsT=hpt[:, kc, cl * 128:(cl + 1) * 128], rhs=W1["v"][:, kc, :],
                        start=(kc == 0), stop=False), r=["W", hk], w=[BK(b)])
                for kc in range(KC):
                    P.pe(lambda e, kc=kc, b=b, cl=cl, hpt=hq_: e.matmul(
                        bk[b][:, :], lhsT=hpt[:, kc, cl * 128:(cl + 1) * 128], rhs=W2["v"][:, kc, :],
                        start=False, stop=(kc == KC - 1)), r=["W", hk], w=[BK(b)])
                P.act(lambda e, b=b, cl=cl: e.copy(out=vtok[:, cl, :], in_=bk[b][:, :]), r=[BK(b)], w=["vtok"])
                P.act(lambda e, b=b, cl=cl: e.copy(out=v32[:, cl, :], in_=bk[b][:, :]), r=[BK(b)], w=["v32"])
            b = nbk()
            proj_fm(bk[b], BK(b), L1a, L1b, 0, 64, hpt, hk, TT)
            P.act(lambda e, b=b: e.activation(out=lw_sb[:, :], in_=bk[b][0:64, 0:TT], func=AF.Tanh),
                  r=[BK(b)], w=["lw"])
            b = nbk()
            proj_fm(bk[b], BK(b), L1a, L1b, 64, 128, hpt, hk, TT)
            P.act(lambda e, b=b: e.copy(out=la_sb[:, :], in_=bk[b][0:64, 0:TT]), r=[BK(b)], w=["la"])
            b = nbk()
            proj_fm(bk[b], BK(b), L1a, L1b, 128, 256, hpt, hk, TT)
            P.act(lambda e, b=b: e.activation(out=lga_sb[:, :], in_=bk[b][:, 0:TT], func=AF.Sigmoid),
                  r=[BK(b)], w=["lg"])
            b = nbk()
            proj_fm(bk[b], BK(b), L1a, L1b, 256, 288, hpt, hk, TT)
            P.act(lambda e, b=b: e.activation(out=lgb_sb[:, :], in_=bk[b][0:32, 0:TT], func=AF.Sigmoid),
                  r=[BK(b)], w=["lg"])
            for cl in range(NCL):
                b = nbk()
                P.pe(lambda e, b=b, cl=cl: e.matmul(bk[b][:, :], lhsT=lga_sb[:, cl * 128:(cl + 1) * 128],
                                                   rhs=g2a_sb[:, :], start=True, stop=False),
                     r=["lg", "const"], w=[BK(b)])
                P.pe(lambda e, b=b, cl=cl: e.matmul(bk[b][:, :], lhsT=lgb_sb[:, cl * 128:(cl + 1) * 128],
                                                   rhs=g2b_sb[:, :], start=False, stop=True),
                     r=["lg", "const"], w=[BK(b)])
                P.act(lambda e, b=b, cl=cl: e.copy(out=gtok[:, cl, :], in_=bk[b][:, :]), r=[BK(b)], w=["gtok"])
            for pc in range(4 if LEVEL >= 2 else 0):
                cvc = lambda w, pc=pc: cv_sb[:, w * 4 + pc:w * 4 + pc + 1]
                b = nbk()
                P.pe(lambda e, b=b, pc=pc: e.matmul(bk[b][:, 0:TT], lhsT=w2_sb[:, pc * 128:(pc + 1) * 128],
                                                   rhs=lw_sb[:, :], start=True, stop=True),
                     r=["lw", "const"], w=[BK(b)])
                P.act(lambda e, b=b, pc=pc: e.activation(out=sc["sg"][:, :], in_=bk[b][:, 0:TT], func=AF.Sigmoid,
                                                         bias=cv_sb[:, pc:pc + 1], scale=1.0),
                      r=[BK(b), "const"], w=["sg"])
                b = nbk()
                P.pe(lambda e, b=b, pc=pc: e.matmul(bk[b][:, 0:TT], lhsT=a2_sb[:, pc * 128:(pc + 1) * 128],
                                                   rhs=la_sb[:, :], start=True, stop=True),
                     r=["la", "const"], w=[BK(b)])
                P.act(lambda e, b=b, pc=pc: e.activation(out=sc["a"][:, :], in_=bk[b][:, 0:TT], func=AF.Sigmoid,
                                                         bias=cv_sb[:, 4 + pc:5 + pc], scale=1.0),
                      r=[BK(b), "const"], w=["a"])
                P.dve(lambda e, pc=pc: e.tensor_scalar(out=sc["kk"][:, :], in0=k32[:, pc, :],
                                                       scalar1=cv_sb[:, 8 + pc:9 + pc], scalar2=None, op0=ALU.mult),
                      r=["k32", "const"], w=["kk"])
                P.act(lambda e: e.activation(out=kk2_sb[:, :], in_=sc["kk"][:, :], func=AF.Square),
                      r=["kk"], w=["kk2"])
                b = nbk()
                P.pe(lambda e, b=b: e.matmul(bk[b][:, 0:TT], lhsT=bo_sb[:, :], rhs=kk2_sb[:, :], start=True, stop=True),
                     r=["kk2", "bo"], w=[BK(b)])
                P.act(lambda e, b=b: e.activation(out=sc["rn"][:, :], in_=bk[b][:, 0:TT], func=AF.Sqrt,
                                                  bias=1e-24, scale=1.0), r=[BK(b)], w=["rn"])
                P.dve(lambda e: e.reciprocal(out=sc["rn"][:, :], in_=sc["rn"][:, :]), r=["rn"], w=["rn"])
                P.dve(lambda e: e.tensor_tensor(out=sc["kkn"][:, :], in0=sc["kk"][:, :], in1=sc["rn"][:, :],
                                                op=ALU.mult), r=["kk", "rn"], w=["kkn"])
                P.dve(lambda e, pc=pc: e.tensor_scalar(out=sc["tmp"][:, :], in0=sc["a"][:, :],
                                                       scalar1=cv_sb[:, 12 + pc:13 + pc],
                                                       scalar2=cv_sb[:, 20 + pc:21 + pc], op0=ALU.mult, op1=ALU.add),
                      r=["a", "const", "cv2"], w=["tmp"])
                P.dve(lambda e, pc=pc: e.tensor_tensor(out=sc["kmod"][:, :], in0=k32[:, pc, :], in1=sc["tmp"][:, :],
                                                       op=ALU.mult), r=["k32", "tmp"], w=["kmod"])
                P.dve(lambda e: e.tensor_tensor(out=sc["bvec"][:, :], in0=sc["kkn"][:, :], in1=sc["a"][:, :],
                                                op=ALU.mult), r=["kkn", "a"], w=["bvec"])
                P.dve(lambda e: e.tensor_tensor_scan(out=sc["cs"][:, :], data0=rmask_sb[:, :], data1=sc["sg"][:, :],
                                                     initial=0.0, op0=ALU.mult, op1=ALU.add),
                      r=["sg", "const"], w=["cs"])
                P.dve(lambda e: e.tensor_tensor(out=sc["tmp2"][:, :], in0=sc["cs"][:, :], in1=sc["sg"][:, :],
                                                op=ALU.subtract), r=["cs", "sg"], w=["tmp2"])
                P.act(lambda e, pc=pc: e.activation(out=Ep[:, pc, :], in_=sc["cs"][:, :], func=AF.Exp, scale=-C0),
                      r=["cs"], w=["Ep"])
                P.act(lambda e: e.activation(out=sc["Em"][:, :], in_=sc["cs"][:, :], func=AF.Exp, scale=C0),
                      r=["cs"], w=["Em"])
                P.act(lambda e: e.activation(out=sc["Epv"][:, :], in_=sc["tmp2"][:, :], func=AF.Exp, scale=-C0),
                      r=["tmp2"], w=["Epv"])
                for cl in range(NCL):
                    cs_ = slice(cl * 128, (cl + 1) * 128)
                    P.dve(lambda e, pc=pc, cl=cl, cs_=cs_: e.scalar_tensor_tensor(
                        out=AR[:, pc, cl, 0:128], in0=sc["kkn"][:, cs_], scalar=-1.0, in1=sc["Epv"][:, cs_],
                        op0=ALU.mult, op1=ALU.mult), r=["kkn", "Epv"], w=["AR"])
                    P.dve(lambda e, pc=pc, cl=cl, cs_=cs_: e.tensor_tensor(
                        out=AR[:, pc, cl, 128:256], in0=r32[:, pc, cs_], in1=Ep[:, pc, cs_], op=ALU.mult),
                        r=["r32", "Ep"], w=["AR"])
                P.dve(lambda e, pc=pc: e.tensor_tensor(out=kt[:, pc, :], in0=sc["kmod"][:, :], in1=sc["Em"][:, :],
                                                       op=ALU.mult), r=["kmod", "Em"], w=["kt"])
                P.dve(lambda e, pc=pc: e.tensor_tensor(out=bt[:, pc, :], in0=sc["bvec"][:, :], in1=sc["Em"][:, :],
                                                       op=ALU.mult), r=["bvec", "Em"], w=["bt"])
                P.dve(lambda e, pc=pc: e.scalar_tensor_tensor(
                    out=rk[:, pc, :], in0=r32[:, pc, :], scalar=cv_sb[:, 16 + pc:17 + pc], in1=sc["kmod"][:, :],
                    op0=ALU.mult, op1=ALU.mult), r=["r32", "kmod", "const"], w=["rk"])
                for hd in range(2):
                    rows = slice(64 * hd, 64 * hd + 64)
                    for cl in range(NCL):
                        cs_ = slice(cl * 128, (cl + 1) * 128)
                        P.pool(lambda e, pc=pc, hd=hd, rows=rows, cl=cl, cs_=cs_: e.tensor_copy(
                            out=at2[rows, pc, hd, cs_], in_=AR[rows, pc, cl, 0:128]), r=["AR"], w=["at2"])
                        P.pool(lambda e, pc=pc, hd=hd, rows=rows, cl=cl, cs_=cs_: e.tensor_copy(
                            out=rt2[rows, pc, hd, cs_], in_=AR[rows, pc, cl, 128:256]), r=["AR"], w=["rt2"])
                    P.pool(lambda e, pc=pc, hd=hd, rows=rows: e.tensor_copy(
                        out=bt2[rows, pc, hd, :], in_=bt[rows, pc, :]), r=["bt"], w=["bt2"])
                    P.pool(lambda e, pc=pc, hd=hd, rows=rows: e.tensor_copy(
                        out=kt2[rows, pc, hd, :], in_=kt[rows, pc, :]), r=["kt"], w=["kt2"])
            for cl in range(NCL if LEVEL >= 3 else 0):
                cs_ = slice(cl * 128, (cl + 1) * 128)
                for pc in range(4):
                    for hd in range(2):
                        rows = slice(64 * hd, 64 * hd + 64)
                        P.pe(lambda e, pc=pc, cl=cl, hd=hd, rows=rows, cs_=cs_: e.matmul(
                            bk[2][:, hd * 256:(hd + 1) * 256], lhsT=bt2[:, pc, hd, cs_], rhs=AR[:, pc, cl, :],
                            start=True, stop=True), r=["bt2", "AR"], w=[BK(2)])
                        P.pe(lambda e, pc=pc, cl=cl, hd=hd, rows=rows, cs_=cs_: e.matmul(
                            bk[3][:, hd * 256:(hd + 1) * 256], lhsT=kt2[:, pc, hd, cs_], rhs=AR[:, pc, cl, :],
                            start=True, stop=True), r=["kt2", "AR"], w=[BK(3)])
                        P.pe(lambda e, pc=pc, cl=cl, hd=hd, rows=rows, cs_=cs_: e.matmul(
                            bk[4][:, hd * 128:(hd + 1) * 128], lhsT=at2[:, pc, hd, cs_], rhs=bt[:, pc, cs_],
                            start=True, stop=True), r=["bt", "at2"], w=[BK(4)])
                    for hd in range(2):
                        P.dve(lambda e, pc=pc, hd=hd: e.tensor_tensor(
                            out=NTb_sb[:, pc, hd * 256:(hd + 1) * 256], in0=bk[2][:, hd * 256:(hd + 1) * 256],
                            in1=mask2_sb[:, :], op=ALU.mult), r=[BK(2), "const"], w=["NTb%d" % pc])
                        P.dve(lambda e, pc=pc, hd=hd: e.tensor_tensor(
                            out=NTk_sb[:, pc, hd * 256:(hd + 1) * 256], in0=bk[3][:, hd * 256:(hd + 1) * 256],
                            in1=mask2_sb[:, :], op=ALU.mult), r=[BK(3), "const"], w=["NTk%d" % pc])
                        P.dve(lambda e, hd=hd: e.tensor_tensor(
                            out=XTs[0][:, hd * 128:(hd + 1) * 128], in0=bk[4][:, hd * 128:(hd + 1) * 128],
                            in1=maskL_sb[:, :], op=ALU.mult), r=[BK(4), "const"], w=["XT0"])
                        P.pool(lambda e, pc=pc, hd=hd: e.tensor_copy(
                            out=Xs[0][:, hd * 128:(hd + 1) * 128], in_=NTb_sb[:, pc, hd * 256:hd * 256 + 128]),
                            r=["NTb%d" % pc], w=["X0"])
                        P.pool(lambda e, pc=pc, hd=hd: e.tensor_tensor(
                            out=Ps[0][:, hd * 128:(hd + 1) * 128], in0=NTb_sb[:, pc, hd * 256:hd * 256 + 128],
                            in1=identf[:, :], op=ALU.add), r=["NTb%d" % pc, "const"], w=["P0"])
                    P.pe(lambda e, pc=pc, cs_=cs_: e.transpose(out=bk5[:, 0:128], in_=bt[:, pc, cs_],
                                                               identity=ident_bf[:, :]), r=["bt", "const"], w=[BK(5)])
                    P.pe(lambda e, pc=pc, cs_=cs_: e.transpose(out=bk5[:, 128:256], in_=kt[:, pc, cs_],
                                                               identity=ident_bf[:, :]), r=["kt", "const"], w=[BK(5)])
                    P.act(lambda e, pc=pc: e.copy(out=btok[:, pc, :], in_=bk5[:, 0:128]), r=[BK(5)], w=["btok%d" % pc])
                    P.act(lambda e, pc=pc: e.copy(out=ktok[:, pc, :], in_=bk5[:, 128:256]), r=[BK(5)], w=["ktok%d" % pc])
                    cur = 0
                    for step in range(6):
                        nxt = 1 - cur
                        last = step == 5
                        for hd in range(2):
                            hs_ = slice(hd * 128, (hd + 1) * 128)
                            if not last:
                                P.pe(lambda e, cur=cur, hs_=hs_, hd=hd: e.matmul(
                                    bk[6][:, hd * 128:(hd + 1) * 128], lhsT=XTs[cur][:, hs_], rhs=Xs[cur][:, hs_],
                                    start=True, stop=True), r=["X%d" % cur, "XT%d" % cur], w=[BK(6)])
                            P.pe(lambda e, cur=cur, hs_=hs_, hd=hd: e.matmul(
                                bk[6][:, 256 + hd * 128:256 + (hd + 1) * 128], lhsT=Xs[cur][:, hs_], rhs=XTs[cur][:, hs_],
                                start=True, stop=True), r=["X%d" % cur, "XT%d" % cur], w=[BK(6)])
                        if not last:
                            P.act(lambda e, nxt=nxt: e.copy(out=Xs[nxt][:, :], in_=bk[6][:, 0:256]),
                                  r=[BK(6)], w=["X%d" % nxt])
                        P.act(lambda e, nxt=nxt: e.copy(out=XTs[nxt][:, :], in_=bk[6][:, 256:512]),
                              r=[BK(6)], w=["XT%d" % nxt])
                        for hd in range(2):
                            hs_ = slice(hd * 128, (hd + 1) * 128)
                            P.pe(lambda e, cur=cur, nxt=nxt, hs_=hs_: e.matmul(
                                bk[7][:, hs_], lhsT=XTs[nxt][:, hs_], rhs=Ps[cur][:, hs_], start=True, stop=True),
                                r=["XT%d" % nxt, "P%d" % cur], w=[BK(7)])
                        if last:
                            P.dve(lambda e, cur=cur, pc=pc: e.tensor_tensor(
                                out=Pfin[:, pc, :], in0=bk[7][:, 0:256], in1=Ps[cur][:, :], op=ALU.add),
                                r=[BK(7), "P%d" % cur], w=["Pfin%d" % pc])
                        else:
                            P.dve(lambda e, cur=cur, nxt=nxt: e.tensor_tensor(
                                out=Ps[nxt][:, :], in0=bk[7][:, 0:256], in1=Ps[cur][:, :], op=ALU.add),
                                r=[BK(7), "P%d" % cur], w=["P%d" % nxt])
                        cur = nxt
                if LEVEL < 4:
                    continue
                for pc in range(4):
                    for hd in range(2):
                        rows = slice(64 * hd, 64 * hd + 64)
                        oc_ = slice(pc * 128 + hd * 64, pc * 128 + hd * 64 + 64)
                        P.pe(lambda e, pc=pc, cl=cl, rows=rows, oc_=oc_, hd=hd, cs_=cs_: e.matmul(
                            bk[0][:, oc_], lhsT=at2[:, pc, hd, cs_], rhs=Hbf[:, pc, hd * 64:hd * 64 + 64],
                            start=True, stop=False), r=["at2", "Hbf%d" % pc], w=[BK(0)])
                        P.pe(lambda e, pc=pc, cl=cl, oc_=oc_, hd=hd: e.matmul(
                            bk[0][:, oc_], lhsT=NTk_sb[:, pc, hd * 256:hd * 256 + 128], rhs=vtok[:, cl, oc_],
                            start=False, stop=True), r=["NTk%d" % pc, "vtok"], w=[BK(0)])
                P.act(lambda e: e.copy(out=Zs[:, :, :], in_=bk[0][:, :]), r=[BK(0)], w=["Zs"])
                for pc in range(4):
                    for hd in range(2):
                        oc_ = slice(pc * 128 + hd * 64, pc * 128 + hd * 64 + 64)
                        P.pe(lambda e, pc=pc, oc_=oc_, hd=hd: e.matmul(
                            bk[1][:, oc_], lhsT=Pfin[:, pc, hd * 128:(hd + 1) * 128], rhs=Zs[:, pc, hd * 64:hd * 64 + 64],
                            start=True, stop=True), r=["Pfin%d" % pc, "Zs"], w=[BK(1)])
                P.dve(lambda e: e.tensor_copy(out=Us[:, :, :], in_=bk[1][:, :]), r=[BK(1)], w=["Us"])
                for pc in range(4):
                    for hd in range(2):
                        rows = slice(64 * hd, 64 * hd + 64)
                        oc_ = slice(pc * 128 + hd * 64, pc * 128 + hd * 64 + 64)
                        P.pe(lambda e, pc=pc, cl=cl, rows=rows, oc_=oc_, hd=hd, cs_=cs_: e.matmul(
                            bk[2][:, oc_], lhsT=rt2[:, pc, hd, cs_], rhs=Hbf[:, pc, hd * 64:hd * 64 + 64],
                            start=True, stop=False), r=["rt2", "Hbf%d" % pc], w=[BK(2)])
                        P.pe(lambda e, pc=pc, oc_=oc_, hd=hd: e.matmul(
                            bk[2][:, oc_], lhsT=NTb_sb[:, pc, hd * 256 + 128:hd * 256 + 256],
                            rhs=Us[:, pc, hd * 64:hd * 64 + 64], start=False, stop=False),
                            r=["NTb%d" % pc, "Us"], w=[BK(2)])
                        P.pe(lambda e, pc=pc, cl=cl, oc_=oc_, hd=hd: e.matmul(
                            bk[2][:, oc_], lhsT=NTk_sb[:, pc, hd * 256 + 128:hd * 256 + 256], rhs=vtok[:, cl, oc_],
                            start=False, stop=True), r=["NTk%d" % pc, "vtok"], w=[BK(2)])
                P.act(lambda e, cl=cl: e.copy(out=ytok[:, cl, :], in_=bk[2][:, :]), r=[BK(2)], w=["ytok"])
                for pc in range(4):
                    pcs = slice(pc * 128, (pc + 1) * 128)
                    P.pe(lambda e, pc=pc, pcs=pcs: e.matmul(bk[3][:, pcs], lhsT=btok[:, pc, :], rhs=Us[:, pc, :],
                                                            start=True, stop=False),
                         r=["btok%d" % pc, "Us"], w=[BK(3)])
                    P.pe(lambda e, pc=pc, pcs=pcs, cl=cl: e.matmul(bk[3][:, pcs], lhsT=ktok[:, pc, :], rhs=vtok[:, cl, pcs],
                                                                   start=False, stop=True),
                         r=["ktok%d" % pc, "vtok"], w=[BK(3)])
                P.dve(lambda e: e.tensor_tensor(out=Htmp[:, :, :], in0=bk[3][:, :], in1=H32[:, :, :], op=ALU.add),
                      r=[BK(3), "H32"], w=["Htmp"])
                for pc in range(4):
                    wl = Ep[:, pc, cl * 128 + 127:cl * 128 + 128]
                    P.dve(lambda e, pc=pc, wl=wl: e.tensor_scalar(out=H32[:, pc, :], in0=Htmp[:, pc, :], scalar1=wl,
                                                                  scalar2=None, op0=ALU.mult),
                          r=["Htmp", "Ep"], w=["H32"])
                    P.act(lambda e, pc=pc, wl=wl: e.activation(out=Hbf[:, pc, :], in_=Htmp[:, pc, :], func=AF.Copy,
                                                               scale=wl), r=["Htmp", "Ep"], w=["Hbf%d" % pc])
            for cl in range(NCL if LEVEL >= 5 else 0):
                cs_ = slice(cl * 128, (cl + 1) * 128)
                o = out_sb[fo % 2]
                ok = "out%d" % (fo % 2)
                fo += 1
                for pc in range(4):
                    P.pe(lambda e, pc=pc, cs_=cs_: e.matmul(bk[4][:, 2 * pc:2 * pc + 2], lhsT=rk[:, pc, cs_],
                                                            rhs=ind_sb[:, :], start=True, stop=True),
                         r=["rk", "const"], w=[BK(4)])
                P.dve(lambda e: e.tensor_copy(out=st_sb[:, 32:40], in_=bk[4][:, 0:8]), r=[BK(4)], w=["bonus"])
                y3 = ytok[:, cl, :].rearrange("p (h n) -> p h n", n=64)
                P.dve(lambda e, y3=y3: e.tensor_reduce(out=st_sb[:, 0:8], in_=y3, axis=AX.X, op=ALU.add),
                      r=["ytok"], w=["st"])
                P.act(lambda e, cl=cl: e.activation(out=ysq[:, :], in_=ytok[:, cl, :], func=AF.Square),
                      r=["ytok"], w=["ysq"])
                P.dve(lambda e: e.tensor_reduce(out=st_sb[:, 8:16], in_=ysq[:, :].rearrange("p (h n) -> p h n", n=64),
                                                axis=AX.X, op=ALU.add), r=["ysq"], w=["st"])
                P.dve(lambda e: e.tensor_scalar(out=st_sb[:, 0:8], in0=st_sb[:, 0:8], scalar1=1.0 / 64, scalar2=None,
                                                op0=ALU.mult), r=["st"], w=["st"])
                P.dve(lambda e: e.tensor_tensor(out=st_sb[:, 16:24], in0=st_sb[:, 0:8], in1=st_sb[:, 0:8], op=ALU.mult),
                      r=["st"], w=["st"])
                P.dve(lambda e: e.scalar_tensor_tensor(out=st_sb[:, 24:32], in0=st_sb[:, 8:16], scalar=1.0 / 64,
                                                       in1=st_sb[:, 16:24], op0=ALU.mult, op1=ALU.subtract),
                      r=["st"], w=["st"])
                P.act(lambda e: e.activation(out=st_sb[:, 24:32], in_=st_sb[:, 24:32], func=AF.Sqrt, bias=RW_LN_EPS,
                                             scale=1.0), r=["st"], w=["st"])
                P.dve(lambda e: e.reciprocal(out=st_sb[:, 24:32], in_=st_sb[:, 24:32]), r=["st"], w=["st"])
                for h in range(8):
                    hs_ = slice(h * 64, (h + 1) * 64)
                    P.dve(lambda e, h=h, hs_=hs_, o=o, cl=cl: e.tensor_scalar(
                        out=o[:, hs_], in0=ytok[:, cl, hs_], scalar1=st_sb[:, h:h + 1], scalar2=st_sb[:, 24 + h:25 + h],
                        op0=ALU.subtract, op1=ALU.mult), r=["ytok", "st"], w=[ok])
                P.dve(lambda e, o=o: e.tensor_tensor(out=o[:, :], in0=o[:, :], in1=lng_sb[:, :], op=ALU.mult),
                      r=[ok, "const"], w=[ok])
                P.dve(lambda e, o=o: e.tensor_tensor(out=o[:, :], in0=o[:, :], in1=lnb_sb[:, :], op=ALU.add),
                      r=[ok, "const"], w=[ok])
                for h in range(8):
                    hs_ = slice(h * 64, (h + 1) * 64)
                    P.dve(lambda e, h=h, hs_=hs_, o=o, cl=cl: e.scalar_tensor_tensor(
                        out=o[:, hs_], in0=v32[:, cl, hs_], scalar=st_sb[:, 32 + h:33 + h], in1=o[:, hs_],
                        op0=ALU.mult, op1=ALU.add), r=["v32", "bonus", ok], w=[ok])
                P.dve(lambda e, o=o, cl=cl: e.tensor_tensor(out=o[:, :], in0=o[:, :], in1=gtok[:, cl, :], op=ALU.mult),
                      r=[ok, "gtok"], w=[ok])
                P.dma("sp", yg[t0 + cl * 128:t0 + (cl + 1) * 128, :], o[:, :], r=[ok], w=["yg"])
        pass


def rw_consts(TT=256):
    ident = np.eye(128, dtype=np.float32)
    p = np.arange(128)
    up_strict = (p[:, None] < p[None, :]).astype(np.float32)
    up_incl = (p[:, None] <= p[None, :]).astype(np.float32)
    mask2 = np.concatenate([up_strict, up_incl], axis=1)
    maskL = (p[:, None] > p[None, :]).astype(np.float32)
    rmask = np.ones((128, TT), np.float32)
    rmask[:, ::CL] = 0.0
    ind = np.zeros((128, 2), np.float32)
    ind[:64, 0] = 1.0
    ind[64:, 1] = 1.0
    return dict(ident=ident, mask2=mask2, maskL=maskL, rmask=rmask, ind=ind)


def rw_inputs(hT_b, hh, p):
    cols = slice(hh * 512, (hh + 1) * 512)
    T = hT_b.shape[1]
    hTp = np.zeros((D, T + 1), np.float32)
    hTp[:, 1:] = hT_b
    vec = lambda v: np.ascontiguousarray(v[cols].reshape(4, 128).T)
    cv = np.concatenate([vec(p["rwkv_w0"][0]), vec(p["rwkv_a0"][0]), vec(p["rwkv_k_k"][0]), vec(p["rwkv_k_a"][0]),
                         vec(p["rwkv_r_k"][0].reshape(-1))], axis=1)
    mu6 = np.ascontiguousarray(p["rwkv_mu"][0].reshape(6, 8, 128).transpose(2, 0, 1).reshape(128, 48))
    m = dict(hTp=hTp,
             wr=np.ascontiguousarray(p["rwkv_w_rkv"][0, 0][:, cols]), wk=np.ascontiguousarray(p["rwkv_w_rkv"][0, 1][:, cols]),
             wv=np.ascontiguousarray(p["rwkv_w_rkv"][0, 2][:, cols]),
             l1=np.ascontiguousarray(np.concatenate([p["rwkv_w1"][0], p["rwkv_a1"][0], p["rwkv_g1"][0]], axis=1)),
             w2c=np.ascontiguousarray(p["rwkv_w2"][0][:, cols]), a2c=np.ascontiguousarray(p["rwkv_a2"][0][:, cols]),
             g2c=np.ascontiguousarray(p["rwkv_g2"][0][:, cols]), mu6=mu6, cv=np.ascontiguousarray(cv),
             lng=np.ascontiguousarray(np.broadcast_to(p["rwkv_ln_g"][0][cols], (128, 512))),
             lnb=np.ascontiguousarray(np.broadcast_to(p["rwkv_ln_b"][0][cols], (128, 512))))
    m.update(rw_consts())
    return m


def _std_io(nc):
    def io(name, shape, kind):
        return nc.dram_tensor(name, list(shape), F32,
                              kind="ExternalInput" if kind == "in" else "ExternalOutput").ap()
    return io


def build_LA(NT, has_add, TB=1024, TT=512):
    nc = bass.Bass("TRN2", target_bir_lowering=False)
    with ExitStack() as es:
        P = Prog(nc, es)
        emit_LA(nc, P, _std_io(nc), NT, has_add, TB, TT)
    return nc


def build_LP(NT, has_v, TB=1024, TT=512):
    nc = bass.Bass("TRN2", target_bir_lowering=False)
    with ExitStack() as es:
        P = Prog(nc, es)
        emit_LP(nc, P, _std_io(nc), NT, has_v, TB, TT)
    return nc


def build_ATT(T, HL=4):
    nc = bass.Bass("TRN2", target_bir_lowering=False)
    with ExitStack() as es:
        P = Prog(nc, es)
        emit_ATT(nc, P, _std_io(nc), T, HL)
    return nc


def build_RW(T, TT=256, LEVEL=9):
    nc = bass.Bass("TRN2", target_bir_lowering=False)
    with ExitStack() as es:
        P = Prog(nc, es)
        emit_RW(nc, P, _std_io(nc), T, TT, LEVEL)
    return nc


_PROGS = {}


def _prog(name, fn):
    if name not in _PROGS:
        _PROGS[name] = fn()
    return _PROGS[name]


def _tile_win(w):
    g = w[:, :FF].reshape(8, 128, 22, 128)
    u = w[:, FF:].reshape(8, 128, 22, 128)
    return np.ascontiguousarray(np.concatenate([g, u], axis=3).transpose(2, 1, 0, 3))


def _tile_wout(w):
    return np.ascontiguousarray(w.reshape(22, 128, 8, 128).transpose(2, 1, 0, 3))


def _tile_sq(w):
    return np.ascontiguousarray(w.reshape(8, 128, 8, 128).transpose(2, 1, 0, 3))


def _gains(g1, g2):
    return np.ascontiguousarray(np.concatenate([g1.reshape(8, 128).T, g2.reshape(8, 128).T], axis=1))


def _run(nc, in_maps):
    return run_bass_kernel_spmd(nc, in_maps, core_ids=list(range(NCORES))).results


def _run_LA(xT_list, w_in, w_out, g1, g2, aT_list=None, w_add=None):
    NT = xT_list[0].shape[1]
    has_add = aT_list is not None
    nc = _prog(("LA", NT, has_add), lambda: build_LA(NT, has_add))
    wi, wo, gg = _tile_win(w_in), _tile_wout(w_out), _gains(g1, g2)
    wa = _tile_sq(w_add) if has_add else None
    maps = []
    for c in range(NCORES):
        m = {"xT": xT_list[c], "gains": gg, "w_in": wi, "w_out": wo}
        if has_add:
            m["aT"] = aT_list[c]
            m["w_add"] = wa
        maps.append(m)
    res = _run(nc, maps)
    return [r["yT"] for r in res], [r["hT"] for r in res]


def _run_LP(hT_list, wn, gn, wv=None):
    NT = hT_list[0].shape[1]
    has_v = wv is not None
    nc = _prog(("LP", NT, has_v), lambda: build_LP(NT, has_v))
    wnt = _tile_sq(wn)
    g = np.ascontiguousarray(np.tile(gn, 2).reshape(128, 1))
    maps = []
    for c in range(NCORES):
        m = {"hT": hT_list[c], "wn": wnt, "gn": g}
        if has_v:
            m["wv"] = np.ascontiguousarray(wv.reshape(8, 128, D).transpose(1, 0, 2))
        maps.append(m)
    res = _run(nc, maps)
    return [r["nT"] for r in res], ([r["v_tok"] for r in res] if has_v else None)


def kernel_unfused(**inp):
    p = {k: np.asarray(v, dtype=np.float32) for k, v in inp.items()}
    x = p["x"]
    B, T, _ = x.shape
    HT = T // 2
    xT = [np.ascontiguousarray(x[c // 2, (c % 2) * HT:(c % 2 + 1) * HT].T) for c in range(NCORES)]

    def full_seq(lst, b):
        return np.concatenate([lst[2 * b], lst[2 * b + 1]], axis=1)

    x1T, h1T = _run_LA(xT, p["ffn_w_in"][0, 0], p["ffn_w_out"][0, 0], p["ffn_norm"][0, 0], p["mix_norm"][0])
    nc_rw = _prog(("RW", T), lambda: build_RW(T))
    rw_maps = [rw_inputs(full_seq(h1T, c // 2), c % 2, p) for c in range(NCORES)]
    yg = [r["yg_tok"] for r in _run(nc_rw, rw_maps)]
    ygT = [np.ascontiguousarray(np.concatenate([yg[2 * b], yg[2 * b + 1]], axis=1).T) for b in range(B)]
    aT = [np.ascontiguousarray(ygT[c // 2][:, (c % 2) * HT:(c % 2 + 1) * HT]) for c in range(NCORES)]
    x3T, hkvT = _run_LA(x1T, p["ffn_w_in"][0, 1], p["ffn_w_out"][0, 1], p["ffn_norm"][0, 1], p["kv_norm"],
                        aT_list=aT, w_add=p["rwkv_w_o"][0])
    kT, v_tok = _run_LP(hkvT, p["w_kv"][:, :D], p["k_norm"], wv=p["w_kv"][:, D:])
    x4T, h2T = _run_LA(x3T, p["ffn_w_in"][1, 0], p["ffn_w_out"][1, 0], p["ffn_norm"][1, 0], p["mix_norm"][1])
    qT, _ = _run_LP(h2T, p["diff_w_q"][0], p["diff_q_norm"][0])
    nc_att = _prog(("ATT", T), lambda: build_ATT(T, 4))
    att_maps = []
    subg = np.ascontiguousarray(np.broadcast_to(p["diff_subln"][0], (128, 128)))
    lam = np.ascontiguousarray(p["diff_lambda"][0].reshape(1, 256))
    for c in range(NCORES):
        b, hh = c // 2, c % 2
        rows = slice(hh * 512, (hh + 1) * 512)
        qaug, kaug, btab, tri = att_consts(T, [hh * 4 + i for i in range(4)])
        att_maps.append({"qT": np.ascontiguousarray(full_seq(qT, b)[rows]), "kT": np.ascontiguousarray(full_seq(kT, b)[rows]),
                         "v_tok": np.ascontiguousarray(np.concatenate([v_tok[2 * b], v_tok[2 * b + 1]], axis=0)[:, rows]),
                         "qaug": qaug, "kaug": kaug, "btab": btab, "tri": tri, "lam": lam, "subg": subg})
    ot = [r["o_tok"] for r in _run(nc_att, att_maps)]
    oT = [np.ascontiguousarray(np.concatenate([ot[2 * b], ot[2 * b + 1]], axis=1).T) for b in range(B)]
    aT = [np.ascontiguousarray(oT[c // 2][:, (c % 2) * HT:(c % 2 + 1) * HT]) for c in range(NCORES)]
    outT, _ = _run_LA(x4T, p["ffn_w_in"][1, 1], p["ffn_w_out"][1, 1], p["ffn_norm"][1, 1], p["ffn_norm"][1, 1],
                      aT_list=aT, w_add=p["diff_w_o"][0])
    out = np.empty((B, T, D), np.float32)
    for c in range(NCORES):
        out[c // 2, (c % 2) * HT:(c % 2 + 1) * HT] = outT[c].T
    return out


RG_PAIRS = [[0, 1], [2, 3], [4, 5], [6, 7]]


CC_MAX_BYTES = 2 * 1024 * 1024


class _Gathered:
    def __init__(self, nc, name, src, R, C):
        self.src, self.R, self.C = src, R, C
        self.RC = min(R, CC_MAX_BYTES // (C * 4))
        assert R % self.RC == 0
        self.nch = R // self.RC
        self.g = nc.dram_tensor(name, [self.nch * 2 * self.RC, C], F32).ap()

    def rows(self, j, r0, r1):
        ch = r0 // self.RC
        assert (r1 - 1) // self.RC == ch
        base = (ch * 2 + j) * self.RC - ch * self.RC
        return self.g[base + r0:base + r1, :]


def _emit_allgather(nc, P, gs):
    with P.stage():
        for G in gs:
            for ch in range(G.nch):
                src = G.src[ch * G.RC:(ch + 1) * G.RC, :]
                dst = G.g[ch * 2 * G.RC:(ch + 1) * 2 * G.RC, :]
                P.add("pool", lambda e, src=src, dst=dst: e.collective_compute(
                    "AllGather", ALU.bypass, replica_groups=RG_PAIRS, ins=[src.opt()], outs=[dst.opt()]),
                    dma="cc")


def _emit_select(nc, P, sel, jobs, F, ident=None):
    with P.stage():
        sel_sb = P.sb("sel_sb", [128, 2], F32)
        P.dma("sp", sel_sb[:, :], sel[:, :], w=["sel"])
        a_sb = [P.sb("a_sb%d" % i, [128, F], F32) for i in range(2)]
        b_sb = [P.sb("b_sb%d" % i, [128, F], F32) for i in range(2)]
        o_sb = [P.sb("o_sb%d" % i, [128, F], F32) for i in range(2)]
        if ident is not None:
            id_sb = P.sb("id_sb", [128, 128], F32)
            P.dma("sp", id_sb[:, :], ident[:, :], w=["ident"])
            t_sb = [P.sb("t_sb%d" % i, [128, 512], F32) for i in range(2)]
            ps_t = [P.ps("ps_t%d" % i, [128, 512]) for i in range(2)]
            P.psum_keys.update(["ps_t0", "ps_t1"])
        for i, (A, B, dst) in enumerate(jobs):
            q = i % 2
            P.dma("sp", a_sb[q][:, :], A, w=["a%d" % q])
            P.dma("sp", b_sb[q][:, :], B, w=["b%d" % q])
            P.dve(lambda e, q=q: e.tensor_scalar(out=a_sb[q][:, :], in0=a_sb[q][:, :], scalar1=sel_sb[:, 0:1],
                                                 scalar2=None, op0=ALU.mult), r=["a%d" % q, "sel"], w=["a%d" % q])
            P.dve(lambda e, q=q: e.scalar_tensor_tensor(out=o_sb[q][:, :], in0=b_sb[q][:, :], scalar=sel_sb[:, 1:2],
                                                        in1=a_sb[q][:, :], op0=ALU.mult, op1=ALU.add),
                  r=["a%d" % q, "b%d" % q, "sel"], w=["o%d" % q])
            if ident is None:
                P.dma("sp", dst, o_sb[q][:, :], r=["o%d" % q], w=["seldst"])
            else:
                for cc in range(4):
                    P.pe(lambda e, q=q, cc=cc: e.transpose(out=ps_t[q][:, cc * 128:(cc + 1) * 128],
                                                           in_=o_sb[q][:, cc * 128:(cc + 1) * 128],
                                                           identity=id_sb[:, :]),
                         r=["o%d" % q, "ident"], w=["ps_t%d" % q])
                P.act(lambda e, q=q: e.copy(out=t_sb[q][:, :], in_=ps_t[q][:, :]), r=["ps_t%d" % q], w=["t%d" % q])
                P.dma("sp", dst, t_sb[q][:, :].rearrange("p (c t) -> p c t", c=4), r=["t%d" % q], w=["seldst"])


class _Skip:
    def __init__(self, P):
        self.P = P

    def __enter__(self):
        self.n = len(self.P.ops)
        self.P.es = ExitStack()
        self.P.es.__enter__()
        return self.P

    def __exit__(self, *a):
        del self.P.ops[self.n:]
        self.P.lastw, self.P.readers = {}, {}
        self.P.es.__exit__(*a)
        self.P.es = self.P.sem_es
        return False


def build_FUSED(T=8192, UPTO=99):
    HT = T // 2
    _st = [0]

    def go():
        _st[0] += 1
        return _st[0] <= UPTO
    nc = bass.Bass("TRN2", target_bir_lowering=False)
    ext_in = lambda n, shp: nc.dram_tensor(n, list(shp), F32, kind="ExternalInput").ap()
    ext_out = lambda n, shp: nc.dram_tensor(n, list(shp), F32, kind="ExternalOutput").ap()
    internal = lambda n, shp: nc.dram_tensor(n, list(shp), F32).ap()
    with ExitStack() as es:
        P = Prog(nc, es)

        def mk_io(prefix, bind):
            def io(name, shape, kind):
                if name in bind:
                    return bind[name]
                assert kind == "in", name
                return ext_in(prefix + name, shape)
            return io

        sel = ext_in("sel", [128, 2])
        ident = ext_in("SEL_ident", [128, 128])
        xT = ext_in("xT", [D, HT])
        outT = ext_out("outT", [D, HT])
        x1T, h1T = internal("x1T", [D, HT]), internal("h1T", [D, HT])
        if go():
            emit_LA(nc, P, mk_io("A1_", {"xT": xT, "yT": x1T, "hT": h1T}), HT, False)
        h1g = _Gathered(nc, "h1g", h1T, D, HT)
        if go():
            _emit_allgather(nc, P, [h1g])
        hTp = internal("hTp", [D, T + 1])
        with (P.stage() if go() else _Skip(P)):
            z_sb = P.sb("z_sb", [128, KC, 1], F32)
            P.pool(lambda e: e.memset(z_sb[:, :, :], 0.0), w=["z"])
            P.add("sp", lambda e: e.dma_start(out=hTp.rearrange("(kc p) t -> p kc t", p=128)[:, :, 0:1],
                                              in_=z_sb[:, :, :], allow_slow_non_contiguous=True),
                  r=["z"], w=["hTp"], dma=True)
            for j in range(2):
                for kc in range(KC):
                    P.dma("sp", hTp[kc * 128:(kc + 1) * 128, 1 + j * HT:1 + (j + 1) * HT],
                          h1g.rows(j, kc * 128, (kc + 1) * 128), w=["hTp"])
        yg_tok = internal("yg_tok", [T, 512])
        if go():
            emit_RW(nc, P, mk_io("RW_", {"hTp": hTp, "yg_tok": yg_tok}), T)
        ygg = _Gathered(nc, "ygg", yg_tok, T, 512)
        if go():
            _emit_allgather(nc, P, [ygg])
        aT2 = internal("aT2", [D, HT])

        def tok2feat_jobs(g, dst):
            jobs = []
            for j in range(2):
                for tb in range(HT // 128):
                    A = g.rows(j, tb * 128, (tb + 1) * 128)
                    B = g.rows(j, HT + tb * 128, HT + (tb + 1) * 128)
                    dd = dst[j * 512:(j + 1) * 512, tb * 128:(tb + 1) * 128].rearrange("(c p) t -> p c t", p=128)
                    jobs.append((A, B, dd))
            return jobs

        if go():
            _emit_select(nc, P, sel, tok2feat_jobs(ygg, aT2), 512, ident=ident)
        x3T, hkvT = internal("x3T", [D, HT]), internal("hkvT", [D, HT])
        if go():
            emit_LA(nc, P, mk_io("A2_", {"xT": x1T, "aT": aT2, "yT": x3T, "hT": hkvT}), HT, True)
        kT, v_tok = internal("kT", [D, HT]), internal("v_tok", [HT, D])
        if go():
            emit_LP(nc, P, mk_io("P1_", {"hT": hkvT, "nT": kT, "v_tok": v_tok}), HT, True)
        x4T, h2T = internal("x4T", [D, HT]), internal("h2T", [D, HT])
        if go():
            emit_LA(nc, P, mk_io("A3_", {"xT": x3T, "yT": x4T, "hT": h2T}), HT, False)
        qT = internal("qT", [D, HT])
        if go():
            emit_LP(nc, P, mk_io("P2_", {"hT": h2T, "nT": qT}), HT, False)
        qg, kg, vg = _Gathered(nc, "qg", qT, D, HT), _Gathered(nc, "kg", kT, D, HT), _Gathered(nc, "vg", v_tok, HT, D)
        if go():
            _emit_allgather(nc, P, [qg, kg, vg])
        Sq, Sk, Sv = internal("Sq", [512, T]), internal("Sk", [512, T]), internal("Sv", [T, 512])
        jobs = []
        FS = min(2048, HT)
        for g, S in ((qg, Sq), (kg, Sk)):
            for j in range(2):
                for rr in range(4):
                    for cb in range(HT // FS):
                        cs = slice(cb * FS, (cb + 1) * FS)
                        jobs.append((g.rows(j, rr * 128, (rr + 1) * 128)[:, cs],
                                     g.rows(j, 512 + rr * 128, 512 + (rr + 1) * 128)[:, cs],
                                     S[rr * 128:(rr + 1) * 128, j * HT + cb * FS:j * HT + (cb + 1) * FS]))
        if go():
            _emit_select(nc, P, sel, jobs, FS)
        jobs = [(vg.rows(tb // (HT // 128), (tb % (HT // 128)) * 128, (tb % (HT // 128) + 1) * 128)[:, 0:512],
                 vg.rows(tb // (HT // 128), (tb % (HT // 128)) * 128, (tb % (HT // 128) + 1) * 128)[:, 512:1024],
                 Sv[tb * 128:(tb + 1) * 128, :]) for tb in range(T // 128)]
        if go():
            _emit_select(nc, P, sel, jobs, 512)
        o_tok = internal("o_tok", [T, 512])
        if go():
            emit_ATT(nc, P, mk_io("AT_", {"qT": Sq, "kT": Sk, "v_tok": Sv, "o_tok": o_tok}), T, 4)
        og = _Gathered(nc, "og", o_tok, T, 512)
        if go():
            _emit_allgather(nc, P, [og])
        aT4 = internal("aT4", [D, HT])
        if go():
            _emit_select(nc, P, sel, tok2feat_jobs(og, aT4), 512, ident=ident)
        hdum = internal("hdum", [D, HT])
        if go():
            emit_LA(nc, P, mk_io("A4_", {"xT": x4T, "aT": aT4, "yT": outT, "hT": hdum}), HT, True)
    return nc


def kernel(**inp):
    p = {k: np.asarray(v, dtype=np.float32) for k, v in inp.items()}
    x = p["x"]
    B, T, _ = x.shape
    HT = T // 2
    import os
    nc = _prog(("FUSED", T), lambda: build_FUSED(T, int(os.environ.get("FUSED_UPTO", "99"))))
    shared = {"SEL_ident": np.eye(128, dtype=np.float32)}

    def la(prefix, l, i, g2, w_add=None):
        shared[prefix + "gains"] = _gains(p["ffn_norm"][l, i], g2)
        shared[prefix + "w_in"] = _tile_win(p["ffn_w_in"][l, i])
        shared[prefix + "w_out"] = _tile_wout(p["ffn_w_out"][l, i])
        if w_add is not None:
            shared[prefix + "w_add"] = _tile_sq(w_add)

    la("A1_", 0, 0, p["mix_norm"][0])
    la("A2_", 0, 1, p["kv_norm"], p["rwkv_w_o"][0])
    la("A3_", 1, 0, p["mix_norm"][1])
    la("A4_", 1, 1, p["ffn_norm"][1, 1], p["diff_w_o"][0])
    shared["P1_wn"] = _tile_sq(p["w_kv"][:, :D])
    shared["P1_gn"] = np.ascontiguousarray(np.tile(p["k_norm"], 2).reshape(128, 1))
    shared["P1_wv"] = np.ascontiguousarray(p["w_kv"][:, D:].reshape(8, 128, D).transpose(1, 0, 2))
    shared["P2_wn"] = _tile_sq(p["diff_w_q"][0])
    shared["P2_gn"] = np.ascontiguousarray(np.tile(p["diff_q_norm"][0], 2).reshape(128, 1))
    shared["AT_lam"] = np.ascontiguousarray(p["diff_lambda"][0].reshape(1, 256))
    shared["AT_subg"] = np.ascontiguousarray(np.broadcast_to(p["diff_subln"][0], (128, 128)))
    dummy_h = np.zeros((D, 1), np.float32)
    maps = []
    for c in range(NCORES):
        b, hh = c // 2, c % 2
        m = dict(shared)
        m["xT"] = np.ascontiguousarray(x[b, hh * HT:(hh + 1) * HT].T)
        s_ = np.zeros((128, 2), np.float32)
        s_[:, hh] = 1.0
        m["sel"] = s_
        rw = rw_inputs(dummy_h, hh, p)
        del rw["hTp"]
        for k_, v_ in rw.items():
            m["RW_" + k_] = v_
        qaug, kaug, btab, tri = att_consts(T, [hh * 4 + i for i in range(4)])
        m.update({"AT_qaug": qaug, "AT_kaug": kaug, "AT_btab": btab, "AT_tri": tri})
        maps.append(m)
    res = _run(nc, maps)
    out = np.empty((B, T, D), np.float32)
    for c in range(NCORES):
        out[c // 2, (c % 2) * HT:(c % 2 + 1) * HT] = res[c]["outT"].T
    return out
```

```python
import numpy as np
from contextlib import ExitStack
import concourse.bass as bass
import concourse.mybir as mybir
from concourse.bass_utils import run_bass_kernel_spmd

F32 = mybir.dt.float32
BF16 = mybir.dt.bfloat16
ALU = mybir.AluOpType
AF = mybir.ActivationFunctionType
AX = mybir.AxisListType

NCORES = 8
SEM_CAP = 8192


class _Op:
    __slots__ = ("eng", "fn", "deps", "dma", "signal", "sem", "val", "idx")


class Prog:
    ENGS = ("pe", "act", "dve", "pool", "sp")

    def __init__(self, nc, es, n_dma_sems=12):
        self.nc = nc
        self.sem_es = es
        self.es = es
        self.ops = []
        self.lastw = {}
        self.readers = {}
        self.n_dma_sems = n_dma_sems
        self.uid = 0
        self.psum_keys = set()
        self.emitted = 0
        self.cnt = {e: 0 for e in self.ENGS}
        self.dma_cnt = [0] * (2 * n_dma_sems)
        self.dma_last = [None] * (2 * n_dma_sems)
        self.n_dma = {"sp": 0, "pool": 0}
        self.n_cc = 0
        self.sems = {}
        self.waited = {e: {} for e in self.ENGS}
        self.nstage = 0

    def sb(self, name, shape, dt):
        return self.es.enter_context(self.nc.sbuf_tensor("g%d_%s" % (self.nstage, name), list(shape), dt))

    def ps(self, name, shape, dt=F32):
        return self.es.enter_context(self.nc.psum_tensor("g%d_%s" % (self.nstage, name), list(shape), dt))

    def _sem(self, key):
        if key not in self.sems:
            self.sems[key] = self.sem_es.enter_context(self.nc.semaphore("s_%s_%s" % key))
        return self.sems[key]

    class _Stage:
        def __init__(self, P):
            self.P = P

        def __enter__(self):
            self.P.es = ExitStack()
            self.P.es.__enter__()
            return self.P

        def __exit__(self, *a):
            if a[0] is None:
                self.P.emit_stage()
            self.P.es.__exit__(*a)
            self.P.es = self.P.sem_es
            return False

    def stage(self):
        return Prog._Stage(self)

    def add(self, eng, fn, r=(), w=(), dma=False):
        op = _Op()
        op.eng, op.fn, op.dma = eng, fn, dma
        op.idx = len(self.ops)
        op.signal = False
        op.sem = op.val = None
        deps = {}
        for k in r:
            d = self.lastw.get(k)
            if d is not None:
                deps[d] = True
        for k in r:
            if k in self.psum_keys:
                for rd in self.readers.get(k, ()):
                    if self.ops[rd].eng != eng:
                        deps[rd] = True
        for k in w:
            d = self.lastw.get(k)
            if d is not None:
                deps[d] = True
            for rd in self.readers.get(k, ()):
                if rd not in deps:
                    deps[rd] = False
        for k in r:
            lst = self.readers.setdefault(k, [])
            if not dma:
                lst[:] = [x for x in lst if self.ops[x].dma or self.ops[x].eng != eng]
            lst.append(op.idx)
        for k in w:
            self.lastw[k] = op.idx
            self.readers[k] = []
        op.deps = deps
        self.ops.append(op)
        return op

    def pe(self, fn, r=(), w=()):
        return self.add("pe", fn, r, w)

    def act(self, fn, r=(), w=()):
        return self.add("act", fn, r, w)

    def dve(self, fn, r=(), w=()):
        return self.add("dve", fn, r, w)

    def pool(self, fn, r=(), w=()):
        return self.add("pool", fn, r, w)

    def dma(self, q, out, in_, r=(), w=()):
        return self.add(q, lambda e: e.dma_start(out=out, in_=in_), r, w, dma=True)

    def finalize(self):
        self.emit_stage()

    def emit_stage(self):
        nc, ops = self.nc, self.ops
        s0 = self.emitted
        stage_ops = ops[s0:]
        self.emitted = len(ops)
        self.nstage += 1
        self.lastw, self.readers = {}, {}
        if not stage_ops:
            return
        for op in stage_ops:
            op.deps = {d: st for d, st in op.deps.items() if d >= s0}
            for d, strict in op.deps.items():
                p = ops[d]
                if p.dma:
                    continue
                if p.eng != op.eng or op.dma:
                    p.signal = True
                elif strict and p.eng != "pe":
                    p.signal = True
        NS2 = 2 * self.n_dma_sems
        for op in stage_ops:
            if op.dma == "cc":
                self.n_cc += 1
                op.sem, op.val = ("cc", 0), self.n_cc
            elif op.dma:
                j = self.n_dma[op.eng] % self.n_dma_sems + (self.n_dma_sems if op.eng == "pool" else 0)
                self.n_dma[op.eng] += 1
                self.dma_cnt[j] += 1
                op.sem, op.val = ("dma", j), 16 * self.dma_cnt[j]
                if self.dma_last[j] is not None and self.dma_last[j] >= s0:
                    op.deps[self.dma_last[j]] = True
                self.dma_last[j] = op.idx
            elif op.signal:
                t = self.cnt[op.eng]
                self.cnt[op.eng] += 1
                op.sem, op.val = (op.eng, t // SEM_CAP), t % SEM_CAP + 1
        per_eng = {e: [] for e in self.ENGS}
        for op in stage_ops:
            per_eng[op.eng].append(op)
        final = {}
        for op in stage_ops:
            if op.dma:
                final[op.sem] = max(final.get(op.sem, 0), op.val)
        sems = self._sem

        def emit(e, eng):
            waited = self.waited[eng]
            for op in per_eng[eng]:
                need = {}
                for d, strict in op.deps.items():
                    p = ops[d]
                    if (not p.dma) and p.eng == eng and not op.dma:
                        if not strict or eng == "pe":
                            continue
                    if p.sem is None:
                        continue
                    if need.get(p.sem, 0) < p.val:
                        need[p.sem] = p.val
                for sk, v in need.items():
                    if waited.get(sk, 0) < v:
                        e.wait_ge(sems(sk), v)
                        waited[sk] = v
                ins = op.fn(e)
                if op.dma == "cc":
                    ins.then_inc(sems(op.sem), 1)
                elif op.dma:
                    ins.then_inc(sems(op.sem), 16)
                elif op.signal:
                    ins.then_inc(sems(op.sem), 1)
            if eng == "sp":
                for sk, v in final.items():
                    if waited.get(sk, 0) < v:
                        e.wait_ge(sems(sk), v)
                        waited[sk] = v

        with nc.Block() as block:
            @block.tensor
            def _(e):
                emit(e, "pe")

            @block.scalar
            def _(e):
                emit(e, "act")

            @block.vector
            def _(e):
                emit(e, "dve")

            @block.gpsimd
            def _(e):
                emit(e, "pool")

            @block.sync
            def _(e):
                emit(e, "sp")


D = 1024
KC = 8
FF = 2816
FC = 22
EPS = 1e-6


def _rmsnorm(P, nc, x_sb, xkey, h_sb, hkey, g_sb, gcol0, sq_sb, ones_sb, ps_ss, rstd_sb, t0, tn, tag, o0=None):
    kq = "sq" + tag
    if o0 is None:
        o0 = t0
    for kc in range(KC):
        P.act(lambda e, kc=kc: e.activation(out=sq_sb[:, kc, 0:tn], in_=x_sb[:, kc, t0:t0 + tn], func=AF.Square),
              r=[xkey], w=[kq])
    for kc in range(KC):
        P.pe(lambda e, kc=kc: e.matmul(ps_ss[:, 0:tn], lhsT=ones_sb[:, :], rhs=sq_sb[:, kc, 0:tn],
                                       start=(kc == 0), stop=(kc == KC - 1)),
             r=[kq, "ones"], w=["ps_ss"])
    P.act(lambda e: e.activation(out=rstd_sb[:, 0:tn], in_=ps_ss[:, 0:tn], func=AF.Sqrt, bias=EPS, scale=1.0),
          r=["ps_ss"], w=["rstd"])
    P.dve(lambda e: e.reciprocal(out=rstd_sb[:, 0:tn], in_=rstd_sb[:, 0:tn]), r=["rstd"], w=["rstd"])
    for kc in range(KC):
        P.dve(lambda e, kc=kc: e.scalar_tensor_tensor(out=h_sb[:, kc, o0:o0 + tn], in0=x_sb[:, kc, t0:t0 + tn],
                                                      scalar=g_sb[:, gcol0 + kc:gcol0 + kc + 1],
                                                      in1=rstd_sb[:, 0:tn], op0=ALU.mult, op1=ALU.mult),
              r=[xkey, "rstd", "gains"], w=[hkey])


def emit_LA(nc, P, io, NT, has_add, TB=1024, TT=512):
    NWB, NWI, NWA = 4, 6, 3
    xT = io("xT", [D, NT], "in")
    gains = io("gains", [128, 2 * KC], "in")
    w_in = io("w_in", [FC, 128, KC, 256], "in")
    w_out = io("w_out", [KC, 128, FC, 128], "in")
    if has_add:
        aT = io("aT", [D, NT], "in")
        w_add = io("w_add", [KC, 128, KC, 128], "in")
    yT = io("yT", [D, NT], "out")
    hT = io("hT", [D, NT], "out")
    xT_v = xT.rearrange("(kc p) t -> p kc t", p=128)
    yT_v = yT.rearrange("(kc p) t -> p kc t", p=128)
    hT_v = hT.rearrange("(kc p) t -> p kc t", p=128)
    NB = NT // TB
    NS = TB // TT
    with P.stage():
        x_sb = P.sb("x_sb", [128, KC, TB], F32)
        h_sb = P.sb("h_sb", [128, KC, TB], BF16)
        act_sb = P.sb("act_sb", [128, FC, TB], BF16)
        sq_sb = P.sb("sq_sb", [128, KC, TT], BF16)
        ho_sb = P.sb("ho_sb", [128, KC, TT], F32)
        rstd_sb = P.sb("rstd_sb", [128, TT], F32)
        silu_sb = [P.sb("silu_sb%d" % i, [128, TT], F32) for i in range(2)]
        g_sb = P.sb("g_sb", [128, 2 * KC], F32)
        ones_sb = P.sb("ones_sb", [128, 128], BF16)
        win_sb = [P.sb("win_sb%d" % i, [128, KC, 256], BF16) for i in range(NWI)]
        wout_sb = [P.sb("wout_sb%d" % i, [128, FC, 128], BF16) for i in range(NWB)]
        if has_add:
            a_sb = P.sb("a_sb", [128, KC, TB], BF16)
            wadd_sb = [P.sb("wadd_sb%d" % i, [128, KC, 128], BF16) for i in range(NWA)]
        ps_g = [P.ps("ps_g%d" % i, [128, TT]) for i in range(2)]
        ps_u = [P.ps("ps_u%d" % i, [128, TT]) for i in range(2)]
        ps_o = [P.ps("ps_o%d" % i, [128, TT]) for i in range(2)]
        ps_ss = P.ps("ps_ss", [128, TT])
        P.psum_keys.update(["ps_g0", "ps_g1", "ps_u0", "ps_u1", "ps_o0", "ps_o1", "ps_ss"])

        P.dma("sp", g_sb[:, :], gains[:, :], w=["gains"])
        P.pool(lambda e: e.memset(ones_sb[:, :], 1.0 / D), w=["ones"])
        nw = [0, 0, 0]
        for b in range(NB):
            tb0 = b * TB
            for kc in range(KC):
                P.dma("sp", x_sb[:, kc, :], xT_v[:, kc, tb0:tb0 + TB], w=["x"])
            if has_add:
                for kc in range(KC):
                    P.dma("pool", a_sb[:, kc, :], aT.rearrange("(kc p) t -> p kc t", p=128)[:, kc, tb0:tb0 + TB],
                          w=["a"])
                for oc in range(KC):
                    s = nw[2] % NWA
                    nw[2] += 1
                    P.dma("pool", wadd_sb[s][:, :, :], w_add[oc], w=["wadd%d" % s])
                    for st in range(NS):
                        t0 = st * TT
                        pb = ps_o[(oc * NS + st) % 2]
                        pk = "ps_o%d" % ((oc * NS + st) % 2)
                        for kc in range(KC):
                            P.pe(lambda e, kc=kc, s=s, pb=pb, t0=t0: e.matmul(
                                pb[:, :], lhsT=wadd_sb[s][:, kc, :], rhs=a_sb[:, kc, t0:t0 + TT],
                                start=(kc == 0), stop=(kc == KC - 1)), r=["wadd%d" % s, "a"], w=[pk])
                        P.dve(lambda e, oc=oc, pb=pb, t0=t0: e.tensor_tensor(
                            out=x_sb[:, oc, t0:t0 + TT], in0=pb[:, :], in1=x_sb[:, oc, t0:t0 + TT], op=ALU.add),
                            r=[pk, "x"], w=["x"])
            for st in range(NS):
                _rmsnorm(P, nc, x_sb, "x", h_sb, "h", g_sb, 0, sq_sb, ones_sb, ps_ss, rstd_sb, st * TT, TT, "")
            for j in range(FC):
                s = nw[0] % NWI
                nw[0] += 1
                P.dma("pool", win_sb[s][:, :, :], w_in[j], w=["win%d" % s])
                for st in range(NS):
                    t0 = st * TT
                    q = (j * NS + st) % 2
                    for kc in range(KC):
                        P.pe(lambda e, kc=kc, s=s, q=q, t0=t0: e.matmul(
                            ps_g[q][:, :], lhsT=win_sb[s][:, kc, 0:128], rhs=h_sb[:, kc, t0:t0 + TT],
                            start=(kc == 0), stop=(kc == KC - 1)), r=["win%d" % s, "h"], w=["ps_g%d" % q])
                    for kc in range(KC):
                        P.pe(lambda e, kc=kc, s=s, q=q, t0=t0: e.matmul(
                            ps_u[q][:, :], lhsT=win_sb[s][:, kc, 128:256], rhs=h_sb[:, kc, t0:t0 + TT],
                            start=(kc == 0), stop=(kc == KC - 1)), r=["win%d" % s, "h"], w=["ps_u%d" % q])
                    P.act(lambda e, q=q: e.activation(out=silu_sb[q][:, :], in_=ps_g[q][:, :], func=AF.Silu),
                          r=["ps_g%d" % q], w=["silu%d" % q])
                    P.dve(lambda e, q=q, j=j, t0=t0: e.tensor_tensor(
                        out=act_sb[:, j, t0:t0 + TT], in0=ps_u[q][:, :], in1=silu_sb[q][:, :], op=ALU.mult),
                        r=["ps_u%d" % q, "silu%d" % q], w=["act"])
            for oc in range(KC):
                s = nw[1] % NWB
                nw[1] += 1
                P.dma("pool", wout_sb[s][:, :, :], w_out[oc], w=["wout%d" % s])
                for st in range(NS):
                    t0 = st * TT
                    q = (oc * NS + st) % 2
                    for j in range(FC):
                        P.pe(lambda e, j=j, s=s, q=q, t0=t0: e.matmul(
                            ps_o[q][:, :], lhsT=wout_sb[s][:, j, :], rhs=act_sb[:, j, t0:t0 + TT],
                            start=(j == 0), stop=(j == FC - 1)), r=["wout%d" % s, "act"], w=["ps_o%d" % q])
                    P.dve(lambda e, oc=oc, q=q, t0=t0: e.scalar_tensor_tensor(
                        out=x_sb[:, oc, t0:t0 + TT], in0=ps_o[q][:, :], scalar=0.5, in1=x_sb[:, oc, t0:t0 + TT],
                        op0=ALU.mult, op1=ALU.add), r=["ps_o%d" % q, "x"], w=["x"])
            for kc in range(KC):
                P.dma("sp", yT_v[:, kc, tb0:tb0 + TB], x_sb[:, kc, :], r=["x"], w=["yT"])
            for st in range(NS):
                _rmsnorm(P, nc, x_sb, "x", ho_sb, "ho", g_sb, KC, sq_sb, ones_sb, ps_ss, rstd_sb, st * TT, TT, "", o0=0)
                for kc in range(KC):
                    P.dma("sp", hT_v[:, kc, tb0 + st * TT:tb0 + (st + 1) * TT], ho_sb[:, kc, :], r=["ho"], w=["hT"])
        pass


def _blockones(P, t, val, key):
    P.pool(lambda e: e.memset(t[:, :], 0.0), w=[key])
    P.pool(lambda e: e.memset(t[0:64, 0:64], val), w=[key])
    P.pool(lambda e: e.memset(t[64:128, 64:128], val), w=[key])


def emit_LP(nc, P, io, NT, has_v, TB=1024, TT=512):
    hT = io("hT", [D, NT], "in")
    wn = io("wn", [KC, 128, KC, 128], "in")
    gn = io("gn", [128, 1], "in")
    nT = io("nT", [D, NT], "out")
    if has_v:
        wv = io("wv", [128, KC, D], "in")
        v_tok = io("v_tok", [NT, D], "out")
    hT_v = hT.rearrange("(kc p) t -> p kc t", p=128)
    nT_v = nT.rearrange("(kc p) t -> p kc t", p=128)
    NB, NS = NT // TB, TB // TT
    with P.stage():
        h_sb = P.sb("h_sb", [128, KC, TB], BF16)
        wn_sb = [P.sb("wn_sb%d" % i, [128, KC, 128], BF16) for i in range(2)]
        sq_sb = [P.sb("sq_sb%d" % i, [128, TT], BF16) for i in range(3)]
        rstd_sb = [P.sb("rstd_sb%d" % i, [128, TT], F32) for i in range(3)]
        o_sb = [P.sb("o_sb%d" % i, [128, TT], F32) for i in range(3)]
        g_sb = P.sb("g_sb", [128, 1], F32)
        bo_sb = P.sb("bo_sb", [128, 128], BF16)
        ps_p = [P.ps("ps_p%d" % i, [128, TT]) for i in range(3)]
        ps_s = [P.ps("ps_s%d" % i, [128, TT]) for i in range(3)]
        P.psum_keys.update(["ps_p0", "ps_p1", "ps_p2", "ps_s0", "ps_s1", "ps_s2", "ps_v0", "ps_v1"])
        if has_v:
            wv_sb = P.sb("wv_sb", [128, KC, D], BF16)
            v_sb = [P.sb("v_sb%d" % i, [128, 512], F32) for i in range(2)]
            ps_v = [P.ps("ps_v%d" % i, [128, 512]) for i in range(2)]
            for kc in range(KC):
                P.dma("pool", wv_sb[:, kc, :], wv[:, kc, :], w=["wv"])
        P.dma("sp", g_sb[:, :], gn[:, :], w=["gn"])
        _blockones(P, bo_sb, 1.0 / 64, "bo")
        it = 0
        nv = 0
        for b in range(NB):
            tb0 = b * TB
            for kc in range(KC):
                P.dma("pool", h_sb[:, kc, :], hT_v[:, kc, tb0:tb0 + TB], w=["h"])
            for oc in range(KC):
                s = oc % 2
                P.dma("pool", wn_sb[s][:, :, :], wn[oc], w=["wn%d" % s])
                for st in range(NS):
                    t0 = st * TT
                    q = it % 3
                    it += 1
                    for kc in range(KC):
                        P.pe(lambda e, kc=kc, s=s, q=q, t0=t0: e.matmul(
                            ps_p[q][:, :], lhsT=wn_sb[s][:, kc, :], rhs=h_sb[:, kc, t0:t0 + TT],
                            start=(kc == 0), stop=(kc == KC - 1)), r=["wn%d" % s, "h"], w=["ps_p%d" % q])
                    P.act(lambda e, q=q: e.activation(out=sq_sb[q][:, :], in_=ps_p[q][:, :], func=AF.Square),
                          r=["ps_p%d" % q], w=["sq%d" % q])
                    P.pe(lambda e, q=q: e.matmul(ps_s[q][:, :], lhsT=bo_sb[:, :], rhs=sq_sb[q][:, :],
                                                 start=True, stop=True), r=["sq%d" % q, "bo"], w=["ps_s%d" % q])
                    P.act(lambda e, q=q: e.activation(out=rstd_sb[q][:, :], in_=ps_s[q][:, :], func=AF.Sqrt,
                                                      bias=EPS, scale=1.0), r=["ps_s%d" % q], w=["rstd%d" % q])
                    P.dve(lambda e, q=q: e.reciprocal(out=rstd_sb[q][:, :], in_=rstd_sb[q][:, :]),
                          r=["rstd%d" % q], w=["rstd%d" % q])
                    P.dve(lambda e, q=q: e.scalar_tensor_tensor(
                        out=o_sb[q][:, :], in0=ps_p[q][:, :], scalar=g_sb[:, 0:1], in1=rstd_sb[q][:, :],
                        op0=ALU.mult, op1=ALU.mult), r=["ps_p%d" % q, "rstd%d" % q, "gn"], w=["o%d" % q])
                    P.dma("sp", nT_v[:, oc, tb0 + t0:tb0 + t0 + TT], o_sb[q][:, :], r=["o%d" % q], w=["nT"])
            if has_v:
                for tbk in range(TB // 128):
                    for half in range(2):
                        q = nv % 2
                        nv += 1
                        for kc in range(KC):
                            P.pe(lambda e, kc=kc, q=q, tbk=tbk, half=half: e.matmul(
                                ps_v[q][:, :], lhsT=h_sb[:, kc, tbk * 128:(tbk + 1) * 128],
                                rhs=wv_sb[:, kc, half * 512:(half + 1) * 512],
                                start=(kc == 0), stop=(kc == KC - 1)), r=["wv", "h"], w=["ps_v%d" % q])
                        P.act(lambda e, q=q: e.copy(out=v_sb[q][:, :], in_=ps_v[q][:, :]),
                              r=["ps_v%d" % q], w=["v%d" % q])
                        P.dma("sp", v_tok[tb0 + tbk * 128:tb0 + (tbk + 1) * 128, half * 512:(half + 1) * 512],
                              v_sb[q][:, :], r=["v%d" % q], w=["v_tok"])
        pass


LAM_INIT1 = 0.8 - 0.6 * float(np.exp(-0.3))
SUBLN_EPS = 1e-5
NDD = 67


def emit_ATT(nc, P, io, T, HL=4, gath=None):
    if gath is None:
        qT = io("qT", [HL * 128, T], "in")
        kT = io("kT", [HL * 128, T], "in")
        v_tok = io("v_tok", [T, HL * 128], "in")
    qaug = io("qaug", [5, T], "in")
    kaug = io("kaug", [HL, 5, T], "in")
    btab = io("btab", [128, HL * NDD], "in")
    tri = io("tri", [128, 128], "in")
    lam = io("lam", [1, 256], "in")
    subg = io("subg", [128, 128], "in")
    o_tok = io("o_tok", [T, HL * 128], "out")
    NQB = T // 512
    NKB = T // 128
    with P.stage():
        q_sbs = [P.sb("q_sb%d" % i, [69, 2, T], BF16) for i in range(2)]
        k_sbs = [P.sb("k_sb%d" % i, [69, 2, T], BF16) for i in range(2)]
        v_sbs = [P.sb("v_sb%d" % i, [128, NKB, 130], BF16) for i in range(2)]
        bt_sb = P.sb("bt_sb", [128, HL * NDD], F32)
        tri_sb = P.sb("tri_sb", [128, 128], BF16)
        z_sb = P.sb("z_sb", [128, 512], BF16)
        pt_sb = [P.sb("pt_sb%d" % i, [128, 512], BF16) for i in range(4)]
        lam_sb = P.sb("lam_sb", [1, 256], F32)
        lt_sb = P.sb("lt_sb", [1, 128], F32)
        ls_sb = P.sb("ls_sb", [1, 4], F32)
        one_row = P.sb("one_row", [1, 128], F32)
        nl_sb = P.sb("nl_sb", [128, 1], F32)
        eps_sb = P.sb("eps_sb", [128, 1], F32)
        sg_sb = P.sb("sg_sb", [128, 128], F32)
        rec_sb = [P.sb("rec_sb%d" % i, [128, 4], F32) for i in range(2)]
        o0_sb = [P.sb("o0_sb%d" % i, [128, 128], F32) for i in range(2)]
        od_sb = [P.sb("od_sb%d" % i, [128, 128], F32) for i in range(2)]
        junk_sb = [P.sb("junk_sb%d" % i, [128, 128], F32) for i in range(2)]
        out_sb = [P.sb("out_sb%d" % i, [128, 128], F32) for i in range(2)]
        oc_sb = [P.sb("oc_sb%d" % i, [128, 512], F32) for i in range(3)]
        ps_S = [P.ps("ps_S%d" % i, [128, 512]) for i in range(4)]
        ps_O = [P.ps("ps_O%d" % i, [128, 512]) for i in range(3)]
        ps_m = P.ps("ps_m", [128, 512])
        P.psum_keys.update(["S0", "S1", "S2", "S3", "O0", "O1", "O2", "ps_m"])

        P.dma("sp", bt_sb[:, :], btab[:, :], w=["bt"])
        P.dma("pool", tri_sb[:, :], tri[:, :], w=["tri"])
        P.dma("sp", lam_sb[:, :], lam[:, :], w=["lam"])
        P.dma("sp", sg_sb[:, :], subg[:, :], w=["sg"])
        P.pool(lambda e: e.memset(z_sb[:, :], 0.0), w=["z"])
        P.pool(lambda e: e.memset(eps_sb[:, :], SUBLN_EPS), w=["epsc"])
        P.pool(lambda e: e.memset(one_row[:, :], 1.0), w=["one_row"])
        P.pool(lambda e: e.memset(v_sbs[0][:, :, 128:130], 1.0), w=["vones"])
        P.pool(lambda e: e.memset(v_sbs[1][:, :, 128:130], 1.0), w=["vones"])
        P.dve(lambda e: e.tensor_tensor(out=lt_sb[:, 0:64], in0=lam_sb[:, 0:64], in1=lam_sb[:, 64:128], op=ALU.mult),
              r=["lam"], w=["lt"])
        P.dve(lambda e: e.tensor_tensor(out=lt_sb[:, 64:128], in0=lam_sb[:, 128:192], in1=lam_sb[:, 192:256],
                                        op=ALU.mult), r=["lam"], w=["lt"])
        P.dve(lambda e: e.tensor_reduce(out=ls_sb[:, 0:1], in_=lt_sb[:, 0:64], axis=AX.X, op=ALU.add),
              r=["lt"], w=["ls"])
        P.dve(lambda e: e.tensor_reduce(out=ls_sb[:, 1:2], in_=lt_sb[:, 64:128], axis=AX.X, op=ALU.add),
              r=["lt"], w=["ls"])
        P.act(lambda e: e.activation(out=ls_sb[:, 0:2], in_=ls_sb[:, 0:2], func=AF.Exp), r=["ls"], w=["ls"])
        P.dve(lambda e: e.scalar_tensor_tensor(out=ls_sb[:, 2:3], in0=ls_sb[:, 1:2], scalar=-LAM_INIT1,
                                               in1=ls_sb[:, 0:1], op0=ALU.add, op1=ALU.subtract),
              r=["ls"], w=["ls"])
        P.pe(lambda e: e.matmul(ps_m[:, 0:1], lhsT=one_row[:, :], rhs=ls_sb[:, 2:3], start=True, stop=True),
             r=["ls", "one_row"], w=["ps_m"])
        P.dve(lambda e: e.tensor_copy(out=nl_sb[:, :], in_=ps_m[:, 0:1]), r=["ps_m"], w=["nl"])
        P.dve(lambda e: e.tensor_scalar(out=sg_sb[:, :], in0=sg_sb[:, :], scalar1=1.0 - LAM_INIT1, scalar2=None,
                                        op0=ALU.mult), r=["sg"], w=["sg"])

        def acc(a):
            return ps_O[a // 3], (a % 3) * 132

        sl = 0
        fin = 0
        if gath is not None:
            qg_, kg_, vg_, sel_ = gath
            HT_ = T // 2
            sel_sb = P.sb("sel_sb", [128, 2], F32)
            P.dma("sp", sel_sb[:, :], sel_[:, :], w=["sel"])
            stg_sb = [P.sb("stg_sb%d" % i, [128, HT_], BF16) for i in range(2)]
            nst = [0]

            def blend(dst, A, B, npart, key):
                q_ = nst[0] % 2
                nst[0] += 1
                P.dma("pool", dst, A, w=[key])
                n = dst.shape[1] if len(dst.shape) == 2 else dst.shape[1] * dst.shape[2]
                st = stg_sb[q_][0:npart, 0:n]
                if len(dst.shape) == 3:
                    st = st.rearrange("p (a b) -> p a b", b=dst.shape[2])
                P.dma("pool", st, B, w=["stg%d" % q_])
                P.dve(lambda e: e.tensor_scalar(out=dst, in0=dst, scalar1=sel_sb[0:npart, 0:1], scalar2=None,
                                                op0=ALU.mult), r=[key, "sel"], w=[key])
                P.dve(lambda e: e.scalar_tensor_tensor(out=dst, in0=st, scalar=sel_sb[0:npart, 1:2], in1=dst,
                                                       op0=ALU.mult, op1=ALU.add), r=[key, "stg%d" % q_, "sel"], w=[key])

        def load_head(hl):
            hb = hl % 2
            for c in range(2):
                r0 = hl * 128 + c * 64
                if gath is None:
                    P.dma("pool", q_sbs[hb][0:64, c, :], qT[r0:r0 + 64, :], w=["q%d" % hb])
                    P.dma("pool", k_sbs[hb][0:64, c, :], kT[r0:r0 + 64, :], w=["k%d" % hb])
                else:
                    for j in range(2):
                        cs = slice(j * HT_, (j + 1) * HT_)
                        blend(q_sbs[hb][0:64, c, cs], qg_.rows(j, r0, r0 + 64), qg_.rows(j, 512 + r0, 512 + r0 + 64),
                              64, "q%d" % hb)
                        blend(k_sbs[hb][0:64, c, cs], kg_.rows(j, r0, r0 + 64), kg_.rows(j, 512 + r0, 512 + r0 + 64),
                              64, "k%d" % hb)
                P.dma("pool", q_sbs[hb][64:69, c, :], qaug[:, :], w=["q%d" % hb])
                P.dma("pool", k_sbs[hb][64:69, c, :], kaug[hl], w=["k%d" % hb])
            if gath is None:
                P.dma("pool", v_sbs[hb][:, :, 0:128],
                      v_tok.rearrange("(blk p) v -> p blk v", p=128)[:, :, hl * 128:(hl + 1) * 128], w=["v%d" % hb])
            else:
                nb_ = HT_ // 128
                for j in range(2):
                    rc = vg_.RC
                    for ch in range(HT_ // rc):
                        nbc = rc // 128
                        blk0 = j * nb_ + ch * nbc
                        rows = vg_.rows(j, ch * rc, (ch + 1) * rc).rearrange("(blk p) v -> p blk v", p=128)
                        blend(v_sbs[hb][:, blk0:blk0 + nbc, 0:128], rows[:, :, hl * 128:(hl + 1) * 128],
                              rows[:, :, 512 + hl * 128:512 + (hl + 1) * 128], 128, "v%d" % hb)

        load_head(0)
        for hl in range(HL):
            hb = hl % 2
            q_sb, k_sb, v_sb = q_sbs[hb], k_sbs[hb], v_sbs[hb]
            qk_, kk_, vk_ = "q%d" % hb, "k%d" % hb, "v%d" % hb
            if hl + 1 < HL:
                load_head(hl + 1)
            for qb in range(NQB):
                for bk in range(3):
                    P.pe(lambda e, bk=bk: e.matmul(ps_O[bk][:, :], lhsT=z_sb[:, 0:128], rhs=z_sb[:, :],
                                                   start=True, stop=False), r=["z"], w=["O%d" % bk])
                nkb = 4 * qb + 4
                units = [(kb, c) for kb in range(nkb) for c in range(2)]
                pend = []

                def emit_pv(u, qb=qb, v_sb=v_sb, vk=vk_):
                    kb, c, s, j0 = u
                    for jj in range(j0 // 128, 4):
                        a = jj * 2 + c
                        pb, off = acc(a)
                        P.pe(lambda e, s=s, jj=jj, kb=kb, pb=pb, off=off, last=(kb == 4 * qb + jj): e.matmul(
                            pb[:, off:off + 129], lhsT=pt_sb[s][:, jj * 128:(jj + 1) * 128],
                            rhs=v_sb[:, kb, 0:129], start=False, stop=last),
                            r=["pt%d" % s, vk, "vones"], w=["O%d" % (a // 3)])

                for kb, c in units:
                    rr = kb - 4 * qb
                    j0 = max(0, 128 * rr)
                    s = sl % 4
                    sl += 1
                    P.pe(lambda e, s=s, c=c, kb=kb, qb=qb, j0=j0, k_sb=k_sb, q_sb=q_sb: e.matmul(
                        ps_S[s][:, j0:512], lhsT=k_sb[0:69, c, kb * 128:(kb + 1) * 128],
                        rhs=q_sb[0:69, c, qb * 512 + j0:(qb + 1) * 512], start=True, stop=True),
                        r=[qk_, kk_], w=["S%d" % s])
                    P.act(lambda e, s=s, j0=j0: e.activation(
                        out=pt_sb[s][:, j0:512], in_=ps_S[s][:, j0:512], func=AF.Exp, scale=0.125),
                        r=["S%d" % s], w=["pt%d" % s])
                    if rr >= 0:
                        P.dve(lambda e, s=s, j0=j0: e.tensor_tensor(
                            out=pt_sb[s][:, j0:j0 + 128], in0=pt_sb[s][:, j0:j0 + 128], in1=tri_sb[:, :],
                            op=ALU.mult), r=["pt%d" % s, "tri"], w=["pt%d" % s])
                    pend.append((kb, c, s, j0))
                    if len(pend) > 2:
                        emit_pv(pend.pop(0))
                while pend:
                    emit_pv(pend.pop(0))
                for bk in range(3):
                    P.dve(lambda e, bk=bk: e.tensor_copy(out=oc_sb[bk][:, :], in_=ps_O[bk][:, :]),
                          r=["O%d" % bk], w=["oc%d" % bk])
                for jj in range(4):
                    f = fin % 2
                    fin += 1
                    pb0, off0 = oc_sb[(jj * 2) // 3], ((jj * 2) % 3) * 132
                    pb1, off1 = oc_sb[(jj * 2 + 1) // 3], ((jj * 2 + 1) % 3) * 132
                    k0, k1 = "oc%d" % ((jj * 2) // 3), "oc%d" % ((jj * 2 + 1) // 3)
                    P.dve(lambda e, f=f, pb0=pb0, off0=off0: e.reciprocal(
                        out=rec_sb[f][:, 0:1], in_=pb0[:, off0 + 128:off0 + 129]), r=[k0], w=["rec%d" % f])
                    P.dve(lambda e, f=f, pb1=pb1, off1=off1: e.reciprocal(
                        out=rec_sb[f][:, 1:2], in_=pb1[:, off1 + 128:off1 + 129]), r=[k1], w=["rec%d" % f])
                    P.dve(lambda e, f=f: e.tensor_tensor(out=rec_sb[f][:, 2:3], in0=rec_sb[f][:, 1:2],
                                                         in1=nl_sb[:, 0:1], op=ALU.mult),
                          r=["rec%d" % f, "nl"], w=["rec%d" % f])
                    P.dve(lambda e, f=f, pb0=pb0, off0=off0: e.tensor_scalar(
                        out=o0_sb[f][:, :], in0=pb0[:, off0:off0 + 128], scalar1=rec_sb[f][:, 0:1], scalar2=None,
                        op0=ALU.mult), r=[k0, "rec%d" % f], w=["o0%d" % f])
                    P.dve(lambda e, f=f, pb1=pb1, off1=off1: e.scalar_tensor_tensor(
                        out=od_sb[f][:, :], in0=pb1[:, off1:off1 + 128], scalar=rec_sb[f][:, 2:3],
                        in1=o0_sb[f][:, :], op0=ALU.mult, op1=ALU.add),
                        r=[k1, "rec%d" % f, "o0%d" % f], w=["od%d" % f])
                    P.dve(lambda e, f=f: e.tensor_tensor(out=junk_sb[f][:, :], in0=od_sb[f][:, :], in1=od_sb[f][:, :],
                                                         op=ALU.mult), r=["od%d" % f], w=["junk%d" % f])
                    P.dve(lambda e, f=f: e.tensor_reduce(out=rec_sb[f][:, 3:4], in_=junk_sb[f][:, :], axis=AX.X,
                                                         op=ALU.add), r=["junk%d" % f], w=["ss%d" % f])
                    P.act(lambda e, f=f: e.activation(out=rec_sb[f][:, 3:4], in_=rec_sb[f][:, 3:4], func=AF.Ln,
                                                      bias=eps_sb[:, 0:1], scale=1.0 / 128),
                          r=["ss%d" % f, "epsc"], w=["ss%d" % f])
                    P.act(lambda e, f=f: e.activation(out=rec_sb[f][:, 3:4], in_=rec_sb[f][:, 3:4], func=AF.Exp,
                                                      scale=-0.5), r=["ss%d" % f], w=["ss%d" % f])
                    P.dve(lambda e, f=f: e.scalar_tensor_tensor(
                        out=out_sb[f][:, :], in0=od_sb[f][:, :], scalar=rec_sb[f][:, 3:4], in1=sg_sb[:, :],
                        op0=ALU.mult, op1=ALU.mult), r=["od%d" % f, "ss%d" % f, "sg"], w=["out%d" % f])
                    P.dma("sp", o_tok[qb * 512 + jj * 128:qb * 512 + (jj + 1) * 128, hl * 128:(hl + 1) * 128],
                          out_sb[f][:, :], r=["out%d" % f], w=["o_tok"])
        pass


def att_consts(T, heads):
    t = np.arange(T)
    one = np.ones_like(t)
    qaug = np.stack([(t % 512) // 16, t % 16, one, one, t // 512]).astype(np.float32)
    kaug = np.zeros((len(heads), 5, T), np.float32)
    btab = np.zeros((128, len(heads) * NDD), np.float32)
    for i, h in enumerate(heads):
        slope = 2.0 ** (-(h + 1))
        kaug[i, 0, :] = -16.0 * slope / 0.125
        kaug[i, 1, :] = -slope / 0.125
        kaug[i, 2, :] = slope * (t % 128) / 0.125
        kaug[i, 3, :] = slope * 128.0 * (t // 128) / 0.125
        kaug[i, 4, :] = -512.0 * slope / 0.125
    tri = (np.arange(128)[:, None] <= np.arange(128)[None, :]).astype(np.float32)
    return qaug, kaug, btab, tri


C0 = float(np.exp(-0.5))
RW_LN_EPS = 64e-5
CL = 128


def emit_RW(nc, P, io, T, TT=256, LEVEL=9):
    dt_in = lambda n, s: io(n, s, "in")
    hTp = dt_in("hTp", [D, T + 1])
    wr_d, wk_d, wv_d = dt_in("wr", [D, 512]), dt_in("wk", [D, 512]), dt_in("wv", [D, 512])
    l1_d = dt_in("l1", [D, 288])
    w2_d, a2_d, g2_d = dt_in("w2c", [64, 512]), dt_in("a2c", [64, 512]), dt_in("g2c", [160, 512])
    mu_d = dt_in("mu6", [128, 48])
    cv_d = dt_in("cv", [128, 20])
    lng_d, lnb_d = dt_in("lng", [128, 512]), dt_in("lnb", [128, 512])
    ident_d, mask2_d, maskL_d = dt_in("ident", [128, 128]), dt_in("mask2", [128, 256]), dt_in("maskL", [128, 128])
    rmask_d, ind_d = dt_in("rmask", [128, TT]), dt_in("ind", [128, 2])
    yg = io("yg_tok", [T, 512], "out")
    hTp_v = hTp.rearrange("(kc p) t -> p kc t", p=128)
    NTI = T // TT
    NCL = TT // CL
    with P.stage():
        sb = P.sb
        W1 = {n: sb("W1" + n, [128, KC, 512], BF16) for n in "rkv"}
        W2 = {n: sb("W2" + n, [128, KC, 512], BF16) for n in "rkv"}
        L1a = sb("L1a", [128, KC, 288], BF16)
        L1b = sb("L1b", [128, KC, 288], BF16)
        stg = [sb("stg%d" % i, [128, 512], F32) for i in range(2)]
        stg1 = sb("stgL", [128, KC, 288], F32)
        w2_sb, a2_sb = sb("w2_sb", [64, 512], BF16), sb("a2_sb", [64, 512], BF16)
        g2a_sb, g2b_sb = sb("g2a_sb", [128, 512], BF16), sb("g2b_sb", [32, 512], BF16)
        mu_sb, omu_sb, cv_sb = sb("mu_sb", [128, 48], F32), sb("omu_sb", [128, 48], F32), sb("cv_sb", [128, 24], F32)
        lng_sb, lnb_sb = sb("lng_sb", [128, 512], F32), sb("lnb_sb", [128, 512], F32)
        ident_bf, identf = sb("ident_bf", [128, 128], BF16), sb("identf", [128, 128], F32)
        mask2_sb, maskL_sb = sb("mask2_sb", [128, 256], F32), sb("maskL_sb", [128, 128], F32)
        rmask_sb, ind_sb, bo_sb = sb("rmask_sb", [128, TT], F32), sb("ind_sb", [128, 2], BF16), sb("bo_sb", [128, 128], BF16)
        hp = [sb("hp%d" % i, [128, KC, TT], BF16) for i in range(2)]
        hq = [sb("hq%d" % i, [128, KC, TT], BF16) for i in range(2)]
        r32, k32 = sb("r32", [128, 4, TT], F32), sb("k32", [128, 4, TT], F32)
        vtok, v32 = sb("vtok", [128, NCL, 512], BF16), sb("v32", [128, NCL, 512], F32)
        gtok = sb("gtok", [128, NCL, 512], F32)
        lw_sb, la_sb = sb("lw_sb", [64, TT], BF16), sb("la_sb", [64, TT], BF16)
        lga_sb, lgb_sb = sb("lga_sb", [128, TT], BF16), sb("lgb_sb", [32, TT], BF16)
        sc = {n: sb("sc_" + n, [128, TT], F32) for n in
              ("sg", "a", "kk", "rn", "kkn", "tmp", "kmod", "bvec", "cs", "tmp2", "Em", "Epv")}
        kk2_sb = sb("kk2_sb", [128, TT], BF16)
        Ep = sb("Ep", [128, 4, TT], F32)
        AR = sb("AR", [128, 4, NCL, 256], BF16)
        bt, kt, rk = sb("bt", [128, 4, TT], BF16), sb("kt", [128, 4, TT], BF16), sb("rk", [128, 4, TT], BF16)
        NTb_sb, NTk_sb = sb("NTb_sb", [128, 4, 512], BF16), sb("NTk_sb", [128, 4, 512], BF16)
        at2, rt2 = sb("at2", [128, 4, 2, TT], BF16), sb("rt2", [128, 4, 2, TT], BF16)
        bt2, kt2 = sb("bt2", [128, 4, 2, TT], BF16), sb("kt2", [128, 4, 2, TT], BF16)
        XX = [[sb("XX%d_%d" % (pc, i), [128, 512], BF16) for i in range(2)] for pc in range(4)]
        Pp = [[sb("Pp%d_%d" % (pc, i), [128, 256], BF16) for i in range(2)] for pc in range(4)]
        Pfin = sb("Pfin", [128, 4, 256], BF16)
        btok, ktok = sb("btok", [128, 4, 128], BF16), sb("ktok", [128, 4, 128], BF16)
        Zs, Us = sb("Zs", [128, 4, 128], BF16), sb("Us", [128, 4, 128], BF16)
        ytok = sb("ytok", [128, NCL, 512], F32)
        H32, Hbf, Htmp = sb("H32", [128, 4, 128], F32), sb("Hbf", [128, 4, 128], BF16), sb("Htmp", [128, 4, 128], F32)
        ysq, st_sb = sb("ysq", [128, 512], F32), sb("st_sb", [128, 40], F32)
        out_sb = [sb("out_sb%d" % i, [128, 512], F32) for i in range(2)]
        bk = [P.ps("bk%d" % i, [128, 512]) for i in range(5)] + [None] + [P.ps("bk%d" % i, [128, 512]) for i in (6, 7)]
        bk5 = P.ps("bk5", [128, 1024], BF16)
        BK = lambda i: "bk%d" % i
        P.psum_keys.update([BK(i) for i in range(8)])

        for dst, src, q in ((mu_sb, mu_d, "sp"), (lng_sb, lng_d, "sp"), (lnb_sb, lnb_d, "sp"),
                            (identf, ident_d, "sp"), (mask2_sb, mask2_d, "sp"), (maskL_sb, maskL_d, "sp"),
                            (rmask_sb, rmask_d, "sp"), (ident_bf, ident_d, "pool"), (ind_sb, ind_d, "pool"),
                            (w2_sb, w2_d, "pool"), (a2_sb, a2_d, "pool")):
            P.dma(q, dst[:, :], src[:, :], w=["const"])
        P.dma("sp", cv_sb[:, 0:20], cv_d[:, :], w=["const"])
        P.dma("pool", g2a_sb[:, :], g2_d[0:128, :], w=["const"])
        P.dma("pool", g2b_sb[:, :], g2_d[128:160, :], w=["const"])
        _blockones(P, bo_sb, 1.0, "bo")
        P.dve(lambda e: e.tensor_scalar(out=omu_sb[:, :], in0=mu_sb[:, :], scalar1=-1.0, scalar2=1.0,
                                        op0=ALU.mult, op1=ALU.add), r=["const"], w=["omu"])
        P.dve(lambda e: e.tensor_scalar(out=cv_sb[:, 20:24], in0=cv_sb[:, 12:16], scalar1=-1.0, scalar2=1.0,
                                        op0=ALU.mult, op1=ALU.add), r=["const"], w=["cv2"])
        P.pool(lambda e: e.memset(H32[:, :, :], 0.0), w=["H32"])
        for tz, kz in ((at2, "at2"), (rt2, "rt2"), (bt2, "bt2"), (kt2, "kt2")):
            P.pool(lambda e, tz=tz: e.memset(tz[:, :, :, :], 0.0), w=[kz])
        P.pool(lambda e: e.memset(Hbf[:, :, :], 0.0), w=["Hbf"])
        ns = 0
        for wi, (n, src) in enumerate((("r", wr_d), ("k", wk_d), ("v", wv_d))):
            for kc in range(KC):
                s = ns % 2
                ns += 1
                P.dma("sp", stg[s][:, :], src[kc * 128:(kc + 1) * 128, :], w=["stg%d" % s])
                P.dve(lambda e, s=s, n=n, kc=kc, wi=wi: e.tensor_scalar(
                    out=W1[n][:, kc, :], in0=stg[s][:, :], scalar1=omu_sb[:, wi * 8 + kc:wi * 8 + kc + 1],
                    scalar2=None, op0=ALU.mult), r=["stg%d" % s, "omu"], w=["W"])
                P.dve(lambda e, s=s, n=n, kc=kc, wi=wi: e.tensor_scalar(
                    out=W2[n][:, kc, :], in0=stg[s][:, :], scalar1=mu_sb[:, wi * 8 + kc:wi * 8 + kc + 1],
                    scalar2=None, op0=ALU.mult), r=["stg%d" % s, "const"], w=["W"])
        P.dma("sp", stg1[:, :, :], l1_d.rearrange("(kc p) l -> p kc l", p=128), w=["stg1"])
        for li, (c0, c1) in enumerate(((0, 64), (64, 128), (128, 288))):
            wi = 3 + li
            for kc in range(KC):
                P.dve(lambda e, kc=kc, wi=wi, c0=c0, c1=c1: e.tensor_scalar(
                    out=L1a[:, kc, c0:c1], in0=stg1[:, kc, c0:c1], scalar1=omu_sb[:, wi * 8 + kc:wi * 8 + kc + 1],
                    scalar2=None, op0=ALU.mult), r=["stg1", "omu"], w=["W"])
                P.dve(lambda e, kc=kc, wi=wi, c0=c0, c1=c1: e.tensor_scalar(
                    out=L1b[:, kc, c0:c1], in0=stg1[:, kc, c0:c1], scalar1=mu_sb[:, wi * 8 + kc:wi * 8 + kc + 1],
                    scalar2=None, op0=ALU.mult), r=["stg1", "const"], w=["W"])

        nb = [0]

        def nbk():
            nb[0] += 1
            return nb[0] % 2

        def proj_fm(pb, key, wa, wb, c0, c1, hpt, hk, n, hqt=None):
            m = c1 - c0
            hpt, hqt = hpt
            for kc in range(KC):
                P.pe(lambda e, kc=kc: e.matmul(pb[0:m, 0:n], lhsT=wa[:, kc, c0:c1], rhs=hpt[:, kc, 0:n],
                                               start=(kc == 0), stop=False), r=["W", hk], w=[key])
            for kc in range(KC):
                P.pe(lambda e, kc=kc: e.matmul(pb[0:m, 0:n], lhsT=wb[:, kc, c0:c1], rhs=hqt[:, kc, 0:n],
                                               start=False, stop=(kc == KC - 1)), r=["W", hk], w=[key])

        fo = 0
        for ti in range(NTI if LEVEL >= 1 else 0):
            t0 = ti * TT
            hs = ti % 2
            hk = "hp%d" % hs
            hp_, hq_ = hp[hs], hq[hs]
            hpt = (hp_, hq_)
            P.dma("pool", hp_[:, :, :], hTp_v[:, :, t0 + 1:t0 + TT + 1], w=[hk])
            P.dma("pool", hq_[:, :, :], hTp_v[:, :, t0:t0 + TT], w=[hk])
            for pc in range(4):
                for n, dst in (("r", r32), ("k", k32)):
                    b = nbk()
                    proj_fm(bk[b], BK(b), W1[n], W2[n], pc * 128, (pc + 1) * 128, hpt, hk, TT)
                    P.act(lambda e, b=b, dst=dst, pc=pc: e.copy(out=dst[:, pc, :], in_=bk[b][:, 0:TT]),
                          r=[BK(b)], w=[n + "32"])
            for cl in range(NCL):
                b = nbk()
                for kc in range(KC):
                    P.pe(lambda e, kc=kc, b=b, cl=cl, hpt=hp_: e.matmul(
                        bk[b][:, :], lhsT=hpt[:, kc, cl * 128:(cl + 1) * 128], rhs=W1["v"][:, kc, :],
                        start=(kc == 0), stop=False), r=["W", hk], w=[BK(b)])
                for kc in range(KC):
                    P.pe(lambda e, kc=kc, b=b, cl=cl, hpt=hq_: e.matmul(
                        bk[b][:, :], lhsT=hpt[:, kc, cl * 128:(cl + 1) * 128], rhs=W2["v"][:, kc, :],
                        start=False, stop=(kc == KC - 1)), r=["W", hk], w=[BK(b)])
                P.act(lambda e, b=b, cl=cl: e.copy(out=vtok[:, cl, :], in_=bk[b][:, :]), r=[BK(b)], w=["vtok"])
                P.act(lambda e, b=b, cl=cl: e.copy(out=v32[:, cl, :], in_=bk[b][:, :]), r=[BK(b)], w=["v32"])
            b = nbk()
            proj_fm(bk[b], BK(b), L1a, L1b, 0, 64, hpt, hk, TT)
            P.act(lambda e, b=b: e.activation(out=lw_sb[:, :], in_=bk[b][0:64, 0:TT], func=AF.Tanh),
                  r=[BK(b)], w=["lw"])
            b = nbk()
            proj_fm(bk[b], BK(b), L1a, L1b, 64, 128, hpt, hk, TT)
            P.act(lambda e, b=b: e.copy(out=la_sb[:, :], in_=bk[b][0:64, 0:TT]), r=[BK(b)], w=["la"])
            b = nbk()
            proj_fm(bk[b], BK(b), L1a, L1b, 128, 256, hpt, hk, TT)
            P.act(lambda e, b=b: e.activation(out=lga_sb[:, :], in_=bk[b][:, 0:TT], func=AF.Sigmoid),
                  r=[BK(b)], w=["lg"])
            b = nbk()
            proj_fm(bk[b], BK(b), L1a, L1b, 256, 288, hpt, hk, TT)
            P.act(lambda e, b=b: e.activation(out=lgb_sb[:, :], in_=bk[b][0:32, 0:TT], func=AF.Sigmoid),
                  r=[BK(b)], w=["lg"])
            for cl in range(NCL):
                b = nbk()
                P.pe(lambda e, b=b, cl=cl: e.matmul(bk[b][:, :], lhsT=lga_sb[:, cl * 128:(cl + 1) * 128],
                                                   rhs=g2a_sb[:, :], start=True, stop=False),
                     r=["lg", "const"], w=[BK(b)])
                P.pe(lambda e, b=b, cl=cl: e.matmul(bk[b][:, :], lhsT=lgb_sb[:, cl * 128:(cl + 1) * 128],
                                                   rhs=g2b_sb[:, :], start=False, stop=True),
                     r=["lg", "const"], w=[BK(b)])
                P.act(lambda e, b=b, cl=cl: e.copy(out=gtok[:, cl, :], in_=bk[b][:, :]), r=[BK(b)], w=["gtok"])
            for pc in range(4 if LEVEL >= 2 else 0):
                cvc = lambda w, pc=pc: cv_sb[:, w * 4 + pc:w * 4 + pc + 1]
                b = nbk()
                P.pe(lambda e, b=b, pc=pc: e.matmul(bk[b][:, 0:TT], lhsT=w2_sb[:, pc * 128:(pc + 1) * 128],
                                                   rhs=lw_sb[:, :], start=True, stop=True),
                     r=["lw", "const"], w=[BK(b)])
                P.act(lambda e, b=b, pc=pc: e.activation(out=sc["sg"][:, :], in_=bk[b][:, 0:TT], func=AF.Sigmoid,
                                                         bias=cv_sb[:, pc:pc + 1], scale=1.0),
                      r=[BK(b), "const"], w=["sg"])
                b = nbk()
                P.pe(lambda e, b=b, pc=pc: e.matmul(bk[b][:, 0:TT], lhsT=a2_sb[:, pc * 128:(pc + 1) * 128],
                                                   rhs=la_sb[:, :], start=True, stop=True),
                     r=["la", "const"], w=[BK(b)])
                P.act(lambda e, b=b, pc=pc: e.activation(out=sc["a"][:, :], in_=bk[b][:, 0:TT], func=AF.Sigmoid,
                                                         bias=cv_sb[:, 4 + pc:5 + pc], scale=1.0),
                      r=[BK(b), "const"], w=["a"])
                P.dve(lambda e, pc=pc: e.tensor_scalar(out=sc["kk"][:, :], in0=k32[:, pc, :],
                                                       scalar1=cv_sb[:, 8 + pc:9 + pc], scalar2=None, op0=ALU.mult),
                      r=["k32", "const"], w=["kk"])
                P.act(lambda e: e.activation(out=kk2_sb[:, :], in_=sc["kk"][:, :], func=AF.Square),
                      r=["kk"], w=["kk2"])
                b = nbk()
                P.pe(lambda e, b=b: e.matmul(bk[b][:, 0:TT], lhsT=bo_sb[:, :], rhs=kk2_sb[:, :], start=True, stop=True),
                     r=["kk2", "bo"], w=[BK(b)])
                P.act(lambda e, b=b: e.activation(out=sc["rn"][:, :], in_=bk[b][:, 0:TT], func=AF.Sqrt,
                                                  bias=1e-24, scale=1.0), r=[BK(b)], w=["rn"])
                P.dve(lambda e: e.reciprocal(out=sc["rn"][:, :], in_=sc["rn"][:, :]), r=["rn"], w=["rn"])
                P.dve(lambda e: e.tensor_tensor(out=sc["kkn"][:, :], in0=sc["kk"][:, :], in1=sc["rn"][:, :],
                                                op=ALU.mult), r=["kk", "rn"], w=["kkn"])
                P.dve(lambda e, pc=pc: e.tensor_scalar(out=sc["tmp"][:, :], in0=sc["a"][:, :],
                                                       scalar1=cv_sb[:, 12 + pc:13 + pc],
                                                       scalar2=cv_sb[:, 20 + pc:21 + pc], op0=ALU.mult, op1=ALU.add),
                      r=["a", "const", "cv2"], w=["tmp"])
                P.dve(lambda e, pc=pc: e.tensor_tensor(out=sc["kmod"][:, :], in0=k32[:, pc, :], in1=sc["tmp"][:, :],
                                                       op=ALU.mult), r=["k32", "tmp"], w=["kmod"])
                P.dve(lambda e: e.tensor_tensor(out=sc["bvec"][:, :], in0=sc["kkn"][:, :], in1=sc["a"][:, :],
                                                op=ALU.mult), r=["kkn", "a"], w=["bvec"])
                P.dve(lambda e: e.tensor_tensor_scan(out=sc["cs"][:, :], data0=rmask_sb[:, :], data1=sc["sg"][:, :],
                                                     initial=0.0, op0=ALU.mult, op1=ALU.add),
                      r=["sg", "const"], w=["cs"])
                P.dve(lambda e: e.tensor_tensor(out=sc["tmp2"][:, :], in0=sc["cs"][:, :], in1=sc["sg"][:, :],
                                                op=ALU.subtract), r=["cs", "sg"], w=["tmp2"])
                P.act(lambda e, pc=pc: e.activation(out=Ep[:, pc, :], in_=sc["cs"][:, :], func=AF.Exp, scale=-C0),
                      r=["cs"], w=["Ep"])
                P.act(lambda e: e.activation(out=sc["Em"][:, :], in_=sc["cs"][:, :], func=AF.Exp, scale=C0),
                      r=["cs"], w=["Em"])
                P.act(lambda e: e.activation(out=sc["Epv"][:, :], in_=sc["tmp2"][:, :], func=AF.Exp, scale=-C0),
                      r=["tmp2"], w=["Epv"])
                for cl in range(NCL):
                    cs_ = slice(cl * 128, (cl + 1) * 128)
                    P.dve(lambda e, pc=pc, cl=cl, cs_=cs_: e.scalar_tensor_tensor(
                        out=AR[:, pc, cl, 0:128], in0=sc["kkn"][:, cs_], scalar=-1.0, in1=sc["Epv"][:, cs_],
                        op0=ALU.mult, op1=ALU.mult), r=["kkn", "Epv"], w=["AR"])
                    P.dve(lambda e, pc=pc, cl=cl, cs_=cs_: e.tensor_tensor(
                        out=AR[:, pc, cl, 128:256], in0=r32[:, pc, cs_], in1=Ep[:, pc, cs_], op=ALU.mult),
                        r=["r32", "Ep"], w=["AR"])
                P.dve(lambda e, pc=pc: e.tensor_tensor(out=kt[:, pc, :], in0=sc["kmod"][:, :], in1=sc["Em"][:, :],
                                                       op=ALU.mult), r=["kmod", "Em"], w=["kt"])
                P.dve(lambda e, pc=pc: e.tensor_tensor(out=bt[:, pc, :], in0=sc["bvec"][:, :], in1=sc["Em"][:, :],
                                                       op=ALU.mult), r=["bvec", "Em"], w=["bt"])
                P.dve(lambda e, pc=pc: e.scalar_tensor_tensor(
                    out=rk[:, pc, :], in0=r32[:, pc, :], scalar=cv_sb[:, 16 + pc:17 + pc], in1=sc["kmod"][:, :],
                    op0=ALU.mult, op1=ALU.mult), r=["r32", "kmod", "const"], w=["rk"])
                for hd in range(2):
                    rows = slice(64 * hd, 64 * hd + 64)
                    for cl in range(NCL):
                        cs_ = slice(cl * 128, (cl + 1) * 128)
                        P.pool(lambda e, pc=pc, hd=hd, rows=rows, cl=cl, cs_=cs_: e.tensor_copy(
                            out=at2[rows, pc, hd, cs_], in_=AR[rows, pc, cl, 0:128]), r=["AR"], w=["at2"])
                        P.pool(lambda e, pc=pc, hd=hd, rows=rows, cl=cl, cs_=cs_: e.tensor_copy(
                            out=rt2[rows, pc, hd, cs_], in_=AR[rows, pc, cl, 128:256]), r=["AR"], w=["rt2"])
                    P.pool(lambda e, pc=pc, hd=hd, rows=rows: e.tensor_copy(
                        out=bt2[rows, pc, hd, :], in_=bt[rows, pc, :]), r=["bt"], w=["bt2"])
                    P.pool(lambda e, pc=pc, hd=hd, rows=rows: e.tensor_copy(
                        out=kt2[rows, pc, hd, :], in_=kt[rows, pc, :]), r=["kt"], w=["kt2"])
            for cl in range(NCL if LEVEL >= 3 else 0):
                cs_ = slice(cl * 128, (cl + 1) * 128)
                for pc in range(4):
                    for hd in range(2):
                        P.pe(lambda e, pc=pc, cl=cl, hd=hd, cs_=cs_: e.matmul(
                            bk[2][:, hd * 256:(hd + 1) * 256], lhsT=bt2[:, pc, hd, cs_], rhs=AR[:, pc, cl, :],
                            start=True, stop=True), r=["bt2", "AR"], w=[BK(2)])
                        P.pe(lambda e, pc=pc, cl=cl, hd=hd, cs_=cs_: e.matmul(
                            bk[3][:, hd * 256:(hd + 1) * 256], lhsT=kt2[:, pc, hd, cs_], rhs=AR[:, pc, cl, :],
                            start=True, stop=True), r=["kt2", "AR"], w=[BK(3)])
                        P.pe(lambda e, pc=pc, cl=cl, hd=hd, cs_=cs_: e.matmul(
                            bk[4][:, hd * 128:(hd + 1) * 128], lhsT=at2[:, pc, hd, cs_], rhs=bt[:, pc, cs_],
                            start=True, stop=True), r=["bt", "at2"], w=[BK(4)])
                    xk0 = "XX%d_0" % pc
                    for hd in range(2):
                        P.dve(lambda e, pc=pc, hd=hd: e.tensor_tensor(
                            out=NTb_sb[:, pc, hd * 256:(hd + 1) * 256], in0=bk[2][:, hd * 256:(hd + 1) * 256],
                            in1=mask2_sb[:, :], op=ALU.mult), r=[BK(2), "const"], w=["NTb%d" % pc])
                        P.dve(lambda e, pc=pc, hd=hd: e.tensor_tensor(
                            out=NTk_sb[:, pc, hd * 256:(hd + 1) * 256], in0=bk[3][:, hd * 256:(hd + 1) * 256],
                            in1=mask2_sb[:, :], op=ALU.mult), r=[BK(3), "const"], w=["NTk%d" % pc])
                        P.dve(lambda e, pc=pc, hd=hd: e.tensor_tensor(
                            out=XX[pc][0][:, 256 + hd * 128:256 + (hd + 1) * 128], in0=bk[4][:, hd * 128:(hd + 1) * 128],
                            in1=maskL_sb[:, :], op=ALU.mult), r=[BK(4), "const"], w=[xk0])
                        P.pool(lambda e, pc=pc, hd=hd: e.tensor_copy(
                            out=XX[pc][0][:, hd * 128:(hd + 1) * 128], in_=NTb_sb[:, pc, hd * 256:hd * 256 + 128]),
                            r=["NTb%d" % pc], w=[xk0])
                        P.pool(lambda e, pc=pc, hd=hd: e.tensor_tensor(
                            out=Pp[pc][0][:, hd * 128:(hd + 1) * 128], in0=NTb_sb[:, pc, hd * 256:hd * 256 + 128],
                            in1=identf[:, :], op=ALU.add), r=["NTb%d" % pc, "const"], w=["Pp%d_0" % pc])
                    P.pe(lambda e, pc=pc, cs_=cs_: e.transpose(out=bk5[:, 0:128], in_=bt[:, pc, cs_],
                                                               identity=ident_bf[:, :]), r=["bt", "const"], w=[BK(5)])
                    P.pe(lambda e, pc=pc, cs_=cs_: e.transpose(out=bk5[:, 128:256], in_=kt[:, pc, cs_],
                                                               identity=ident_bf[:, :]), r=["kt", "const"], w=[BK(5)])
                    P.act(lambda e, pc=pc: e.copy(out=btok[:, pc, :], in_=bk5[:, 0:128]), r=[BK(5)], w=["btok%d" % pc])
                    P.act(lambda e, pc=pc: e.copy(out=ktok[:, pc, :], in_=bk5[:, 128:256]), r=[BK(5)], w=["ktok%d" % pc])
                cur = 0
                for step in range(6):
                    nxt = 1 - cur
                    last = step == 5
                    for pc in range(4):
                        bx, bxk = (bk[6], BK(6)) if pc % 2 == 0 else (bk[0], BK(0))
                        xc, xn = "XX%d_%d" % (pc, cur), "XX%d_%d" % (pc, nxt)
                        for hd in range(2):
                            hs_ = slice(hd * 128, (hd + 1) * 128)
                            hx_ = slice(256 + hd * 128, 256 + (hd + 1) * 128)
                            if not last:
                                P.pe(lambda e, pc=pc, cur=cur, hs_=hs_, hx_=hx_, bx=bx: e.matmul(
                                    bx[:, hs_], lhsT=XX[pc][cur][:, hx_], rhs=XX[pc][cur][:, hs_],
                                    start=True, stop=True), r=[xc], w=[bxk])
                            P.pe(lambda e, pc=pc, cur=cur, hs_=hs_, hx_=hx_, bx=bx: e.matmul(
                                bx[:, hx_], lhsT=XX[pc][cur][:, hs_], rhs=XX[pc][cur][:, hx_],
                                start=True, stop=True), r=[xc], w=[bxk])
                        if not last:
                            P.act(lambda e, pc=pc, nxt=nxt, bx=bx: e.copy(out=XX[pc][nxt][:, :], in_=bx[:, :]),
                                  r=[bxk], w=[xn])
                        else:
                            P.act(lambda e, pc=pc, nxt=nxt, bx=bx: e.copy(out=XX[pc][nxt][:, 256:512], in_=bx[:, 256:512]),
                                  r=[bxk], w=[xn])
                    for pc in range(4):
                        bp, bpk = (bk[7], BK(7)) if pc % 2 == 0 else (bk[1], BK(1))
                        xn = "XX%d_%d" % (pc, nxt)
                        pk_c, pk_n = "Pp%d_%d" % (pc, cur), "Pp%d_%d" % (pc, nxt)
                        for hd in range(2):
                            hs_ = slice(hd * 128, (hd + 1) * 128)
                            hx_ = slice(256 + hd * 128, 256 + (hd + 1) * 128)
                            P.pe(lambda e, pc=pc, cur=cur, nxt=nxt, hs_=hs_, hx_=hx_, bp=bp: e.matmul(
                                bp[:, hs_], lhsT=XX[pc][nxt][:, hx_], rhs=Pp[pc][cur][:, hs_], start=True, stop=True),
                                r=[xn, pk_c], w=[bpk])
                        if last:
                            P.dve(lambda e, cur=cur, pc=pc, bp=bp: e.tensor_tensor(
                                out=Pfin[:, pc, :], in0=bp[:, 0:256], in1=Pp[pc][cur][:, :], op=ALU.add),
                                r=[bpk, pk_c], w=["Pfin%d" % pc])
                        else:
                            P.dve(lambda e, cur=cur, nxt=nxt, pc=pc, bp=bp: e.tensor_tensor(
                                out=Pp[pc][nxt][:, :], in0=bp[:, 0:256], in1=Pp[pc][cur][:, :], op=ALU.add),
                                r=[bpk, pk_c], w=[pk_n])
                    cur = nxt
                if LEVEL < 4:
                    continue
                for pc in range(4):
                    for hd in range(2):
                        rows = slice(64 * hd, 64 * hd + 64)
                        oc_ = slice(pc * 128 + hd * 64, pc * 128 + hd * 64 + 64)
                        P.pe(lambda e, pc=pc, cl=cl, rows=rows, oc_=oc_, hd=hd, cs_=cs_: e.matmul(
                            bk[0][:, oc_], lhsT=at2[:, pc, hd, cs_], rhs=Hbf[:, pc, hd * 64:hd * 64 + 64],
                            start=True, stop=False), r=["at2", "Hbf%d" % pc], w=[BK(0)])
                        P.pe(lambda e, pc=pc, cl=cl, oc_=oc_, hd=hd: e.matmul(
                            bk[0][:, oc_], lhsT=NTk_sb[:, pc, hd * 256:hd * 256 + 128], rhs=vtok[:, cl, oc_],
                            start=False, stop=True), r=["NTk%d" % pc, "vtok"], w=[BK(0)])
                P.act(lambda e: e.copy(out=Zs[:, :, :], in_=bk[0][:, :]), r=[BK(0)], w=["Zs"])
                for pc in range(4):
                    for hd in range(2):
                        oc_ = slice(pc * 128 + hd * 64, pc * 128 + hd * 64 + 64)
                        P.pe(lambda e, pc=pc, oc_=oc_, hd=hd: e.matmul(
                            bk[1][:, oc_], lhsT=Pfin[:, pc, hd * 128:(hd + 1) * 128], rhs=Zs[:, pc, hd * 64:hd * 64 + 64],
                            start=True, stop=True), r=["Pfin%d" % pc, "Zs"], w=[BK(1)])
                P.dve(lambda e: e.tensor_copy(out=Us[:, :, :], in_=bk[1][:, :]), r=[BK(1)], w=["Us"])
                for pc in range(4):
                    for hd in range(2):
                        rows = slice(64 * hd, 64 * hd + 64)
                        oc_ = slice(pc * 128 + hd * 64, pc * 128 + hd * 64 + 64)
                        P.pe(lambda e, pc=pc, cl=cl, rows=rows, oc_=oc_, hd=hd, cs_=cs_: e.matmul(
                            bk[2][:, oc_], lhsT=rt2[:, pc, hd, cs_], rhs=Hbf[:, pc, hd * 64:hd * 64 + 64],
                            start=True, stop=False), r=["rt2", "Hbf%d" % pc], w=[BK(2)])
                        P.pe(lambda e, pc=pc, oc_=oc_, hd=hd: e.matmul(
                            bk[2][:, oc_], lhsT=NTb_sb[:, pc, hd * 256 + 128:hd * 256 + 256],
                            rhs=Us[:, pc, hd * 64:hd * 64 + 64], start=False, stop=False),
                            r=["NTb%d" % pc, "Us"], w=[BK(2)])
                        P.pe(lambda e, pc=pc, cl=cl, oc_=oc_, hd=hd: e.matmul(
                            bk[2][:, oc_], lhsT=NTk_sb[:, pc, hd * 256 + 128:hd * 256 + 256], rhs=vtok[:, cl, oc_],
                            start=False, stop=True), r=["NTk%d" % pc, "vtok"], w=[BK(2)])
                P.act(lambda e, cl=cl: e.copy(out=ytok[:, cl, :], in_=bk[2][:, :]), r=[BK(2)], w=["ytok"])
                for pc in range(4):
                    pcs = slice(pc * 128, (pc + 1) * 128)
                    P.pe(lambda e, pc=pc, pcs=pcs: e.matmul(bk[3][:, pcs], lhsT=btok[:, pc, :], rhs=Us[:, pc, :],
                                                            start=True, stop=False),
                         r=["btok%d" % pc, "Us"], w=[BK(3)])
                    P.pe(lambda e, pc=pc, pcs=pcs, cl=cl: e.matmul(bk[3][:, pcs], lhsT=ktok[:, pc, :], rhs=vtok[:, cl, pcs],
                                                                   start=False, stop=True),
                         r=["ktok%d" % pc, "vtok"], w=[BK(3)])
                P.dve(lambda e: e.tensor_tensor(out=Htmp[:, :, :], in0=bk[3][:, :], in1=H32[:, :, :], op=ALU.add),
                      r=[BK(3), "H32"], w=["Htmp"])
                for pc in range(4):
                    wl = Ep[:, pc, cl * 128 + 127:cl * 128 + 128]
                    P.dve(lambda e, pc=pc, wl=wl: e.tensor_scalar(out=H32[:, pc, :], in0=Htmp[:, pc, :], scalar1=wl,
                                                                  scalar2=None, op0=ALU.mult),
                          r=["Htmp", "Ep"], w=["H32"])
                    P.act(lambda e, pc=pc, wl=wl: e.activation(out=Hbf[:, pc, :], in_=Htmp[:, pc, :], func=AF.Copy,
                                                               scale=wl), r=["Htmp", "Ep"], w=["Hbf%d" % pc])
            for cl in range(NCL if LEVEL >= 5 else 0):
                cs_ = slice(cl * 128, (cl + 1) * 128)
                o = out_sb[fo % 2]
                ok = "out%d" % (fo % 2)
                fo += 1
                for pc in range(4):
                    P.pe(lambda e, pc=pc, cs_=cs_: e.matmul(bk[4][:, 2 * pc:2 * pc + 2], lhsT=rk[:, pc, cs_],
                                                            rhs=ind_sb[:, :], start=True, stop=True),
                         r=["rk", "const"], w=[BK(4)])
                P.dve(lambda e: e.tensor_copy(out=st_sb[:, 32:40], in_=bk[4][:, 0:8]), r=[BK(4)], w=["bonus"])
                y3 = ytok[:, cl, :].rearrange("p (h n) -> p h n", n=64)
                P.dve(lambda e, y3=y3: e.tensor_reduce(out=st_sb[:, 0:8], in_=y3, axis=AX.X, op=ALU.add),
                      r=["ytok"], w=["st"])
                P.act(lambda e, cl=cl: e.activation(out=ysq[:, :], in_=ytok[:, cl, :], func=AF.Square),
                      r=["ytok"], w=["ysq"])
                P.dve(lambda e: e.tensor_reduce(out=st_sb[:, 8:16], in_=ysq[:, :].rearrange("p (h n) -> p h n", n=64),
                                                axis=AX.X, op=ALU.add), r=["ysq"], w=["st"])
                P.dve(lambda e: e.tensor_scalar(out=st_sb[:, 0:8], in0=st_sb[:, 0:8], scalar1=1.0 / 64, scalar2=None,
                                                op0=ALU.mult), r=["st"], w=["st"])
                P.dve(lambda e: e.tensor_tensor(out=st_sb[:, 16:24], in0=st_sb[:, 0:8], in1=st_sb[:, 0:8], op=ALU.mult),
                      r=["st"], w=["st"])
                P.dve(lambda e: e.scalar_tensor_tensor(out=st_sb[:, 24:32], in0=st_sb[:, 8:16], scalar=1.0 / 64,
                                                       in1=st_sb[:, 16:24], op0=ALU.mult, op1=ALU.subtract),
                      r=["st"], w=["st"])
                P.act(lambda e: e.activation(out=st_sb[:, 24:32], in_=st_sb[:, 24:32], func=AF.Sqrt, bias=RW_LN_EPS,
                                             scale=1.0), r=["st"], w=["st"])
                P.dve(lambda e: e.reciprocal(out=st_sb[:, 24:32], in_=st_sb[:, 24:32]), r=["st"], w=["st"])
                for h in range(8):
                    hs_ = slice(h * 64, (h + 1) * 64)
                    P.dve(lambda e, h=h, hs_=hs_, o=o, cl=cl: e.tensor_scalar(
                        out=o[:, hs_], in0=ytok[:, cl, hs_], scalar1=st_sb[:, h:h + 1], scalar2=st_sb[:, 24 + h:25 + h],
                        op0=ALU.subtract, op1=ALU.mult), r=["ytok", "st"], w=[ok])
                P.dve(lambda e, o=o: e.tensor_tensor(out=o[:, :], in0=o[:, :], in1=lng_sb[:, :], op=ALU.mult),
                      r=[ok, "const"], w=[ok])
                P.dve(lambda e, o=o: e.tensor_tensor(out=o[:, :], in0=o[:, :], in1=lnb_sb[:, :], op=ALU.add),
                      r=[ok, "const"], w=[ok])
                for h in range(8):
                    hs_ = slice(h * 64, (h + 1) * 64)
                    P.dve(lambda e, h=h, hs_=hs_, o=o, cl=cl: e.scalar_tensor_tensor(
                        out=o[:, hs_], in0=v32[:, cl, hs_], scalar=st_sb[:, 32 + h:33 + h], in1=o[:, hs_],
                        op0=ALU.mult, op1=ALU.add), r=["v32", "bonus", ok], w=[ok])
                P.dve(lambda e, o=o, cl=cl: e.tensor_tensor(out=o[:, :], in0=o[:, :], in1=gtok[:, cl, :], op=ALU.mult),
                      r=[ok, "gtok"], w=[ok])
                P.dma("sp", yg[t0 + cl * 128:t0 + (cl + 1) * 128, :], o[:, :], r=[ok], w=["yg"])
        pass


def rw_consts(TT=256):
    ident = np.eye(128, dtype=np.float32)
    p = np.arange(128)
    up_strict = (p[:, None] < p[None, :]).astype(np.float32)
    up_incl = (p[:, None] <= p[None, :]).astype(np.float32)
    mask2 = np.concatenate([up_strict, up_incl], axis=1)
    maskL = (p[:, None] > p[None, :]).astype(np.float32)
    rmask = np.ones((128, TT), np.float32)
    rmask[:, ::CL] = 0.0
    ind = np.zeros((128, 2), np.float32)
    ind[:64, 0] = 1.0
    ind[64:, 1] = 1.0
    return dict(ident=ident, mask2=mask2, maskL=maskL, rmask=rmask, ind=ind)


def rw_inputs(hT_b, hh, p):
    cols = slice(hh * 512, (hh + 1) * 512)
    T = hT_b.shape[1]
    hTp = np.zeros((D, T + 1), np.float32)
    hTp[:, 1:] = hT_b
    vec = lambda v: np.ascontiguousarray(v[cols].reshape(4, 128).T)
    cv = np.concatenate([vec(p["rwkv_w0"][0]), vec(p["rwkv_a0"][0]), vec(p["rwkv_k_k"][0]), vec(p["rwkv_k_a"][0]),
                         vec(p["rwkv_r_k"][0].reshape(-1))], axis=1)
    mu6 = np.ascontiguousarray(p["rwkv_mu"][0].reshape(6, 8, 128).transpose(2, 0, 1).reshape(128, 48))
    m = dict(hTp=hTp,
             wr=np.ascontiguousarray(p["rwkv_w_rkv"][0, 0][:, cols]), wk=np.ascontiguousarray(p["rwkv_w_rkv"][0, 1][:, cols]),
             wv=np.ascontiguousarray(p["rwkv_w_rkv"][0, 2][:, cols]),
             l1=np.ascontiguousarray(np.concatenate([p["rwkv_w1"][0], p["rwkv_a1"][0], p["rwkv_g1"][0]], axis=1)),
             w2c=np.ascontiguousarray(p["rwkv_w2"][0][:, cols]), a2c=np.ascontiguousarray(p["rwkv_a2"][0][:, cols]),
             g2c=np.ascontiguousarray(p["rwkv_g2"][0][:, cols]), mu6=mu6, cv=np.ascontiguousarray(cv),
             lng=np.ascontiguousarray(np.broadcast_to(p["rwkv_ln_g"][0][cols], (128, 512))),
             lnb=np.ascontiguousarray(np.broadcast_to(p["rwkv_ln_b"][0][cols], (128, 512))))
    m.update(rw_consts())
    return m


def _std_io(nc):
    def io(name, shape, kind):
        return nc.dram_tensor(name, list(shape), F32,
                              kind="ExternalInput" if kind == "in" else "ExternalOutput").ap()
    return io


def build_LA(NT, has_add, TB=1024, TT=512):
    nc = bass.Bass("TRN2", target_bir_lowering=False)
    with ExitStack() as es:
        P = Prog(nc, es)
        emit_LA(nc, P, _std_io(nc), NT, has_add, TB, TT)
    return nc


def build_LP(NT, has_v, TB=1024, TT=512):
    nc = bass.Bass("TRN2", target_bir_lowering=False)
    with ExitStack() as es:
        P = Prog(nc, es)
        emit_LP(nc, P, _std_io(nc), NT, has_v, TB, TT)
    return nc


def build_ATT(T, HL=4):
    nc = bass.Bass("TRN2", target_bir_lowering=False)
    with ExitStack() as es:
        P = Prog(nc, es)
        emit_ATT(nc, P, _std_io(nc), T, HL)
    return nc


def build_RW(T, TT=256, LEVEL=9):
    nc = bass.Bass("TRN2", target_bir_lowering=False)
    with ExitStack() as es:
        P = Prog(nc, es)
        emit_RW(nc, P, _std_io(nc), T, TT, LEVEL)
    return nc


_PROGS = {}


def _prog(name, fn):
    if name not in _PROGS:
        _PROGS[name] = fn()
    return _PROGS[name]


def _tile_win(w):
    g = w[:, :FF].reshape(8, 128, 22, 128)
    u = w[:, FF:].reshape(8, 128, 22, 128)
    return np.ascontiguousarray(np.concatenate([g, u], axis=3).transpose(2, 1, 0, 3))


def _tile_wout(w):
    return np.ascontiguousarray(w.reshape(22, 128, 8, 128).transpose(2, 1, 0, 3))


def _tile_sq(w):
    return np.ascontiguousarray(w.reshape(8, 128, 8, 128).transpose(2, 1, 0, 3))


def _gains(g1, g2):
    return np.ascontiguousarray(np.concatenate([g1.reshape(8, 128).T, g2.reshape(8, 128).T], axis=1))


def _run(nc, in_maps):
    return run_bass_kernel_spmd(nc, in_maps, core_ids=list(range(NCORES))).results


def _run_LA(xT_list, w_in, w_out, g1, g2, aT_list=None, w_add=None):
    NT = xT_list[0].shape[1]
    has_add = aT_list is not None
    nc = _prog(("LA", NT, has_add), lambda: build_LA(NT, has_add))
    wi, wo, gg = _tile_win(w_in), _tile_wout(w_out), _gains(g1, g2)
    wa = _tile_sq(w_add) if has_add else None
    maps = []
    for c in range(NCORES):
        m = {"xT": xT_list[c], "gains": gg, "w_in": wi, "w_out": wo}
        if has_add:
            m["aT"] = aT_list[c]
            m["w_add"] = wa
        maps.append(m)
    res = _run(nc, maps)
    return [r["yT"] for r in res], [r["hT"] for r in res]


def _run_LP(hT_list, wn, gn, wv=None):
    NT = hT_list[0].shape[1]
    has_v = wv is not None
    nc = _prog(("LP", NT, has_v), lambda: build_LP(NT, has_v))
    wnt = _tile_sq(wn)
    g = np.ascontiguousarray(np.tile(gn, 2).reshape(128, 1))
    maps = []
    for c in range(NCORES):
        m = {"hT": hT_list[c], "wn": wnt, "gn": g}
        if has_v:
            m["wv"] = np.ascontiguousarray(wv.reshape(8, 128, D).transpose(1, 0, 2))
        maps.append(m)
    res = _run(nc, maps)
    return [r["nT"] for r in res], ([r["v_tok"] for r in res] if has_v else None)


def kernel_unfused(**inp):
    p = {k: np.asarray(v, dtype=np.float32) for k, v in inp.items()}
    x = p["x"]
    B, T, _ = x.shape
    HT = T // 2
    xT = [np.ascontiguousarray(x[c // 2, (c % 2) * HT:(c % 2 + 1) * HT].T) for c in range(NCORES)]

    def full_seq(lst, b):
        return np.concatenate([lst[2 * b], lst[2 * b + 1]], axis=1)

    x1T, h1T = _run_LA(xT, p["ffn_w_in"][0, 0], p["ffn_w_out"][0, 0], p["ffn_norm"][0, 0], p["mix_norm"][0])
    nc_rw = _prog(("RW", T), lambda: build_RW(T))
    rw_maps = [rw_inputs(full_seq(h1T, c // 2), c % 2, p) for c in range(NCORES)]
    yg = [r["yg_tok"] for r in _run(nc_rw, rw_maps)]
    ygT = [np.ascontiguousarray(np.concatenate([yg[2 * b], yg[2 * b + 1]], axis=1).T) for b in range(B)]
    aT = [np.ascontiguousarray(ygT[c // 2][:, (c % 2) * HT:(c % 2 + 1) * HT]) for c in range(NCORES)]
    x3T, hkvT = _run_LA(x1T, p["ffn_w_in"][0, 1], p["ffn_w_out"][0, 1], p["ffn_norm"][0, 1], p["kv_norm"],
                        aT_list=aT, w_add=p["rwkv_w_o"][0])
    kT, v_tok = _run_LP(hkvT, p["w_kv"][:, :D], p["k_norm"], wv=p["w_kv"][:, D:])
    x4T, h2T = _run_LA(x3T, p["ffn_w_in"][1, 0], p["ffn_w_out"][1, 0], p["ffn_norm"][1, 0], p["mix_norm"][1])
    qT, _ = _run_LP(h2T, p["diff_w_q"][0], p["diff_q_norm"][0])
    nc_att = _prog(("ATT", T), lambda: build_ATT(T, 4))
    att_maps = []
    subg = np.ascontiguousarray(np.broadcast_to(p["diff_subln"][0], (128, 128)))
    lam = np.ascontiguousarray(p["diff_lambda"][0].reshape(1, 256))
    for c in range(NCORES):
        b, hh = c // 2, c % 2
        rows = slice(hh * 512, (hh + 1) * 512)
        qaug, kaug, btab, tri = att_consts(T, [hh * 4 + i for i in range(4)])
        att_maps.append({"qT": np.ascontiguousarray(full_seq(qT, b)[rows]), "kT": np.ascontiguousarray(full_seq(kT, b)[rows]),
                         "v_tok": np.ascontiguousarray(np.concatenate([v_tok[2 * b], v_tok[2 * b + 1]], axis=0)[:, rows]),
                         "qaug": qaug, "kaug": kaug, "btab": btab, "tri": tri, "lam": lam, "subg": subg})
    ot = [r["o_tok"] for r in _run(nc_att, att_maps)]
    oT = [np.ascontiguousarray(np.concatenate([ot[2 * b], ot[2 * b + 1]], axis=1).T) for b in range(B)]
    aT = [np.ascontiguousarray(oT[c // 2][:, (c % 2) * HT:(c % 2 + 1) * HT]) for c in range(NCORES)]
    outT, _ = _run_LA(x4T, p["ffn_w_in"][1, 1], p["ffn_w_out"][1, 1], p["ffn_norm"][1, 1], p["ffn_norm"][1, 1],
                      aT_list=aT, w_add=p["diff_w_o"][0])
    out = np.empty((B, T, D), np.float32)
    for c in range(NCORES):
        out[c // 2, (c % 2) * HT:(c % 2 + 1) * HT] = outT[c].T
    return out


RG_PAIRS = [[0, 1], [2, 3], [4, 5], [6, 7]]


CC_MAX_BYTES = 2 * 1024 * 1024


class _Gathered:
    def __init__(self, nc, name, src, R, C):
        self.src, self.R, self.C = src, R, C
        self.RC = min(R, CC_MAX_BYTES // (C * 4))
        assert R % self.RC == 0
        self.nch = R // self.RC
        self.g = nc.dram_tensor(name, [self.nch * 2 * self.RC, C], F32).ap()

    def rows(self, j, r0, r1):
        ch = r0 // self.RC
        assert (r1 - 1) // self.RC == ch
        base = (ch * 2 + j) * self.RC - ch * self.RC
        return self.g[base + r0:base + r1, :]


def _emit_allgather(nc, P, gs):
    with P.stage():
        for G in gs:
            for ch in range(G.nch):
                src = G.src[ch * G.RC:(ch + 1) * G.RC, :]
                dst = G.g[ch * 2 * G.RC:(ch + 1) * 2 * G.RC, :]
                P.add("pool", lambda e, src=src, dst=dst: e.collective_compute(
                    "AllGather", ALU.bypass, replica_groups=RG_PAIRS, ins=[src.opt()], outs=[dst.opt()]),
                    dma="cc")


def _emit_select(nc, P, sel, jobs, F, ident=None):
    NSB = 4 if F <= 1024 else 3
    with P.stage():
        sel_sb = P.sb("sel_sb", [128, 2], F32)
        P.dma("sp", sel_sb[:, :], sel[:, :], w=["sel"])
        a_sb = [P.sb("a_sb%d" % i, [128, F], F32) for i in range(NSB)]
        b_sb = [P.sb("b_sb%d" % i, [128, F], F32) for i in range(NSB)]
        o_sb = [P.sb("o_sb%d" % i, [128, F], F32) for i in range(NSB)]
        if ident is not None:
            id_sb = P.sb("id_sb", [128, 128], F32)
            P.dma("sp", id_sb[:, :], ident[:, :], w=["ident"])
            t_sb = [P.sb("t_sb%d" % i, [128, 512], F32) for i in range(2)]
            ps_t = [P.ps("ps_t%d" % i, [128, 512]) for i in range(2)]
            P.psum_keys.update(["ps_t0", "ps_t1"])
        for i, (A, B, dst) in enumerate(jobs):
            q = i % NSB
            P.dma("sp", a_sb[q][:, :], A, w=["a%d" % q])
            P.dma("sp", b_sb[q][:, :], B, w=["b%d" % q])
            P.dve(lambda e, q=q: e.tensor_scalar(out=a_sb[q][:, :], in0=a_sb[q][:, :], scalar1=sel_sb[:, 0:1],
                                                 scalar2=None, op0=ALU.mult), r=["a%d" % q, "sel"], w=["a%d" % q])
            P.dve(lambda e, q=q: e.scalar_tensor_tensor(out=o_sb[q][:, :], in0=b_sb[q][:, :], scalar=sel_sb[:, 1:2],
                                                        in1=a_sb[q][:, :], op0=ALU.mult, op1=ALU.add),
                  r=["a%d" % q, "b%d" % q, "sel"], w=["o%d" % q])
            if ident is None:
                P.dma("sp", dst, o_sb[q][:, :], r=["o%d" % q], w=["seldst"])
            else:
                for cc in range(4):
                    P.pe(lambda e, q=q, cc=cc: e.transpose(out=ps_t[q % 2][:, cc * 128:(cc + 1) * 128],
                                                           in_=o_sb[q][:, cc * 128:(cc + 1) * 128],
                                                           identity=id_sb[:, :]),
                         r=["o%d" % q, "ident"], w=["ps_t%d" % (q % 2)])
                P.act(lambda e, q=q: e.copy(out=t_sb[q % 2][:, :], in_=ps_t[q % 2][:, :]), r=["ps_t%d" % (q % 2)], w=["t%d" % (q % 2)])
                P.dma("sp", dst, t_sb[q % 2][:, :].rearrange("p (c t) -> p c t", c=4), r=["t%d" % (q % 2)], w=["seldst"])


class _Skip:
    def __init__(self, P):
        self.P = P

    def __enter__(self):
        self.n = len(self.P.ops)
        self.P.es = ExitStack()
        self.P.es.__enter__()
        return self.P

    def __exit__(self, *a):
        del self.P.ops[self.n:]
        self.P.lastw, self.P.readers = {}, {}
        self.P.es.__exit__(*a)
        self.P.es = self.P.sem_es
        return False


def build_FUSED(T=8192, UPTO=99):
    HT = T // 2
    _st = [0]

    def go():
        _st[0] += 1
        return _st[0] <= UPTO
    nc = bass.Bass("TRN2", target_bir_lowering=False)
    ext_in = lambda n, shp: nc.dram_tensor(n, list(shp), F32, kind="ExternalInput").ap()
    ext_out = lambda n, shp: nc.dram_tensor(n, list(shp), F32, kind="ExternalOutput").ap()
    internal = lambda n, shp: nc.dram_tensor(n, list(shp), F32).ap()
    with ExitStack() as es:
        P = Prog(nc, es)

        def mk_io(prefix, bind):
            def io(name, shape, kind):
                if name in bind:
                    return bind[name]
                assert kind == "in", name
                return ext_in(prefix + name, shape)
            return io

        sel = ext_in("sel", [128, 2])
        ident = ext_in("SEL_ident", [128, 128])
        xT = ext_in("xT", [D, HT])
        outT = ext_out("outT", [D, HT])
        x1T, h1T = internal("x1T", [D, HT]), internal("h1T", [D, HT])
        if go():
            emit_LA(nc, P, mk_io("A1_", {"xT": xT, "yT": x1T, "hT": h1T}), HT, False)
        h1g = _Gathered(nc, "h1g", h1T, D, HT)
        if go():
            _emit_allgather(nc, P, [h1g])
        hTp = internal("hTp", [D, T + 1])
        with (P.stage() if go() else _Skip(P)):
            z_sb = P.sb("z_sb", [128, KC, 1], F32)
            P.pool(lambda e: e.memset(z_sb[:, :, :], 0.0), w=["z"])
            P.add("sp", lambda e: e.dma_start(out=hTp.rearrange("(kc p) t -> p kc t", p=128)[:, :, 0:1],
                                              in_=z_sb[:, :, :], allow_slow_non_contiguous=True),
                  r=["z"], w=["hTp"], dma=True)
            for j in range(2):
                for kc in range(KC):
                    P.dma("sp", hTp[kc * 128:(kc + 1) * 128, 1 + j * HT:1 + (j + 1) * HT],
                          h1g.rows(j, kc * 128, (kc + 1) * 128), w=["hTp"])
        yg_tok = internal("yg_tok", [T, 512])
        if go():
            emit_RW(nc, P, mk_io("RW_", {"hTp": hTp, "yg_tok": yg_tok}), T)
        ygg = _Gathered(nc, "ygg", yg_tok, T, 512)
        if go():
            _emit_allgather(nc, P, [ygg])
        aT2 = internal("aT2", [D, HT])

        def tok2feat_jobs(g, dst):
            jobs = []
            for j in range(2):
                for tb in range(HT // 128):
                    A = g.rows(j, tb * 128, (tb + 1) * 128)
                    B = g.rows(j, HT + tb * 128, HT + (tb + 1) * 128)
                    dd = dst[j * 512:(j + 1) * 512, tb * 128:(tb + 1) * 128].rearrange("(c p) t -> p c t", p=128)
                    jobs.append((A, B, dd))
            return jobs

        if go():
            _emit_select(nc, P, sel, tok2feat_jobs(ygg, aT2), 512, ident=ident)
        x3T, hkvT = internal("x3T", [D, HT]), internal("hkvT", [D, HT])
        if go():
            emit_LA(nc, P, mk_io("A2_", {"xT": x1T, "aT": aT2, "yT": x3T, "hT": hkvT}), HT, True)
        kT, v_tok = internal("kT", [D, HT]), internal("v_tok", [HT, D])
        if go():
            emit_LP(nc, P, mk_io("P1_", {"hT": hkvT, "nT": kT, "v_tok": v_tok}), HT, True)
        x4T, h2T = internal("x4T", [D, HT]), internal("h2T", [D, HT])
        if go():
            emit_LA(nc, P, mk_io("A3_", {"xT": x3T, "yT": x4T, "hT": h2T}), HT, False)
        qT = internal("qT", [D, HT])
        if go():
            emit_LP(nc, P, mk_io("P2_", {"hT": h2T, "nT": qT}), HT, False)
        qg, kg, vg = _Gathered(nc, "qg", qT, D, HT), _Gathered(nc, "kg", kT, D, HT), _Gathered(nc, "vg", v_tok, HT, D)
        if go():
            _emit_allgather(nc, P, [qg, kg, vg])
        o_tok = internal("o_tok", [T, 512])
        if go():
            emit_ATT(nc, P, mk_io("AT_", {"o_tok": o_tok}), T, 4, gath=(qg, kg, vg, sel))
        og = _Gathered(nc, "og", o_tok, T, 512)
        if go():
            _emit_allgather(nc, P, [og])
        aT4 = internal("aT4", [D, HT])
        if go():
            _emit_select(nc, P, sel, tok2feat_jobs(og, aT4), 512, ident=ident)
        hdum = internal("hdum", [D, HT])
        if go():
            emit_LA(nc, P, mk_io("A4_", {"xT": x4T, "aT": aT4, "yT": outT, "hT": hdum}), HT, True)
    return nc


def kernel(**inp):
    p = {k: np.asarray(v, dtype=np.float32) for k, v in inp.items()}
    x = p["x"]
    B, T, _ = x.shape
    HT = T // 2
    import os
    nc = _prog(("FUSED", T), lambda: build_FUSED(T, int(os.environ.get("FUSED_UPTO", "99"))))
    shared = {"SEL_ident": np.eye(128, dtype=np.float32)}

    def la(prefix, l, i, g2, w_add=None):
        shared[prefix + "gains"] = _gains(p["ffn_norm"][l, i], g2)
        shared[prefix + "w_in"] = _tile_win(p["ffn_w_in"][l, i])
        shared[prefix + "w_out"] = _tile_wout(p["ffn_w_out"][l, i])
        if w_add is not None:
            shared[prefix + "w_add"] = _tile_sq(w_add)

    la("A1_", 0, 0, p["mix_norm"][0])
    la("A2_", 0, 1, p["kv_norm"], p["rwkv_w_o"][0])
    la("A3_", 1, 0, p["mix_norm"][1])
    la("A4_", 1, 1, p["ffn_norm"][1, 1], p["diff_w_o"][0])
    shared["P1_wn"] = _tile_sq(p["w_kv"][:, :D])
    shared["P1_gn"] = np.ascontiguousarray(np.tile(p["k_norm"], 2).reshape(128, 1))
    shared["P1_wv"] = np.ascontiguousarray(p["w_kv"][:, D:].reshape(8, 128, D).transpose(1, 0, 2))
    shared["P2_wn"] = _tile_sq(p["diff_w_q"][0])
    shared["P2_gn"] = np.ascontiguousarray(np.tile(p["diff_q_norm"][0], 2).reshape(128, 1))
    shared["AT_lam"] = np.ascontiguousarray(p["diff_lambda"][0].reshape(1, 256))
    shared["AT_subg"] = np.ascontiguousarray(np.broadcast_to(p["diff_subln"][0], (128, 128)))
    dummy_h = np.zeros((D, 1), np.float32)
    maps = []
    for c in range(NCORES):
        b, hh = c // 2, c % 2
        m = dict(shared)
        m["xT"] = np.ascontiguousarray(x[b, hh * HT:(hh + 1) * HT].T)
        s_ = np.zeros((128, 2), np.float32)
        s_[:, hh] = 1.0
        m["sel"] = s_
        rw = rw_inputs(dummy_h, hh, p)
        del rw["hTp"]
        for k_, v_ in rw.items():
            m["RW_" + k_] = v_
        qaug, kaug, btab, tri = att_consts(T, [hh * 4 + i for i in range(4)])
        m.update({"AT_qaug": qaug, "AT_kaug": kaug, "AT_btab": btab, "AT_tri": tri})
        maps.append(m)
    res = _run(nc, maps)
    out = np.empty((B, T, D), np.float32)
    for c in range(NCORES):
        out[c // 2, (c % 2) * HT:(c % 2 + 1) * HT] = res[c]["outT"].T
    return out
```

```python
import numpy as np
from contextlib import ExitStack
import concourse.bass as bass
import concourse.mybir as mybir
from concourse.bass_utils import run_bass_kernel_spmd

F32 = mybir.dt.float32
BF16 = mybir.dt.bfloat16
ALU = mybir.AluOpType
AF = mybir.ActivationFunctionType
AX = mybir.AxisListType

NCORES = 8
SEM_CAP = 8192


class _Op:
    __slots__ = ("eng", "fn", "deps", "dma", "signal", "sem", "val", "idx")


class Prog:
    ENGS = ("pe", "act", "dve", "pool", "sp")

    def __init__(self, nc, es, n_dma_sems=12):
        self.nc = nc
        self.sem_es = es
        self.es = es
        self.ops = []
        self.lastw = {}
        self.readers = {}
        self.n_dma_sems = n_dma_sems
        self.uid = 0
        self.psum_keys = set()
        self.emitted = 0
        self.cnt = {e: 0 for e in self.ENGS}
        self.dma_cnt = [0] * (2 * n_dma_sems)
        self.dma_last = [None] * (2 * n_dma_sems)
        self.n_dma = {"sp": 0, "pool": 0}
        self.n_cc = 0
        self.sems = {}
        self.waited = {e: {} for e in self.ENGS}
        self.nstage = 0

    def sb(self, name, shape, dt):
        return self.es.enter_context(self.nc.sbuf_tensor("g%d_%s" % (self.nstage, name), list(shape), dt))

    def ps(self, name, shape, dt=F32):
        return self.es.enter_context(self.nc.psum_tensor("g%d_%s" % (self.nstage, name), list(shape), dt))

    def _sem(self, key):
        if key not in self.sems:
            self.sems[key] = self.sem_es.enter_context(self.nc.semaphore("s_%s_%s" % key))
        return self.sems[key]

    class _Stage:
        def __init__(self, P):
            self.P = P

        def __enter__(self):
            self.P.es = ExitStack()
            self.P.es.__enter__()
            return self.P

        def __exit__(self, *a):
            if a[0] is None:
                self.P.emit_stage()
            self.P.es.__exit__(*a)
            self.P.es = self.P.sem_es
            return False

    def stage(self):
        return Prog._Stage(self)

    def add(self, eng, fn, r=(), w=(), dma=False):
        op = _Op()
        op.eng, op.fn, op.dma = eng, fn, dma
        op.idx = len(self.ops)
        op.signal = False
        op.sem = op.val = None
        deps = {}
        for k in r:
            d = self.lastw.get(k)
            if d is not None:
                deps[d] = True
        for k in r:
            if k in self.psum_keys:
                for rd in self.readers.get(k, ()):
                    if self.ops[rd].eng != eng:
                        deps[rd] = True
        for k in w:
            d = self.lastw.get(k)
            if d is not None:
                deps[d] = True
            for rd in self.readers.get(k, ()):
                if rd not in deps:
                    deps[rd] = False
        for k in r:
            lst = self.readers.setdefault(k, [])
            if not dma:
                lst[:] = [x for x in lst if self.ops[x].dma or self.ops[x].eng != eng]
            lst.append(op.idx)
        for k in w:
            self.lastw[k] = op.idx
            self.readers[k] = []
        op.deps = deps
        self.ops.append(op)
        return op

    def pe(self, fn, r=(), w=()):
        return self.add("pe", fn, r, w)

    def act(self, fn, r=(), w=()):
        return self.add("act", fn, r, w)

    def dve(self, fn, r=(), w=()):
        return self.add("dve", fn, r, w)

    def pool(self, fn, r=(), w=()):
        return self.add("pool", fn, r, w)

    def dma(self, q, out, in_, r=(), w=()):
        return self.add(q, lambda e: e.dma_start(out=out, in_=in_), r, w, dma=True)

    def finalize(self):
        self.emit_stage()

    def emit_stage(self):
        nc, ops = self.nc, self.ops
        s0 = self.emitted
        stage_ops = ops[s0:]
        self.emitted = len(ops)
        self.nstage += 1
        self.lastw, self.readers = {}, {}
        if not stage_ops:
            return
        for op in stage_ops:
            op.deps = {d: st for d, st in op.deps.items() if d >= s0}
            for d, strict in op.deps.items():
                p = ops[d]
                if p.dma:
                    continue
                if p.eng != op.eng or op.dma:
                    p.signal = True
                elif strict and p.eng != "pe":
                    p.signal = True
        NS2 = 2 * self.n_dma_sems
        for op in stage_ops:
            if op.dma == "cc":
                self.n_cc += 1
                op.sem, op.val = ("cc", 0), self.n_cc
            elif op.dma:
                j = self.n_dma[op.eng] % self.n_dma_sems + (self.n_dma_sems if op.eng == "pool" else 0)
                self.n_dma[op.eng] += 1
                self.dma_cnt[j] += 1
                op.sem, op.val = ("dma", j), 16 * self.dma_cnt[j]
                if self.dma_last[j] is not None and self.dma_last[j] >= s0:
                    op.deps[self.dma_last[j]] = True
                self.dma_last[j] = op.idx
            elif op.signal:
                t = self.cnt[op.eng]
                self.cnt[op.eng] += 1
                op.sem, op.val = (op.eng, t // SEM_CAP), t % SEM_CAP + 1
        per_eng = {e: [] for e in self.ENGS}
        for op in stage_ops:
            per_eng[op.eng].append(op)
        final = {}
        for op in stage_ops:
            if op.dma:
                final[op.sem] = max(final.get(op.sem, 0), op.val)
        sems = self._sem

        def emit(e, eng):
            waited = self.waited[eng]
            for op in per_eng[eng]:
                need = {}
                for d, strict in op.deps.items():
                    p = ops[d]
                    if (not p.dma) and p.eng == eng and not op.dma:
                        if not strict or eng == "pe":
                            continue
                    if p.sem is None:
                        continue
                    if need.get(p.sem, 0) < p.val:
                        need[p.sem] = p.val
                for sk, v in need.items():
                    if waited.get(sk, 0) < v:
                        e.wait_ge(sems(sk), v)
                        waited[sk] = v
                ins = op.fn(e)
                if op.dma == "cc":
                    ins.then_inc(sems(op.sem), 1)
                elif op.dma:
                    ins.then_inc(sems(op.sem), 16)
                elif op.signal:
                    ins.then_inc(sems(op.sem), 1)
            if eng == "sp":
                for sk, v in final.items():
                    if waited.get(sk, 0) < v:
                        e.wait_ge(sems(sk), v)
                        waited[sk] = v

        with nc.Block() as block:
            @block.tensor
            def _(e):
                emit(e, "pe")

            @block.scalar
            def _(e):
                emit(e, "act")

            @block.vector
            def _(e):
                emit(e, "dve")

            @block.gpsimd
            def _(e):
                emit(e, "pool")

            @block.sync
            def _(e):
                emit(e, "sp")


D = 1024
KC = 8
FF = 2816
FC = 22
EPS = 1e-6


def _rmsnorm(P, nc, x_sb, xkey, h_sb, hkey, g_sb, gcol0, sq_sb, ones_sb, ps_ss, rstd_sb, t0, tn, tag, o0=None):
    kq = "sq" + tag
    if o0 is None:
        o0 = t0
    for kc in range(KC):
        P.act(lambda e, kc=kc: e.activation(out=sq_sb[:, kc, 0:tn], in_=x_sb[:, kc, t0:t0 + tn], func=AF.Square),
              r=[xkey], w=[kq])
    for kc in range(KC):
        P.pe(lambda e, kc=kc: e.matmul(ps_ss[:, 0:tn], lhsT=ones_sb[:, :], rhs=sq_sb[:, kc, 0:tn],
                                       start=(kc == 0), stop=(kc == KC - 1)),
             r=[kq, "ones"], w=["ps_ss"])
    P.act(lambda e: e.activation(out=rstd_sb[:, 0:tn], in_=ps_ss[:, 0:tn], func=AF.Sqrt, bias=EPS, scale=1.0),
          r=["ps_ss"], w=["rstd"])
    P.dve(lambda e: e.reciprocal(out=rstd_sb[:, 0:tn], in_=rstd_sb[:, 0:tn]), r=["rstd"], w=["rstd"])
    for kc in range(KC):
        P.dve(lambda e, kc=kc: e.scalar_tensor_tensor(out=h_sb[:, kc, o0:o0 + tn], in0=x_sb[:, kc, t0:t0 + tn],
                                                      scalar=g_sb[:, gcol0 + kc:gcol0 + kc + 1],
                                                      in1=rstd_sb[:, 0:tn], op0=ALU.mult, op1=ALU.mult),
              r=[xkey, "rstd", "gains"], w=[hkey])


def emit_LA(nc, P, io, NT, has_add, TB=1024, TT=512):
    NWB, NWI, NWA = 4, 6, 3
    xT = io("xT", [D, NT], "in")
    gains = io("gains", [128, 2 * KC], "in")
    w_in = io("w_in", [FC, 128, KC, 256], "in")
    w_out = io("w_out", [KC, 128, FC, 128], "in")
    if has_add:
        aT = io("aT", [D, NT], "in")
        w_add = io("w_add", [KC, 128, KC, 128], "in")
    yT = io("yT", [D, NT], "out")
    hT = io("hT", [D, NT], "out")
    xT_v = xT.rearrange("(kc p) t -> p kc t", p=128)
    yT_v = yT.rearrange("(kc p) t -> p kc t", p=128)
    hT_v = hT.rearrange("(kc p) t -> p kc t", p=128)
    NB = NT // TB
    NS = TB // TT
    with P.stage():
        x_sb = P.sb("x_sb", [128, KC, TB], F32)
        h_sb = P.sb("h_sb", [128, KC, TB], BF16)
        act_sb = P.sb("act_sb", [128, FC, TB], BF16)
        sq_sb = P.sb("sq_sb", [128, KC, TT], BF16)
        ho_sb = P.sb("ho_sb", [128, KC, TT], F32)
        rstd_sb = P.sb("rstd_sb", [128, TT], F32)
        silu_sb = [P.sb("silu_sb%d" % i, [128, TT], F32) for i in range(2)]
        g_sb = P.sb("g_sb", [128, 2 * KC], F32)
        ones_sb = P.sb("ones_sb", [128, 128], BF16)
        win_sb = [P.sb("win_sb%d" % i, [128, KC, 256], BF16) for i in range(NWI)]
        wout_sb = [P.sb("wout_sb%d" % i, [128, FC, 128], BF16) for i in range(NWB)]
        if has_add:
            a_sb = P.sb("a_sb", [128, KC, TB], BF16)
            wadd_sb = [P.sb("wadd_sb%d" % i, [128, KC, 128], BF16) for i in range(NWA)]
        ps_g = [P.ps("ps_g%d" % i, [128, TT]) for i in range(2)]
        ps_u = [P.ps("ps_u%d" % i, [128, TT]) for i in range(2)]
        ps_o = [P.ps("ps_o%d" % i, [128, TT]) for i in range(2)]
        ps_ss = P.ps("ps_ss", [128, TT])
        P.psum_keys.update(["ps_g0", "ps_g1", "ps_u0", "ps_u1", "ps_o0", "ps_o1", "ps_ss"])

        P.dma("sp", g_sb[:, :], gains[:, :], w=["gains"])
        P.pool(lambda e: e.memset(ones_sb[:, :], 1.0 / D), w=["ones"])
        nw = [0, 0, 0]
        for b in range(NB):
            tb0 = b * TB
            for kc in range(KC):
                P.dma("sp", x_sb[:, kc, :], xT_v[:, kc, tb0:tb0 + TB], w=["x"])
            if has_add:
                for kc in range(KC):
                    P.dma("pool", a_sb[:, kc, :], aT.rearrange("(kc p) t -> p kc t", p=128)[:, kc, tb0:tb0 + TB],
                          w=["a"])
                for oc in range(KC):
                    s = nw[2] % NWA
                    nw[2] += 1
                    P.dma("pool", wadd_sb[s][:, :, :], w_add[oc], w=["wadd%d" % s])
                    for st in range(NS):
                        t0 = st * TT
                        pb = ps_o[(oc * NS + st) % 2]
                        pk = "ps_o%d" % ((oc * NS + st) % 2)
                        for kc in range(KC):
                            P.pe(lambda e, kc=kc, s=s, pb=pb, t0=t0: e.matmul(
                                pb[:, :], lhsT=wadd_sb[s][:, kc, :], rhs=a_sb[:, kc, t0:t0 + TT],
                                start=(kc == 0), stop=(kc == KC - 1)), r=["wadd%d" % s, "a"], w=[pk])
                        P.dve(lambda e, oc=oc, pb=pb, t0=t0: e.tensor_tensor(
                            out=x_sb[:, oc, t0:t0 + TT], in0=pb[:, :], in1=x_sb[:, oc, t0:t0 + TT], op=ALU.add),
                            r=[pk, "x"], w=["x"])
            for st in range(NS):
                _rmsnorm(P, nc, x_sb, "x", h_sb, "h", g_sb, 0, sq_sb, ones_sb, ps_ss, rstd_sb, st * TT, TT, "")
            for j in range(FC):
                s = nw[0] % NWI
                nw[0] += 1
                P.dma("pool", win_sb[s][:, :, :], w_in[j], w=["win%d" % s])
                for st in range(NS):
                    t0 = st * TT
                    q = (j * NS + st) % 2
                    for kc in range(KC):
                        P.pe(lambda e, kc=kc, s=s, q=q, t0=t0: e.matmul(
                            ps_g[q][:, :], lhsT=win_sb[s][:, kc, 0:128], rhs=h_sb[:, kc, t0:t0 + TT],
                            start=(kc == 0), stop=(kc == KC - 1)), r=["win%d" % s, "h"], w=["ps_g%d" % q])
                    for kc in range(KC):
                        P.pe(lambda e, kc=kc, s=s, q=q, t0=t0: e.matmul(
                            ps_u[q][:, :], lhsT=win_sb[s][:, kc, 128:256], rhs=h_sb[:, kc, t0:t0 + TT],
                            start=(kc == 0), stop=(kc == KC - 1)), r=["win%d" % s, "h"], w=["ps_u%d" % q])
                    P.act(lambda e, q=q: e.activation(out=silu_sb[q][:, :], in_=ps_g[q][:, :], func=AF.Silu),
                          r=["ps_g%d" % q], w=["silu%d" % q])
                    P.dve(lambda e, q=q, j=j, t0=t0: e.tensor_tensor(
                        out=act_sb[:, j, t0:t0 + TT], in0=ps_u[q][:, :], in1=silu_sb[q][:, :], op=ALU.mult),
                        r=["ps_u%d" % q, "silu%d" % q], w=["act"])
            for oc in range(KC):
                s = nw[1] % NWB
                nw[1] += 1
                P.dma("pool", wout_sb[s][:, :, :], w_out[oc], w=["wout%d" % s])
                for st in range(NS):
                    t0 = st * TT
                    q = (oc * NS + st) % 2
                    for j in range(FC):
                        P.pe(lambda e, j=j, s=s, q=q, t0=t0: e.matmul(
                            ps_o[q][:, :], lhsT=wout_sb[s][:, j, :], rhs=act_sb[:, j, t0:t0 + TT],
                            start=(j == 0), stop=(j == FC - 1)), r=["wout%d" % s, "act"], w=["ps_o%d" % q])
                    P.dve(lambda e, oc=oc, q=q, t0=t0: e.scalar_tensor_tensor(
                        out=x_sb[:, oc, t0:t0 + TT], in0=ps_o[q][:, :], scalar=0.5, in1=x_sb[:, oc, t0:t0 + TT],
                        op0=ALU.mult, op1=ALU.add), r=["ps_o%d" % q, "x"], w=["x"])
            for kc in range(KC):
                P.dma("sp", yT_v[:, kc, tb0:tb0 + TB], x_sb[:, kc, :], r=["x"], w=["yT"])
            for st in range(NS):
                _rmsnorm(P, nc, x_sb, "x", ho_sb, "ho", g_sb, KC, sq_sb, ones_sb, ps_ss, rstd_sb, st * TT, TT, "", o0=0)
                for kc in range(KC):
                    P.dma("sp", hT_v[:, kc, tb0 + st * TT:tb0 + (st + 1) * TT], ho_sb[:, kc, :], r=["ho"], w=["hT"])
        pass


def _blockones(P, t, val, key):
    P.pool(lambda e: e.memset(t[:, :], 0.0), w=[key])
    P.pool(lambda e: e.memset(t[0:64, 0:64], val), w=[key])
    P.pool(lambda e: e.memset(t[64:128, 64:128], val), w=[key])


def emit_LP(nc, P, io, NT, has_v, TB=1024, TT=512):
    hT = io("hT", [D, NT], "in")
    wn = io("wn", [KC, 128, KC, 128], "in")
    gn = io("gn", [128, 1], "in")
    nT = io("nT", [D, NT], "out")
    if has_v:
        wv = io("wv", [128, KC, D], "in")
        v_tok = io("v_tok", [NT, D], "out")
    hT_v = hT.rearrange("(kc p) t -> p kc t", p=128)
    nT_v = nT.rearrange("(kc p) t -> p kc t", p=128)
    NB, NS = NT // TB, TB // TT
    with P.stage():
        h_sb = P.sb("h_sb", [128, KC, TB], BF16)
        wn_sb = [P.sb("wn_sb%d" % i, [128, KC, 128], BF16) for i in range(2)]
        sq_sb = [P.sb("sq_sb%d" % i, [128, TT], BF16) for i in range(3)]
        rstd_sb = [P.sb("rstd_sb%d" % i, [128, TT], F32) for i in range(3)]
        o_sb = [P.sb("o_sb%d" % i, [128, TT], F32) for i in range(3)]
        g_sb = P.sb("g_sb", [128, 1], F32)
        bo_sb = P.sb("bo_sb", [128, 128], BF16)
        ps_p = [P.ps("ps_p%d" % i, [128, TT]) for i in range(3)]
        ps_s = [P.ps("ps_s%d" % i, [128, TT]) for i in range(3)]
        P.psum_keys.update(["ps_p0", "ps_p1", "ps_p2", "ps_s0", "ps_s1", "ps_s2", "ps_v0", "ps_v1"])
        if has_v:
            wv_sb = P.sb("wv_sb", [128, KC, D], BF16)
            v_sb = [P.sb("v_sb%d" % i, [128, 512], F32) for i in range(2)]
            ps_v = [P.ps("ps_v%d" % i, [128, 512]) for i in range(2)]
            for kc in range(KC):
                P.dma("pool", wv_sb[:, kc, :], wv[:, kc, :], w=["wv"])
        P.dma("sp", g_sb[:, :], gn[:, :], w=["gn"])
        _blockones(P, bo_sb, 1.0 / 64, "bo")
        it = 0
        nv = 0
        for b in range(NB):
            tb0 = b * TB
            for kc in range(KC):
                P.dma("pool", h_sb[:, kc, :], hT_v[:, kc, tb0:tb0 + TB], w=["h"])
            pend = []

            def lp_tail(u):
                oc, t0, q = u
                P.pe(lambda e, q=q: e.matmul(ps_s[q][:, :], lhsT=bo_sb[:, :], rhs=sq_sb[q][:, :],
                                             start=True, stop=True), r=["sq%d" % q, "bo"], w=["ps_s%d" % q])
                P.act(lambda e, q=q: e.activation(out=rstd_sb[q][:, :], in_=ps_s[q][:, :], func=AF.Sqrt,
                                                  bias=EPS, scale=1.0), r=["ps_s%d" % q], w=["rstd%d" % q])
                P.dve(lambda e, q=q: e.reciprocal(out=rstd_sb[q][:, :], in_=rstd_sb[q][:, :]),
                      r=["rstd%d" % q], w=["rstd%d" % q])
                P.dve(lambda e, q=q: e.scalar_tensor_tensor(
                    out=o_sb[q][:, :], in0=ps_p[q][:, :], scalar=g_sb[:, 0:1], in1=rstd_sb[q][:, :],
                    op0=ALU.mult, op1=ALU.mult), r=["ps_p%d" % q, "rstd%d" % q, "gn"], w=["o%d" % q])
                P.dma("sp", nT_v[:, oc, tb0 + t0:tb0 + t0 + TT], o_sb[q][:, :], r=["o%d" % q], w=["nT"])

            for oc in range(KC):
                s = oc % 2
                P.dma("pool", wn_sb[s][:, :, :], wn[oc], w=["wn%d" % s])
                for st in range(NS):
                    t0 = st * TT
                    q = it % 3
                    it += 1
                    for kc in range(KC):
                        P.pe(lambda e, kc=kc, s=s, q=q, t0=t0: e.matmul(
                            ps_p[q][:, :], lhsT=wn_sb[s][:, kc, :], rhs=h_sb[:, kc, t0:t0 + TT],
                            start=(kc == 0), stop=(kc == KC - 1)), r=["wn%d" % s, "h"], w=["ps_p%d" % q])
                    P.act(lambda e, q=q: e.activation(out=sq_sb[q][:, :], in_=ps_p[q][:, :], func=AF.Square),
                          r=["ps_p%d" % q], w=["sq%d" % q])
                    pend.append((oc, t0, q))
                    if len(pend) > 1:
                        lp_tail(pend.pop(0))
            while pend:
                lp_tail(pend.pop(0))
            if has_v:
                for tbk in range(TB // 128):
                    for half in range(2):
                        q = nv % 2
                        nv += 1
                        for kc in range(KC):
                            P.pe(lambda e, kc=kc, q=q, tbk=tbk, half=half: e.matmul(
                                ps_v[q][:, :], lhsT=h_sb[:, kc, tbk * 128:(tbk + 1) * 128],
                                rhs=wv_sb[:, kc, half * 512:(half + 1) * 512],
                                start=(kc == 0), stop=(kc == KC - 1)), r=["wv", "h"], w=["ps_v%d" % q])
                        P.act(lambda e, q=q: e.copy(out=v_sb[q][:, :], in_=ps_v[q][:, :]),
                              r=["ps_v%d" % q], w=["v%d" % q])
                        P.dma("sp", v_tok[tb0 + tbk * 128:tb0 + (tbk + 1) * 128, half * 512:(half + 1) * 512],
                              v_sb[q][:, :], r=["v%d" % q], w=["v_tok"])
        pass


LAM_INIT1 = 0.8 - 0.6 * float(np.exp(-0.3))
SUBLN_EPS = 1e-5
NDD = 67


def emit_ATT(nc, P, io, T, HL=4, gath=None):
    if gath is None:
        qT = io("qT", [HL * 128, T], "in")
        kT = io("kT", [HL * 128, T], "in")
        v_tok = io("v_tok", [T, HL * 128], "in")
    qaug = io("qaug", [5, T], "in")
    kaug = io("kaug", [HL, 5, T], "in")
    btab = io("btab", [128, HL * NDD], "in")
    tri = io("tri", [128, 128], "in")
    lam = io("lam", [1, 256], "in")
    subg = io("subg", [128, 128], "in")
    o_tok = io("o_tok", [T, HL * 128], "out")
    NQB = T // 512
    NKB = T // 128
    with P.stage():
        q_sbs = [P.sb("q_sb%d" % i, [69, 2, T], BF16) for i in range(2)]
        k_sbs = [P.sb("k_sb%d" % i, [69, 2, T], BF16) for i in range(2)]
        v_sbs = [P.sb("v_sb%d" % i, [128, NKB, 130], BF16) for i in range(2)]
        bt_sb = P.sb("bt_sb", [128, HL * NDD], F32)
        tri_sb = P.sb("tri_sb", [128, 128], BF16)
        z_sb = P.sb("z_sb", [128, 512], BF16)
        pt_sb = [P.sb("pt_sb%d" % i, [128, 512], BF16) for i in range(4)]
        lam_sb = P.sb("lam_sb", [1, 256], F32)
        lt_sb = P.sb("lt_sb", [1, 128], F32)
        ls_sb = P.sb("ls_sb", [1, 4], F32)
        one_row = P.sb("one_row", [1, 128], F32)
        nl_sb = P.sb("nl_sb", [128, 1], F32)
        eps_sb = P.sb("eps_sb", [128, 1], F32)
        sg_sb = P.sb("sg_sb", [128, 128], F32)
        rec_sb = [P.sb("rec_sb%d" % i, [128, 4], F32) for i in range(2)]
        o0_sb = [P.sb("o0_sb%d" % i, [128, 128], F32) for i in range(2)]
        od_sb = [P.sb("od_sb%d" % i, [128, 128], F32) for i in range(2)]
        junk_sb = [P.sb("junk_sb%d" % i, [128, 128], F32) for i in range(2)]
        out_sb = [P.sb("out_sb%d" % i, [128, 128], F32) for i in range(2)]
        oc_sb = [P.sb("oc_sb%d" % i, [128, 512], F32) for i in range(3)]
        ps_S = [P.ps("ps_S%d" % i, [128, 512]) for i in range(4)]
        ps_O = [P.ps("ps_O%d" % i, [128, 512]) for i in range(3)]
        ps_m = P.ps("ps_m", [128, 512])
        P.psum_keys.update(["S0", "S1", "S2", "S3", "O0", "O1", "O2", "ps_m"])

        P.dma("sp", bt_sb[:, :], btab[:, :], w=["bt"])
        P.dma("pool", tri_sb[:, :], tri[:, :], w=["tri"])
        P.dma("sp", lam_sb[:, :], lam[:, :], w=["lam"])
        P.dma("sp", sg_sb[:, :], subg[:, :], w=["sg"])
        P.pool(lambda e: e.memset(z_sb[:, :], 0.0), w=["z"])
        P.pool(lambda e: e.memset(eps_sb[:, :], SUBLN_EPS), w=["epsc"])
        P.pool(lambda e: e.memset(one_row[:, :], 1.0), w=["one_row"])
        P.pool(lambda e: e.memset(v_sbs[0][:, :, 128:130], 1.0), w=["vones"])
        P.pool(lambda e: e.memset(v_sbs[1][:, :, 128:130], 1.0), w=["vones"])
        P.dve(lambda e: e.tensor_tensor(out=lt_sb[:, 0:64], in0=lam_sb[:, 0:64], in1=lam_sb[:, 64:128], op=ALU.mult),
              r=["lam"], w=["lt"])
        P.dve(lambda e: e.tensor_tensor(out=lt_sb[:, 64:128], in0=lam_sb[:, 128:192], in1=lam_sb[:, 192:256],
                                        op=ALU.mult), r=["lam"], w=["lt"])
        P.dve(lambda e: e.tensor_reduce(out=ls_sb[:, 0:1], in_=lt_sb[:, 0:64], axis=AX.X, op=ALU.add),
              r=["lt"], w=["ls"])
        P.dve(lambda e: e.tensor_reduce(out=ls_sb[:, 1:2], in_=lt_sb[:, 64:128], axis=AX.X, op=ALU.add),
              r=["lt"], w=["ls"])
        P.act(lambda e: e.activation(out=ls_sb[:, 0:2], in_=ls_sb[:, 0:2], func=AF.Exp), r=["ls"], w=["ls"])
        P.dve(lambda e: e.scalar_tensor_tensor(out=ls_sb[:, 2:3], in0=ls_sb[:, 1:2], scalar=-LAM_INIT1,
                                               in1=ls_sb[:, 0:1], op0=ALU.add, op1=ALU.subtract),
              r=["ls"], w=["ls"])
        P.pe(lambda e: e.matmul(ps_m[:, 0:1], lhsT=one_row[:, :], rhs=ls_sb[:, 2:3], start=True, stop=True),
             r=["ls", "one_row"], w=["ps_m"])
        P.dve(lambda e: e.tensor_copy(out=nl_sb[:, :], in_=ps_m[:, 0:1]), r=["ps_m"], w=["nl"])
        P.dve(lambda e: e.tensor_scalar(out=sg_sb[:, :], in0=sg_sb[:, :], scalar1=1.0 - LAM_INIT1, scalar2=None,
                                        op0=ALU.mult), r=["sg"], w=["sg"])

        def acc(a):
            return ps_O[a // 3], (a % 3) * 132

        sl = 0
        fin = 0
        if gath is not None:
            qg_, kg_, vg_, sel_ = gath
            HT_ = T // 2
            sel_sb = P.sb("sel_sb", [128, 2], F32)
            P.dma("sp", sel_sb[:, :], sel_[:, :], w=["sel"])
            stg_sb = [P.sb("stg_sb%d" % i, [128, HT_], BF16) for i in range(2)]
            nst = [0]

            def blend(dst, A, B, npart, key):
                q_ = nst[0] % 2
                nst[0] += 1
                P.dma("pool", dst, A, w=[key])
                n = dst.shape[1] if len(dst.shape) == 2 else dst.shape[1] * dst.shape[2]
                st = stg_sb[q_][0:npart, 0:n]
                if len(dst.shape) == 3:
                    st = st.rearrange("p (a b) -> p a b", b=dst.shape[2])
                P.dma("pool", st, B, w=["stg%d" % q_])
                P.dve(lambda e: e.tensor_scalar(out=dst, in0=dst, scalar1=sel_sb[0:npart, 0:1], scalar2=None,
                                                op0=ALU.mult), r=[key, "sel"], w=[key])
                P.dve(lambda e: e.scalar_tensor_tensor(out=dst, in0=st, scalar=sel_sb[0:npart, 1:2], in1=dst,
                                                       op0=ALU.mult, op1=ALU.add), r=[key, "stg%d" % q_, "sel"], w=[key])

        def load_head(hl):
            hb = hl % 2
            for c in range(2):
                r0 = hl * 128 + c * 64
                if gath is None:
                    P.dma("pool", q_sbs[hb][0:64, c, :], qT[r0:r0 + 64, :], w=["q%d" % hb])
                    P.dma("pool", k_sbs[hb][0:64, c, :], kT[r0:r0 + 64, :], w=["k%d" % hb])
                else:
                    for j in range(2):
                        cs = slice(j * HT_, (j + 1) * HT_)
                        blend(q_sbs[hb][0:64, c, cs], qg_.rows(j, r0, r0 + 64), qg_.rows(j, 512 + r0, 512 + r0 + 64),
                              64, "q%d" % hb)
                        blend(k_sbs[hb][0:64, c, cs], kg_.rows(j, r0, r0 + 64), kg_.rows(j, 512 + r0, 512 + r0 + 64),
                              64, "k%d" % hb)
                P.dma("pool", q_sbs[hb][64:69, c, :], qaug[:, :], w=["q%d" % hb])
                P.dma("pool", k_sbs[hb][64:69, c, :], kaug[hl], w=["k%d" % hb])
            if gath is None:
                P.dma("pool", v_sbs[hb][:, :, 0:128],
                      v_tok.rearrange("(blk p) v -> p blk v", p=128)[:, :, hl * 128:(hl + 1) * 128], w=["v%d" % hb])
            else:
                nb_ = HT_ // 128
                for j in range(2):
                    rc = vg_.RC
                    for ch in range(HT_ // rc):
                        nbc = rc // 128
                        blk0 = j * nb_ + ch * nbc
                        rows = vg_.rows(j, ch * rc, (ch + 1) * rc).rearrange("(blk p) v -> p blk v", p=128)
                        blend(v_sbs[hb][:, blk0:blk0 + nbc, 0:128], rows[:, :, hl * 128:(hl + 1) * 128],
                              rows[:, :, 512 + hl * 128:512 + (hl + 1) * 128], 128, "v%d" % hb)

        load_head(0)
        for hl in range(HL):
            hb = hl % 2
            q_sb, k_sb, v_sb = q_sbs[hb], k_sbs[hb], v_sbs[hb]
            qk_, kk_, vk_ = "q%d" % hb, "k%d" % hb, "v%d" % hb
            if hl + 1 < HL:
                load_head(hl + 1)
            for qb in range(NQB):
                for bk in range(3):
                    P.pe(lambda e, bk=bk: e.matmul(ps_O[bk][:, :], lhsT=z_sb[:, 0:128], rhs=z_sb[:, :],
                                                   start=True, stop=False), r=["z"], w=["O%d" % bk])
                nkb = 4 * qb + 4
                units = [(kb, c) for kb in range(nkb) for c in range(2)]
                pend = []

                def emit_pv(u, qb=qb, v_sb=v_sb, vk=vk_):
                    kb, c, s, j0 = u
                    for jj in range(j0 // 128, 4):
                        a = jj * 2 + c
                        pb, off = acc(a)
                        P.pe(lambda e, s=s, jj=jj, kb=kb, pb=pb, off=off, last=(kb == 4 * qb + jj): e.matmul(
                            pb[:, off:off + 129], lhsT=pt_sb[s][:, jj * 128:(jj + 1) * 128],
                            rhs=v_sb[:, kb, 0:129], start=False, stop=last),
                            r=["pt%d" % s, vk, "vones"], w=["O%d" % (a // 3)])

                for kb, c in units:
                    rr = kb - 4 * qb
                    j0 = max(0, 128 * rr)
                    s = sl % 4
                    sl += 1
                    P.pe(lambda e, s=s, c=c, kb=kb, qb=qb, j0=j0, k_sb=k_sb, q_sb=q_sb: e.matmul(
                        ps_S[s][:, j0:512], lhsT=k_sb[0:69, c, kb * 128:(kb + 1) * 128],
                        rhs=q_sb[0:69, c, qb * 512 + j0:(qb + 1) * 512], start=True, stop=True),
                        r=[qk_, kk_], w=["S%d" % s])
                    P.act(lambda e, s=s, j0=j0: e.activation(
                        out=pt_sb[s][:, j0:512], in_=ps_S[s][:, j0:512], func=AF.Exp, scale=0.125),
                        r=["S%d" % s], w=["pt%d" % s])
                    if rr >= 0:
                        P.dve(lambda e, s=s, j0=j0: e.tensor_tensor(
                            out=pt_sb[s][:, j0:j0 + 128], in0=pt_sb[s][:, j0:j0 + 128], in1=tri_sb[:, :],
                            op=ALU.mult), r=["pt%d" % s, "tri"], w=["pt%d" % s])
                    pend.append((kb, c, s, j0))
                    if len(pend) > 2:
                        emit_pv(pend.pop(0))
                while pend:
                    emit_pv(pend.pop(0))
                for bk in range(3):
                    P.dve(lambda e, bk=bk: e.tensor_copy(out=oc_sb[bk][:, :], in_=ps_O[bk][:, :]),
                          r=["O%d" % bk], w=["oc%d" % bk])
                for jj in range(4):
                    f = fin % 2
                    fin += 1
                    pb0, off0 = oc_sb[(jj * 2) // 3], ((jj * 2) % 3) * 132
                    pb1, off1 = oc_sb[(jj * 2 + 1) // 3], ((jj * 2 + 1) % 3) * 132
                    k0, k1 = "oc%d" % ((jj * 2) // 3), "oc%d" % ((jj * 2 + 1) // 3)
                    P.dve(lambda e, f=f, pb0=pb0, off0=off0: e.reciprocal(
                        out=rec_sb[f][:, 0:1], in_=pb0[:, off0 + 128:off0 + 129]), r=[k0], w=["rec%d" % f])
                    P.dve(lambda e, f=f, pb1=pb1, off1=off1: e.reciprocal(
                        out=rec_sb[f][:, 1:2], in_=pb1[:, off1 + 128:off1 + 129]), r=[k1], w=["rec%d" % f])
                    P.dve(lambda e, f=f: e.tensor_tensor(out=rec_sb[f][:, 2:3], in0=rec_sb[f][:, 1:2],
                                                         in1=nl_sb[:, 0:1], op=ALU.mult),
                          r=["rec%d" % f, "nl"], w=["rec%d" % f])
                    P.dve(lambda e, f=f, pb0=pb0, off0=off0: e.tensor_scalar(
                        out=o0_sb[f][:, :], in0=pb0[:, off0:off0 + 128], scalar1=rec_sb[f][:, 0:1], scalar2=None,
                        op0=ALU.mult), r=[k0, "rec%d" % f], w=["o0%d" % f])
                    P.dve(lambda e, f=f, pb1=pb1, off1=off1: e.scalar_tensor_tensor(
                        out=od_sb[f][:, :], in0=pb1[:, off1:off1 + 128], scalar=rec_sb[f][:, 2:3],
                        in1=o0_sb[f][:, :], op0=ALU.mult, op1=ALU.add),
                        r=[k1, "rec%d" % f, "o0%d" % f], w=["od%d" % f])
                    P.dve(lambda e, f=f: e.tensor_tensor(out=junk_sb[f][:, :], in0=od_sb[f][:, :], in1=od_sb[f][:, :],
                                                         op=ALU.mult), r=["od%d" % f], w=["junk%d" % f])
                    P.dve(lambda e, f=f: e.tensor_reduce(out=rec_sb[f][:, 3:4], in_=junk_sb[f][:, :], axis=AX.X,
                                                         op=ALU.add), r=["junk%d" % f], w=["ss%d" % f])
                    P.act(lambda e, f=f: e.activation(out=rec_sb[f][:, 3:4], in_=rec_sb[f][:, 3:4], func=AF.Ln,
                                                      bias=eps_sb[:, 0:1], scale=1.0 / 128),
                          r=["ss%d" % f, "epsc"], w=["ss%d" % f])
                    P.act(lambda e, f=f: e.activation(out=rec_sb[f][:, 3:4], in_=rec_sb[f][:, 3:4], func=AF.Exp,
                                                      scale=-0.5), r=["ss%d" % f], w=["ss%d" % f])
                    P.dve(lambda e, f=f: e.scalar_tensor_tensor(
                        out=out_sb[f][:, :], in0=od_sb[f][:, :], scalar=rec_sb[f][:, 3:4], in1=sg_sb[:, :],
                        op0=ALU.mult, op1=ALU.mult), r=["od%d" % f, "ss%d" % f, "sg"], w=["out%d" % f])
                    P.dma("sp", o_tok[qb * 512 + jj * 128:qb * 512 + (jj + 1) * 128, hl * 128:(hl + 1) * 128],
                          out_sb[f][:, :], r=["out%d" % f], w=["o_tok"])
        pass


def att_consts(T, heads):
    t = np.arange(T)
    one = np.ones_like(t)
    qaug = np.stack([(t % 512) // 16, t % 16, one, one, t // 512]).astype(np.float32)
    kaug = np.zeros((len(heads), 5, T), np.float32)
    btab = np.zeros((128, len(heads) * NDD), np.float32)
    for i, h in enumerate(heads):
        slope = 2.0 ** (-(h + 1))
        kaug[i, 0, :] = -16.0 * slope / 0.125
        kaug[i, 1, :] = -slope / 0.125
        kaug[i, 2, :] = slope * (t % 128) / 0.125
        kaug[i, 3, :] = slope * 128.0 * (t // 128) / 0.125
        kaug[i, 4, :] = -512.0 * slope / 0.125
    tri = (np.arange(128)[:, None] <= np.arange(128)[None, :]).astype(np.float32)
    return qaug, kaug, btab, tri


C0 = float(np.exp(-0.5))
RW_LN_EPS = 64e-5
CL = 128


def emit_RW(nc, P, io, T, TT=256, LEVEL=9):
    dt_in = lambda n, s: io(n, s, "in")
    hTp = dt_in("hTp", [D, T + 1])
    wr_d, wk_d, wv_d = dt_in("wr", [D, 512]), dt_in("wk", [D, 512]), dt_in("wv", [D, 512])
    l1_d = dt_in("l1", [D, 288])
    w2_d, a2_d, g2_d = dt_in("w2c", [64, 512]), dt_in("a2c", [64, 512]), dt_in("g2c", [160, 512])
    mu_d = dt_in("mu6", [128, 48])
    cv_d = dt_in("cv", [128, 20])
    lng_d, lnb_d = dt_in("lng", [128, 512]), dt_in("lnb", [128, 512])
    ident_d, mask2_d, maskL_d = dt_in("ident", [128, 128]), dt_in("mask2", [128, 256]), dt_in("maskL", [128, 128])
    rmask_d, ind_d = dt_in("rmask", [128, TT]), dt_in("ind", [128, 2])
    yg = io("yg_tok", [T, 512], "out")
    hTp_v = hTp.rearrange("(kc p) t -> p kc t", p=128)
    NTI = T // TT
    NCL = TT // CL
    with P.stage():
        sb = P.sb
        W1 = {n: sb("W1" + n, [128, KC, 512], BF16) for n in "rkv"}
        W2 = {n: sb("W2" + n, [128, KC, 512], BF16) for n in "rkv"}
        L1a = sb("L1a", [128, KC, 288], BF16)
        L1b = sb("L1b", [128, KC, 288], BF16)
        stg = [sb("stg%d" % i, [128, 512], F32) for i in range(2)]
        w2_sb, a2_sb = sb("w2_sb", [64, 512], BF16), sb("a2_sb", [64, 512], BF16)
        g2a_sb, g2b_sb = sb("g2a_sb", [128, 512], BF16), sb("g2b_sb", [32, 512], BF16)
        mu_sb, omu_sb, cv_sb = sb("mu_sb", [128, 48], F32), sb("omu_sb", [128, 48], F32), sb("cv_sb", [128, 24], F32)
        lng_sb, lnb_sb = sb("lng_sb", [128, 512], F32), sb("lnb_sb", [128, 512], F32)
        ident_bf, identf = sb("ident_bf", [128, 128], BF16), sb("identf", [128, 128], F32)
        mask2_sb, maskL_sb = sb("mask2_sb", [128, 256], F32), sb("maskL_sb", [128, 128], F32)
        rmask_sb, ind_sb, bo_sb = sb("rmask_sb", [128, TT], F32), sb("ind_sb", [128, 2], BF16), sb("bo_sb", [128, 128], BF16)
        hp = [sb("hp%d" % i, [128, KC, TT], BF16) for i in range(2)]
        hq = [sb("hq%d" % i, [128, KC, TT], BF16) for i in range(2)]
        r32, k32 = sb("r32", [128, 4, TT], F32), sb("k32", [128, 4, TT], F32)
        vtok, v32 = sb("vtok", [128, NCL, 512], BF16), sb("v32", [128, NCL, 512], F32)
        gtok = sb("gtok", [128, NCL, 512], F32)
        lw_sb, la_sb = sb("lw_sb", [64, TT], BF16), sb("la_sb", [64, TT], BF16)
        lga_sb, lgb_sb = sb("lga_sb", [128, TT], BF16), sb("lgb_sb", [32, TT], BF16)
        sc = {n: sb("sc_" + n, [128, TT], F32) for n in
              ("sg", "a", "kk", "rn", "kkn", "tmp", "kmod", "bvec", "cs", "tmp2", "Em", "Epv")}
        kk2_sb = sb("kk2_sb", [128, TT], BF16)
        sc2 = {n: sb("sc2_" + n, [128, TT], F32) for n in sc}
        kk2b_sb = sb("kk2b_sb", [128, TT], BF16)
        Ep = sb("Ep", [128, 4, TT], F32)
        AR = sb("AR", [128, 4, NCL, 256], BF16)
        bt, kt, rk = sb("bt", [128, 4, TT], BF16), sb("kt", [128, 4, TT], BF16), sb("rk", [128, 4, TT], BF16)
        NTb_sb, NTk_sb = sb("NTb_sb", [128, 4, 512], BF16), sb("NTk_sb", [128, 4, 512], BF16)
        at2, rt2 = sb("at2", [128, 4, 2, TT], BF16), sb("rt2", [128, 4, 2, TT], BF16)
        bt2, kt2 = sb("bt2", [128, 4, 2, TT], BF16), sb("kt2", [128, 4, 2, TT], BF16)
        XX = [[sb("XX%d_%d" % (pc, i), [128, 512], BF16) for i in range(2)] for pc in range(4)]
        Pp = [[sb("Pp%d_%d" % (pc, i), [128, 256], BF16) for i in range(2)] for pc in range(4)]
        Pfin = sb("Pfin", [128, 4, 256], BF16)
        btok, ktok = sb("btok", [128, 4, 128], BF16), sb("ktok", [128, 4, 128], BF16)
        Zs, Us = sb("Zs", [128, 4, 128], BF16), sb("Us", [128, 4, 128], BF16)
        ytok = sb("ytok", [128, NCL, 512], F32)
        H32, Hbf, Htmp = sb("H32", [128, 4, 128], F32), sb("Hbf", [128, 4, 128], BF16), sb("Htmp", [128, 4, 128], F32)
        ysq, st_sb = sb("ysq", [128, 512], F32), sb("st_sb", [128, 40], F32)
        out_sb = [sb("out_sb%d" % i, [128, 512], F32) for i in range(2)]
        bk = [P.ps("bk%d" % i, [128, 512]) for i in range(5)] + [None] + [P.ps("bk%d" % i, [128, 512]) for i in (6, 7)]
        bk5 = P.ps("bk5", [128, 1024], BF16)
        BK = lambda i: "bk%d" % i
        P.psum_keys.update([BK(i) for i in range(8)])

        for dst, src, q in ((mu_sb, mu_d, "sp"), (lng_sb, lng_d, "sp"), (lnb_sb, lnb_d, "sp"),
                            (identf, ident_d, "sp"), (mask2_sb, mask2_d, "sp"), (maskL_sb, maskL_d, "sp"),
                            (rmask_sb, rmask_d, "sp"), (ident_bf, ident_d, "pool"), (ind_sb, ind_d, "pool"),
                            (w2_sb, w2_d, "pool"), (a2_sb, a2_d, "pool")):
            P.dma(q, dst[:, :], src[:, :], w=["const"])
        P.dma("sp", cv_sb[:, 0:20], cv_d[:, :], w=["const"])
        P.dma("pool", g2a_sb[:, :], g2_d[0:128, :], w=["const"])
        P.dma("pool", g2b_sb[:, :], g2_d[128:160, :], w=["const"])
        _blockones(P, bo_sb, 1.0, "bo")
        P.dve(lambda e: e.tensor_scalar(out=omu_sb[:, :], in0=mu_sb[:, :], scalar1=-1.0, scalar2=1.0,
                                        op0=ALU.mult, op1=ALU.add), r=["const"], w=["omu"])
        P.dve(lambda e: e.tensor_scalar(out=cv_sb[:, 20:24], in0=cv_sb[:, 12:16], scalar1=-1.0, scalar2=1.0,
                                        op0=ALU.mult, op1=ALU.add), r=["const"], w=["cv2"])
        P.pool(lambda e: e.memset(H32[:, :, :], 0.0), w=["H32"])
        for tz, kz in ((at2, "at2"), (rt2, "rt2"), (bt2, "bt2"), (kt2, "kt2")):
            P.pool(lambda e, tz=tz: e.memset(tz[:, :, :, :], 0.0), w=[kz])
        P.pool(lambda e: e.memset(Hbf[:, :, :], 0.0), w=["Hbf"])
        ns = 0
        for wi, (n, src) in enumerate((("r", wr_d), ("k", wk_d), ("v", wv_d))):
            for kc in range(KC):
                s = ns % 2
                ns += 1
                P.dma("sp", stg[s][:, :], src[kc * 128:(kc + 1) * 128, :], w=["stg%d" % s])
                P.dve(lambda e, s=s, n=n, kc=kc, wi=wi: e.tensor_scalar(
                    out=W1[n][:, kc, :], in0=stg[s][:, :], scalar1=omu_sb[:, wi * 8 + kc:wi * 8 + kc + 1],
                    scalar2=None, op0=ALU.mult), r=["stg%d" % s, "omu"], w=["W"])
                P.dve(lambda e, s=s, n=n, kc=kc, wi=wi: e.tensor_scalar(
                    out=W2[n][:, kc, :], in0=stg[s][:, :], scalar1=mu_sb[:, wi * 8 + kc:wi * 8 + kc + 1],
                    scalar2=None, op0=ALU.mult), r=["stg%d" % s, "const"], w=["W"])
        for kc in range(KC):
            s_ = ns % 2
            ns += 1
            P.dma("sp", stg[s_][:, 0:288], l1_d[kc * 128:(kc + 1) * 128, :], w=["stg%d" % s_])
            for li, (c0, c1) in enumerate(((0, 64), (64, 128), (128, 288))):
                wi = 3 + li
                P.dve(lambda e, kc=kc, wi=wi, c0=c0, c1=c1, s_=s_: e.tensor_scalar(
                    out=L1a[:, kc, c0:c1], in0=stg[s_][:, c0:c1], scalar1=omu_sb[:, wi * 8 + kc:wi * 8 + kc + 1],
                    scalar2=None, op0=ALU.mult), r=["stg%d" % s_, "omu"], w=["W"])
                P.dve(lambda e, kc=kc, wi=wi, c0=c0, c1=c1, s_=s_: e.tensor_scalar(
                    out=L1b[:, kc, c0:c1], in0=stg[s_][:, c0:c1], scalar1=mu_sb[:, wi * 8 + kc:wi * 8 + kc + 1],
                    scalar2=None, op0=ALU.mult), r=["stg%d" % s_, "const"], w=["W"])

        nb = [0]

        def nbk():
            nb[0] += 1
            return nb[0] % 2

        def proj_fm(pb, key, wa, wb, c0, c1, hpt, hk, n, hqt=None):
            m = c1 - c0
            hpt, hqt = hpt
            for kc in range(KC):
                P.pe(lambda e, kc=kc: e.matmul(pb[0:m, 0:n], lhsT=wa[:, kc, c0:c1], rhs=hpt[:, kc, 0:n],
                                               start=(kc == 0), stop=False), r=["W", hk], w=[key])
            for kc in range(KC):
                P.pe(lambda e, kc=kc: e.matmul(pb[0:m, 0:n], lhsT=wb[:, kc, c0:c1], rhs=hqt[:, kc, 0:n],
                                               start=False, stop=(kc == KC - 1)), r=["W", hk], w=[key])

        fo = 0
        for ti in range(NTI if LEVEL >= 1 else 0):
            t0 = ti * TT
            hs = ti % 2
            hk = "hp%d" % hs
            hp_, hq_ = hp[hs], hq[hs]
            hpt = (hp_, hq_)
            P.dma("pool", hp_[:, :, :], hTp_v[:, :, t0 + 1:t0 + TT + 1], w=[hk])
            P.dma("pool", hq_[:, :, :], hTp_v[:, :, t0:t0 + TT], w=[hk])
            for pc in range(4):
                for n, dst in (("r", r32), ("k", k32)):
                    b = nbk()
                    proj_fm(bk[b], BK(b), W1[n], W2[n], pc * 128, (pc + 1) * 128, hpt, hk, TT)
                    P.act(lambda e, b=b, dst=dst, pc=pc: e.copy(out=dst[:, pc, :], in_=bk[b][:, 0:TT]),
                          r=[BK(b)], w=[n + "32"])
            for cl in range(NCL):
                b = nbk()
                for kc in range(KC):
                    P.pe(lambda e, kc=kc, b=b, cl=cl, hpt=hp_: e.matmul(
                        bk[b][:, :], lhsT=hpt[:, kc, cl * 128:(cl + 1) * 128], rhs=W1["v"][:, kc, :],
                        start=(kc == 0), stop=False), r=["W", hk], w=[BK(b)])
                for kc in range(KC):
                    P.pe(lambda e, kc=kc, b=b, cl=cl, hpt=hq_: e.matmul(
                        bk[b][:, :], lhsT=hpt[:, kc, cl * 128:(cl + 1) * 128], rhs=W2["v"][:, kc, :],
                        start=False, stop=(kc == KC - 1)), r=["W", hk], w=[BK(b)])
                P.act(lambda e, b=b, cl=cl: e.copy(out=vtok[:, cl, :], in_=bk[b][:, :]), r=[BK(b)], w=["vtok"])
                P.act(lambda e, b=b, cl=cl: e.copy(out=v32[:, cl, :], in_=bk[b][:, :]), r=[BK(b)], w=["v32"])
            b = nbk()
            proj_fm(bk[b], BK(b), L1a, L1b, 0, 64, hpt, hk, TT)
            P.act(lambda e, b=b: e.activation(out=lw_sb[:, :], in_=bk[b][0:64, 0:TT], func=AF.Tanh),
                  r=[BK(b)], w=["lw"])
            b = nbk()
            proj_fm(bk[b], BK(b), L1a, L1b, 64, 128, hpt, hk, TT)
            P.act(lambda e, b=b: e.copy(out=la_sb[:, :], in_=bk[b][0:64, 0:TT]), r=[BK(b)], w=["la"])
            b = nbk()
            proj_fm(bk[b], BK(b), L1a, L1b, 128, 256, hpt, hk, TT)
            P.act(lambda e, b=b: e.activation(out=lga_sb[:, :], in_=bk[b][:, 0:TT], func=AF.Sigmoid),
                  r=[BK(b)], w=["lg"])
            b = nbk()
            proj_fm(bk[b], BK(b), L1a, L1b, 256, 288, hpt, hk, TT)
            P.act(lambda e, b=b: e.activation(out=lgb_sb[:, :], in_=bk[b][0:32, 0:TT], func=AF.Sigmoid),
                  r=[BK(b)], w=["lg"])
            for cl in range(NCL):
                b = nbk()
                P.pe(lambda e, b=b, cl=cl: e.matmul(bk[b][:, :], lhsT=lga_sb[:, cl * 128:(cl + 1) * 128],
                                                   rhs=g2a_sb[:, :], start=True, stop=False),
                     r=["lg", "const"], w=[BK(b)])
                P.pe(lambda e, b=b, cl=cl: e.matmul(bk[b][:, :], lhsT=lgb_sb[:, cl * 128:(cl + 1) * 128],
                                                   rhs=g2b_sb[:, :], start=False, stop=True),
                     r=["lg", "const"], w=[BK(b)])
                P.act(lambda e, b=b, cl=cl: e.copy(out=gtok[:, cl, :], in_=bk[b][:, :]), r=[BK(b)], w=["gtok"])
            def a5_gen(pc, sc, kk2_sb, sx):
                cvc = lambda w, pc=pc: cv_sb[:, w * 4 + pc:w * 4 + pc + 1]
                b = nbk()
                P.pe(lambda e, b=b, pc=pc: e.matmul(bk[b][:, 0:TT], lhsT=w2_sb[:, pc * 128:(pc + 1) * 128],
                                                   rhs=lw_sb[:, :], start=True, stop=True),
                     r=["lw", "const"], w=[BK(b)])
                yield
                P.act(lambda e, b=b, pc=pc: e.activation(out=sc["sg"][:, :], in_=bk[b][:, 0:TT], func=AF.Sigmoid,
                                                         bias=cv_sb[:, pc:pc + 1], scale=1.0),
                      r=[BK(b), "const"], w=["sg" + sx])
                yield
                b = nbk()
                P.pe(lambda e, b=b, pc=pc: e.matmul(bk[b][:, 0:TT], lhsT=a2_sb[:, pc * 128:(pc + 1) * 128],
                                                   rhs=la_sb[:, :], start=True, stop=True),
                     r=["la", "const"], w=[BK(b)])
                yield
                P.act(lambda e, b=b, pc=pc: e.activation(out=sc["a"][:, :], in_=bk[b][:, 0:TT], func=AF.Sigmoid,
                                                         bias=cv_sb[:, 4 + pc:5 + pc], scale=1.0),
                      r=[BK(b), "const"], w=["a" + sx])
                yield
                P.dve(lambda e, pc=pc: e.tensor_scalar(out=sc["kk"][:, :], in0=k32[:, pc, :],
                                                       scalar1=cv_sb[:, 8 + pc:9 + pc], scalar2=None, op0=ALU.mult),
                      r=["k32", "const"], w=["kk" + sx])
                yield
                P.act(lambda e: e.activation(out=kk2_sb[:, :], in_=sc["kk"][:, :], func=AF.Square),
                      r=["kk" + sx], w=["kk2" + sx])
                yield
                b = nbk()
                P.pe(lambda e, b=b: e.matmul(bk[b][:, 0:TT], lhsT=bo_sb[:, :], rhs=kk2_sb[:, :], start=True, stop=True),
                     r=["kk2" + sx, "bo"], w=[BK(b)])
                yield
                P.act(lambda e, b=b: e.activation(out=sc["rn"][:, :], in_=bk[b][:, 0:TT], func=AF.Sqrt,
                                                  bias=1e-24, scale=1.0), r=[BK(b)], w=["rn" + sx])
                yield
                P.dve(lambda e: e.reciprocal(out=sc["rn"][:, :], in_=sc["rn"][:, :]), r=["rn" + sx], w=["rn" + sx])
                yield
                P.dve(lambda e: e.tensor_tensor(out=sc["kkn"][:, :], in0=sc["kk"][:, :], in1=sc["rn"][:, :],
                                                op=ALU.mult), r=["kk" + sx, "rn" + sx], w=["kkn" + sx])
                yield
                P.dve(lambda e, pc=pc: e.tensor_scalar(out=sc["tmp"][:, :], in0=sc["a"][:, :],
                                                       scalar1=cv_sb[:, 12 + pc:13 + pc],
                                                       scalar2=cv_sb[:, 20 + pc:21 + pc], op0=ALU.mult, op1=ALU.add),
                      r=["a" + sx, "const", "cv2"], w=["tmp" + sx])
                yield
                P.dve(lambda e, pc=pc: e.tensor_tensor(out=sc["kmod"][:, :], in0=k32[:, pc, :], in1=sc["tmp"][:, :],
                                                       op=ALU.mult), r=["k32", "tmp" + sx], w=["kmod" + sx])
                yield
                P.dve(lambda e: e.tensor_tensor(out=sc["bvec"][:, :], in0=sc["kkn"][:, :], in1=sc["a"][:, :],
                                                op=ALU.mult), r=["kkn" + sx, "a" + sx], w=["bvec" + sx])
                yield
                P.dve(lambda e: e.tensor_tensor_scan(out=sc["cs"][:, :], data0=rmask_sb[:, :], data1=sc["sg"][:, :],
                                                     initial=0.0, op0=ALU.mult, op1=ALU.add),
                      r=["sg" + sx, "const"], w=["cs" + sx])
                yield
                P.dve(lambda e: e.tensor_tensor(out=sc["tmp2"][:, :], in0=sc["cs"][:, :], in1=sc["sg"][:, :],
                                                op=ALU.subtract), r=["cs" + sx, "sg" + sx], w=["tmp2" + sx])
                yield
                P.act(lambda e, pc=pc: e.activation(out=Ep[:, pc, :], in_=sc["cs"][:, :], func=AF.Exp, scale=-C0),
                      r=["cs" + sx], w=["Ep"])
                yield
                P.act(lambda e: e.activation(out=sc["Em"][:, :], in_=sc["cs"][:, :], func=AF.Exp, scale=C0),
                      r=["cs" + sx], w=["Em" + sx])
                yield
                P.act(lambda e: e.activation(out=sc["Epv"][:, :], in_=sc["tmp2"][:, :], func=AF.Exp, scale=-C0),
                      r=["tmp2" + sx], w=["Epv" + sx])
                yield
                for cl in range(NCL):
                    cs_ = slice(cl * 128, (cl + 1) * 128)
                    P.dve(lambda e, pc=pc, cl=cl, cs_=cs_: e.scalar_tensor_tensor(
                        out=AR[:, pc, cl, 0:128], in0=sc["kkn"][:, cs_], scalar=-1.0, in1=sc["Epv"][:, cs_],
                        op0=ALU.mult, op1=ALU.mult), r=["kkn" + sx, "Epv" + sx], w=["AR"])
                    P.dve(lambda e, pc=pc, cl=cl, cs_=cs_: e.tensor_tensor(
                        out=AR[:, pc, cl, 128:256], in0=r32[:, pc, cs_], in1=Ep[:, pc, cs_], op=ALU.mult),
                        r=["r32", "Ep"], w=["AR"])
                P.dve(lambda e, pc=pc: e.tensor_tensor(out=kt[:, pc, :], in0=sc["kmod"][:, :], in1=sc["Em"][:, :],
                                                       op=ALU.mult), r=["kmod" + sx, "Em" + sx], w=["kt"])
                yield
                P.dve(lambda e, pc=pc: e.tensor_tensor(out=bt[:, pc, :], in0=sc["bvec"][:, :], in1=sc["Em"][:, :],
                                                       op=ALU.mult), r=["bvec" + sx, "Em" + sx], w=["bt"])
                yield
                P.dve(lambda e, pc=pc: e.scalar_tensor_tensor(
                    out=rk[:, pc, :], in0=r32[:, pc, :], scalar=cv_sb[:, 16 + pc:17 + pc], in1=sc["kmod"][:, :],
                    op0=ALU.mult, op1=ALU.mult), r=["r32", "kmod" + sx, "const"], w=["rk"])
                yield
                for hd in range(2):
                    rows = slice(64 * hd, 64 * hd + 64)
                    for cl in range(NCL):
                        cs_ = slice(cl * 128, (cl + 1) * 128)
                        P.pool(lambda e, pc=pc, hd=hd, rows=rows, cl=cl, cs_=cs_: e.tensor_copy(
                            out=at2[rows, pc, hd, cs_], in_=AR[rows, pc, cl, 0:128]), r=["AR"], w=["at2"])
                        P.pool(lambda e, pc=pc, hd=hd, rows=rows, cl=cl, cs_=cs_: e.tensor_copy(
                            out=rt2[rows, pc, hd, cs_], in_=AR[rows, pc, cl, 128:256]), r=["AR"], w=["rt2"])
                    P.pool(lambda e, pc=pc, hd=hd, rows=rows: e.tensor_copy(
                        out=bt2[rows, pc, hd, :], in_=bt[rows, pc, :]), r=["bt"], w=["bt2"])
                    P.pool(lambda e, pc=pc, hd=hd, rows=rows: e.tensor_copy(
                        out=kt2[rows, pc, hd, :], in_=kt[rows, pc, :]), r=["kt"], w=["kt2"])
            if LEVEL >= 2:
                for pa in (0, 2):
                    gens = [a5_gen(pa, sc, kk2_sb, "_0"), a5_gen(pa + 1, sc2, kk2b_sb, "_1")]
                    alive = list(gens)
                    while alive:
                        for g in list(alive):
                            try:
                                next(g)
                            except StopIteration:
                                alive.remove(g)
            for cl in range(NCL if LEVEL >= 3 else 0):
                cs_ = slice(cl * 128, (cl + 1) * 128)
                for pc in range(4):
                    for hd in range(2):
                        P.pe(lambda e, pc=pc, cl=cl, hd=hd, cs_=cs_: e.matmul(
                            bk[2][:, hd * 256:(hd + 1) * 256], lhsT=bt2[:, pc, hd, cs_], rhs=AR[:, pc, cl, :],
                            start=True, stop=True), r=["bt2", "AR"], w=[BK(2)])
                        P.pe(lambda e, pc=pc, cl=cl, hd=hd, cs_=cs_: e.matmul(
                            bk[3][:, hd * 256:(hd + 1) * 256], lhsT=kt2[:, pc, hd, cs_], rhs=AR[:, pc, cl, :],
                            start=True, stop=True), r=["kt2", "AR"], w=[BK(3)])
                        P.pe(lambda e, pc=pc, cl=cl, hd=hd, cs_=cs_: e.matmul(
                            bk[4][:, hd * 128:(hd + 1) * 128], lhsT=at2[:, pc, hd, cs_], rhs=bt[:, pc, cs_],
                            start=True, stop=True), r=["bt", "at2"], w=[BK(4)])
                    xk0 = "XX%d_0" % pc
                    for hd in range(2):
                        P.dve(lambda e, pc=pc, hd=hd: e.tensor_tensor(
                            out=NTb_sb[:, pc, hd * 256:(hd + 1) * 256], in0=bk[2][:, hd * 256:(hd + 1) * 256],
                            in1=mask2_sb[:, :], op=ALU.mult), r=[BK(2), "const"], w=["NTb%d" % pc])
                        P.dve(lambda e, pc=pc, hd=hd: e.tensor_tensor(
                            out=NTk_sb[:, pc, hd * 256:(hd + 1) * 256], in0=bk[3][:, hd * 256:(hd + 1) * 256],
                            in1=mask2_sb[:, :], op=ALU.mult), r=[BK(3), "const"], w=["NTk%d" % pc])
                        P.dve(lambda e, pc=pc, hd=hd: e.tensor_tensor(
                            out=XX[pc][0][:, 256 + hd * 128:256 + (hd + 1) * 128], in0=bk[4][:, hd * 128:(hd + 1) * 128],
                            in1=maskL_sb[:, :], op=ALU.mult), r=[BK(4), "const"], w=[xk0])
                        P.pool(lambda e, pc=pc, hd=hd: e.tensor_copy(
                            out=XX[pc][0][:, hd * 128:(hd + 1) * 128], in_=NTb_sb[:, pc, hd * 256:hd * 256 + 128]),
                            r=["NTb%d" % pc], w=[xk0])
                        P.pool(lambda e, pc=pc, hd=hd: e.tensor_tensor(
                            out=Pp[pc][0][:, hd * 128:(hd + 1) * 128], in0=NTb_sb[:, pc, hd * 256:hd * 256 + 128],
                            in1=identf[:, :], op=ALU.add), r=["NTb%d" % pc, "const"], w=["Pp%d_0" % pc])
                    P.pe(lambda e, pc=pc, cs_=cs_: e.transpose(out=bk5[:, 0:128], in_=bt[:, pc, cs_],
                                                               identity=ident_bf[:, :]), r=["bt", "const"], w=[BK(5)])
                    P.pe(lambda e, pc=pc, cs_=cs_: e.transpose(out=bk5[:, 128:256], in_=kt[:, pc, cs_],
                                                               identity=ident_bf[:, :]), r=["kt", "const"], w=[BK(5)])
                    P.act(lambda e, pc=pc: e.copy(out=btok[:, pc, :], in_=bk5[:, 0:128]), r=[BK(5)], w=["btok%d" % pc])
                    P.act(lambda e, pc=pc: e.copy(out=ktok[:, pc, :], in_=bk5[:, 128:256]), r=[BK(5)], w=["ktok%d" % pc])
                cur = 0
                for step in range(6):
                    nxt = 1 - cur
                    last = step == 5
                    for pc in range(4):
                        bx, bxk = (bk[6], BK(6)) if pc % 2 == 0 else (bk[0], BK(0))
                        xc, xn = "XX%d_%d" % (pc, cur), "XX%d_%d" % (pc, nxt)
                        for hd in range(2):
                            hs_ = slice(hd * 128, (hd + 1) * 128)
                            hx_ = slice(256 + hd * 128, 256 + (hd + 1) * 128)
                            if not last:
                                P.pe(lambda e, pc=pc, cur=cur, hs_=hs_, hx_=hx_, bx=bx: e.matmul(
                                    bx[:, hs_], lhsT=XX[pc][cur][:, hx_], rhs=XX[pc][cur][:, hs_],
                                    start=True, stop=True), r=[xc], w=[bxk])
                            P.pe(lambda e, pc=pc, cur=cur, hs_=hs_, hx_=hx_, bx=bx: e.matmul(
                                bx[:, hx_], lhsT=XX[pc][cur][:, hs_], rhs=XX[pc][cur][:, hx_],
                                start=True, stop=True), r=[xc], w=[bxk])
                        if not last:
                            P.act(lambda e, pc=pc, nxt=nxt, bx=bx: e.copy(out=XX[pc][nxt][:, :], in_=bx[:, :]),
                                  r=[bxk], w=[xn])
                        else:
                            P.act(lambda e, pc=pc, nxt=nxt, bx=bx: e.copy(out=XX[pc][nxt][:, 256:512], in_=bx[:, 256:512]),
                                  r=[bxk], w=[xn])
                    for pc in range(4):
                        bp, bpk = (bk[7], BK(7)) if pc % 2 == 0 else (bk[1], BK(1))
                        xn = "XX%d_%d" % (pc, nxt)
                        pk_c, pk_n = "Pp%d_%d" % (pc, cur), "Pp%d_%d" % (pc, nxt)
                        for hd in range(2):
                            hs_ = slice(hd * 128, (hd + 1) * 128)
                            hx_ = slice(256 + hd * 128, 256 + (hd + 1) * 128)
                            P.pe(lambda e, pc=pc, cur=cur, nxt=nxt, hs_=hs_, hx_=hx_, bp=bp: e.matmul(
                                bp[:, hs_], lhsT=XX[pc][nxt][:, hx_], rhs=Pp[pc][cur][:, hs_], start=True, stop=True),
                                r=[xn, pk_c], w=[bpk])
                        if last:
                            P.dve(lambda e, cur=cur, pc=pc, bp=bp: e.tensor_tensor(
                                out=Pfin[:, pc, :], in0=bp[:, 0:256], in1=Pp[pc][cur][:, :], op=ALU.add),
                                r=[bpk, pk_c], w=["Pfin%d" % pc])
                        else:
                            P.dve(lambda e, cur=cur, nxt=nxt, pc=pc, bp=bp: e.tensor_tensor(
                                out=Pp[pc][nxt][:, :], in0=bp[:, 0:256], in1=Pp[pc][cur][:, :], op=ALU.add),
                                r=[bpk, pk_c], w=[pk_n])
                    cur = nxt
                if LEVEL < 4:
                    continue
                for pc in range(4):
                    for hd in range(2):
                        rows = slice(64 * hd, 64 * hd + 64)
                        oc_ = slice(pc * 128 + hd * 64, pc * 128 + hd * 64 + 64)
                        P.pe(lambda e, pc=pc, cl=cl, rows=rows, oc_=oc_, hd=hd, cs_=cs_: e.matmul(
                            bk[0][:, oc_], lhsT=at2[:, pc, hd, cs_], rhs=Hbf[:, pc, hd * 64:hd * 64 + 64],
                            start=True, stop=False), r=["at2", "Hbf%d" % pc], w=[BK(0)])
                        P.pe(lambda e, pc=pc, cl=cl, oc_=oc_, hd=hd: e.matmul(
                            bk[0][:, oc_], lhsT=NTk_sb[:, pc, hd * 256:hd * 256 + 128], rhs=vtok[:, cl, oc_],
                            start=False, stop=True), r=["NTk%d" % pc, "vtok"], w=[BK(0)])
                P.act(lambda e: e.copy(out=Zs[:, :, :], in_=bk[0][:, :]), r=[BK(0)], w=["Zs"])
                for pc in range(4):
                    for hd in range(2):
                        oc_ = slice(pc * 128 + hd * 64, pc * 128 + hd * 64 + 64)
                        P.pe(lambda e, pc=pc, oc_=oc_, hd=hd: e.matmul(
                            bk[1][:, oc_], lhsT=Pfin[:, pc, hd * 128:(hd + 1) * 128], rhs=Zs[:, pc, hd * 64:hd * 64 + 64],
                            start=True, stop=True), r=["Pfin%d" % pc, "Zs"], w=[BK(1)])
                P.dve(lambda e: e.tensor_copy(out=Us[:, :, :], in_=bk[1][:, :]), r=[BK(1)], w=["Us"])
                for pc in range(4):
                    for hd in range(2):
                        rows = slice(64 * hd, 64 * hd + 64)
                        oc_ = slice(pc * 128 + hd * 64, pc * 128 + hd * 64 + 64)
                        P.pe(lambda e, pc=pc, cl=cl, rows=rows, oc_=oc_, hd=hd, cs_=cs_: e.matmul(
                            bk[2][:, oc_], lhsT=rt2[:, pc, hd, cs_], rhs=Hbf[:, pc, hd * 64:hd * 64 + 64],
                            start=True, stop=False), r=["rt2", "Hbf%d" % pc], w=[BK(2)])
                        P.pe(lambda e, pc=pc, oc_=oc_, hd=hd: e.matmul(
                            bk[2][:, oc_], lhsT=NTb_sb[:, pc, hd * 256 + 128:hd * 256 + 256],
                            rhs=Us[:, pc, hd * 64:hd * 64 + 64], start=False, stop=False),
                            r=["NTb%d" % pc, "Us"], w=[BK(2)])
                        P.pe(lambda e, pc=pc, cl=cl, oc_=oc_, hd=hd: e.matmul(
                            bk[2][:, oc_], lhsT=NTk_sb[:, pc, hd * 256 + 128:hd * 256 + 256], rhs=vtok[:, cl, oc_],
                            start=False, stop=True), r=["NTk%d" % pc, "vtok"], w=[BK(2)])
                P.act(lambda e, cl=cl: e.copy(out=ytok[:, cl, :], in_=bk[2][:, :]), r=[BK(2)], w=["ytok"])
                for pc in range(4):
                    pcs = slice(pc * 128, (pc + 1) * 128)
                    P.pe(lambda e, pc=pc, pcs=pcs: e.matmul(bk[3][:, pcs], lhsT=btok[:, pc, :], rhs=Us[:, pc, :],
                                                            start=True, stop=False),
                         r=["btok%d" % pc, "Us"], w=[BK(3)])
                    P.pe(lambda e, pc=pc, pcs=pcs, cl=cl: e.matmul(bk[3][:, pcs], lhsT=ktok[:, pc, :], rhs=vtok[:, cl, pcs],
                                                                   start=False, stop=True),
                         r=["ktok%d" % pc, "vtok"], w=[BK(3)])
                P.dve(lambda e: e.tensor_tensor(out=Htmp[:, :, :], in0=bk[3][:, :], in1=H32[:, :, :], op=ALU.add),
                      r=[BK(3), "H32"], w=["Htmp"])
                for pc in range(4):
                    wl = Ep[:, pc, cl * 128 + 127:cl * 128 + 128]
                    P.dve(lambda e, pc=pc, wl=wl: e.tensor_scalar(out=H32[:, pc, :], in0=Htmp[:, pc, :], scalar1=wl,
                                                                  scalar2=None, op0=ALU.mult),
                          r=["Htmp", "Ep"], w=["H32"])
                    P.act(lambda e, pc=pc, wl=wl: e.activation(out=Hbf[:, pc, :], in_=Htmp[:, pc, :], func=AF.Copy,
                                                               scale=wl), r=["Htmp", "Ep"], w=["Hbf%d" % pc])
            for cl in range(NCL if LEVEL >= 5 else 0):
                cs_ = slice(cl * 128, (cl + 1) * 128)
                o = out_sb[fo % 2]
                ok = "out%d" % (fo % 2)
                fo += 1
                for pc in range(4):
                    P.pe(lambda e, pc=pc, cs_=cs_: e.matmul(bk[4][:, 2 * pc:2 * pc + 2], lhsT=rk[:, pc, cs_],
                                                            rhs=ind_sb[:, :], start=True, stop=True),
                         r=["rk", "const"], w=[BK(4)])
                P.dve(lambda e: e.tensor_copy(out=st_sb[:, 32:40], in_=bk[4][:, 0:8]), r=[BK(4)], w=["bonus"])
                y3 = ytok[:, cl, :].rearrange("p (h n) -> p h n", n=64)
                P.dve(lambda e, y3=y3: e.tensor_reduce(out=st_sb[:, 0:8], in_=y3, axis=AX.X, op=ALU.add),
                      r=["ytok"], w=["st"])
                P.act(lambda e, cl=cl: e.activation(out=ysq[:, :], in_=ytok[:, cl, :], func=AF.Square),
                      r=["ytok"], w=["ysq"])
                P.dve(lambda e: e.tensor_reduce(out=st_sb[:, 8:16], in_=ysq[:, :].rearrange("p (h n) -> p h n", n=64),
                                                axis=AX.X, op=ALU.add), r=["ysq"], w=["st"])
                P.dve(lambda e: e.tensor_scalar(out=st_sb[:, 0:8], in0=st_sb[:, 0:8], scalar1=1.0 / 64, scalar2=None,
                                                op0=ALU.mult), r=["st"], w=["st"])
                P.dve(lambda e: e.tensor_tensor(out=st_sb[:, 16:24], in0=st_sb[:, 0:8], in1=st_sb[:, 0:8], op=ALU.mult),
                      r=["st"], w=["st"])
                P.dve(lambda e: e.scalar_tensor_tensor(out=st_sb[:, 24:32], in0=st_sb[:, 8:16], scalar=1.0 / 64,
                                                       in1=st_sb[:, 16:24], op0=ALU.mult, op1=ALU.subtract),
                      r=["st"], w=["st"])
                P.act(lambda e: e.activation(out=st_sb[:, 24:32], in_=st_sb[:, 24:32], func=AF.Sqrt, bias=RW_LN_EPS,
                                             scale=1.0), r=["st"], w=["st"])
                P.dve(lambda e: e.reciprocal(out=st_sb[:, 24:32], in_=st_sb[:, 24:32]), r=["st"], w=["st"])
                for h in range(8):
                    hs_ = slice(h * 64, (h + 1) * 64)
                    P.dve(lambda e, h=h, hs_=hs_, o=o, cl=cl: e.tensor_scalar(
                        out=o[:, hs_], in0=ytok[:, cl, hs_], scalar1=st_sb[:, h:h + 1], scalar2=st_sb[:, 24 + h:25 + h],
                        op0=ALU.subtract, op1=ALU.mult), r=["ytok", "st"], w=[ok])
                P.dve(lambda e, o=o: e.tensor_tensor(out=o[:, :], in0=o[:, :], in1=lng_sb[:, :], op=ALU.mult),
                      r=[ok, "const"], w=[ok])
                P.dve(lambda e, o=o: e.tensor_tensor(out=o[:, :], in0=o[:, :], in1=lnb_sb[:, :], op=ALU.add),
                      r=[ok, "const"], w=[ok])
                for h in range(8):
                    hs_ = slice(h * 64, (h + 1) * 64)
                    P.dve(lambda e, h=h, hs_=hs_, o=o, cl=cl: e.scalar_tensor_tensor(
                        out=o[:, hs_], in0=v32[:, cl, hs_], scalar=st_sb[:, 32 + h:33 + h], in1=o[:, hs_],
                        op0=ALU.mult, op1=ALU.add), r=["v32", "bonus", ok], w=[ok])
                P.dve(lambda e, o=o, cl=cl: e.tensor_tensor(out=o[:, :], in0=o[:, :], in1=gtok[:, cl, :], op=ALU.mult),
                      r=[ok, "gtok"], w=[ok])
                P.dma("sp", yg[t0 + cl * 128:t0 + (cl + 1) * 128, :], o[:, :], r=[ok], w=["yg"])
        pass


def rw_consts(TT=256):
    ident = np.eye(128, dtype=np.float32)
    p = np.arange(128)
    up_strict = (p[:, None] < p[None, :]).astype(np.float32)
    up_incl = (p[:, None] <= p[None, :]).astype(np.float32)
    mask2 = np.concatenate([up_strict, up_incl], axis=1)
    maskL = (p[:, None] > p[None, :]).astype(np.float32)
    rmask = np.ones((128, TT), np.float32)
    rmask[:, ::CL] = 0.0
    ind = np.zeros((128, 2), np.float32)
    ind[:64, 0] = 1.0
    ind[64:, 1] = 1.0
    return dict(ident=ident, mask2=mask2, maskL=maskL, rmask=rmask, ind=ind)


def rw_inputs(hT_b, hh, p):
    cols = slice(hh * 512, (hh + 1) * 512)
    T = hT_b.shape[1]
    hTp = np.zeros((D, T + 1), np.float32)
    hTp[:, 1:] = hT_b
    vec = lambda v: np.ascontiguousarray(v[cols].reshape(4, 128).T)
    cv = np.concatenate([vec(p["rwkv_w0"][0]), vec(p["rwkv_a0"][0]), vec(p["rwkv_k_k"][0]), vec(p["rwkv_k_a"][0]),
                         vec(p["rwkv_r_k"][0].reshape(-1))], axis=1)
    mu6 = np.ascontiguousarray(p["rwkv_mu"][0].reshape(6, 8, 128).transpose(2, 0, 1).reshape(128, 48))
    m = dict(hTp=hTp,
             wr=np.ascontiguousarray(p["rwkv_w_rkv"][0, 0][:, cols]), wk=np.ascontiguousarray(p["rwkv_w_rkv"][0, 1][:, cols]),
             wv=np.ascontiguousarray(p["rwkv_w_rkv"][0, 2][:, cols]),
             l1=np.ascontiguousarray(np.concatenate([p["rwkv_w1"][0], p["rwkv_a1"][0], p["rwkv_g1"][0]], axis=1)),
             w2c=np.ascontiguousarray(p["rwkv_w2"][0][:, cols]), a2c=np.ascontiguousarray(p["rwkv_a2"][0][:, cols]),
             g2c=np.ascontiguousarray(p["rwkv_g2"][0][:, cols]), mu6=mu6, cv=np.ascontiguousarray(cv),
             lng=np.ascontiguousarray(np.broadcast_to(p["rwkv_ln_g"][0][cols], (128, 512))),
             lnb=np.ascontiguousarray(np.broadcast_to(p["rwkv_ln_b"][0][cols], (128, 512))))
    m.update(rw_consts())
    return m


def _std_io(nc):
    def io(name, shape, kind):
        return nc.dram_tensor(name, list(shape), F32,
                              kind="ExternalInput" if kind == "in" else "ExternalOutput").ap()
    return io


def build_LA(NT, has_add, TB=1024, TT=512):
    nc = bass.Bass("TRN2", target_bir_lowering=False)
    with ExitStack() as es:
        P = Prog(nc, es)
        emit_LA(nc, P, _std_io(nc), NT, has_add, TB, TT)
    return nc


def build_LP(NT, has_v, TB=1024, TT=512):
    nc = bass.Bass("TRN2", target_bir_lowering=False)
    with ExitStack() as es:
        P = Prog(nc, es)
        emit_LP(nc, P, _std_io(nc), NT, has_v, TB, TT)
    return nc


def build_ATT(T, HL=4):
    nc = bass.Bass("TRN2", target_bir_lowering=False)
    with ExitStack() as es:
        P = Prog(nc, es)
        emit_ATT(nc, P, _std_io(nc), T, HL)
    return nc


def build_RW(T, TT=256, LEVEL=9):
    nc = bass.Bass("TRN2", target_bir_lowering=False)
    with ExitStack() as es:
        P = Prog(nc, es)
        emit_RW(nc, P, _std_io(nc), T, TT, LEVEL)
    return nc


_PROGS = {}


def _prog(name, fn):
    if name not in _PROGS:
        _PROGS[name] = fn()
    return _PROGS[name]


def _tile_win(w):
    g = w[:, :FF].reshape(8, 128, 22, 128)
    u = w[:, FF:].reshape(8, 128, 22, 128)
    return np.ascontiguousarray(np.concatenate([g, u], axis=3).transpose(2, 1, 0, 3))


def _tile_wout(w):
    return np.ascontiguousarray(w.reshape(22, 128, 8, 128).transpose(2, 1, 0, 3))


def _tile_sq(w):
    return np.ascontiguousarray(w.reshape(8, 128, 8, 128).transpose(2, 1, 0, 3))


def _gains(g1, g2):
    return np.ascontiguousarray(np.concatenate([g1.reshape(8, 128).T, g2.reshape(8, 128).T], axis=1))


def _run(nc, in_maps):
    return run_bass_kernel_spmd(nc, in_maps, core_ids=list(range(NCORES))).results


def _run_LA(xT_list, w_in, w_out, g1, g2, aT_list=None, w_add=None):
    NT = xT_list[0].shape[1]
    has_add = aT_list is not None
    nc = _prog(("LA", NT, has_add), lambda: build_LA(NT, has_add))
    wi, wo, gg = _tile_win(w_in), _tile_wout(w_out), _gains(g1, g2)
    wa = _tile_sq(w_add) if has_add else None
    maps = []
    for c in range(NCORES):
        m = {"xT": xT_list[c], "gains": gg, "w_in": wi, "w_out": wo}
        if has_add:
            m["aT"] = aT_list[c]
            m["w_add"] = wa
        maps.append(m)
    res = _run(nc, maps)
    return [r["yT"] for r in res], [r["hT"] for r in res]


def _run_LP(hT_list, wn, gn, wv=None):
    NT = hT_list[0].shape[1]
    has_v = wv is not None
    nc = _prog(("LP", NT, has_v), lambda: build_LP(NT, has_v))
    wnt = _tile_sq(wn)
    g = np.ascontiguousarray(np.tile(gn, 2).reshape(128, 1))
    maps = []
    for c in range(NCORES):
        m = {"hT": hT_list[c], "wn": wnt, "gn": g}
        if has_v:
            m["wv"] = np.ascontiguousarray(wv.reshape(8, 128, D).transpose(1, 0, 2))
        maps.append(m)
    res = _run(nc, maps)
    return [r["nT"] for r in res], ([r["v_tok"] for r in res] if has_v else None)


def kernel_unfused(**inp):
    p = {k: np.asarray(v, dtype=np.float32) for k, v in inp.items()}
    x = p["x"]
    B, T, _ = x.shape
    HT = T // 2
    xT = [np.ascontiguousarray(x[c // 2, (c % 2) * HT:(c % 2 + 1) * HT].T) for c in range(NCORES)]

    def full_seq(lst, b):
        return np.concatenate([lst[2 * b], lst[2 * b + 1]], axis=1)

    x1T, h1T = _run_LA(xT, p["ffn_w_in"][0, 0], p["ffn_w_out"][0, 0], p["ffn_norm"][0, 0], p["mix_norm"][0])
    nc_rw = _prog(("RW", T), lambda: build_RW(T))
    rw_maps = [rw_inputs(full_seq(h1T, c // 2), c % 2, p) for c in range(NCORES)]
    yg = [r["yg_tok"] for r in _run(nc_rw, rw_maps)]
    ygT = [np.ascontiguousarray(np.concatenate([yg[2 * b], yg[2 * b + 1]], axis=1).T) for b in range(B)]
    aT = [np.ascontiguousarray(ygT[c // 2][:, (c % 2) * HT:(c % 2 + 1) * HT]) for c in range(NCORES)]
    x3T, hkvT = _run_LA(x1T, p["ffn_w_in"][0, 1], p["ffn_w_out"][0, 1], p["ffn_norm"][0, 1], p["kv_norm"],
                        aT_list=aT, w_add=p["rwkv_w_o"][0])
    kT, v_tok = _run_LP(hkvT, p["w_kv"][:, :D], p["k_norm"], wv=p["w_kv"][:, D:])
    x4T, h2T = _run_LA(x3T, p["ffn_w_in"][1, 0], p["ffn_w_out"][1, 0], p["ffn_norm"][1, 0], p["mix_norm"][1])
    qT, _ = _run_LP(h2T, p["diff_w_q"][0], p["diff_q_norm"][0])
    nc_att = _prog(("ATT", T), lambda: build_ATT(T, 4))
    att_maps = []
    subg = np.ascontiguousarray(np.broadcast_to(p["diff_subln"][0], (128, 128)))
    lam = np.ascontiguousarray(p["diff_lambda"][0].reshape(1, 256))
    for c in range(NCORES):
        b, hh = c // 2, c % 2
        rows = slice(hh * 512, (hh + 1) * 512)
        qaug, kaug, btab, tri = att_consts(T, [hh * 4 + i for i in range(4)])
        att_maps.append({"qT": np.ascontiguousarray(full_seq(qT, b)[rows]), "kT": np.ascontiguousarray(full_seq(kT, b)[rows]),
                         "v_tok": np.ascontiguousarray(np.concatenate([v_tok[2 * b], v_tok[2 * b + 1]], axis=0)[:, rows]),
                         "qaug": qaug, "kaug": kaug, "btab": btab, "tri": tri, "lam": lam, "subg": subg})
    ot = [r["o_tok"] for r in _run(nc_att, att_maps)]
    oT = [np.ascontiguousarray(np.concatenate([ot[2 * b], ot[2 * b + 1]], axis=1).T) for b in range(B)]
    aT = [np.ascontiguousarray(oT[c // 2][:, (c % 2) * HT:(c % 2 + 1) * HT]) for c in range(NCORES)]
    outT, _ = _run_LA(x4T, p["ffn_w_in"][1, 1], p["ffn_w_out"][1, 1], p["ffn_norm"][1, 1], p["ffn_norm"][1, 1],
                      aT_list=aT, w_add=p["diff_w_o"][0])
    out = np.empty((B, T, D), np.float32)
    for c in range(NCORES):
        out[c // 2, (c % 2) * HT:(c % 2 + 1) * HT] = outT[c].T
    return out


RG_PAIRS = [[0, 1], [2, 3], [4, 5], [6, 7]]


CC_MAX_BYTES = 2 * 1024 * 1024


class _Gathered:
    def __init__(self, nc, name, src, R, C):
        self.src, self.R, self.C = src, R, C
        self.RC = min(R, CC_MAX_BYTES // (C * 4))
        assert R % self.RC == 0
        self.nch = R // self.RC
        self.g = nc.dram_tensor(name, [self.nch * 2 * self.RC, C], F32).ap()

    def rows(self, j, r0, r1):
        ch = r0 // self.RC
        assert (r1 - 1) // self.RC == ch
        base = (ch * 2 + j) * self.RC - ch * self.RC
        return self.g[base + r0:base + r1, :]


def _emit_allgather(nc, P, gs):
    with P.stage():
        for G in gs:
            for ch in range(G.nch):
                src = G.src[ch * G.RC:(ch + 1) * G.RC, :]
                dst = G.g[ch * 2 * G.RC:(ch + 1) * 2 * G.RC, :]
                P.add("pool", lambda e, src=src, dst=dst: e.collective_compute(
                    "AllGather", ALU.bypass, replica_groups=RG_PAIRS, ins=[src.opt()], outs=[dst.opt()]),
                    dma="cc")


def _emit_select(nc, P, sel, jobs, F, ident=None):
    NSB = 4 if F <= 1024 else 3
    with P.stage():
        sel_sb = P.sb("sel_sb", [128, 2], F32)
        P.dma("sp", sel_sb[:, :], sel[:, :], w=["sel"])
        a_sb = [P.sb("a_sb%d" % i, [128, F], F32) for i in range(NSB)]
        b_sb = [P.sb("b_sb%d" % i, [128, F], F32) for i in range(NSB)]
        o_sb = [P.sb("o_sb%d" % i, [128, F], F32) for i in range(NSB)]
        if ident is not None:
            id_sb = P.sb("id_sb", [128, 128], F32)
            P.dma("sp", id_sb[:, :], ident[:, :], w=["ident"])
            t_sb = [P.sb("t_sb%d" % i, [128, 512], F32) for i in range(2)]
            ps_t = [P.ps("ps_t%d" % i, [128, 512]) for i in range(2)]
            P.psum_keys.update(["ps_t0", "ps_t1"])
        for i, (A, B, dst) in enumerate(jobs):
            q = i % NSB
            P.dma("sp", a_sb[q][:, :], A, w=["a%d" % q])
            P.dma("sp", b_sb[q][:, :], B, w=["b%d" % q])
            P.dve(lambda e, q=q: e.tensor_scalar(out=a_sb[q][:, :], in0=a_sb[q][:, :], scalar1=sel_sb[:, 0:1],
                                                 scalar2=None, op0=ALU.mult), r=["a%d" % q, "sel"], w=["a%d" % q])
            P.dve(lambda e, q=q: e.scalar_tensor_tensor(out=o_sb[q][:, :], in0=b_sb[q][:, :], scalar=sel_sb[:, 1:2],
                                                        in1=a_sb[q][:, :], op0=ALU.mult, op1=ALU.add),
                  r=["a%d" % q, "b%d" % q, "sel"], w=["o%d" % q])
            if ident is None:
                P.dma("sp", dst, o_sb[q][:, :], r=["o%d" % q], w=["seldst"])
            else:
                for cc in range(4):
                    P.pe(lambda e, q=q, cc=cc: e.transpose(out=ps_t[q % 2][:, cc * 128:(cc + 1) * 128],
                                                           in_=o_sb[q][:, cc * 128:(cc + 1) * 128],
                                                           identity=id_sb[:, :]),
                         r=["o%d" % q, "ident"], w=["ps_t%d" % (q % 2)])
                P.act(lambda e, q=q: e.copy(out=t_sb[q % 2][:, :], in_=ps_t[q % 2][:, :]), r=["ps_t%d" % (q % 2)], w=["t%d" % (q % 2)])
                P.dma("sp", dst, t_sb[q % 2][:, :].rearrange("p (c t) -> p c t", c=4), r=["t%d" % (q % 2)], w=["seldst"])


class _Skip:
    def __init__(self, P):
        self.P = P

    def __enter__(self):
        self.n = len(self.P.ops)
        self.P.es = ExitStack()
        self.P.es.__enter__()
        return self.P

    def __exit__(self, *a):
        del self.P.ops[self.n:]
        self.P.lastw, self.P.readers = {}, {}
        self.P.es.__exit__(*a)
        self.P.es = self.P.sem_es
        return False


def build_FUSED(T=8192, UPTO=99):
    HT = T // 2
    _st = [0]

    def go():
        _st[0] += 1
        return _st[0] <= UPTO
    nc = bass.Bass("TRN2", target_bir_lowering=False)
    ext_in = lambda n, shp: nc.dram_tensor(n, list(shp), F32, kind="ExternalInput").ap()
    ext_out = lambda n, shp: nc.dram_tensor(n, list(shp), F32, kind="ExternalOutput").ap()
    internal = lambda n, shp: nc.dram_tensor(n, list(shp), F32).ap()
    with ExitStack() as es:
        P = Prog(nc, es)

        def mk_io(prefix, bind):
            def io(name, shape, kind):
                if name in bind:
                    return bind[name]
                assert kind == "in", name
                return ext_in(prefix + name, shape)
            return io

        sel = ext_in("sel", [128, 2])
        ident = ext_in("SEL_ident", [128, 128])
        xT = ext_in("xT", [D, HT])
        outT = ext_out("outT", [D, HT])
        x1T, h1T = internal("x1T", [D, HT]), internal("h1T", [D, HT])
        if go():
            emit_LA(nc, P, mk_io("A1_", {"xT": xT, "yT": x1T, "hT": h1T}), HT, False)
        h1g = _Gathered(nc, "h1g", h1T, D, HT)
        if go():
            _emit_allgather(nc, P, [h1g])
        hTp = internal("hTp", [D, T + 1])
        with (P.stage() if go() else _Skip(P)):
            z_sb = P.sb("z_sb", [128, KC, 1], F32)
            P.pool(lambda e: e.memset(z_sb[:, :, :], 0.0), w=["z"])
            P.add("sp", lambda e: e.dma_start(out=hTp.rearrange("(kc p) t -> p kc t", p=128)[:, :, 0:1],
                                              in_=z_sb[:, :, :], allow_slow_non_contiguous=True),
                  r=["z"], w=["hTp"], dma=True)
            for j in range(2):
                for kc in range(KC):
                    P.dma("sp", hTp[kc * 128:(kc + 1) * 128, 1 + j * HT:1 + (j + 1) * HT],
                          h1g.rows(j, kc * 128, (kc + 1) * 128), w=["hTp"])
        yg_tok = internal("yg_tok", [T, 512])
        if go():
            emit_RW(nc, P, mk_io("RW_", {"hTp": hTp, "yg_tok": yg_tok}), T)
        ygg = _Gathered(nc, "ygg", yg_tok, T, 512)
        if go():
            _emit_allgather(nc, P, [ygg])
        aT2 = internal("aT2", [D, HT])

        def tok2feat_jobs(g, dst):
            jobs = []
            for j in range(2):
                for tb in range(HT // 128):
                    A = g.rows(j, tb * 128, (tb + 1) * 128)
                    B = g.rows(j, HT + tb * 128, HT + (tb + 1) * 128)
                    dd = dst[j * 512:(j + 1) * 512, tb * 128:(tb + 1) * 128].rearrange("(c p) t -> p c t", p=128)
                    jobs.append((A, B, dd))
            return jobs

        if go():
            _emit_select(nc, P, sel, tok2feat_jobs(ygg, aT2), 512, ident=ident)
        x3T, hkvT = internal("x3T", [D, HT]), internal("hkvT", [D, HT])
        if go():
            emit_LA(nc, P, mk_io("A2_", {"xT": x1T, "aT": aT2, "yT": x3T, "hT": hkvT}), HT, True)
        kT, v_tok = internal("kT", [D, HT]), internal("v_tok", [HT, D])
        if go():
            emit_LP(nc, P, mk_io("P1_", {"hT": hkvT, "nT": kT, "v_tok": v_tok}), HT, True)
        x4T, h2T = internal("x4T", [D, HT]), internal("h2T", [D, HT])
        if go():
            emit_LA(nc, P, mk_io("A3_", {"xT": x3T, "yT": x4T, "hT": h2T}), HT, False)
        qT = internal("qT", [D, HT])
        if go():
            emit_LP(nc, P, mk_io("P2_", {"hT": h2T, "nT": qT}), HT, False)
        qg, kg, vg = _Gathered(nc, "qg", qT, D, HT), _Gathered(nc, "kg", kT, D, HT), _Gathered(nc, "vg", v_tok, HT, D)
        if go():
            _emit_allgather(nc, P, [qg, kg, vg])
        o_tok = internal("o_tok", [T, 512])
        if go():
            emit_ATT(nc, P, mk_io("AT_", {"o_tok": o_tok}), T, 4, gath=(qg, kg, vg, sel))
        og = _Gathered(nc, "og", o_tok, T, 512)
        if go():
            _emit_allgather(nc, P, [og])
        aT4 = internal("aT4", [D, HT])
        if go():
            _emit_select(nc, P, sel, tok2feat_jobs(og, aT4), 512, ident=ident)
        hdum = internal("hdum", [D, HT])
        if go():
            emit_LA(nc, P, mk_io("A4_", {"xT": x4T, "aT": aT4, "yT": outT, "hT": hdum}), HT, True)
    return nc


def kernel(**inp):
    p = {k: np.asarray(v, dtype=np.float32) for k, v in inp.items()}
    x = p["x"]
    B, T, _ = x.shape
    HT = T // 2
    import os
    nc = _prog(("FUSED", T), lambda: build_FUSED(T, int(os.environ.get("FUSED_UPTO", "99"))))
    shared = {"SEL_ident": np.eye(128, dtype=np.float32)}

    def la(prefix, l, i, g2, w_add=None):
        shared[prefix + "gains"] = _gains(p["ffn_norm"][l, i], g2)
        shared[prefix + "w_in"] = _tile_win(p["ffn_w_in"][l, i])
        shared[prefix + "w_out"] = _tile_wout(p["ffn_w_out"][l, i])
        if w_add is not None:
            shared[prefix + "w_add"] = _tile_sq(w_add)

    la("A1_", 0, 0, p["mix_norm"][0])
    la("A2_", 0, 1, p["kv_norm"], p["rwkv_w_o"][0])
    la("A3_", 1, 0, p["mix_norm"][1])
    la("A4_", 1, 1, p["ffn_norm"][1, 1], p["diff_w_o"][0])
    shared["P1_wn"] = _tile_sq(p["w_kv"][:, :D])
    shared["P1_gn"] = np.ascontiguousarray(np.tile(p["k_norm"], 2).reshape(128, 1))
    shared["P1_wv"] = np.ascontiguousarray(p["w_kv"][:, D:].reshape(8, 128, D).transpose(1, 0, 2))
    shared["P2_wn"] = _tile_sq(p["diff_w_q"][0])
    shared["P2_gn"] = np.ascontiguousarray(np.tile(p["diff_q_norm"][0], 2).reshape(128, 1))
    shared["AT_lam"] = np.ascontiguousarray(p["diff_lambda"][0].reshape(1, 256))
    shared["AT_subg"] = np.ascontiguousarray(np.broadcast_to(p["diff_subln"][0], (128, 128)))
    dummy_h = np.zeros((D, 1), np.float32)
    maps = []
    for c in range(NCORES):
        b, hh = c // 2, c % 2
        m = dict(shared)
        m["xT"] = np.ascontiguousarray(x[b, hh * HT:(hh + 1) * HT].T)
        s_ = np.zeros((128, 2), np.float32)
        s_[:, hh] = 1.0
        m["sel"] = s_
        rw = rw_inputs(dummy_h, hh, p)
        del rw["hTp"]
        for k_, v_ in rw.items():
            m["RW_" + k_] = v_
        qaug, kaug, btab, tri = att_consts(T, [hh * 4 + i for i in range(4)])
        m.update({"AT_qaug": qaug, "AT_kaug": kaug, "AT_btab": btab, "AT_tri": tri})
        maps.append(m)
    res = _run(nc, maps)
    out = np.empty((B, T, D), np.float32)
    for c in range(NCORES):
        out[c // 2, (c % 2) * HT:(c % 2 + 1) * HT] = res[c]["outT"].T
    return out
```

```python
import numpy as np
from contextlib import ExitStack
import concourse.bass as bass
import concourse.mybir as mybir
from concourse.bass_utils import run_bass_kernel_spmd

F32 = mybir.dt.float32
BF16 = mybir.dt.bfloat16
ALU = mybir.AluOpType
AF = mybir.ActivationFunctionType
AX = mybir.AxisListType

NCORES = 8
SEM_CAP = 8192
ATTACH_WAIT = True


class _Op:
    __slots__ = ("eng", "fn", "deps", "dma", "signal", "sem", "val", "idx")


class Prog:
    ENGS = ("pe", "act", "dve", "pool", "sp")

    def __init__(self, nc, es, n_dma_sems=12):
        self.nc = nc
        self.sem_es = es
        self.es = es
        self.ops = []
        self.lastw = {}
        self.readers = {}
        self.n_dma_sems = n_dma_sems
        self.uid = 0
        self.psum_keys = set()
        self.emitted = 0
        self.cnt = {e: 0 for e in self.ENGS}
        self.dma_cnt = [0] * (2 * n_dma_sems)
        self.dma_last = [None] * (2 * n_dma_sems)
        self.n_dma = {"sp": 0, "pool": 0}
        self.n_cc = 0
        self.sems = {}
        self.waited = {e: {} for e in self.ENGS}
        self.nstage = 0

    def sb(self, name, shape, dt):
        return self.es.enter_context(self.nc.sbuf_tensor("g%d_%s" % (self.nstage, name), list(shape), dt))

    def ps(self, name, shape, dt=F32):
        return self.es.enter_context(self.nc.psum_tensor("g%d_%s" % (self.nstage, name), list(shape), dt))

    def _sem(self, key):
        if key not in self.sems:
            self.sems[key] = self.sem_es.enter_context(self.nc.semaphore("s_%s_%s" % key))
        return self.sems[key]

    class _Stage:
        def __init__(self, P):
            self.P = P

        def __enter__(self):
            self.P.es = ExitStack()
            self.P.es.__enter__()
            return self.P

        def __exit__(self, *a):
            if a[0] is None:
                self.P.emit_stage()
            self.P.es.__exit__(*a)
            self.P.es = self.P.sem_es
            return False

    def stage(self):
        return Prog._Stage(self)

    def add(self, eng, fn, r=(), w=(), dma=False):
        op = _Op()
        op.eng, op.fn, op.dma = eng, fn, dma
        op.idx = len(self.ops)
        op.signal = False
        op.sem = op.val = None
        deps = {}
        for k in r:
            d = self.lastw.get(k)
            if d is not None:
                deps[d] = True
        for k in r:
            if k in self.psum_keys:
                for rd in self.readers.get(k, ()):
                    if self.ops[rd].eng != eng:
                        deps[rd] = True
        for k in w:
            d = self.lastw.get(k)
            if d is not None:
                deps[d] = True
            for rd in self.readers.get(k, ()):
                if rd not in deps:
                    deps[rd] = False
        for k in r:
            lst = self.readers.setdefault(k, [])
            if not dma:
                lst[:] = [x for x in lst if self.ops[x].dma or self.ops[x].eng != eng]
            lst.append(op.idx)
        for k in w:
            self.lastw[k] = op.idx
            self.readers[k] = []
        op.deps = deps
        self.ops.append(op)
        return op

    def pe(self, fn, r=(), w=()):
        return self.add("pe", fn, r, w)

    def act(self, fn, r=(), w=()):
        return self.add("act", fn, r, w)

    def dve(self, fn, r=(), w=()):
        return self.add("dve", fn, r, w)

    def pool(self, fn, r=(), w=()):
        return self.add("pool", fn, r, w)

    def dma(self, q, out, in_, r=(), w=()):
        return self.add(q, lambda e: e.dma_start(out=out, in_=in_), r, w, dma=True)

    def finalize(self):
        self.emit_stage()

    def emit_stage(self):
        nc, ops = self.nc, self.ops
        s0 = self.emitted
        stage_ops = ops[s0:]
        self.emitted = len(ops)
        self.nstage += 1
        self.lastw, self.readers = {}, {}
        if not stage_ops:
            return
        for op in stage_ops:
            op.deps = {d: st for d, st in op.deps.items() if d >= s0}
            for d, strict in op.deps.items():
                p = ops[d]
                if p.dma:
                    continue
                if p.eng != op.eng or op.dma:
                    p.signal = True
                elif strict and p.eng != "pe":
                    p.signal = True
        NS2 = 2 * self.n_dma_sems
        for op in stage_ops:
            if op.dma == "cc":
                self.n_cc += 1
                op.sem, op.val = ("cc", 0), self.n_cc
            elif op.dma:
                j = self.n_dma[op.eng] % self.n_dma_sems + (self.n_dma_sems if op.eng == "pool" else 0)
                self.n_dma[op.eng] += 1
                self.dma_cnt[j] += 1
                op.sem, op.val = ("dma", j), 16 * self.dma_cnt[j]
                if self.dma_last[j] is not None and self.dma_last[j] >= s0:
                    op.deps[self.dma_last[j]] = True
                self.dma_last[j] = op.idx
            elif op.signal:
                t = self.cnt[op.eng]
                self.cnt[op.eng] += 1
                op.sem, op.val = (op.eng, t // SEM_CAP), t % SEM_CAP + 1
        per_eng = {e: [] for e in self.ENGS}
        for op in stage_ops:
            per_eng[op.eng].append(op)
        final = {}
        for op in stage_ops:
            if op.dma:
                final[op.sem] = max(final.get(op.sem, 0), op.val)
        sems = self._sem

        def emit(e, eng):
            waited = self.waited[eng]
            for op in per_eng[eng]:
                need = {}
                for d, strict in op.deps.items():
                    p = ops[d]
                    if (not p.dma) and p.eng == eng and not op.dma:
                        if not strict or eng == "pe":
                            continue
                    if p.sem is None:
                        continue
                    if need.get(p.sem, 0) < p.val:
                        need[p.sem] = p.val
                todo = [(sk, v) for sk, v in need.items() if waited.get(sk, 0) < v]
                attach = None
                if ATTACH_WAIT and todo and eng in ("act", "dve", "pool") and not op.dma:
                    attach = todo.pop()
                for sk, v in todo:
                    e.wait_ge(sems(sk), v)
                    waited[sk] = v
                ins = op.fn(e)
                if attach is not None:
                    ins._wait_ge(sems(attach[0]), attach[1])
                    waited[attach[0]] = attach[1]
                if op.dma == "cc":
                    ins.then_inc(sems(op.sem), 1)
                elif op.dma:
                    ins.then_inc(sems(op.sem), 16)
                elif op.signal:
                    ins.then_inc(sems(op.sem), 1)
            if eng == "sp":
                for sk, v in final.items():
                    if waited.get(sk, 0) < v:
                        e.wait_ge(sems(sk), v)
                        waited[sk] = v

        with nc.Block() as block:
            @block.tensor
            def _(e):
                emit(e, "pe")

            @block.scalar
            def _(e):
                emit(e, "act")

            @block.vector
            def _(e):
                emit(e, "dve")

            @block.gpsimd
            def _(e):
                emit(e, "pool")

            @block.sync
            def _(e):
                emit(e, "sp")


D = 1024
KC = 8
FF = 2816
FC = 22
EPS = 1e-6


def _rmsnorm(P, nc, x_sb, xkey, h_sb, hkey, g_sb, gcol0, sq_sb, ones_sb, ps_ss, rstd_sb, t0, tn, tag, o0=None):
    kq = "sq" + tag
    if o0 is None:
        o0 = t0
    for kc in range(KC):
        P.act(lambda e, kc=kc: e.activation(out=sq_sb[:, kc, 0:tn], in_=x_sb[:, kc, t0:t0 + tn], func=AF.Square),
              r=[xkey], w=[kq])
    for kc in range(KC):
        P.pe(lambda e, kc=kc: e.matmul(ps_ss[:, 0:tn], lhsT=ones_sb[:, :], rhs=sq_sb[:, kc, 0:tn],
                                       start=(kc == 0), stop=(kc == KC - 1)),
             r=[kq, "ones"], w=["ps_ss"])
    P.act(lambda e: e.activation(out=rstd_sb[:, 0:tn], in_=ps_ss[:, 0:tn], func=AF.Sqrt, bias=EPS, scale=1.0),
          r=["ps_ss"], w=["rstd"])
    P.dve(lambda e: e.reciprocal(out=rstd_sb[:, 0:tn], in_=rstd_sb[:, 0:tn]), r=["rstd"], w=["rstd"])
    for kc in range(KC):
        P.dve(lambda e, kc=kc: e.scalar_tensor_tensor(out=h_sb[:, kc, o0:o0 + tn], in0=x_sb[:, kc, t0:t0 + tn],
                                                      scalar=g_sb[:, gcol0 + kc:gcol0 + kc + 1],
                                                      in1=rstd_sb[:, 0:tn], op0=ALU.mult, op1=ALU.mult),
              r=[xkey, "rstd", "gains"], w=[hkey])


def emit_LA(nc, P, io, NT, has_add, TB=1024, TT=512):
    NWB, NWI, NWA = 4, 6, 3
    xT = io("xT", [D, NT], "in")
    gains = io("gains", [128, 2 * KC], "in")
    w_in = io("w_in", [FC, 128, KC, 256], "in")
    w_out = io("w_out", [KC, 128, FC, 128], "in")
    if has_add:
        aT = io("aT", [D, NT], "in")
        w_add = io("w_add", [KC, 128, KC, 128], "in")
    yT = io("yT", [D, NT], "out")
    hT = io("hT", [D, NT], "out")
    xT_v = xT.rearrange("(kc p) t -> p kc t", p=128)
    yT_v = yT.rearrange("(kc p) t -> p kc t", p=128)
    hT_v = hT.rearrange("(kc p) t -> p kc t", p=128)
    NB = NT // TB
    NS = TB // TT
    with P.stage():
        x_sb = P.sb("x_sb", [128, KC, TB], F32)
        h_sb = P.sb("h_sb", [128, KC, TB], BF16)
        act_sb = P.sb("act_sb", [128, FC, TB], BF16)
        sq_sb = P.sb("sq_sb", [128, KC, TT], BF16)
        ho_sb = P.sb("ho_sb", [128, KC, TT], F32)
        rstd_sb = P.sb("rstd_sb", [128, TT], F32)
        silu_sb = [P.sb("silu_sb%d" % i, [128, TT], F32) for i in range(2)]
        g_sb = P.sb("g_sb", [128, 2 * KC], F32)
        ones_sb = P.sb("ones_sb", [128, 128], BF16)
        win_sb = [P.sb("win_sb%d" % i, [128, KC, 256], BF16) for i in range(NWI)]
        wout_sb = [P.sb("wout_sb%d" % i, [128, FC, 128], BF16) for i in range(NWB)]
        if has_add:
            a_sb = P.sb("a_sb", [128, KC, TB], BF16)
            wadd_sb = [P.sb("wadd_sb%d" % i, [128, KC, 128], BF16) for i in range(NWA)]
        ps_g = [P.ps("ps_g%d" % i, [128, TT]) for i in range(2)]
        ps_u = [P.ps("ps_u%d" % i, [128, TT]) for i in range(2)]
        ps_o = [P.ps("ps_o%d" % i, [128, TT]) for i in range(2)]
        ps_ss = P.ps("ps_ss", [128, TT])
        P.psum_keys.update(["ps_g0", "ps_g1", "ps_u0", "ps_u1", "ps_o0", "ps_o1", "ps_ss"])

        P.dma("sp", g_sb[:, :], gains[:, :], w=["gains"])
        P.pool(lambda e: e.memset(ones_sb[:, :], 1.0 / D), w=["ones"])
        nw = [0, 0, 0]
        for b in range(NB):
            tb0 = b * TB
            for kc in range(KC):
                P.dma("sp", x_sb[:, kc, :], xT_v[:, kc, tb0:tb0 + TB], w=["x"])
            if has_add:
                for kc in range(KC):
                    P.dma("pool", a_sb[:, kc, :], aT.rearrange("(kc p) t -> p kc t", p=128)[:, kc, tb0:tb0 + TB],
                          w=["a"])
                for oc in range(KC):
                    s = nw[2] % NWA
                    nw[2] += 1
                    P.dma("pool", wadd_sb[s][:, :, :], w_add[oc], w=["wadd%d" % s])
                    for st in range(NS):
                        t0 = st * TT
                        pb = ps_o[(oc * NS + st) % 2]
                        pk = "ps_o%d" % ((oc * NS + st) % 2)
                        for kc in range(KC):
                            P.pe(lambda e, kc=kc, s=s, pb=pb, t0=t0: e.matmul(
                                pb[:, :], lhsT=wadd_sb[s][:, kc, :], rhs=a_sb[:, kc, t0:t0 + TT],
                                start=(kc == 0), stop=(kc == KC - 1)), r=["wadd%d" % s, "a"], w=[pk])
                        P.dve(lambda e, oc=oc, pb=pb, t0=t0: e.tensor_tensor(
                            out=x_sb[:, oc, t0:t0 + TT], in0=pb[:, :], in1=x_sb[:, oc, t0:t0 + TT], op=ALU.add),
                            r=[pk, "x"], w=["x"])
            for st in range(NS):
                _rmsnorm(P, nc, x_sb, "x", h_sb, "h", g_sb, 0, sq_sb, ones_sb, ps_ss, rstd_sb, st * TT, TT, "")
            for j in range(FC):
                s = nw[0] % NWI
                nw[0] += 1
                P.dma("pool", win_sb[s][:, :, :], w_in[j], w=["win%d" % s])
                for st in range(NS):
                    t0 = st * TT
                    q = (j * NS + st) % 2
                    for kc in range(KC):
                        P.pe(lambda e, kc=kc, s=s, q=q, t0=t0: e.matmul(
                            ps_g[q][:, :], lhsT=win_sb[s][:, kc, 0:128], rhs=h_sb[:, kc, t0:t0 + TT],
                            start=(kc == 0), stop=(kc == KC - 1)), r=["win%d" % s, "h"], w=["ps_g%d" % q])
                    for kc in range(KC):
                        P.pe(lambda e, kc=kc, s=s, q=q, t0=t0: e.matmul(
                            ps_u[q][:, :], lhsT=win_sb[s][:, kc, 128:256], rhs=h_sb[:, kc, t0:t0 + TT],
                            start=(kc == 0), stop=(kc == KC - 1)), r=["win%d" % s, "h"], w=["ps_u%d" % q])
                    P.act(lambda e, q=q: e.activation(out=silu_sb[q][:, :], in_=ps_g[q][:, :], func=AF.Silu),
                          r=["ps_g%d" % q], w=["silu%d" % q])
                    P.dve(lambda e, q=q, j=j, t0=t0: e.tensor_tensor(
                        out=act_sb[:, j, t0:t0 + TT], in0=ps_u[q][:, :], in1=silu_sb[q][:, :], op=ALU.mult),
                        r=["ps_u%d" % q, "silu%d" % q], w=["act"])
            for oc in range(KC):
                s = nw[1] % NWB
                nw[1] += 1
                P.dma("pool", wout_sb[s][:, :, :], w_out[oc], w=["wout%d" % s])
                for st in range(NS):
                    t0 = st * TT
                    q = (oc * NS + st) % 2
                    for j in range(FC):
                        P.pe(lambda e, j=j, s=s, q=q, t0=t0: e.matmul(
                            ps_o[q][:, :], lhsT=wout_sb[s][:, j, :], rhs=act_sb[:, j, t0:t0 + TT],
                            start=(j == 0), stop=(j == FC - 1)), r=["wout%d" % s, "act"], w=["ps_o%d" % q])
                    P.dve(lambda e, oc=oc, q=q, t0=t0: e.scalar_tensor_tensor(
                        out=x_sb[:, oc, t0:t0 + TT], in0=ps_o[q][:, :], scalar=0.5, in1=x_sb[:, oc, t0:t0 + TT],
                        op0=ALU.mult, op1=ALU.add), r=["ps_o%d" % q, "x"], w=["x"])
            for kc in range(KC):
                P.dma("sp", yT_v[:, kc, tb0:tb0 + TB], x_sb[:, kc, :], r=["x"], w=["yT"])
            for st in range(NS):
                _rmsnorm(P, nc, x_sb, "x", ho_sb, "ho", g_sb, KC, sq_sb, ones_sb, ps_ss, rstd_sb, st * TT, TT, "", o0=0)
                for kc in range(KC):
                    P.dma("sp", hT_v[:, kc, tb0 + st * TT:tb0 + (st + 1) * TT], ho_sb[:, kc, :], r=["ho"], w=["hT"])
        pass


def _blockones(P, t, val, key):
    P.pool(lambda e: e.memset(t[:, :], 0.0), w=[key])
    P.pool(lambda e: e.memset(t[0:64, 0:64], val), w=[key])
    P.pool(lambda e: e.memset(t[64:128, 64:128], val), w=[key])


def emit_LP(nc, P, io, NT, has_v, TB=1024, TT=512):
    hT = io("hT", [D, NT], "in")
    wn = io("wn", [KC, 128, KC, 128], "in")
    gn = io("gn", [128, 1], "in")
    nT = io("nT", [D, NT], "out")
    if has_v:
        wv = io("wv", [128, KC, D], "in")
        v_tok = io("v_tok", [NT, D], "out")
    hT_v = hT.rearrange("(kc p) t -> p kc t", p=128)
    nT_v = nT.rearrange("(kc p) t -> p kc t", p=128)
    NB, NS = NT // TB, TB // TT
    with P.stage():
        h_sb = P.sb("h_sb", [128, KC, TB], BF16)
        wn_sb = [P.sb("wn_sb%d" % i, [128, KC, 128], BF16) for i in range(2)]
        sq_sb = [P.sb("sq_sb%d" % i, [128, TT], BF16) for i in range(3)]
        rstd_sb = [P.sb("rstd_sb%d" % i, [128, TT], F32) for i in range(3)]
        o_sb = [P.sb("o_sb%d" % i, [128, TT], F32) for i in range(3)]
        g_sb = P.sb("g_sb", [128, 1], F32)
        bo_sb = P.sb("bo_sb", [128, 128], BF16)
        ps_p = [P.ps("ps_p%d" % i, [128, TT]) for i in range(3)]
        ps_s = [P.ps("ps_s%d" % i, [128, TT]) for i in range(3)]
        P.psum_keys.update(["ps_p0", "ps_p1", "ps_p2", "ps_s0", "ps_s1", "ps_s2", "ps_v0", "ps_v1"])
        if has_v:
            wv_sb = P.sb("wv_sb", [128, KC, D], BF16)
            v_sb = [P.sb("v_sb%d" % i, [128, 512], F32) for i in range(2)]
            ps_v = [P.ps("ps_v%d" % i, [128, 512]) for i in range(2)]
            for kc in range(KC):
                P.dma("pool", wv_sb[:, kc, :], wv[:, kc, :], w=["wv"])
        P.dma("sp", g_sb[:, :], gn[:, :], w=["gn"])
        _blockones(P, bo_sb, 1.0 / 64, "bo")
        it = 0
        nv = 0
        for b in range(NB):
            tb0 = b * TB
            for kc in range(KC):
                P.dma("pool", h_sb[:, kc, :], hT_v[:, kc, tb0:tb0 + TB], w=["h"])
            pend = []

            def lp_tail(u):
                oc, t0, q = u
                P.pe(lambda e, q=q: e.matmul(ps_s[q][:, :], lhsT=bo_sb[:, :], rhs=sq_sb[q][:, :],
                                             start=True, stop=True), r=["sq%d" % q, "bo"], w=["ps_s%d" % q])
                P.act(lambda e, q=q: e.activation(out=rstd_sb[q][:, :], in_=ps_s[q][:, :], func=AF.Sqrt,
                                                  bias=EPS, scale=1.0), r=["ps_s%d" % q], w=["rstd%d" % q])
                P.dve(lambda e, q=q: e.reciprocal(out=rstd_sb[q][:, :], in_=rstd_sb[q][:, :]),
                      r=["rstd%d" % q], w=["rstd%d" % q])
                P.dve(lambda e, q=q: e.scalar_tensor_tensor(
                    out=o_sb[q][:, :], in0=ps_p[q][:, :], scalar=g_sb[:, 0:1], in1=rstd_sb[q][:, :],
                    op0=ALU.mult, op1=ALU.mult), r=["ps_p%d" % q, "rstd%d" % q, "gn"], w=["o%d" % q])
                P.dma("sp", nT_v[:, oc, tb0 + t0:tb0 + t0 + TT], o_sb[q][:, :], r=["o%d" % q], w=["nT"])

            for oc in range(KC):
                s = oc % 2
                P.dma("pool", wn_sb[s][:, :, :], wn[oc], w=["wn%d" % s])
                for st in range(NS):
                    t0 = st * TT
                    q = it % 3
                    it += 1
                    for kc in range(KC):
                        P.pe(lambda e, kc=kc, s=s, q=q, t0=t0: e.matmul(
                            ps_p[q][:, :], lhsT=wn_sb[s][:, kc, :], rhs=h_sb[:, kc, t0:t0 + TT],
                            start=(kc == 0), stop=(kc == KC - 1)), r=["wn%d" % s, "h"], w=["ps_p%d" % q])
                    P.act(lambda e, q=q: e.activation(out=sq_sb[q][:, :], in_=ps_p[q][:, :], func=AF.Square),
                          r=["ps_p%d" % q], w=["sq%d" % q])
                    pend.append((oc, t0, q))
                    if len(pend) > 1:
                        lp_tail(pend.pop(0))
            while pend:
                lp_tail(pend.pop(0))
            if has_v:
                for tbk in range(TB // 128):
                    for half in range(2):
                        q = nv % 2
                        nv += 1
                        for kc in range(KC):
                            P.pe(lambda e, kc=kc, q=q, tbk=tbk, half=half: e.matmul(
                                ps_v[q][:, :], lhsT=h_sb[:, kc, tbk * 128:(tbk + 1) * 128],
                                rhs=wv_sb[:, kc, half * 512:(half + 1) * 512],
                                start=(kc == 0), stop=(kc == KC - 1)), r=["wv", "h"], w=["ps_v%d" % q])
                        P.act(lambda e, q=q: e.copy(out=v_sb[q][:, :], in_=ps_v[q][:, :]),
                              r=["ps_v%d" % q], w=["v%d" % q])
                        P.dma("sp", v_tok[tb0 + tbk * 128:tb0 + (tbk + 1) * 128, half * 512:(half + 1) * 512],
                              v_sb[q][:, :], r=["v%d" % q], w=["v_tok"])
        pass


LAM_INIT1 = 0.8 - 0.6 * float(np.exp(-0.3))
SUBLN_EPS = 1e-5
NDD = 67


def emit_ATT(nc, P, io, T, HL=4, gath=None):
    if gath is None:
        qT = io("qT", [HL * 128, T], "in")
        kT = io("kT", [HL * 128, T], "in")
        v_tok = io("v_tok", [T, HL * 128], "in")
    qaug = io("qaug", [5, T], "in")
    kaug = io("kaug", [HL, 5, T], "in")
    btab = io("btab", [128, HL * NDD], "in")
    tri = io("tri", [128, 128], "in")
    lam = io("lam", [1, 256], "in")
    subg = io("subg", [128, 128], "in")
    o_tok = io("o_tok", [T, HL * 128], "out")
    NQB = T // 512
    NKB = T // 128
    with P.stage():
        q_sbs = [P.sb("q_sb%d" % i, [69, 2, T], BF16) for i in range(2)]
        k_sbs = [P.sb("k_sb%d" % i, [69, 2, T], BF16) for i in range(2)]
        v_sbs = [P.sb("v_sb%d" % i, [128, NKB, 130], BF16) for i in range(2)]
        bt_sb = P.sb("bt_sb", [128, HL * NDD], F32)
        tri_sb = P.sb("tri_sb", [128, 128], BF16)
        z_sb = P.sb("z_sb", [128, 512], BF16)
        pt_sb = [P.sb("pt_sb%d" % i, [128, 512], BF16) for i in range(6)]
        lam_sb = P.sb("lam_sb", [1, 256], F32)
        lt_sb = P.sb("lt_sb", [1, 128], F32)
        ls_sb = P.sb("ls_sb", [1, 4], F32)
        one_row = P.sb("one_row", [1, 128], F32)
        nl_sb = P.sb("nl_sb", [128, 1], F32)
        eps_sb = P.sb("eps_sb", [128, 1], F32)
        sg_sb = P.sb("sg_sb", [128, 128], F32)
        rec_sb = [P.sb("rec_sb%d" % i, [128, 4], F32) for i in range(2)]
        o0_sb = [P.sb("o0_sb%d" % i, [128, 128], F32) for i in range(2)]
        od_sb = [P.sb("od_sb%d" % i, [128, 128], F32) for i in range(2)]
        junk_sb = [P.sb("junk_sb%d" % i, [128, 128], F32) for i in range(2)]
        out_sb = [P.sb("out_sb%d" % i, [128, 128], F32) for i in range(2)]
        oc_sb = [P.sb("oc_sb%d" % i, [128, 512], F32) for i in range(3)]
        ps_S = [P.ps("ps_S%d" % i, [128, 512]) for i in range(4)]
        ps_O = [P.ps("ps_O%d" % i, [128, 512]) for i in range(3)]
        ps_m = P.ps("ps_m", [128, 512])
        P.psum_keys.update(["S0", "S1", "S2", "S3", "O0", "O1", "O2", "ps_m"])

        P.dma("sp", bt_sb[:, :], btab[:, :], w=["bt"])
        P.dma("pool", tri_sb[:, :], tri[:, :], w=["tri"])
        P.dma("sp", lam_sb[:, :], lam[:, :], w=["lam"])
        P.dma("sp", sg_sb[:, :], subg[:, :], w=["sg"])
        P.pool(lambda e: e.memset(z_sb[:, :], 0.0), w=["z"])
        P.pool(lambda e: e.memset(eps_sb[:, :], SUBLN_EPS), w=["epsc"])
        P.pool(lambda e: e.memset(one_row[:, :], 1.0), w=["one_row"])
        P.pool(lambda e: e.memset(v_sbs[0][:, :, 128:130], 1.0), w=["vones"])
        P.pool(lambda e: e.memset(v_sbs[1][:, :, 128:130], 1.0), w=["vones"])
        P.dve(lambda e: e.tensor_tensor(out=lt_sb[:, 0:64], in0=lam_sb[:, 0:64], in1=lam_sb[:, 64:128], op=ALU.mult),
              r=["lam"], w=["lt"])
        P.dve(lambda e: e.tensor_tensor(out=lt_sb[:, 64:128], in0=lam_sb[:, 128:192], in1=lam_sb[:, 192:256],
                                        op=ALU.mult), r=["lam"], w=["lt"])
        P.dve(lambda e: e.tensor_reduce(out=ls_sb[:, 0:1], in_=lt_sb[:, 0:64], axis=AX.X, op=ALU.add),
              r=["lt"], w=["ls"])
        P.dve(lambda e: e.tensor_reduce(out=ls_sb[:, 1:2], in_=lt_sb[:, 64:128], axis=AX.X, op=ALU.add),
              r=["lt"], w=["ls"])
        P.act(lambda e: e.activation(out=ls_sb[:, 0:2], in_=ls_sb[:, 0:2], func=AF.Exp), r=["ls"], w=["ls"])
        P.dve(lambda e: e.scalar_tensor_tensor(out=ls_sb[:, 2:3], in0=ls_sb[:, 1:2], scalar=-LAM_INIT1,
                                               in1=ls_sb[:, 0:1], op0=ALU.add, op1=ALU.subtract),
              r=["ls"], w=["ls"])
        P.pe(lambda e: e.matmul(ps_m[:, 0:1], lhsT=one_row[:, :], rhs=ls_sb[:, 2:3], start=True, stop=True),
             r=["ls", "one_row"], w=["ps_m"])
        P.dve(lambda e: e.tensor_copy(out=nl_sb[:, :], in_=ps_m[:, 0:1]), r=["ps_m"], w=["nl"])
        P.dve(lambda e: e.tensor_scalar(out=sg_sb[:, :], in0=sg_sb[:, :], scalar1=1.0 - LAM_INIT1, scalar2=None,
                                        op0=ALU.mult), r=["sg"], w=["sg"])

        def acc(a):
            return ps_O[a // 3], (a % 3) * 132

        sl = 0
        fin = 0
        if gath is not None:
            qg_, kg_, vg_, sel_ = gath
            HT_ = T // 2
            sel_sb = P.sb("sel_sb", [128, 2], F32)
            P.dma("sp", sel_sb[:, :], sel_[:, :], w=["sel"])
            stg_sb = [P.sb("stg_sb%d" % i, [128, HT_], BF16) for i in range(2)]
            nst = [0]

            def blend(dst, A, B, npart, key):
                q_ = nst[0] % 2
                nst[0] += 1
                P.dma("pool", dst, A, w=[key])
                n = dst.shape[1] if len(dst.shape) == 2 else dst.shape[1] * dst.shape[2]
                st = stg_sb[q_][0:npart, 0:n]
                if len(dst.shape) == 3:
                    st = st.rearrange("p (a b) -> p a b", b=dst.shape[2])
                P.dma("pool", st, B, w=["stg%d" % q_])
                P.dve(lambda e: e.tensor_scalar(out=dst, in0=dst, scalar1=sel_sb[0:npart, 0:1], scalar2=None,
                                                op0=ALU.mult), r=[key, "sel"], w=[key])
                P.dve(lambda e: e.scalar_tensor_tensor(out=dst, in0=st, scalar=sel_sb[0:npart, 1:2], in1=dst,
                                                       op0=ALU.mult, op1=ALU.add), r=[key, "stg%d" % q_, "sel"], w=[key])

        def load_head(hl):
            hb = hl % 2
            for c in range(2):
                r0 = hl * 128 + c * 64
                if gath is None:
                    P.dma("pool", q_sbs[hb][0:64, c, :], qT[r0:r0 + 64, :], w=["q%d" % hb])
                    P.dma("pool", k_sbs[hb][0:64, c, :], kT[r0:r0 + 64, :], w=["k%d" % hb])
                else:
                    for j in range(2):
                        cs = slice(j * HT_, (j + 1) * HT_)
                        blend(q_sbs[hb][0:64, c, cs], qg_.rows(j, r0, r0 + 64), qg_.rows(j, 512 + r0, 512 + r0 + 64),
                              64, "q%d" % hb)
                        blend(k_sbs[hb][0:64, c, cs], kg_.rows(j, r0, r0 + 64), kg_.rows(j, 512 + r0, 512 + r0 + 64),
                              64, "k%d" % hb)
                P.dma("pool", q_sbs[hb][64:69, c, :], qaug[:, :], w=["q%d" % hb])
                P.dma("pool", k_sbs[hb][64:69, c, :], kaug[hl], w=["k%d" % hb])
            if gath is None:
                P.dma("pool", v_sbs[hb][:, :, 0:128],
                      v_tok.rearrange("(blk p) v -> p blk v", p=128)[:, :, hl * 128:(hl + 1) * 128], w=["v%d" % hb])
            else:
                nb_ = HT_ // 128
                for j in range(2):
                    rc = vg_.RC
                    for ch in range(HT_ // rc):
                        nbc = rc // 128
                        blk0 = j * nb_ + ch * nbc
                        rows = vg_.rows(j, ch * rc, (ch + 1) * rc).rearrange("(blk p) v -> p blk v", p=128)
                        blend(v_sbs[hb][:, blk0:blk0 + nbc, 0:128], rows[:, :, hl * 128:(hl + 1) * 128],
                              rows[:, :, 512 + hl * 128:512 + (hl + 1) * 128], 128, "v%d" % hb)

        load_head(0)
        for hl in range(HL):
            hb = hl % 2
            q_sb, k_sb, v_sb = q_sbs[hb], k_sbs[hb], v_sbs[hb]
            qk_, kk_, vk_ = "q%d" % hb, "k%d" % hb, "v%d" % hb
            if hl + 1 < HL:
                load_head(hl + 1)
            for qb in range(NQB):
                for bk in range(3):
                    P.pe(lambda e, bk=bk: e.matmul(ps_O[bk][:, :], lhsT=z_sb[:, 0:128], rhs=z_sb[:, :],
                                                   start=True, stop=False), r=["z"], w=["O%d" % bk])
                nkb = 4 * qb + 4
                units = [(kb, c) for kb in range(nkb) for c in range(2)]
                pend = []

                def emit_pv(u, qb=qb, v_sb=v_sb, vk=vk_):
                    kb, c, s, j0 = u
                    for jj in range(j0 // 128, 4):
                        a = jj * 2 + c
                        pb, off = acc(a)
                        P.pe(lambda e, s=s, jj=jj, kb=kb, pb=pb, off=off, last=(kb == 4 * qb + jj): e.matmul(
                            pb[:, off:off + 129], lhsT=pt_sb[s][:, jj * 128:(jj + 1) * 128],
                            rhs=v_sb[:, kb, 0:129], start=False, stop=last),
                            r=["pt%d" % s, vk, "vones"], w=["O%d" % (a // 3)])

                for kb, c in units:
                    rr = kb - 4 * qb
                    j0 = max(0, 128 * rr)
                    s = sl % 4
                    sp_ = sl % 6
                    sl += 1
                    P.pe(lambda e, s=s, c=c, kb=kb, qb=qb, j0=j0, k_sb=k_sb, q_sb=q_sb: e.matmul(
                        ps_S[s][:, j0:512], lhsT=k_sb[0:69, c, kb * 128:(kb + 1) * 128],
                        rhs=q_sb[0:69, c, qb * 512 + j0:(qb + 1) * 512], start=True, stop=True),
                        r=[qk_, kk_], w=["S%d" % s])
                    P.act(lambda e, s=s, sp_=sp_, j0=j0: e.activation(
                        out=pt_sb[sp_][:, j0:512], in_=ps_S[s][:, j0:512], func=AF.Exp, scale=0.125),
                        r=["S%d" % s], w=["pt%d" % sp_])
                    if rr >= 0:
                        P.dve(lambda e, sp_=sp_, j0=j0: e.tensor_tensor(
                            out=pt_sb[sp_][:, j0:j0 + 128], in0=pt_sb[sp_][:, j0:j0 + 128], in1=tri_sb[:, :],
                            op=ALU.mult), r=["pt%d" % sp_, "tri"], w=["pt%d" % sp_])
                    pend.append((kb, c, sp_, j0))
                    if len(pend) > 3:
                        emit_pv(pend.pop(0))
                while pend:
                    emit_pv(pend.pop(0))
                for bk in range(3):
                    P.dve(lambda e, bk=bk: e.tensor_copy(out=oc_sb[bk][:, :], in_=ps_O[bk][:, :]),
                          r=["O%d" % bk], w=["oc%d" % bk])
                for jj in range(4):
                    f = fin % 2
                    fin += 1
                    pb0, off0 = oc_sb[(jj * 2) // 3], ((jj * 2) % 3) * 132
                    pb1, off1 = oc_sb[(jj * 2 + 1) // 3], ((jj * 2 + 1) % 3) * 132
                    k0, k1 = "oc%d" % ((jj * 2) // 3), "oc%d" % ((jj * 2 + 1) // 3)
                    P.dve(lambda e, f=f, pb0=pb0, off0=off0: e.reciprocal(
                        out=rec_sb[f][:, 0:1], in_=pb0[:, off0 + 128:off0 + 129]), r=[k0], w=["rec%d" % f])
                    P.dve(lambda e, f=f, pb1=pb1, off1=off1: e.reciprocal(
                        out=rec_sb[f][:, 1:2], in_=pb1[:, off1 + 128:off1 + 129]), r=[k1], w=["rec%d" % f])
                    P.dve(lambda e, f=f: e.tensor_tensor(out=rec_sb[f][:, 2:3], in0=rec_sb[f][:, 1:2],
                                                         in1=nl_sb[:, 0:1], op=ALU.mult),
                          r=["rec%d" % f, "nl"], w=["rec%d" % f])
                    P.dve(lambda e, f=f, pb0=pb0, off0=off0: e.tensor_scalar(
                        out=o0_sb[f][:, :], in0=pb0[:, off0:off0 + 128], scalar1=rec_sb[f][:, 0:1], scalar2=None,
                        op0=ALU.mult), r=[k0, "rec%d" % f], w=["o0%d" % f])
                    P.dve(lambda e, f=f, pb1=pb1, off1=off1: e.scalar_tensor_tensor(
                        out=od_sb[f][:, :], in0=pb1[:, off1:off1 + 128], scalar=rec_sb[f][:, 2:3],
                        in1=o0_sb[f][:, :], op0=ALU.mult, op1=ALU.add),
                        r=[k1, "rec%d" % f, "o0%d" % f], w=["od%d" % f])
                    P.dve(lambda e, f=f: e.tensor_tensor(out=junk_sb[f][:, :], in0=od_sb[f][:, :], in1=od_sb[f][:, :],
                                                         op=ALU.mult), r=["od%d" % f], w=["junk%d" % f])
                    P.dve(lambda e, f=f: e.tensor_reduce(out=rec_sb[f][:, 3:4], in_=junk_sb[f][:, :], axis=AX.X,
                                                         op=ALU.add), r=["junk%d" % f], w=["ss%d" % f])
                    P.act(lambda e, f=f: e.activation(out=rec_sb[f][:, 3:4], in_=rec_sb[f][:, 3:4], func=AF.Ln,
                                                      bias=eps_sb[:, 0:1], scale=1.0 / 128),
                          r=["ss%d" % f, "epsc"], w=["ss%d" % f])
                    P.act(lambda e, f=f: e.activation(out=rec_sb[f][:, 3:4], in_=rec_sb[f][:, 3:4], func=AF.Exp,
                                                      scale=-0.5), r=["ss%d" % f], w=["ss%d" % f])
                    P.dve(lambda e, f=f: e.scalar_tensor_tensor(
                        out=out_sb[f][:, :], in0=od_sb[f][:, :], scalar=rec_sb[f][:, 3:4], in1=sg_sb[:, :],
                        op0=ALU.mult, op1=ALU.mult), r=["od%d" % f, "ss%d" % f, "sg"], w=["out%d" % f])
                    P.dma("sp", o_tok[qb * 512 + jj * 128:qb * 512 + (jj + 1) * 128, hl * 128:(hl + 1) * 128],
                          out_sb[f][:, :], r=["out%d" % f], w=["o_tok"])
        pass


def att_consts(T, heads):
    t = np.arange(T)
    one = np.ones_like(t)
    qaug = np.stack([(t % 512) // 16, t % 16, one, one, t // 512]).astype(np.float32)
    kaug = np.zeros((len(heads), 5, T), np.float32)
    btab = np.zeros((128, len(heads) * NDD), np.float32)
    for i, h in enumerate(heads):
        slope = 2.0 ** (-(h + 1))
        kaug[i, 0, :] = -16.0 * slope / 0.125
        kaug[i, 1, :] = -slope / 0.125
        kaug[i, 2, :] = slope * (t % 128) / 0.125
        kaug[i, 3, :] = slope * 128.0 * (t // 128) / 0.125
        kaug[i, 4, :] = -512.0 * slope / 0.125
    tri = (np.arange(128)[:, None] <= np.arange(128)[None, :]).astype(np.float32)
    return qaug, kaug, btab, tri


C0 = float(np.exp(-0.5))
RW_LN_EPS = 64e-5
CL = 128


def emit_RW(nc, P, io, T, TT=256, LEVEL=9):
    dt_in = lambda n, s: io(n, s, "in")
    hTp = dt_in("hTp", [D, T + 1])
    wr_d, wk_d, wv_d = dt_in("wr", [D, 512]), dt_in("wk", [D, 512]), dt_in("wv", [D, 512])
    l1_d = dt_in("l1", [D, 288])
    w2_d, a2_d, g2_d = dt_in("w2c", [64, 512]), dt_in("a2c", [64, 512]), dt_in("g2c", [160, 512])
    mu_d = dt_in("mu6", [128, 48])
    cv_d = dt_in("cv", [128, 20])
    lng_d, lnb_d = dt_in("lng", [128, 512]), dt_in("lnb", [128, 512])
    ident_d, mask2_d, maskL_d = dt_in("ident", [128, 128]), dt_in("mask2", [128, 256]), dt_in("maskL", [128, 128])
    rmask_d, ind_d = dt_in("rmask", [128, TT]), dt_in("ind", [128, 2])
    yg = io("yg_tok", [T, 512], "out")
    hTp_v = hTp.rearrange("(kc p) t -> p kc t", p=128)
    NTI = T // TT
    NCL = TT // CL
    with P.stage():
        sb = P.sb
        W1 = {n: sb("W1" + n, [128, KC, 512], BF16) for n in "rkv"}
        W2 = {n: sb("W2" + n, [128, KC, 512], BF16) for n in "rkv"}
        L1a = sb("L1a", [128, KC, 288], BF16)
        L1b = sb("L1b", [128, KC, 288], BF16)
        stg = [sb("stg%d" % i, [128, 512], F32) for i in range(2)]
        w2_sb, a2_sb = sb("w2_sb", [64, 512], BF16), sb("a2_sb", [64, 512], BF16)
        g2a_sb, g2b_sb = sb("g2a_sb", [128, 512], BF16), sb("g2b_sb", [32, 512], BF16)
        mu_sb, omu_sb, cv_sb = sb("mu_sb", [128, 48], F32), sb("omu_sb", [128, 48], F32), sb("cv_sb", [128, 24], F32)
        lng_sb, lnb_sb = sb("lng_sb", [128, 512], F32), sb("lnb_sb", [128, 512], F32)
        ident_bf, identf = sb("ident_bf", [128, 128], BF16), sb("identf", [128, 128], F32)
        mask2_sb, maskL_sb = sb("mask2_sb", [128, 256], F32), sb("maskL_sb", [128, 128], F32)
        rmask_sb, ind_sb, bo_sb = sb("rmask_sb", [128, TT], F32), sb("ind_sb", [128, 2], BF16), sb("bo_sb", [128, 128], BF16)
        hp = [sb("hp%d" % i, [128, KC, TT], BF16) for i in range(2)]
        hq = [sb("hq%d" % i, [128, KC, TT], BF16) for i in range(2)]
        r32, k32 = sb("r32", [128, 4, TT], F32), sb("k32", [128, 4, TT], F32)
        vtok, v32 = sb("vtok", [128, NCL, 512], BF16), sb("v32", [128, NCL, 512], F32)
        gtok = sb("gtok", [128, NCL, 512], F32)
        lw_sb, la_sb = sb("lw_sb", [64, TT], BF16), sb("la_sb", [64, TT], BF16)
        lga_sb, lgb_sb = sb("lga_sb", [128, TT], BF16), sb("lgb_sb", [32, TT], BF16)
        sc = {n: sb("sc_" + n, [128, TT], F32) for n in
              ("sg", "a", "kk", "rn", "kkn", "tmp", "kmod", "bvec", "cs", "tmp2", "Em", "Epv")}
        kk2_sb = sb("kk2_sb", [128, TT], BF16)
        sc2 = {n: sb("sc2_" + n, [128, TT], F32) for n in sc}
        kk2b_sb = sb("kk2b_sb", [128, TT], BF16)
        Ep = sb("Ep", [128, 4, TT], F32)
        AR = sb("AR", [128, 4, NCL, 256], BF16)
        bt, kt, rk = sb("bt", [128, 4, TT], BF16), sb("kt", [128, 4, TT], BF16), sb("rk", [128, 4, TT], BF16)
        NTb_sb, NTk_sb = sb("NTb_sb", [128, 4, 512], BF16), sb("NTk_sb", [128, 4, 512], BF16)
        at2, rt2 = sb("at2", [128, 4, 2, TT], BF16), sb("rt2", [128, 4, 2, TT], BF16)
        bt2, kt2 = sb("bt2", [128, 4, 2, TT], BF16), sb("kt2", [128, 4, 2, TT], BF16)
        XX = [[sb("XX%d_%d" % (pc, i), [128, 512], BF16) for i in range(2)] for pc in range(4)]
        Pp = [[sb("Pp%d_%d" % (pc, i), [128, 256], BF16) for i in range(2)] for pc in range(4)]
        Pfin = sb("Pfin", [128, 4, 256], BF16)
        btok, ktok = sb("btok", [128, 4, 128], BF16), sb("ktok", [128, 4, 128], BF16)
        Zs, Us = sb("Zs", [128, 4, 128], BF16), sb("Us", [128, 4, 128], BF16)
        ytok = sb("ytok", [128, NCL, 512], F32)
        H32, Hbf, Htmp = sb("H32", [128, 4, 128], F32), sb("Hbf", [128, 4, 128], BF16), sb("Htmp", [128, 4, 128], F32)
        ysq, st_sb = sb("ysq", [128, 512], F32), sb("st_sb", [128, 40], F32)
        out_sb = [sb("out_sb%d" % i, [128, 512], F32) for i in range(2)]
        bk = [P.ps("bk%d" % i, [128, 512]) for i in range(5)] + [None] + [P.ps("bk%d" % i, [128, 512]) for i in (6, 7)]
        bk5 = P.ps("bk5", [128, 1024], BF16)
        BK = lambda i: "bk%d" % i
        P.psum_keys.update([BK(i) for i in range(8)])

        for dst, src, q in ((mu_sb, mu_d, "sp"), (lng_sb, lng_d, "sp"), (lnb_sb, lnb_d, "sp"),
                            (identf, ident_d, "sp"), (mask2_sb, mask2_d, "sp"), (maskL_sb, maskL_d, "sp"),
                            (rmask_sb, rmask_d, "sp"), (ident_bf, ident_d, "pool"), (ind_sb, ind_d, "pool"),
                            (w2_sb, w2_d, "pool"), (a2_sb, a2_d, "pool")):
            P.dma(q, dst[:, :], src[:, :], w=["const"])
        P.dma("sp", cv_sb[:, 0:20], cv_d[:, :], w=["const"])
        P.dma("pool", g2a_sb[:, :], g2_d[0:128, :], w=["const"])
        P.dma("pool", g2b_sb[:, :], g2_d[128:160, :], w=["const"])
        _blockones(P, bo_sb, 1.0, "bo")
        P.dve(lambda e: e.tensor_scalar(out=omu_sb[:, :], in0=mu_sb[:, :], scalar1=-1.0, scalar2=1.0,
                                        op0=ALU.mult, op1=ALU.add), r=["const"], w=["omu"])
        P.dve(lambda e: e.tensor_scalar(out=cv_sb[:, 20:24], in0=cv_sb[:, 12:16], scalar1=-1.0, scalar2=1.0,
                                        op0=ALU.mult, op1=ALU.add), r=["const"], w=["cv2"])
        P.pool(lambda e: e.memset(H32[:, :, :], 0.0), w=["H32"])
        for tz, kz in ((at2, "at2"), (rt2, "rt2"), (bt2, "bt2"), (kt2, "kt2")):
            P.pool(lambda e, tz=tz: e.memset(tz[:, :, :, :], 0.0), w=[kz])
        P.pool(lambda e: e.memset(Hbf[:, :, :], 0.0), w=["Hbf"])
        ns = 0
        for wi, (n, src) in enumerate((("r", wr_d), ("k", wk_d), ("v", wv_d))):
            for kc in range(KC):
                s = ns % 2
                ns += 1
                P.dma("sp", stg[s][:, :], src[kc * 128:(kc + 1) * 128, :], w=["stg%d" % s])
                P.dve(lambda e, s=s, n=n, kc=kc, wi=wi: e.tensor_scalar(
                    out=W1[n][:, kc, :], in0=stg[s][:, :], scalar1=omu_sb[:, wi * 8 + kc:wi * 8 + kc + 1],
                    scalar2=None, op0=ALU.mult), r=["stg%d" % s, "omu"], w=["W"])
                P.dve(lambda e, s=s, n=n, kc=kc, wi=wi: e.tensor_scalar(
                    out=W2[n][:, kc, :], in0=stg[s][:, :], scalar1=mu_sb[:, wi * 8 + kc:wi * 8 + kc + 1],
                    scalar2=None, op0=ALU.mult), r=["stg%d" % s, "const"], w=["W"])
        for kc in range(KC):
            s_ = ns % 2
            ns += 1
            P.dma("sp", stg[s_][:, 0:288], l1_d[kc * 128:(kc + 1) * 128, :], w=["stg%d" % s_])
            for li, (c0, c1) in enumerate(((0, 64), (64, 128), (128, 288))):
                wi = 3 + li
                P.dve(lambda e, kc=kc, wi=wi, c0=c0, c1=c1, s_=s_: e.tensor_scalar(
                    out=L1a[:, kc, c0:c1], in0=stg[s_][:, c0:c1], scalar1=omu_sb[:, wi * 8 + kc:wi * 8 + kc + 1],
                    scalar2=None, op0=ALU.mult), r=["stg%d" % s_, "omu"], w=["W"])
                P.dve(lambda e, kc=kc, wi=wi, c0=c0, c1=c1, s_=s_: e.tensor_scalar(
                    out=L1b[:, kc, c0:c1], in0=stg[s_][:, c0:c1], scalar1=mu_sb[:, wi * 8 + kc:wi * 8 + kc + 1],
                    scalar2=None, op0=ALU.mult), r=["stg%d" % s_, "const"], w=["W"])

        nb = [0]

        def nbk():
            nb[0] += 1
            return nb[0] % 2

        def proj_fm(pb, key, wa, wb, c0, c1, hpt, hk, n, hqt=None):
            m = c1 - c0
            hpt, hqt = hpt
            for kc in range(KC):
                P.pe(lambda e, kc=kc: e.matmul(pb[0:m, 0:n], lhsT=wa[:, kc, c0:c1], rhs=hpt[:, kc, 0:n],
                                               start=(kc == 0), stop=False), r=["W", hk], w=[key])
            for kc in range(KC):
                P.pe(lambda e, kc=kc: e.matmul(pb[0:m, 0:n], lhsT=wb[:, kc, c0:c1], rhs=hqt[:, kc, 0:n],
                                               start=False, stop=(kc == KC - 1)), r=["W", hk], w=[key])

        fo = 0
        for ti in range(NTI if LEVEL >= 1 else 0):
            t0 = ti * TT
            hs = ti % 2
            hk = "hp%d" % hs
            hp_, hq_ = hp[hs], hq[hs]
            hpt = (hp_, hq_)
            P.dma("pool", hp_[:, :, :], hTp_v[:, :, t0 + 1:t0 + TT + 1], w=[hk])
            P.dma("pool", hq_[:, :, :], hTp_v[:, :, t0:t0 + TT], w=[hk])
            for pc in range(4):
                for n, dst in (("r", r32), ("k", k32)):
                    b = nbk()
                    proj_fm(bk[b], BK(b), W1[n], W2[n], pc * 128, (pc + 1) * 128, hpt, hk, TT)
                    P.act(lambda e, b=b, dst=dst, pc=pc: e.copy(out=dst[:, pc, :], in_=bk[b][:, 0:TT]),
                          r=[BK(b)], w=[n + "32"])
            for cl in range(NCL):
                b = nbk()
                for kc in range(KC):
                    P.pe(lambda e, kc=kc, b=b, cl=cl, hpt=hp_: e.matmul(
                        bk[b][:, :], lhsT=hpt[:, kc, cl * 128:(cl + 1) * 128], rhs=W1["v"][:, kc, :],
                        start=(kc == 0), stop=False), r=["W", hk], w=[BK(b)])
                for kc in range(KC):
                    P.pe(lambda e, kc=kc, b=b, cl=cl, hpt=hq_: e.matmul(
                        bk[b][:, :], lhsT=hpt[:, kc, cl * 128:(cl + 1) * 128], rhs=W2["v"][:, kc, :],
                        start=False, stop=(kc == KC - 1)), r=["W", hk], w=[BK(b)])
                P.act(lambda e, b=b, cl=cl: e.copy(out=vtok[:, cl, :], in_=bk[b][:, :]), r=[BK(b)], w=["vtok"])
                P.act(lambda e, b=b, cl=cl: e.copy(out=v32[:, cl, :], in_=bk[b][:, :]), r=[BK(b)], w=["v32"])
            b = nbk()
            proj_fm(bk[b], BK(b), L1a, L1b, 0, 64, hpt, hk, TT)
            P.act(lambda e, b=b: e.activation(out=lw_sb[:, :], in_=bk[b][0:64, 0:TT], func=AF.Tanh),
                  r=[BK(b)], w=["lw"])
            b = nbk()
            proj_fm(bk[b], BK(b), L1a, L1b, 64, 128, hpt, hk, TT)
            P.act(lambda e, b=b: e.copy(out=la_sb[:, :], in_=bk[b][0:64, 0:TT]), r=[BK(b)], w=["la"])
            b = nbk()
            proj_fm(bk[b], BK(b), L1a, L1b, 128, 256, hpt, hk, TT)
            P.act(lambda e, b=b: e.activation(out=lga_sb[:, :], in_=bk[b][:, 0:TT], func=AF.Sigmoid),
                  r=[BK(b)], w=["lg"])
            b = nbk()
            proj_fm(bk[b], BK(b), L1a, L1b, 256, 288, hpt, hk, TT)
            P.act(lambda e, b=b: e.activation(out=lgb_sb[:, :], in_=bk[b][0:32, 0:TT], func=AF.Sigmoid),
                  r=[BK(b)], w=["lg"])
            for cl in range(NCL):
                b = nbk()
                P.pe(lambda e, b=b, cl=cl: e.matmul(bk[b][:, :], lhsT=lga_sb[:, cl * 128:(cl + 1) * 128],
                                                   rhs=g2a_sb[:, :], start=True, stop=False),
                     r=["lg", "const"], w=[BK(b)])
                P.pe(lambda e, b=b, cl=cl: e.matmul(bk[b][:, :], lhsT=lgb_sb[:, cl * 128:(cl + 1) * 128],
                                                   rhs=g2b_sb[:, :], start=False, stop=True),
                     r=["lg", "const"], w=[BK(b)])
                P.act(lambda e, b=b, cl=cl: e.copy(out=gtok[:, cl, :], in_=bk[b][:, :]), r=[BK(b)], w=["gtok"])
            def a5_gen(pc, sc, kk2_sb, sx):
                cvc = lambda w, pc=pc: cv_sb[:, w * 4 + pc:w * 4 + pc + 1]
                b = nbk()
                P.pe(lambda e, b=b, pc=pc: e.matmul(bk[b][:, 0:TT], lhsT=w2_sb[:, pc * 128:(pc + 1) * 128],
                                                   rhs=lw_sb[:, :], start=True, stop=True),
                     r=["lw", "const"], w=[BK(b)])
                yield
                P.act(lambda e, b=b, pc=pc: e.activation(out=sc["sg"][:, :], in_=bk[b][:, 0:TT], func=AF.Sigmoid,
                                                         bias=cv_sb[:, pc:pc + 1], scale=1.0),
                      r=[BK(b), "const"], w=["sg" + sx])
                yield
                b = nbk()
                P.pe(lambda e, b=b, pc=pc: e.matmul(bk[b][:, 0:TT], lhsT=a2_sb[:, pc * 128:(pc + 1) * 128],
                                                   rhs=la_sb[:, :], start=True, stop=True),
                     r=["la", "const"], w=[BK(b)])
                yield
                P.act(lambda e, b=b, pc=pc: e.activation(out=sc["a"][:, :], in_=bk[b][:, 0:TT], func=AF.Sigmoid,
                                                         bias=cv_sb[:, 4 + pc:5 + pc], scale=1.0),
                      r=[BK(b), "const"], w=["a" + sx])
                yield
                P.dve(lambda e, pc=pc: e.tensor_scalar(out=sc["kk"][:, :], in0=k32[:, pc, :],
                                                       scalar1=cv_sb[:, 8 + pc:9 + pc], scalar2=None, op0=ALU.mult),
                      r=["k32", "const"], w=["kk" + sx])
                yield
                P.act(lambda e: e.activation(out=kk2_sb[:, :], in_=sc["kk"][:, :], func=AF.Square),
                      r=["kk" + sx], w=["kk2" + sx])
                yield
                b = nbk()
                P.pe(lambda e, b=b: e.matmul(bk[b][:, 0:TT], lhsT=bo_sb[:, :], rhs=kk2_sb[:, :], start=True, stop=True),
                     r=["kk2" + sx, "bo"], w=[BK(b)])
                yield
                P.act(lambda e, b=b: e.activation(out=sc["rn"][:, :], in_=bk[b][:, 0:TT], func=AF.Sqrt,
                                                  bias=1e-24, scale=1.0), r=[BK(b)], w=["rn" + sx])
                yield
                P.dve(lambda e: e.reciprocal(out=sc["rn"][:, :], in_=sc["rn"][:, :]), r=["rn" + sx], w=["rn" + sx])
                yield
                P.dve(lambda e: e.tensor_tensor(out=sc["kkn"][:, :], in0=sc["kk"][:, :], in1=sc["rn"][:, :],
                                                op=ALU.mult), r=["kk" + sx, "rn" + sx], w=["kkn" + sx])
                yield
                P.dve(lambda e, pc=pc: e.tensor_scalar(out=sc["tmp"][:, :], in0=sc["a"][:, :],
                                                       scalar1=cv_sb[:, 12 + pc:13 + pc],
                                                       scalar2=cv_sb[:, 20 + pc:21 + pc], op0=ALU.mult, op1=ALU.add),
                      r=["a" + sx, "const", "cv2"], w=["tmp" + sx])
                yield
                P.dve(lambda e, pc=pc: e.tensor_tensor(out=sc["kmod"][:, :], in0=k32[:, pc, :], in1=sc["tmp"][:, :],
                                                       op=ALU.mult), r=["k32", "tmp" + sx], w=["kmod" + sx])
                yield
                P.dve(lambda e: e.tensor_tensor(out=sc["bvec"][:, :], in0=sc["kkn"][:, :], in1=sc["a"][:, :],
                                                op=ALU.mult), r=["kkn" + sx, "a" + sx], w=["bvec" + sx])
                yield
                P.dve(lambda e: e.tensor_tensor_scan(out=sc["cs"][:, :], data0=rmask_sb[:, :], data1=sc["sg"][:, :],
                                                     initial=0.0, op0=ALU.mult, op1=ALU.add),
                      r=["sg" + sx, "const"], w=["cs" + sx])
                yield
                P.dve(lambda e: e.tensor_tensor(out=sc["tmp2"][:, :], in0=sc["cs"][:, :], in1=sc["sg"][:, :],
                                                op=ALU.subtract), r=["cs" + sx, "sg" + sx], w=["tmp2" + sx])
                yield
                P.act(lambda e, pc=pc: e.activation(out=Ep[:, pc, :], in_=sc["cs"][:, :], func=AF.Exp, scale=-C0),
                      r=["cs" + sx], w=["Ep"])
                yield
                P.act(lambda e: e.activation(out=sc["Em"][:, :], in_=sc["cs"][:, :], func=AF.Exp, scale=C0),
                      r=["cs" + sx], w=["Em" + sx])
                yield
                P.act(lambda e: e.activation(out=sc["Epv"][:, :], in_=sc["tmp2"][:, :], func=AF.Exp, scale=-C0),
                      r=["tmp2" + sx], w=["Epv" + sx])
                yield
                for cl in range(NCL):
                    cs_ = slice(cl * 128, (cl + 1) * 128)
                    P.dve(lambda e, pc=pc, cl=cl, cs_=cs_: e.scalar_tensor_tensor(
                        out=AR[:, pc, cl, 0:128], in0=sc["kkn"][:, cs_], scalar=-1.0, in1=sc["Epv"][:, cs_],
                        op0=ALU.mult, op1=ALU.mult), r=["kkn" + sx, "Epv" + sx], w=["AR"])
                    P.dve(lambda e, pc=pc, cl=cl, cs_=cs_: e.tensor_tensor(
                        out=AR[:, pc, cl, 128:256], in0=r32[:, pc, cs_], in1=Ep[:, pc, cs_], op=ALU.mult),
                        r=["r32", "Ep"], w=["AR"])
                P.dve(lambda e, pc=pc: e.tensor_tensor(out=kt[:, pc, :], in0=sc["kmod"][:, :], in1=sc["Em"][:, :],
                                                       op=ALU.mult), r=["kmod" + sx, "Em" + sx], w=["kt"])
                yield
                P.dve(lambda e, pc=pc: e.tensor_tensor(out=bt[:, pc, :], in0=sc["bvec"][:, :], in1=sc["Em"][:, :],
                                                       op=ALU.mult), r=["bvec" + sx, "Em" + sx], w=["bt"])
                yield
                P.dve(lambda e, pc=pc: e.scalar_tensor_tensor(
                    out=rk[:, pc, :], in0=r32[:, pc, :], scalar=cv_sb[:, 16 + pc:17 + pc], in1=sc["kmod"][:, :],
                    op0=ALU.mult, op1=ALU.mult), r=["r32", "kmod" + sx, "const"], w=["rk"])
                yield
                for hd in range(2):
                    rows = slice(64 * hd, 64 * hd + 64)
                    for cl in range(NCL):
                        cs_ = slice(cl * 128, (cl + 1) * 128)
                        P.pool(lambda e, pc=pc, hd=hd, rows=rows, cl=cl, cs_=cs_: e.tensor_copy(
                            out=at2[rows, pc, hd, cs_], in_=AR[rows, pc, cl, 0:128]), r=["AR"], w=["at2"])
                        P.pool(lambda e, pc=pc, hd=hd, rows=rows, cl=cl, cs_=cs_: e.tensor_copy(
                            out=rt2[rows, pc, hd, cs_], in_=AR[rows, pc, cl, 128:256]), r=["AR"], w=["rt2"])
                    P.pool(lambda e, pc=pc, hd=hd, rows=rows: e.tensor_copy(
                        out=bt2[rows, pc, hd, :], in_=bt[rows, pc, :]), r=["bt"], w=["bt2"])
                    P.pool(lambda e, pc=pc, hd=hd, rows=rows: e.tensor_copy(
                        out=kt2[rows, pc, hd, :], in_=kt[rows, pc, :]), r=["kt"], w=["kt2"])
            if LEVEL >= 2:
                for pa in (0, 2):
                    gens = [a5_gen(pa, sc, kk2_sb, "_0"), a5_gen(pa + 1, sc2, kk2b_sb, "_1")]
                    alive = list(gens)
                    while alive:
                        for g in list(alive):
                            try:
                                next(g)
                            except StopIteration:
                                alive.remove(g)
            for cl in range(NCL if LEVEL >= 3 else 0):
                cs_ = slice(cl * 128, (cl + 1) * 128)
                for pc in range(4):
                    for hd in range(2):
                        P.pe(lambda e, pc=pc, cl=cl, hd=hd, cs_=cs_: e.matmul(
                            bk[2][:, hd * 256:(hd + 1) * 256], lhsT=bt2[:, pc, hd, cs_], rhs=AR[:, pc, cl, :],
                            start=True, stop=True), r=["bt2", "AR"], w=[BK(2)])
                        P.pe(lambda e, pc=pc, cl=cl, hd=hd, cs_=cs_: e.matmul(
                            bk[3][:, hd * 256:(hd + 1) * 256], lhsT=kt2[:, pc, hd, cs_], rhs=AR[:, pc, cl, :],
                            start=True, stop=True), r=["kt2", "AR"], w=[BK(3)])
                        P.pe(lambda e, pc=pc, cl=cl, hd=hd, cs_=cs_: e.matmul(
                            bk[4][:, hd * 128:(hd + 1) * 128], lhsT=at2[:, pc, hd, cs_], rhs=bt[:, pc, cs_],
                            start=True, stop=True), r=["bt", "at2"], w=[BK(4)])
                    xk0 = "XX%d_0" % pc
                    for hd in range(2):
                        P.dve(lambda e, pc=pc, hd=hd: e.tensor_tensor(
                            out=NTb_sb[:, pc, hd * 256:(hd + 1) * 256], in0=bk[2][:, hd * 256:(hd + 1) * 256],
                            in1=mask2_sb[:, :], op=ALU.mult), r=[BK(2), "const"], w=["NTb%d" % pc])
                        P.dve(lambda e, pc=pc, hd=hd: e.tensor_tensor(
                            out=NTk_sb[:, pc, hd * 256:(hd + 1) * 256], in0=bk[3][:, hd * 256:(hd + 1) * 256],
                            in1=mask2_sb[:, :], op=ALU.mult), r=[BK(3), "const"], w=["NTk%d" % pc])
                        P.dve(lambda e, pc=pc, hd=hd: e.tensor_tensor(
                            out=XX[pc][0][:, 256 + hd * 128:256 + (hd + 1) * 128], in0=bk[4][:, hd * 128:(hd + 1) * 128],
                            in1=maskL_sb[:, :], op=ALU.mult), r=[BK(4), "const"], w=[xk0])
                        P.pool(lambda e, pc=pc, hd=hd: e.tensor_copy(
                            out=XX[pc][0][:, hd * 128:(hd + 1) * 128], in_=NTb_sb[:, pc, hd * 256:hd * 256 + 128]),
                            r=["NTb%d" % pc], w=[xk0])
                        P.pool(lambda e, pc=pc, hd=hd: e.tensor_tensor(
                            out=Pp[pc][0][:, hd * 128:(hd + 1) * 128], in0=NTb_sb[:, pc, hd * 256:hd * 256 + 128],
                            in1=identf[:, :], op=ALU.add), r=["NTb%d" % pc, "const"], w=["Pp%d_0" % pc])
                    P.pe(lambda e, pc=pc, cs_=cs_: e.transpose(out=bk5[:, 0:128], in_=bt[:, pc, cs_],
                                                               identity=ident_bf[:, :]), r=["bt", "const"], w=[BK(5)])
                    P.pe(lambda e, pc=pc, cs_=cs_: e.transpose(out=bk5[:, 128:256], in_=kt[:, pc, cs_],
                                                               identity=ident_bf[:, :]), r=["kt", "const"], w=[BK(5)])
                    P.act(lambda e, pc=pc: e.copy(out=btok[:, pc, :], in_=bk5[:, 0:128]), r=[BK(5)], w=["btok%d" % pc])
                    P.act(lambda e, pc=pc: e.copy(out=ktok[:, pc, :], in_=bk5[:, 128:256]), r=[BK(5)], w=["ktok%d" % pc])
                cur = 0
                for step in range(6):
                    nxt = 1 - cur
                    last = step == 5
                    for pc in range(4):
                        bx, bxk = (bk[6], BK(6)) if pc % 2 == 0 else (bk[0], BK(0))
                        xc, xn = "XX%d_%d" % (pc, cur), "XX%d_%d" % (pc, nxt)
                        for hd in range(2):
                            hs_ = slice(hd * 128, (hd + 1) * 128)
                            hx_ = slice(256 + hd * 128, 256 + (hd + 1) * 128)
                            if not last:
                                P.pe(lambda e, pc=pc, cur=cur, hs_=hs_, hx_=hx_, bx=bx: e.matmul(
                                    bx[:, hs_], lhsT=XX[pc][cur][:, hx_], rhs=XX[pc][cur][:, hs_],
                                    start=True, stop=True), r=[xc], w=[bxk])
                            P.pe(lambda e, pc=pc, cur=cur, hs_=hs_, hx_=hx_, bx=bx: e.matmul(
                                bx[:, hx_], lhsT=XX[pc][cur][:, hs_], rhs=XX[pc][cur][:, hx_],
                                start=True, stop=True), r=[xc], w=[bxk])
                        if not last:
                            P.act(lambda e, pc=pc, nxt=nxt, bx=bx: e.copy(out=XX[pc][nxt][:, :], in_=bx[:, :]),
                                  r=[bxk], w=[xn])
                        else:
                            P.act(lambda e, pc=pc, nxt=nxt, bx=bx: e.copy(out=XX[pc][nxt][:, 256:512], in_=bx[:, 256:512]),
                                  r=[bxk], w=[xn])
                    for pc in range(4):
                        bp, bpk = (bk[7], BK(7)) if pc % 2 == 0 else (bk[1], BK(1))
                        xn = "XX%d_%d" % (pc, nxt)
                        pk_c, pk_n = "Pp%d_%d" % (pc, cur), "Pp%d_%d" % (pc, nxt)
                        for hd in range(2):
                            hs_ = slice(hd * 128, (hd + 1) * 128)
                            hx_ = slice(256 + hd * 128, 256 + (hd + 1) * 128)
                            P.pe(lambda e, pc=pc, cur=cur, nxt=nxt, hs_=hs_, hx_=hx_, bp=bp: e.matmul(
                                bp[:, hs_], lhsT=XX[pc][nxt][:, hx_], rhs=Pp[pc][cur][:, hs_], start=True, stop=True),
                                r=[xn, pk_c], w=[bpk])
                        if last:
                            P.dve(lambda e, cur=cur, pc=pc, bp=bp: e.tensor_tensor(
                                out=Pfin[:, pc, :], in0=bp[:, 0:256], in1=Pp[pc][cur][:, :], op=ALU.add),
                                r=[bpk, pk_c], w=["Pfin%d" % pc])
                        else:
                            P.dve(lambda e, cur=cur, nxt=nxt, pc=pc, bp=bp: e.tensor_tensor(
                                out=Pp[pc][nxt][:, :], in0=bp[:, 0:256], in1=Pp[pc][cur][:, :], op=ALU.add),
                                r=[bpk, pk_c], w=[pk_n])
                    cur = nxt
                if LEVEL < 4:
                    continue
                for pc in range(4):
                    for hd in range(2):
                        rows = slice(64 * hd, 64 * hd + 64)
                        oc_ = slice(pc * 128 + hd * 64, pc * 128 + hd * 64 + 64)
                        P.pe(lambda e, pc=pc, cl=cl, rows=rows, oc_=oc_, hd=hd, cs_=cs_: e.matmul(
                            bk[0][:, oc_], lhsT=at2[:, pc, hd, cs_], rhs=Hbf[:, pc, hd * 64:hd * 64 + 64],
                            start=True, stop=False), r=["at2", "Hbf%d" % pc], w=[BK(0)])
                        P.pe(lambda e, pc=pc, cl=cl, oc_=oc_, hd=hd: e.matmul(
                            bk[0][:, oc_], lhsT=NTk_sb[:, pc, hd * 256:hd * 256 + 128], rhs=vtok[:, cl, oc_],
                            start=False, stop=True), r=["NTk%d" % pc, "vtok"], w=[BK(0)])
                P.act(lambda e: e.copy(out=Zs[:, :, :], in_=bk[0][:, :]), r=[BK(0)], w=["Zs"])
                for pc in range(4):
                    for hd in range(2):
                        oc_ = slice(pc * 128 + hd * 64, pc * 128 + hd * 64 + 64)
                        P.pe(lambda e, pc=pc, oc_=oc_, hd=hd: e.matmul(
                            bk[1][:, oc_], lhsT=Pfin[:, pc, hd * 128:(hd + 1) * 128], rhs=Zs[:, pc, hd * 64:hd * 64 + 64],
                            start=True, stop=True), r=["Pfin%d" % pc, "Zs"], w=[BK(1)])
                P.dve(lambda e: e.tensor_copy(out=Us[:, :, :], in_=bk[1][:, :]), r=[BK(1)], w=["Us"])
                for pc in range(4):
                    for hd in range(2):
                        rows = slice(64 * hd, 64 * hd + 64)
                        oc_ = slice(pc * 128 + hd * 64, pc * 128 + hd * 64 + 64)
                        P.pe(lambda e, pc=pc, cl=cl, rows=rows, oc_=oc_, hd=hd, cs_=cs_: e.matmul(
                            bk[2][:, oc_], lhsT=rt2[:, pc, hd, cs_], rhs=Hbf[:, pc, hd * 64:hd * 64 + 64],
                            start=True, stop=False), r=["rt2", "Hbf%d" % pc], w=[BK(2)])
                        P.pe(lambda e, pc=pc, oc_=oc_, hd=hd: e.matmul(
                            bk[2][:, oc_], lhsT=NTb_sb[:, pc, hd * 256 + 128:hd * 256 + 256],
                            rhs=Us[:, pc, hd * 64:hd * 64 + 64], start=False, stop=False),
                            r=["NTb%d" % pc, "Us"], w=[BK(2)])
                        P.pe(lambda e, pc=pc, cl=cl, oc_=oc_, hd=hd: e.matmul(
                            bk[2][:, oc_], lhsT=NTk_sb[:, pc, hd * 256 + 128:hd * 256 + 256], rhs=vtok[:, cl, oc_],
                            start=False, stop=True), r=["NTk%d" % pc, "vtok"], w=[BK(2)])
                P.act(lambda e, cl=cl: e.copy(out=ytok[:, cl, :], in_=bk[2][:, :]), r=[BK(2)], w=["ytok"])
                for pc in range(4):
                    pcs = slice(pc * 128, (pc + 1) * 128)
                    P.pe(lambda e, pc=pc, pcs=pcs: e.matmul(bk[3][:, pcs], lhsT=btok[:, pc, :], rhs=Us[:, pc, :],
                                                            start=True, stop=False),
                         r=["btok%d" % pc, "Us"], w=[BK(3)])
                    P.pe(lambda e, pc=pc, pcs=pcs, cl=cl: e.matmul(bk[3][:, pcs], lhsT=ktok[:, pc, :], rhs=vtok[:, cl, pcs],
                                                                   start=False, stop=True),
                         r=["ktok%d" % pc, "vtok"], w=[BK(3)])
                P.dve(lambda e: e.tensor_tensor(out=Htmp[:, :, :], in0=bk[3][:, :], in1=H32[:, :, :], op=ALU.add),
                      r=[BK(3), "H32"], w=["Htmp"])
                for pc in range(4):
                    wl = Ep[:, pc, cl * 128 + 127:cl * 128 + 128]
                    P.dve(lambda e, pc=pc, wl=wl: e.tensor_scalar(out=H32[:, pc, :], in0=Htmp[:, pc, :], scalar1=wl,
                                                                  scalar2=None, op0=ALU.mult),
                          r=["Htmp", "Ep"], w=["H32"])
                    P.act(lambda e, pc=pc, wl=wl: e.activation(out=Hbf[:, pc, :], in_=Htmp[:, pc, :], func=AF.Copy,
                                                               scale=wl), r=["Htmp", "Ep"], w=["Hbf%d" % pc])
            for cl in range(NCL if LEVEL >= 5 else 0):
                cs_ = slice(cl * 128, (cl + 1) * 128)
                o = out_sb[fo % 2]
                ok = "out%d" % (fo % 2)
                fo += 1
                for pc in range(4):
                    P.pe(lambda e, pc=pc, cs_=cs_: e.matmul(bk[4][:, 2 * pc:2 * pc + 2], lhsT=rk[:, pc, cs_],
                                                            rhs=ind_sb[:, :], start=True, stop=True),
                         r=["rk", "const"], w=[BK(4)])
                P.dve(lambda e: e.tensor_copy(out=st_sb[:, 32:40], in_=bk[4][:, 0:8]), r=[BK(4)], w=["bonus"])
                y3 = ytok[:, cl, :].rearrange("p (h n) -> p h n", n=64)
                P.dve(lambda e, y3=y3: e.tensor_reduce(out=st_sb[:, 0:8], in_=y3, axis=AX.X, op=ALU.add),
                      r=["ytok"], w=["st"])
                P.act(lambda e, cl=cl: e.activation(out=ysq[:, :], in_=ytok[:, cl, :], func=AF.Square),
                      r=["ytok"], w=["ysq"])
                P.dve(lambda e: e.tensor_reduce(out=st_sb[:, 8:16], in_=ysq[:, :].rearrange("p (h n) -> p h n", n=64),
                                                axis=AX.X, op=ALU.add), r=["ysq"], w=["st"])
                P.dve(lambda e: e.tensor_scalar(out=st_sb[:, 0:8], in0=st_sb[:, 0:8], scalar1=1.0 / 64, scalar2=None,
                                                op0=ALU.mult), r=["st"], w=["st"])
                P.dve(lambda e: e.tensor_tensor(out=st_sb[:, 16:24], in0=st_sb[:, 0:8], in1=st_sb[:, 0:8], op=ALU.mult),
                      r=["st"], w=["st"])
                P.dve(lambda e: e.scalar_tensor_tensor(out=st_sb[:, 24:32], in0=st_sb[:, 8:16], scalar=1.0 / 64,
                                                       in1=st_sb[:, 16:24], op0=ALU.mult, op1=ALU.subtract),
                      r=["st"], w=["st"])
                P.act(lambda e: e.activation(out=st_sb[:, 24:32], in_=st_sb[:, 24:32], func=AF.Sqrt, bias=RW_LN_EPS,
                                             scale=1.0), r=["st"], w=["st"])
                P.dve(lambda e: e.reciprocal(out=st_sb[:, 24:32], in_=st_sb[:, 24:32]), r=["st"], w=["st"])
                for h in range(8):
                    hs_ = slice(h * 64, (h + 1) * 64)
                    P.dve(lambda e, h=h, hs_=hs_, o=o, cl=cl: e.tensor_scalar(
                        out=o[:, hs_], in0=ytok[:, cl, hs_], scalar1=st_sb[:, h:h + 1], scalar2=st_sb[:, 24 + h:25 + h],
                        op0=ALU.subtract, op1=ALU.mult), r=["ytok", "st"], w=[ok])
                P.dve(lambda e, o=o: e.tensor_tensor(out=o[:, :], in0=o[:, :], in1=lng_sb[:, :], op=ALU.mult),
                      r=[ok, "const"], w=[ok])
                P.dve(lambda e, o=o: e.tensor_tensor(out=o[:, :], in0=o[:, :], in1=lnb_sb[:, :], op=ALU.add),
                      r=[ok, "const"], w=[ok])
                for h in range(8):
                    hs_ = slice(h * 64, (h + 1) * 64)
                    P.dve(lambda e, h=h, hs_=hs_, o=o, cl=cl: e.scalar_tensor_tensor(
                        out=o[:, hs_], in0=v32[:, cl, hs_], scalar=st_sb[:, 32 + h:33 + h], in1=o[:, hs_],
                        op0=ALU.mult, op1=ALU.add), r=["v32", "bonus", ok], w=[ok])
                P.dve(lambda e, o=o, cl=cl: e.tensor_tensor(out=o[:, :], in0=o[:, :], in1=gtok[:, cl, :], op=ALU.mult),
                      r=[ok, "gtok"], w=[ok])
                P.dma("sp", yg[t0 + cl * 128:t0 + (cl + 1) * 128, :], o[:, :], r=[ok], w=["yg"])
        pass


def rw_consts(TT=256):
    ident = np.eye(128, dtype=np.float32)
    p = np.arange(128)
    up_strict = (p[:, None] < p[None, :]).astype(np.float32)
    up_incl = (p[:, None] <= p[None, :]).astype(np.float32)
    mask2 = np.concatenate([up_strict, up_incl], axis=1)
    maskL = (p[:, None] > p[None, :]).astype(np.float32)
    rmask = np.ones((128, TT), np.float32)
    rmask[:, ::CL] = 0.0
    ind = np.zeros((128, 2), np.float32)
    ind[:64, 0] = 1.0
    ind[64:, 1] = 1.0
    return dict(ident=ident, mask2=mask2, maskL=maskL, rmask=rmask, ind=ind)


def rw_inputs(hT_b, hh, p):
    cols = slice(hh * 512, (hh + 1) * 512)
    T = hT_b.shape[1]
    hTp = np.zeros((D, T + 1), np.float32)
    hTp[:, 1:] = hT_b
    vec = lambda v: np.ascontiguousarray(v[cols].reshape(4, 128).T)
    cv = np.concatenate([vec(p["rwkv_w0"][0]), vec(p["rwkv_a0"][0]), vec(p["rwkv_k_k"][0]), vec(p["rwkv_k_a"][0]),
                         vec(p["rwkv_r_k"][0].reshape(-1))], axis=1)
    mu6 = np.ascontiguousarray(p["rwkv_mu"][0].reshape(6, 8, 128).transpose(2, 0, 1).reshape(128, 48))
    m = dict(hTp=hTp,
             wr=np.ascontiguousarray(p["rwkv_w_rkv"][0, 0][:, cols]), wk=np.ascontiguousarray(p["rwkv_w_rkv"][0, 1][:, cols]),
             wv=np.ascontiguousarray(p["rwkv_w_rkv"][0, 2][:, cols]),
             l1=np.ascontiguousarray(np.concatenate([p["rwkv_w1"][0], p["rwkv_a1"][0], p["rwkv_g1"][0]], axis=1)),
             w2c=np.ascontiguousarray(p["rwkv_w2"][0][:, cols]), a2c=np.ascontiguousarray(p["rwkv_a2"][0][:, cols]),
             g2c=np.ascontiguousarray(p["rwkv_g2"][0][:, cols]), mu6=mu6, cv=np.ascontiguousarray(cv),
             lng=np.ascontiguousarray(np.broadcast_to(p["rwkv_ln_g"][0][cols], (128, 512))),
             lnb=np.ascontiguousarray(np.broadcast_to(p["rwkv_ln_b"][0][cols], (128, 512))))
    m.update(rw_consts())
    return m


def _std_io(nc):
    def io(name, shape, kind):
        return nc.dram_tensor(name, list(shape), F32,
                              kind="ExternalInput" if kind == "in" else "ExternalOutput").ap()
    return io


def build_LA(NT, has_add, TB=1024, TT=512):
    nc = bass.Bass("TRN2", target_bir_lowering=False)
    with ExitStack() as es:
        P = Prog(nc, es)
        emit_LA(nc, P, _std_io(nc), NT, has_add, TB, TT)
    return nc


def build_LP(NT, has_v, TB=1024, TT=512):
    nc = bass.Bass("TRN2", target_bir_lowering=False)
    with ExitStack() as es:
        P = Prog(nc, es)
        emit_LP(nc, P, _std_io(nc), NT, has_v, TB, TT)
    return nc


def build_ATT(T, HL=4):
    nc = bass.Bass("TRN2", target_bir_lowering=False)
    with ExitStack() as es:
        P = Prog(nc, es)
        emit_ATT(nc, P, _std_io(nc), T, HL)
    return nc


def build_RW(T, TT=256, LEVEL=9):
    nc = bass.Bass("TRN2", target_bir_lowering=False)
    with ExitStack() as es:
        P = Prog(nc, es)
        emit_RW(nc, P, _std_io(nc), T, TT, LEVEL)
    return nc


_PROGS = {}


def _prog(name, fn):
    if name not in _PROGS:
        _PROGS[name] = fn()
    return _PROGS[name]


def _tile_win(w):
    g = w[:, :FF].reshape(8, 128, 22, 128)
    u = w[:, FF:].reshape(8, 128, 22, 128)
    return np.ascontiguousarray(np.concatenate([g, u], axis=3).transpose(2, 1, 0, 3))


def _tile_wout(w):
    return np.ascontiguousarray(w.reshape(22, 128, 8, 128).transpose(2, 1, 0, 3))


def _tile_sq(w):
    return np.ascontiguousarray(w.reshape(8, 128, 8, 128).transpose(2, 1, 0, 3))


def _gains(g1, g2):
    return np.ascontiguousarray(np.concatenate([g1.reshape(8, 128).T, g2.reshape(8, 128).T], axis=1))


def _run(nc, in_maps):
    return run_bass_kernel_spmd(nc, in_maps, core_ids=list(range(NCORES))).results


def _run_LA(xT_list, w_in, w_out, g1, g2, aT_list=None, w_add=None):
    NT = xT_list[0].shape[1]
    has_add = aT_list is not None
    nc = _prog(("LA", NT, has_add), lambda: build_LA(NT, has_add))
    wi, wo, gg = _tile_win(w_in), _tile_wout(w_out), _gains(g1, g2)
    wa = _tile_sq(w_add) if has_add else None
    maps = []
    for c in range(NCORES):
        m = {"xT": xT_list[c], "gains": gg, "w_in": wi, "w_out": wo}
        if has_add:
            m["aT"] = aT_list[c]
            m["w_add"] = wa
        maps.append(m)
    res = _run(nc, maps)
    return [r["yT"] for r in res], [r["hT"] for r in res]


def _run_LP(hT_list, wn, gn, wv=None):
    NT = hT_list[0].shape[1]
    has_v = wv is not None
    nc = _prog(("LP", NT, has_v), lambda: build_LP(NT, has_v))
    wnt = _tile_sq(wn)
    g = np.ascontiguousarray(np.tile(gn, 2).reshape(128, 1))
    maps = []
    for c in range(NCORES):
        m = {"hT": hT_list[c], "wn": wnt, "gn": g}
        if has_v:
            m["wv"] = np.ascontiguousarray(wv.reshape(8, 128, D).transpose(1, 0, 2))
        maps.append(m)
    res = _run(nc, maps)
    return [r["nT"] for r in res], ([r["v_tok"] for r in res] if has_v else None)


def kernel_unfused(**inp):
    p = {k: np.asarray(v, dtype=np.float32) for k, v in inp.items()}
    x = p["x"]
    B, T, _ = x.shape
    HT = T // 2
    xT = [np.ascontiguousarray(x[c // 2, (c % 2) * HT:(c % 2 + 1) * HT].T) for c in range(NCORES)]

    def full_seq(lst, b):
        return np.concatenate([lst[2 * b], lst[2 * b + 1]], axis=1)

    x1T, h1T = _run_LA(xT, p["ffn_w_in"][0, 0], p["ffn_w_out"][0, 0], p["ffn_norm"][0, 0], p["mix_norm"][0])
    nc_rw = _prog(("RW", T), lambda: build_RW(T))
    rw_maps = [rw_inputs(full_seq(h1T, c // 2), c % 2, p) for c in range(NCORES)]
    yg = [r["yg_tok"] for r in _run(nc_rw, rw_maps)]
    ygT = [np.ascontiguousarray(np.concatenate([yg[2 * b], yg[2 * b + 1]], axis=1).T) for b in range(B)]
    aT = [np.ascontiguousarray(ygT[c // 2][:, (c % 2) * HT:(c % 2 + 1) * HT]) for c in range(NCORES)]
    x3T, hkvT = _run_LA(x1T, p["ffn_w_in"][0, 1], p["ffn_w_out"][0, 1], p["ffn_norm"][0, 1], p["kv_norm"],
                        aT_list=aT, w_add=p["rwkv_w_o"][0])
    kT, v_tok = _run_LP(hkvT, p["w_kv"][:, :D], p["k_norm"], wv=p["w_kv"][:, D:])
    x4T, h2T = _run_LA(x3T, p["ffn_w_in"][1, 0], p["ffn_w_out"][1, 0], p["ffn_norm"][1, 0], p["mix_norm"][1])
    qT, _ = _run_LP(h2T, p["diff_w_q"][0], p["diff_q_norm"][0])
    nc_att = _prog(("ATT", T), lambda: build_ATT(T, 4))
    att_maps = []
    subg = np.ascontiguousarray(np.broadcast_to(p["diff_subln"][0], (128, 128)))
    lam = np.ascontiguousarray(p["diff_lambda"][0].reshape(1, 256))
    for c in range(NCORES):
        b, hh = c // 2, c % 2
        rows = slice(hh * 512, (hh + 1) * 512)
        qaug, kaug, btab, tri = att_consts(T, [hh * 4 + i for i in range(4)])
        att_maps.append({"qT": np.ascontiguousarray(full_seq(qT, b)[rows]), "kT": np.ascontiguousarray(full_seq(kT, b)[rows]),
                         "v_tok": np.ascontiguousarray(np.concatenate([v_tok[2 * b], v_tok[2 * b + 1]], axis=0)[:, rows]),
                         "qaug": qaug, "kaug": kaug, "btab": btab, "tri": tri, "lam": lam, "subg": subg})
    ot = [r["o_tok"] for r in _run(nc_att, att_maps)]
    oT = [np.ascontiguousarray(np.concatenate([ot[2 * b], ot[2 * b + 1]], axis=1).T) for b in range(B)]
    aT = [np.ascontiguousarray(oT[c // 2][:, (c % 2) * HT:(c % 2 + 1) * HT]) for c in range(NCORES)]
    outT, _ = _run_LA(x4T, p["ffn_w_in"][1, 1], p["ffn_w_out"][1, 1], p["ffn_norm"][1, 1], p["ffn_norm"][1, 1],
                      aT_list=aT, w_add=p["diff_w_o"][0])
    out = np.empty((B, T, D), np.float32)
    for c in range(NCORES):
        out[c // 2, (c % 2) * HT:(c % 2 + 1) * HT] = outT[c].T
    return out


RG_PAIRS = [[0, 1], [2, 3], [4, 5], [6, 7]]


CC_MAX_BYTES = 2 * 1024 * 1024


class _Gathered:
    def __init__(self, nc, name, src, R, C):
        self.src, self.R, self.C = src, R, C
        self.RC = min(R, CC_MAX_BYTES // (C * 4))
        assert R % self.RC == 0
        self.nch = R // self.RC
        self.g = nc.dram_tensor(name, [self.nch * 2 * self.RC, C], F32).ap()

    def rows(self, j, r0, r1):
        ch = r0 // self.RC
        assert (r1 - 1) // self.RC == ch
        base = (ch * 2 + j) * self.RC - ch * self.RC
        return self.g[base + r0:base + r1, :]


def _emit_allgather(nc, P, gs):
    with P.stage():
        for G in gs:
            for ch in range(G.nch):
                src = G.src[ch * G.RC:(ch + 1) * G.RC, :]
                dst = G.g[ch * 2 * G.RC:(ch + 1) * 2 * G.RC, :]
                P.add("pool", lambda e, src=src, dst=dst: e.collective_compute(
                    "AllGather", ALU.bypass, replica_groups=RG_PAIRS, ins=[src.opt()], outs=[dst.opt()]),
                    dma="cc")


def _emit_select(nc, P, sel, jobs, F, ident=None):
    NSB = 4 if F <= 1024 else 3
    with P.stage():
        sel_sb = P.sb("sel_sb", [128, 2], F32)
        P.dma("sp", sel_sb[:, :], sel[:, :], w=["sel"])
        a_sb = [P.sb("a_sb%d" % i, [128, F], F32) for i in range(NSB)]
        b_sb = [P.sb("b_sb%d" % i, [128, F], F32) for i in range(NSB)]
        o_sb = [P.sb("o_sb%d" % i, [128, F], F32) for i in range(NSB)]
        if ident is not None:
            id_sb = P.sb("id_sb", [128, 128], F32)
            P.dma("sp", id_sb[:, :], ident[:, :], w=["ident"])
            t_sb = [P.sb("t_sb%d" % i, [128, 512], F32) for i in range(2)]
            ps_t = [P.ps("ps_t%d" % i, [128, 512]) for i in range(2)]
            P.psum_keys.update(["ps_t0", "ps_t1"])
        for i, (A, B, dst) in enumerate(jobs):
            q = i % NSB
            P.dma("sp", a_sb[q][:, :], A, w=["a%d" % q])
            P.dma("sp", b_sb[q][:, :], B, w=["b%d" % q])
            P.dve(lambda e, q=q: e.tensor_scalar(out=a_sb[q][:, :], in0=a_sb[q][:, :], scalar1=sel_sb[:, 0:1],
                                                 scalar2=None, op0=ALU.mult), r=["a%d" % q, "sel"], w=["a%d" % q])
            P.dve(lambda e, q=q: e.scalar_tensor_tensor(out=o_sb[q][:, :], in0=b_sb[q][:, :], scalar=sel_sb[:, 1:2],
                                                        in1=a_sb[q][:, :], op0=ALU.mult, op1=ALU.add),
                  r=["a%d" % q, "b%d" % q, "sel"], w=["o%d" % q])
            if ident is None:
                P.dma("sp", dst, o_sb[q][:, :], r=["o%d" % q], w=["seldst"])
            else:
                for cc in range(4):
                    P.pe(lambda e, q=q, cc=cc: e.transpose(out=ps_t[q % 2][:, cc * 128:(cc + 1) * 128],
                                                           in_=o_sb[q][:, cc * 128:(cc + 1) * 128],
                                                           identity=id_sb[:, :]),
                         r=["o%d" % q, "ident"], w=["ps_t%d" % (q % 2)])
                P.act(lambda e, q=q: e.copy(out=t_sb[q % 2][:, :], in_=ps_t[q % 2][:, :]), r=["ps_t%d" % (q % 2)], w=["t%d" % (q % 2)])
                P.dma("sp", dst, t_sb[q % 2][:, :].rearrange("p (c t) -> p c t", c=4), r=["t%d" % (q % 2)], w=["seldst"])


class _Skip:
    def __init__(self, P):
        self.P = P

    def __enter__(self):
        self.n = len(self.P.ops)
        self.P.es = ExitStack()
        self.P.es.__enter__()
        return self.P

    def __exit__(self, *a):
        del self.P.ops[self.n:]
        self.P.lastw, self.P.readers = {}, {}
        self.P.es.__exit__(*a)
        self.P.es = self.P.sem_es
        return False


def build_FUSED(T=8192, UPTO=99):
    HT = T // 2
    _st = [0]

    def go():
        _st[0] += 1
        return _st[0] <= UPTO
    nc = bass.Bass("TRN2", target_bir_lowering=False)
    ext_in = lambda n, shp: nc.dram_tensor(n, list(shp), F32, kind="ExternalInput").ap()
    ext_out = lambda n, shp: nc.dram_tensor(n, list(shp), F32, kind="ExternalOutput").ap()
    internal = lambda n, shp: nc.dram_tensor(n, list(shp), F32).ap()
    with ExitStack() as es:
        P = Prog(nc, es)

        def mk_io(prefix, bind):
            def io(name, shape, kind):
                if name in bind:
                    return bind[name]
                assert kind == "in", name
                return ext_in(prefix + name, shape)
            return io

        sel = ext_in("sel", [128, 2])
        ident = ext_in("SEL_ident", [128, 128])
        xT = ext_in("xT", [D, HT])
        outT = ext_out("outT", [D, HT])
        x1T, h1T = internal("x1T", [D, HT]), internal("h1T", [D, HT])
        if go():
            emit_LA(nc, P, mk_io("A1_", {"xT": xT, "yT": x1T, "hT": h1T}), HT, False)
        h1g = _Gathered(nc, "h1g", h1T, D, HT)
        if go():
            _emit_allgather(nc, P, [h1g])
        hTp = internal("hTp", [D, T + 1])
        with (P.stage() if go() else _Skip(P)):
            z_sb = P.sb("z_sb", [128, KC, 1], F32)
            P.pool(lambda e: e.memset(z_sb[:, :, :], 0.0), w=["z"])
            P.add("sp", lambda e: e.dma_start(out=hTp.rearrange("(kc p) t -> p kc t", p=128)[:, :, 0:1],
                                              in_=z_sb[:, :, :], allow_slow_non_contiguous=True),
                  r=["z"], w=["hTp"], dma=True)
            for j in range(2):
                for kc in range(KC):
                    P.dma("sp", hTp[kc * 128:(kc + 1) * 128, 1 + j * HT:1 + (j + 1) * HT],
                          h1g.rows(j, kc * 128, (kc + 1) * 128), w=["hTp"])
        yg_tok = internal("yg_tok", [T, 512])
        if go():
            emit_RW(nc, P, mk_io("RW_", {"hTp": hTp, "yg_tok": yg_tok}), T)
        ygg = _Gathered(nc, "ygg", yg_tok, T, 512)
        if go():
            _emit_allgather(nc, P, [ygg])
        aT2 = internal("aT2", [D, HT])

        def tok2feat_jobs(g, dst):
            jobs = []
            for j in range(2):
                for tb in range(HT // 128):
                    A = g.rows(j, tb * 128, (tb + 1) * 128)
                    B = g.rows(j, HT + tb * 128, HT + (tb + 1) * 128)
                    dd = dst[j * 512:(j + 1) * 512, tb * 128:(tb + 1) * 128].rearrange("(c p) t -> p c t", p=128)
                    jobs.append((A, B, dd))
            return jobs

        if go():
            _emit_select(nc, P, sel, tok2feat_jobs(ygg, aT2), 512, ident=ident)
        x3T, hkvT = internal("x3T", [D, HT]), internal("hkvT", [D, HT])
        if go():
            emit_LA(nc, P, mk_io("A2_", {"xT": x1T, "aT": aT2, "yT": x3T, "hT": hkvT}), HT, True)
        kT, v_tok = internal("kT", [D, HT]), internal("v_tok", [HT, D])
        if go():
            emit_LP(nc, P, mk_io("P1_", {"hT": hkvT, "nT": kT, "v_tok": v_tok}), HT, True)
        x4T, h2T = internal("x4T", [D, HT]), internal("h2T", [D, HT])
        if go():
            emit_LA(nc, P, mk_io("A3_", {"xT": x3T, "yT": x4T, "hT": h2T}), HT, False)
        qT = internal("qT", [D, HT])
        if go():
            emit_LP(nc, P, mk_io("P2_", {"hT": h2T, "nT": qT}), HT, False)
        qg, kg, vg = _Gathered(nc, "qg", qT, D, HT), _Gathered(nc, "kg", kT, D, HT), _Gathered(nc, "vg", v_tok, HT, D)
        if go():
            _emit_allgather(nc, P, [qg, kg, vg])
        o_tok = internal("o_tok", [T, 512])
        if go():
            emit_ATT(nc, P, mk_io("AT_", {"o_tok": o_tok}), T, 4, gath=(qg, kg, vg, sel))
        og = _Gathered(nc, "og", o_tok, T, 512)
        if go():
            _emit_allgather(nc, P, [og])
        aT4 = internal("aT4", [D, HT])
        if go():
            _emit_select(nc, P, sel, tok2feat_jobs(og, aT4), 512, ident=ident)
        hdum = internal("hdum", [D, HT])
        if go():
            emit_LA(nc, P, mk_io("A4_", {"xT": x4T, "aT": aT4, "yT": outT, "hT": hdum}), HT, True)
    return nc


def kernel(**inp):
    p = {k: np.asarray(v, dtype=np.float32) for k, v in inp.items()}
    x = p["x"]
    B, T, _ = x.shape
    HT = T // 2
    import os
    nc = _prog(("FUSED", T), lambda: build_FUSED(T, int(os.environ.get("FUSED_UPTO", "99"))))
    shared = {"SEL_ident": np.eye(128, dtype=np.float32)}

    def la(prefix, l, i, g2, w_add=None):
        shared[prefix + "gains"] = _gains(p["ffn_norm"][l, i], g2)
        shared[prefix + "w_in"] = _tile_win(p["ffn_w_in"][l, i])
        shared[prefix + "w_out"] = _tile_wout(p["ffn_w_out"][l, i])
        if w_add is not None:
            shared[prefix + "w_add"] = _tile_sq(w_add)

    la("A1_", 0, 0, p["mix_norm"][0])
    la("A2_", 0, 1, p["kv_norm"], p["rwkv_w_o"][0])
    la("A3_", 1, 0, p["mix_norm"][1])
    la("A4_", 1, 1, p["ffn_norm"][1, 1], p["diff_w_o"][0])
    shared["P1_wn"] = _tile_sq(p["w_kv"][:, :D])
    shared["P1_gn"] = np.ascontiguousarray(np.tile(p["k_norm"], 2).reshape(128, 1))
    shared["P1_wv"] = np.ascontiguousarray(p["w_kv"][:, D:].reshape(8, 128, D).transpose(1, 0, 2))
    shared["P2_wn"] = _tile_sq(p["diff_w_q"][0])
    shared["P2_gn"] = np.ascontiguousarray(np.tile(p["diff_q_norm"][0], 2).reshape(128, 1))
    shared["AT_lam"] = np.ascontiguousarray(p["diff_lambda"][0].reshape(1, 256))
    shared["AT_subg"] = np.ascontiguousarray(np.broadcast_to(p["diff_subln"][0], (128, 128)))
    dummy_h = np.zeros((D, 1), np.float32)
    maps = []
    for c in range(NCORES):
        b, hh = c // 2, c % 2
        m = dict(shared)
        m["xT"] = np.ascontiguousarray(x[b, hh * HT:(hh + 1) * HT].T)
        s_ = np.zeros((128, 2), np.float32)
        s_[:, hh] = 1.0
        m["sel"] = s_
        rw = rw_inputs(dummy_h, hh, p)
        del rw["hTp"]
        for k_, v_ in rw.items():
            m["RW_" + k_] = v_
        qaug, kaug, btab, tri = att_consts(T, [hh * 4 + i for i in range(4)])
        m.update({"AT_qaug": qaug, "AT_kaug": kaug, "AT_btab": btab, "AT_tri": tri})
        maps.append(m)
    res = _run(nc, maps)
    out = np.empty((B, T, D), np.float32)
    for c in range(NCORES):
        out[c // 2, (c % 2) * HT:(c % 2 + 1) * HT] = res[c]["outT"].T
    return out
```

```python
import numpy as np
from contextlib import ExitStack
import concourse.bass as bass
import concourse.mybir as mybir
from concourse.bass_utils import run_bass_kernel_spmd

F32 = mybir.dt.float32
BF16 = mybir.dt.bfloat16
ALU = mybir.AluOpType
AF = mybir.ActivationFunctionType
AX = mybir.AxisListType

NCORES = 8
SEM_CAP = 8192
ATTACH_WAIT = True


class _Op:
    __slots__ = ("eng", "fn", "deps", "dma", "signal", "sem", "val", "idx")


class Prog:
    ENGS = ("pe", "act", "dve", "pool", "sp")

    def __init__(self, nc, es, n_dma_sems=12):
        self.nc = nc
        self.sem_es = es
        self.es = es
        self.ops = []
        self.lastw = {}
        self.readers = {}
        self.n_dma_sems = n_dma_sems
        self.uid = 0
        self.psum_keys = set()
        self.emitted = 0
        self.cnt = {e: 0 for e in self.ENGS}
        self.dma_cnt = [0] * (2 * n_dma_sems)
        self.dma_last = [None] * (2 * n_dma_sems)
        self.n_dma = {"sp": 0, "pool": 0}
        self.n_cc = 0
        self.sems = {}
        self.waited = {e: {} for e in self.ENGS}
        self.nstage = 0

    def sb(self, name, shape, dt):
        return self.es.enter_context(self.nc.sbuf_tensor("g%d_%s" % (self.nstage, name), list(shape), dt))

    def ps(self, name, shape, dt=F32):
        return self.es.enter_context(self.nc.psum_tensor("g%d_%s" % (self.nstage, name), list(shape), dt))

    def _sem(self, key):
        if key not in self.sems:
            self.sems[key] = self.sem_es.enter_context(self.nc.semaphore("s_%s_%s" % key))
        return self.sems[key]

    class _Stage:
        def __init__(self, P):
            self.P = P

        def __enter__(self):
            self.P.es = ExitStack()
            self.P.es.__enter__()
            return self.P

        def __exit__(self, *a):
            if a[0] is None:
                self.P.emit_stage()
            self.P.es.__exit__(*a)
            self.P.es = self.P.sem_es
            return False

    def stage(self):
        return Prog._Stage(self)

    def add(self, eng, fn, r=(), w=(), dma=False):
        op = _Op()
        op.eng, op.fn, op.dma = eng, fn, dma
        op.idx = len(self.ops)
        op.signal = False
        op.sem = op.val = None
        deps = {}
        for k in r:
            d = self.lastw.get(k)
            if d is not None:
                deps[d] = True
        for k in r:
            if k in self.psum_keys:
                for rd in self.readers.get(k, ()):
                    if self.ops[rd].eng != eng:
                        deps[rd] = True
        for k in w:
            d = self.lastw.get(k)
            if d is not None:
                deps[d] = True
            for rd in self.readers.get(k, ()):
                if rd not in deps:
                    deps[rd] = False
        for k in r:
            lst = self.readers.setdefault(k, [])
            if not dma:
                lst[:] = [x for x in lst if self.ops[x].dma or self.ops[x].eng != eng]
            lst.append(op.idx)
        for k in w:
            self.lastw[k] = op.idx
            self.readers[k] = []
        op.deps = deps
        self.ops.append(op)
        return op

    def pe(self, fn, r=(), w=()):
        return self.add("pe", fn, r, w)

    def act(self, fn, r=(), w=()):
        return self.add("act", fn, r, w)

    def dve(self, fn, r=(), w=()):
        return self.add("dve", fn, r, w)

    def pool(self, fn, r=(), w=()):
        return self.add("pool", fn, r, w)

    def dma(self, q, out, in_, r=(), w=()):
        return self.add(q, lambda e: e.dma_start(out=out, in_=in_), r, w, dma=True)

    def finalize(self):
        self.emit_stage()

    def emit_stage(self):
        nc, ops = self.nc, self.ops
        s0 = self.emitted
        stage_ops = ops[s0:]
        self.emitted = len(ops)
        self.nstage += 1
        self.lastw, self.readers = {}, {}
        if not stage_ops:
            return
        for op in stage_ops:
            op.deps = {d: st for d, st in op.deps.items() if d >= s0}
            for d, strict in op.deps.items():
                p = ops[d]
                if p.dma:
                    continue
                if p.eng != op.eng or op.dma:
                    p.signal = True
                elif strict and p.eng != "pe":
                    p.signal = True
        NS2 = 2 * self.n_dma_sems
        for op in stage_ops:
            if op.dma == "cc":
                self.n_cc += 1
                op.sem, op.val = ("cc", 0), self.n_cc
            elif op.dma:
                j = self.n_dma[op.eng] % self.n_dma_sems + (self.n_dma_sems if op.eng == "pool" else 0)
                self.n_dma[op.eng] += 1
                self.dma_cnt[j] += 1
                op.sem, op.val = ("dma", j), 16 * self.dma_cnt[j]
                if self.dma_last[j] is not None and self.dma_last[j] >= s0:
                    op.deps[self.dma_last[j]] = True
                self.dma_last[j] = op.idx
            elif op.signal:
                t = self.cnt[op.eng]
                self.cnt[op.eng] += 1
                op.sem, op.val = (op.eng, t // SEM_CAP), t % SEM_CAP + 1
        per_eng = {e: [] for e in self.ENGS}
        for op in stage_ops:
            per_eng[op.eng].append(op)
        final = {}
        for op in stage_ops:
            if op.dma:
                final[op.sem] = max(final.get(op.sem, 0), op.val)
        sems = self._sem

        def emit(e, eng):
            waited = self.waited[eng]
            for op in per_eng[eng]:
                need = {}
                for d, strict in op.deps.items():
                    p = ops[d]
                    if (not p.dma) and p.eng == eng and not op.dma:
                        if not strict or eng == "pe":
                            continue
                    if p.sem is None:
                        continue
                    if need.get(p.sem, 0) < p.val:
                        need[p.sem] = p.val
                todo = [(sk, v) for sk, v in need.items() if waited.get(sk, 0) < v]
                attach = None
                if ATTACH_WAIT and todo and eng in ("act", "dve", "pool") and not op.dma:
                    attach = todo.pop()
                for sk, v in todo:
                    e.wait_ge(sems(sk), v)
                    waited[sk] = v
                ins = op.fn(e)
                if attach is not None:
                    ins._wait_ge(sems(attach[0]), attach[1])
                    waited[attach[0]] = attach[1]
                if op.dma == "cc":
                    ins.then_inc(sems(op.sem), 1)
                elif op.dma:
                    ins.then_inc(sems(op.sem), 16)
                elif op.signal:
                    ins.then_inc(sems(op.sem), 1)
            if eng == "sp":
                for sk, v in final.items():
                    if waited.get(sk, 0) < v:
                        e.wait_ge(sems(sk), v)
                        waited[sk] = v

        with nc.Block() as block:
            @block.tensor
            def _(e):
                emit(e, "pe")

            @block.scalar
            def _(e):
                emit(e, "act")

            @block.vector
            def _(e):
                emit(e, "dve")

            @block.gpsimd
            def _(e):
                emit(e, "pool")

            @block.sync
            def _(e):
                emit(e, "sp")


D = 1024
KC = 8
FF = 2816
FC = 22
EPS = 1e-6


def _rmsnorm(P, nc, x_sb, xkey, h_sb, hkey, g_sb, gcol0, sq_sb, ones_sb, ps_ss, rstd_sb, t0, tn, tag, o0=None):
    kq = "sq" + tag
    if o0 is None:
        o0 = t0
    for kc in range(KC):
        P.act(lambda e, kc=kc: e.activation(out=sq_sb[:, kc, 0:tn], in_=x_sb[:, kc, t0:t0 + tn], func=AF.Square),
              r=[xkey], w=[kq])
    for kc in range(KC):
        P.pe(lambda e, kc=kc: e.matmul(ps_ss[:, 0:tn], lhsT=ones_sb[:, :], rhs=sq_sb[:, kc, 0:tn],
                                       start=(kc == 0), stop=(kc == KC - 1)),
             r=[kq, "ones"], w=["ps_ss"])
    P.act(lambda e: e.activation(out=rstd_sb[:, 0:tn], in_=ps_ss[:, 0:tn], func=AF.Sqrt, bias=EPS, scale=1.0),
          r=["ps_ss"], w=["rstd"])
    P.dve(lambda e: e.reciprocal(out=rstd_sb[:, 0:tn], in_=rstd_sb[:, 0:tn]), r=["rstd"], w=["rstd"])
    for kc in range(KC):
        P.dve(lambda e, kc=kc: e.scalar_tensor_tensor(out=h_sb[:, kc, o0:o0 + tn], in0=x_sb[:, kc, t0:t0 + tn],
                                                      scalar=g_sb[:, gcol0 + kc:gcol0 + kc + 1],
                                                      in1=rstd_sb[:, 0:tn], op0=ALU.mult, op1=ALU.mult),
              r=[xkey, "rstd", "gains"], w=[hkey])


def emit_LA(nc, P, io, NT, has_add, TB=1024, TT=512, pre=None):
    NWB, NWI, NWA = 4, 6, 3
    xT = io("xT", [D, NT], "in")
    gains = io("gains", [128, 2 * KC], "in")
    w_in = io("w_in", [FC, 128, KC, 256], "in")
    w_out = io("w_out", [KC, 128, FC, 128], "in")
    if has_add:
        aT = io("aT", [D, NT], "in")
        w_add = io("w_add", [KC, 128, KC, 128], "in")
    yT = io("yT", [D, NT], "out")
    hT = io("hT", [D, NT], "out")
    xT_v = xT.rearrange("(kc p) t -> p kc t", p=128)
    yT_v = yT.rearrange("(kc p) t -> p kc t", p=128)
    hT_v = hT.rearrange("(kc p) t -> p kc t", p=128)
    NB = NT // TB
    NS = TB // TT
    with P.stage():
        if pre is not None:
            pre()
        x_sb = P.sb("x_sb", [128, KC, TB], F32)
        h_sb = P.sb("h_sb", [128, KC, TB], BF16)
        act_sb = P.sb("act_sb", [128, FC, TB], BF16)
        sq_sb = P.sb("sq_sb", [128, KC, TT], BF16)
        ho_sb = P.sb("ho_sb", [128, KC, TT], F32)
        rstd_sb = P.sb("rstd_sb", [128, TT], F32)
        silu_sb = [P.sb("silu_sb%d" % i, [128, TT], F32) for i in range(2)]
        g_sb = P.sb("g_sb", [128, 2 * KC], F32)
        ones_sb = P.sb("ones_sb", [128, 128], BF16)
        win_sb = [P.sb("win_sb%d" % i, [128, KC, 256], BF16) for i in range(NWI)]
        wout_sb = [P.sb("wout_sb%d" % i, [128, FC, 128], BF16) for i in range(NWB)]
        if has_add:
            a_sb = P.sb("a_sb", [128, KC, TB], BF16)
            wadd_sb = [P.sb("wadd_sb%d" % i, [128, KC, 128], BF16) for i in range(NWA)]
        ps_g = [P.ps("ps_g%d" % i, [128, TT]) for i in range(2)]
        ps_u = [P.ps("ps_u%d" % i, [128, TT]) for i in range(2)]
        ps_o = [P.ps("ps_o%d" % i, [128, TT]) for i in range(2)]
        ps_ss = P.ps("ps_ss", [128, TT])
        P.psum_keys.update(["ps_g0", "ps_g1", "ps_u0", "ps_u1", "ps_o0", "ps_o1", "ps_ss"])

        P.dma("sp", g_sb[:, :], gains[:, :], w=["gains"])
        P.pool(lambda e: e.memset(ones_sb[:, :], 1.0 / D), w=["ones"])
        nw = [0, 0, 0]
        for b in range(NB):
            tb0 = b * TB
            for kc in range(KC):
                P.dma("sp", x_sb[:, kc, :], xT_v[:, kc, tb0:tb0 + TB], w=["x"])
            if has_add:
                for kc in range(KC):
                    P.dma("pool", a_sb[:, kc, :], aT.rearrange("(kc p) t -> p kc t", p=128)[:, kc, tb0:tb0 + TB],
                          w=["a"])
                for oc in range(KC):
                    s = nw[2] % NWA
                    nw[2] += 1
                    P.dma("pool", wadd_sb[s][:, :, :], w_add[oc], w=["wadd%d" % s])
                    for st in range(NS):
                        t0 = st * TT
                        pb = ps_o[(oc * NS + st) % 2]
                        pk = "ps_o%d" % ((oc * NS + st) % 2)
                        for kc in range(KC):
                            P.pe(lambda e, kc=kc, s=s, pb=pb, t0=t0: e.matmul(
                                pb[:, :], lhsT=wadd_sb[s][:, kc, :], rhs=a_sb[:, kc, t0:t0 + TT],
                                start=(kc == 0), stop=(kc == KC - 1)), r=["wadd%d" % s, "a"], w=[pk])
                        P.dve(lambda e, oc=oc, pb=pb, t0=t0: e.tensor_tensor(
                            out=x_sb[:, oc, t0:t0 + TT], in0=pb[:, :], in1=x_sb[:, oc, t0:t0 + TT], op=ALU.add),
                            r=[pk, "x"], w=["x"])
            for st in range(NS):
                _rmsnorm(P, nc, x_sb, "x", h_sb, "h", g_sb, 0, sq_sb, ones_sb, ps_ss, rstd_sb, st * TT, TT, "")
            for j in range(FC):
                s = nw[0] % NWI
                nw[0] += 1
                P.dma("pool", win_sb[s][:, :, :], w_in[j], w=["win%d" % s])
                for st in range(NS):
                    t0 = st * TT
                    q = (j * NS + st) % 2
                    for kc in range(KC):
                        P.pe(lambda e, kc=kc, s=s, q=q, t0=t0: e.matmul(
                            ps_g[q][:, :], lhsT=win_sb[s][:, kc, 0:128], rhs=h_sb[:, kc, t0:t0 + TT],
                            start=(kc == 0), stop=(kc == KC - 1)), r=["win%d" % s, "h"], w=["ps_g%d" % q])
                    for kc in range(KC):
                        P.pe(lambda e, kc=kc, s=s, q=q, t0=t0: e.matmul(
                            ps_u[q][:, :], lhsT=win_sb[s][:, kc, 128:256], rhs=h_sb[:, kc, t0:t0 + TT],
                            start=(kc == 0), stop=(kc == KC - 1)), r=["win%d" % s, "h"], w=["ps_u%d" % q])
                    P.act(lambda e, q=q: e.activation(out=silu_sb[q][:, :], in_=ps_g[q][:, :], func=AF.Silu),
                          r=["ps_g%d" % q], w=["silu%d" % q])
                    P.dve(lambda e, q=q, j=j, t0=t0: e.tensor_tensor(
                        out=act_sb[:, j, t0:t0 + TT], in0=ps_u[q][:, :], in1=silu_sb[q][:, :], op=ALU.mult),
                        r=["ps_u%d" % q, "silu%d" % q], w=["act"])
            for oc in range(KC):
                s = nw[1] % NWB
                nw[1] += 1
                P.dma("pool", wout_sb[s][:, :, :], w_out[oc], w=["wout%d" % s])
                for st in range(NS):
                    t0 = st * TT
                    q = (oc * NS + st) % 2
                    for j in range(FC):
                        P.pe(lambda e, j=j, s=s, q=q, t0=t0: e.matmul(
                            ps_o[q][:, :], lhsT=wout_sb[s][:, j, :], rhs=act_sb[:, j, t0:t0 + TT],
                            start=(j == 0), stop=(j == FC - 1)), r=["wout%d" % s, "act"], w=["ps_o%d" % q])
                    P.dve(lambda e, oc=oc, q=q, t0=t0: e.scalar_tensor_tensor(
                        out=x_sb[:, oc, t0:t0 + TT], in0=ps_o[q][:, :], scalar=0.5, in1=x_sb[:, oc, t0:t0 + TT],
                        op0=ALU.mult, op1=ALU.add), r=["ps_o%d" % q, "x"], w=["x"])
            for kc in range(KC):
                P.dma("sp", yT_v[:, kc, tb0:tb0 + TB], x_sb[:, kc, :], r=["x"], w=["yT"])
            for st in range(NS):
                _rmsnorm(P, nc, x_sb, "x", ho_sb, "ho", g_sb, KC, sq_sb, ones_sb, ps_ss, rstd_sb, st * TT, TT, "", o0=0)
                for kc in range(KC):
                    P.dma("sp", hT_v[:, kc, tb0 + st * TT:tb0 + (st + 1) * TT], ho_sb[:, kc, :], r=["ho"], w=["hT"])
        pass


def _blockones(P, t, val, key):
    P.pool(lambda e: e.memset(t[:, :], 0.0), w=[key])
    P.pool(lambda e: e.memset(t[0:64, 0:64], val), w=[key])
    P.pool(lambda e: e.memset(t[64:128, 64:128], val), w=[key])


def emit_LP(nc, P, io, NT, has_v, TB=1024, TT=512):
    hT = io("hT", [D, NT], "in")
    wn = io("wn", [KC, 128, KC, 128], "in")
    gn = io("gn", [128, 1], "in")
    nT = io("nT", [D, NT], "out")
    if has_v:
        wv = io("wv", [128, KC, D], "in")
        v_tok = io("v_tok", [NT, D], "out")
    hT_v = hT.rearrange("(kc p) t -> p kc t", p=128)
    nT_v = nT.rearrange("(kc p) t -> p kc t", p=128)
    NB, NS = NT // TB, TB // TT
    with P.stage():
        h_sb = P.sb("h_sb", [128, KC, TB], BF16)
        wn_sb = [P.sb("wn_sb%d" % i, [128, KC, 128], BF16) for i in range(2)]
        sq_sb = [P.sb("sq_sb%d" % i, [128, TT], BF16) for i in range(3)]
        rstd_sb = [P.sb("rstd_sb%d" % i, [128, TT], F32) for i in range(3)]
        o_sb = [P.sb("o_sb%d" % i, [128, TT], F32) for i in range(3)]
        g_sb = P.sb("g_sb", [128, 1], F32)
        bo_sb = P.sb("bo_sb", [128, 128], BF16)
        ps_p = [P.ps("ps_p%d" % i, [128, TT]) for i in range(3)]
        ps_s = [P.ps("ps_s%d" % i, [128, TT]) for i in range(3)]
        P.psum_keys.update(["ps_p0", "ps_p1", "ps_p2", "ps_s0", "ps_s1", "ps_s2", "ps_v0", "ps_v1"])
        if has_v:
            wv_sb = P.sb("wv_sb", [128, KC, D], BF16)
            v_sb = [P.sb("v_sb%d" % i, [128, 512], F32) for i in range(2)]
            ps_v = [P.ps("ps_v%d" % i, [128, 512]) for i in range(2)]
            for kc in range(KC):
                P.dma("pool", wv_sb[:, kc, :], wv[:, kc, :], w=["wv"])
        P.dma("sp", g_sb[:, :], gn[:, :], w=["gn"])
        _blockones(P, bo_sb, 1.0 / 64, "bo")
        it = 0
        nv = 0
        for b in range(NB):
            tb0 = b * TB
            for kc in range(KC):
                P.dma("pool", h_sb[:, kc, :], hT_v[:, kc, tb0:tb0 + TB], w=["h"])
            pend = []

            def lp_tail(u):
                oc, t0, q = u
                P.pe(lambda e, q=q: e.matmul(ps_s[q][:, :], lhsT=bo_sb[:, :], rhs=sq_sb[q][:, :],
                                             start=True, stop=True), r=["sq%d" % q, "bo"], w=["ps_s%d" % q])
                P.act(lambda e, q=q: e.activation(out=rstd_sb[q][:, :], in_=ps_s[q][:, :], func=AF.Sqrt,
                                                  bias=EPS, scale=1.0), r=["ps_s%d" % q], w=["rstd%d" % q])
                P.dve(lambda e, q=q: e.reciprocal(out=rstd_sb[q][:, :], in_=rstd_sb[q][:, :]),
                      r=["rstd%d" % q], w=["rstd%d" % q])
                P.dve(lambda e, q=q: e.scalar_tensor_tensor(
                    out=o_sb[q][:, :], in0=ps_p[q][:, :], scalar=g_sb[:, 0:1], in1=rstd_sb[q][:, :],
                    op0=ALU.mult, op1=ALU.mult), r=["ps_p%d" % q, "rstd%d" % q, "gn"], w=["o%d" % q])
                P.dma("sp", nT_v[:, oc, tb0 + t0:tb0 + t0 + TT], o_sb[q][:, :], r=["o%d" % q], w=["nT"])

            for oc in range(KC):
                s = oc % 2
                P.dma("pool", wn_sb[s][:, :, :], wn[oc], w=["wn%d" % s])
                for st in range(NS):
                    t0 = st * TT
                    q = it % 3
                    it += 1
                    for kc in range(KC):
                        P.pe(lambda e, kc=kc, s=s, q=q, t0=t0: e.matmul(
                            ps_p[q][:, :], lhsT=wn_sb[s][:, kc, :], rhs=h_sb[:, kc, t0:t0 + TT],
                            start=(kc == 0), stop=(kc == KC - 1)), r=["wn%d" % s, "h"], w=["ps_p%d" % q])
                    P.act(lambda e, q=q: e.activation(out=sq_sb[q][:, :], in_=ps_p[q][:, :], func=AF.Square),
                          r=["ps_p%d" % q], w=["sq%d" % q])
                    pend.append((oc, t0, q))
                    if len(pend) > 1:
                        lp_tail(pend.pop(0))
            while pend:
                lp_tail(pend.pop(0))
            if has_v:
                for tbk in range(TB // 128):
                    for half in range(2):
                        q = nv % 2
                        nv += 1
                        for kc in range(KC):
                            P.pe(lambda e, kc=kc, q=q, tbk=tbk, half=half: e.matmul(
                                ps_v[q][:, :], lhsT=h_sb[:, kc, tbk * 128:(tbk + 1) * 128],
                                rhs=wv_sb[:, kc, half * 512:(half + 1) * 512],
                                start=(kc == 0), stop=(kc == KC - 1)), r=["wv", "h"], w=["ps_v%d" % q])
                        P.act(lambda e, q=q: e.copy(out=v_sb[q][:, :], in_=ps_v[q][:, :]),
                              r=["ps_v%d" % q], w=["v%d" % q])
                        P.dma("sp", v_tok[tb0 + tbk * 128:tb0 + (tbk + 1) * 128, half * 512:(half + 1) * 512],
                              v_sb[q][:, :], r=["v%d" % q], w=["v_tok"])
        pass


LAM_INIT1 = 0.8 - 0.6 * float(np.exp(-0.3))
SUBLN_EPS = 1e-5
NDD = 67


def emit_ATT(nc, P, io, T, HL=4, gath=None):
    if gath is None:
        qT = io("qT", [HL * 128, T], "in")
        kT = io("kT", [HL * 128, T], "in")
        v_tok = io("v_tok", [T, HL * 128], "in")
    qaug = io("qaug", [5, T], "in")
    kaug = io("kaug", [HL, 5, T], "in")
    btab = io("btab", [128, HL * NDD], "in")
    tri = io("tri", [128, 128], "in")
    lam = io("lam", [1, 256], "in")
    subg = io("subg", [128, 128], "in")
    o_tok = io("o_tok", [T, HL * 128], "out")
    NQB = T // 512
    NKB = T // 128
    with P.stage():
        q_sbs = [P.sb("q_sb%d" % i, [69, 2, T], BF16) for i in range(2)]
        k_sbs = [P.sb("k_sb%d" % i, [69, 2, T], BF16) for i in range(2)]
        v_sbs = [P.sb("v_sb%d" % i, [128, NKB, 130], BF16) for i in range(2)]
        bt_sb = P.sb("bt_sb", [128, HL * NDD], F32)
        tri_sb = P.sb("tri_sb", [128, 128], BF16)
        z_sb = P.sb("z_sb", [128, 512], BF16)
        pt_sb = [P.sb("pt_sb%d" % i, [128, 512], BF16) for i in range(6)]
        lam_sb = P.sb("lam_sb", [1, 256], F32)
        lt_sb = P.sb("lt_sb", [1, 128], F32)
        ls_sb = P.sb("ls_sb", [1, 4], F32)
        one_row = P.sb("one_row", [1, 128], F32)
        nl_sb = P.sb("nl_sb", [128, 1], F32)
        eps_sb = P.sb("eps_sb", [128, 1], F32)
        sg_sb = P.sb("sg_sb", [128, 128], F32)
        rec_sb = [P.sb("rec_sb%d" % i, [128, 4], F32) for i in range(2)]
        o0_sb = [P.sb("o0_sb%d" % i, [128, 128], F32) for i in range(2)]
        od_sb = [P.sb("od_sb%d" % i, [128, 128], F32) for i in range(2)]
        junk_sb = [P.sb("junk_sb%d" % i, [128, 128], F32) for i in range(2)]
        out_sb = [P.sb("out_sb%d" % i, [128, 128], F32) for i in range(2)]
        oc_sb = [P.sb("oc_sb%d" % i, [128, 512], F32) for i in range(3)]
        ps_S = [P.ps("ps_S%d" % i, [128, 512]) for i in range(4)]
        ps_O = [P.ps("ps_O%d" % i, [128, 512]) for i in range(3)]
        ps_m = P.ps("ps_m", [128, 512])
        P.psum_keys.update(["S0", "S1", "S2", "S3", "O0", "O1", "O2", "ps_m"])

        P.dma("sp", bt_sb[:, :], btab[:, :], w=["bt"])
        P.dma("pool", tri_sb[:, :], tri[:, :], w=["tri"])
        P.dma("sp", lam_sb[:, :], lam[:, :], w=["lam"])
        P.dma("sp", sg_sb[:, :], subg[:, :], w=["sg"])
        P.pool(lambda e: e.memset(z_sb[:, :], 0.0), w=["z"])
        P.pool(lambda e: e.memset(eps_sb[:, :], SUBLN_EPS), w=["epsc"])
        P.pool(lambda e: e.memset(one_row[:, :], 1.0), w=["one_row"])
        P.pool(lambda e: e.memset(v_sbs[0][:, :, 128:130], 1.0), w=["vones"])
        P.pool(lambda e: e.memset(v_sbs[1][:, :, 128:130], 1.0), w=["vones"])
        P.dve(lambda e: e.tensor_tensor(out=lt_sb[:, 0:64], in0=lam_sb[:, 0:64], in1=lam_sb[:, 64:128], op=ALU.mult),
              r=["lam"], w=["lt"])
        P.dve(lambda e: e.tensor_tensor(out=lt_sb[:, 64:128], in0=lam_sb[:, 128:192], in1=lam_sb[:, 192:256],
                                        op=ALU.mult), r=["lam"], w=["lt"])
        P.dve(lambda e: e.tensor_reduce(out=ls_sb[:, 0:1], in_=lt_sb[:, 0:64], axis=AX.X, op=ALU.add),
              r=["lt"], w=["ls"])
        P.dve(lambda e: e.tensor_reduce(out=ls_sb[:, 1:2], in_=lt_sb[:, 64:128], axis=AX.X, op=ALU.add),
              r=["lt"], w=["ls"])
        P.act(lambda e: e.activation(out=ls_sb[:, 0:2], in_=ls_sb[:, 0:2], func=AF.Exp), r=["ls"], w=["ls"])
        P.dve(lambda e: e.scalar_tensor_tensor(out=ls_sb[:, 2:3], in0=ls_sb[:, 1:2], scalar=-LAM_INIT1,
                                               in1=ls_sb[:, 0:1], op0=ALU.add, op1=ALU.subtract),
              r=["ls"], w=["ls"])
        P.pe(lambda e: e.matmul(ps_m[:, 0:1], lhsT=one_row[:, :], rhs=ls_sb[:, 2:3], start=True, stop=True),
             r=["ls", "one_row"], w=["ps_m"])
        P.dve(lambda e: e.tensor_copy(out=nl_sb[:, :], in_=ps_m[:, 0:1]), r=["ps_m"], w=["nl"])
        P.dve(lambda e: e.tensor_scalar(out=sg_sb[:, :], in0=sg_sb[:, :], scalar1=1.0 - LAM_INIT1, scalar2=None,
                                        op0=ALU.mult), r=["sg"], w=["sg"])

        def acc(a):
            return ps_O[a // 3], (a % 3) * 132

        sl = 0
        fin = 0
        if gath is not None:
            qg_, kg_, vg_, sel_ = gath
            HT_ = T // 2
            sel_sb = P.sb("sel_sb", [128, 2], F32)
            P.dma("sp", sel_sb[:, :], sel_[:, :], w=["sel"])
            stg_sb = [P.sb("stg_sb%d" % i, [128, HT_], BF16) for i in range(2)]
            nst = [0]

            def blend(dst, A, B, npart, key):
                q_ = nst[0] % 2
                nst[0] += 1
                P.dma("pool", dst, A, w=[key])
                n = dst.shape[1] if len(dst.shape) == 2 else dst.shape[1] * dst.shape[2]
                st = stg_sb[q_][0:npart, 0:n]
                if len(dst.shape) == 3:
                    st = st.rearrange("p (a b) -> p a b", b=dst.shape[2])
                P.dma("pool", st, B, w=["stg%d" % q_])
                P.dve(lambda e: e.tensor_scalar(out=dst, in0=dst, scalar1=sel_sb[0:npart, 0:1], scalar2=None,
                                                op0=ALU.mult), r=[key, "sel"], w=[key])
                P.dve(lambda e: e.scalar_tensor_tensor(out=dst, in0=st, scalar=sel_sb[0:npart, 1:2], in1=dst,
                                                       op0=ALU.mult, op1=ALU.add), r=[key, "stg%d" % q_, "sel"], w=[key])

        def load_head(hl):
            hb = hl % 2
            for c in range(2):
                r0 = hl * 128 + c * 64
                if gath is None:
                    P.dma("pool", q_sbs[hb][0:64, c, :], qT[r0:r0 + 64, :], w=["q%d" % hb])
                    P.dma("pool", k_sbs[hb][0:64, c, :], kT[r0:r0 + 64, :], w=["k%d" % hb])
                else:
                    for j in range(2):
                        cs = slice(j * HT_, (j + 1) * HT_)
                        blend(q_sbs[hb][0:64, c, cs], qg_.rows(j, r0, r0 + 64), qg_.rows(j, 512 + r0, 512 + r0 + 64),
                              64, "q%d" % hb)
                        blend(k_sbs[hb][0:64, c, cs], kg_.rows(j, r0, r0 + 64), kg_.rows(j, 512 + r0, 512 + r0 + 64),
                              64, "k%d" % hb)
                P.dma("pool", q_sbs[hb][64:69, c, :], qaug[:, :], w=["q%d" % hb])
                P.dma("pool", k_sbs[hb][64:69, c, :], kaug[hl], w=["k%d" % hb])
            if gath is None:
                P.dma("pool", v_sbs[hb][:, :, 0:128],
                      v_tok.rearrange("(blk p) v -> p blk v", p=128)[:, :, hl * 128:(hl + 1) * 128], w=["v%d" % hb])
            else:
                nb_ = HT_ // 128
                for j in range(2):
                    rc = vg_.RC
                    for ch in range(HT_ // rc):
                        nbc = rc // 128
                        blk0 = j * nb_ + ch * nbc
                        rows = vg_.rows(j, ch * rc, (ch + 1) * rc).rearrange("(blk p) v -> p blk v", p=128)
                        blend(v_sbs[hb][:, blk0:blk0 + nbc, 0:128], rows[:, :, hl * 128:(hl + 1) * 128],
                              rows[:, :, 512 + hl * 128:512 + (hl + 1) * 128], 128, "v%d" % hb)

        load_head(0)
        for hl in range(HL):
            hb = hl % 2
            q_sb, k_sb, v_sb = q_sbs[hb], k_sbs[hb], v_sbs[hb]
            qk_, kk_, vk_ = "q%d" % hb, "k%d" % hb, "v%d" % hb
            if hl + 1 < HL:
                load_head(hl + 1)
            for qb in range(NQB):
                for bk in range(3):
                    P.pe(lambda e, bk=bk: e.matmul(ps_O[bk][:, :], lhsT=z_sb[:, 0:128], rhs=z_sb[:, :],
                                                   start=True, stop=False), r=["z"], w=["O%d" % bk])
                nkb = 4 * qb + 4
                units = [(kb, c) for kb in range(nkb) for c in range(2)]
                pend = []

                def emit_pv(u, qb=qb, v_sb=v_sb, vk=vk_):
                    kb, c, s, j0 = u
                    for jj in range(j0 // 128, 4):
                        a = jj * 2 + c
                        pb, off = acc(a)
                        P.pe(lambda e, s=s, jj=jj, kb=kb, pb=pb, off=off, last=(kb == 4 * qb + jj): e.matmul(
                            pb[:, off:off + 129], lhsT=pt_sb[s][:, jj * 128:(jj + 1) * 128],
                            rhs=v_sb[:, kb, 0:129], start=False, stop=last),
                            r=["pt%d" % s, vk, "vones"], w=["O%d" % (a // 3)])

                for kb, c in units:
                    rr = kb - 4 * qb
                    j0 = max(0, 128 * rr)
                    s = sl % 4
                    sp_ = sl % 6
                    sl += 1
                    P.pe(lambda e, s=s, c=c, kb=kb, qb=qb, j0=j0, k_sb=k_sb, q_sb=q_sb: e.matmul(
                        ps_S[s][:, j0:512], lhsT=k_sb[0:69, c, kb * 128:(kb + 1) * 128],
                        rhs=q_sb[0:69, c, qb * 512 + j0:(qb + 1) * 512], start=True, stop=True),
                        r=[qk_, kk_], w=["S%d" % s])
                    P.act(lambda e, s=s, sp_=sp_, j0=j0: e.activation(
                        out=pt_sb[sp_][:, j0:512], in_=ps_S[s][:, j0:512], func=AF.Exp, scale=0.125),
                        r=["S%d" % s], w=["pt%d" % sp_])
                    if rr >= 0:
                        P.dve(lambda e, sp_=sp_, j0=j0: e.tensor_tensor(
                            out=pt_sb[sp_][:, j0:j0 + 128], in0=pt_sb[sp_][:, j0:j0 + 128], in1=tri_sb[:, :],
                            op=ALU.mult), r=["pt%d" % sp_, "tri"], w=["pt%d" % sp_])
                    pend.append((kb, c, sp_, j0))
                    if len(pend) > 3:
                        emit_pv(pend.pop(0))
                while pend:
                    emit_pv(pend.pop(0))
                for bk in range(3):
                    P.dve(lambda e, bk=bk: e.tensor_copy(out=oc_sb[bk][:, :], in_=ps_O[bk][:, :]),
                          r=["O%d" % bk], w=["oc%d" % bk])
                for jj in range(4):
                    f = fin % 2
                    fin += 1
                    pb0, off0 = oc_sb[(jj * 2) // 3], ((jj * 2) % 3) * 132
                    pb1, off1 = oc_sb[(jj * 2 + 1) // 3], ((jj * 2 + 1) % 3) * 132
                    k0, k1 = "oc%d" % ((jj * 2) // 3), "oc%d" % ((jj * 2 + 1) // 3)
                    P.dve(lambda e, f=f, pb0=pb0, off0=off0: e.reciprocal(
                        out=rec_sb[f][:, 0:1], in_=pb0[:, off0 + 128:off0 + 129]), r=[k0], w=["rec%d" % f])
                    P.dve(lambda e, f=f, pb1=pb1, off1=off1: e.reciprocal(
                        out=rec_sb[f][:, 1:2], in_=pb1[:, off1 + 128:off1 + 129]), r=[k1], w=["rec%d" % f])
                    P.dve(lambda e, f=f: e.tensor_tensor(out=rec_sb[f][:, 2:3], in0=rec_sb[f][:, 1:2],
                                                         in1=nl_sb[:, 0:1], op=ALU.mult),
                          r=["rec%d" % f, "nl"], w=["rec%d" % f])
                    P.dve(lambda e, f=f, pb0=pb0, off0=off0: e.tensor_scalar(
                        out=o0_sb[f][:, :], in0=pb0[:, off0:off0 + 128], scalar1=rec_sb[f][:, 0:1], scalar2=None,
                        op0=ALU.mult), r=[k0, "rec%d" % f], w=["o0%d" % f])
                    P.dve(lambda e, f=f, pb1=pb1, off1=off1: e.scalar_tensor_tensor(
                        out=od_sb[f][:, :], in0=pb1[:, off1:off1 + 128], scalar=rec_sb[f][:, 2:3],
                        in1=o0_sb[f][:, :], op0=ALU.mult, op1=ALU.add),
                        r=[k1, "rec%d" % f, "o0%d" % f], w=["od%d" % f])
                    P.dve(lambda e, f=f: e.tensor_tensor(out=junk_sb[f][:, :], in0=od_sb[f][:, :], in1=od_sb[f][:, :],
                                                         op=ALU.mult), r=["od%d" % f], w=["junk%d" % f])
                    P.dve(lambda e, f=f: e.tensor_reduce(out=rec_sb[f][:, 3:4], in_=junk_sb[f][:, :], axis=AX.X,
                                                         op=ALU.add), r=["junk%d" % f], w=["ss%d" % f])
                    P.act(lambda e, f=f: e.activation(out=rec_sb[f][:, 3:4], in_=rec_sb[f][:, 3:4], func=AF.Ln,
                                                      bias=eps_sb[:, 0:1], scale=1.0 / 128),
                          r=["ss%d" % f, "epsc"], w=["ss%d" % f])
                    P.act(lambda e, f=f: e.activation(out=rec_sb[f][:, 3:4], in_=rec_sb[f][:, 3:4], func=AF.Exp,
                                                      scale=-0.5), r=["ss%d" % f], w=["ss%d" % f])
                    P.dve(lambda e, f=f: e.scalar_tensor_tensor(
                        out=out_sb[f][:, :], in0=od_sb[f][:, :], scalar=rec_sb[f][:, 3:4], in1=sg_sb[:, :],
                        op0=ALU.mult, op1=ALU.mult), r=["od%d" % f, "ss%d" % f, "sg"], w=["out%d" % f])
                    P.dma("sp", o_tok[qb * 512 + jj * 128:qb * 512 + (jj + 1) * 128, hl * 128:(hl + 1) * 128],
                          out_sb[f][:, :], r=["out%d" % f], w=["o_tok"])
        pass


def att_consts(T, heads):
    t = np.arange(T)
    one = np.ones_like(t)
    qaug = np.stack([(t % 512) // 16, t % 16, one, one, t // 512]).astype(np.float32)
    kaug = np.zeros((len(heads), 5, T), np.float32)
    btab = np.zeros((128, len(heads) * NDD), np.float32)
    for i, h in enumerate(heads):
        slope = 2.0 ** (-(h + 1))
        kaug[i, 0, :] = -16.0 * slope / 0.125
        kaug[i, 1, :] = -slope / 0.125
        kaug[i, 2, :] = slope * (t % 128) / 0.125
        kaug[i, 3, :] = slope * 128.0 * (t // 128) / 0.125
        kaug[i, 4, :] = -512.0 * slope / 0.125
    tri = (np.arange(128)[:, None] <= np.arange(128)[None, :]).astype(np.float32)
    return qaug, kaug, btab, tri


C0 = float(np.exp(-0.5))
RW_LN_EPS = 64e-5
CL = 128


def emit_RW(nc, P, io, T, TT=256, LEVEL=9):
    dt_in = lambda n, s: io(n, s, "in")
    hTp = dt_in("hTp", [D, T + 1])
    wr_d, wk_d, wv_d = dt_in("wr", [D, 512]), dt_in("wk", [D, 512]), dt_in("wv", [D, 512])
    l1_d = dt_in("l1", [D, 288])
    w2_d, a2_d, g2_d = dt_in("w2c", [64, 512]), dt_in("a2c", [64, 512]), dt_in("g2c", [160, 512])
    mu_d = dt_in("mu6", [128, 48])
    cv_d = dt_in("cv", [128, 20])
    lng_d, lnb_d = dt_in("lng", [128, 512]), dt_in("lnb", [128, 512])
    ident_d, mask2_d, maskL_d = dt_in("ident", [128, 128]), dt_in("mask2", [128, 256]), dt_in("maskL", [128, 128])
    rmask_d, ind_d = dt_in("rmask", [128, TT]), dt_in("ind", [128, 2])
    yg = io("yg_tok", [T, 512], "out")
    hTp_v = hTp.rearrange("(kc p) t -> p kc t", p=128)
    NTI = T // TT
    NCL = TT // CL
    with P.stage():
        sb = P.sb
        W1 = {n: sb("W1" + n, [128, KC, 512], BF16) for n in "rkv"}
        W2 = {n: sb("W2" + n, [128, KC, 512], BF16) for n in "rkv"}
        L1a = sb("L1a", [128, KC, 288], BF16)
        L1b = sb("L1b", [128, KC, 288], BF16)
        stg = [sb("stg%d" % i, [128, 512], F32) for i in range(2)]
        w2_sb, a2_sb = sb("w2_sb", [64, 512], BF16), sb("a2_sb", [64, 512], BF16)
        g2a_sb, g2b_sb = sb("g2a_sb", [128, 512], BF16), sb("g2b_sb", [32, 512], BF16)
        mu_sb, omu_sb, cv_sb = sb("mu_sb", [128, 48], F32), sb("omu_sb", [128, 48], F32), sb("cv_sb", [128, 24], F32)
        lng_sb, lnb_sb = sb("lng_sb", [128, 512], F32), sb("lnb_sb", [128, 512], F32)
        ident_bf, identf = sb("ident_bf", [128, 128], BF16), sb("identf", [128, 128], F32)
        mask2_sb, maskL_sb = sb("mask2_sb", [128, 256], F32), sb("maskL_sb", [128, 128], F32)
        rmask_sb, ind_sb, bo_sb = sb("rmask_sb", [128, TT], F32), sb("ind_sb", [128, 2], BF16), sb("bo_sb", [128, 128], BF16)
        hp = [sb("hp%d" % i, [128, KC, TT], BF16) for i in range(2)]
        hq = [sb("hq%d" % i, [128, KC, TT], BF16) for i in range(2)]
        r32, k32 = sb("r32", [128, 4, TT], F32), sb("k32", [128, 4, TT], F32)
        vtok, v32 = sb("vtok", [128, NCL, 512], BF16), sb("v32", [128, NCL, 512], F32)
        gtok = sb("gtok", [128, NCL, 512], F32)
        lw_sb, la_sb = sb("lw_sb", [64, TT], BF16), sb("la_sb", [64, TT], BF16)
        lga_sb, lgb_sb = sb("lga_sb", [128, TT], BF16), sb("lgb_sb", [32, TT], BF16)
        sc = {n: sb("sc_" + n, [128, TT], F32) for n in
              ("sg", "a", "kk", "rn", "kkn", "tmp", "kmod", "bvec", "cs", "tmp2", "Em", "Epv")}
        kk2_sb = sb("kk2_sb", [128, TT], BF16)
        sc2 = {n: sb("sc2_" + n, [128, TT], F32) for n in sc}
        kk2b_sb = sb("kk2b_sb", [128, TT], BF16)
        Ep = sb("Ep", [128, 4, TT], F32)
        AR = sb("AR", [128, 4, NCL, 256], BF16)
        bt, kt, rk = sb("bt", [128, 4, TT], BF16), sb("kt", [128, 4, TT], BF16), sb("rk", [128, 4, TT], BF16)
        NTb_sb, NTk_sb = sb("NTb_sb", [128, 4, 512], BF16), sb("NTk_sb", [128, 4, 512], BF16)
        at2, rt2 = sb("at2", [128, 4, 2, TT], BF16), sb("rt2", [128, 4, 2, TT], BF16)
        bt2, kt2 = sb("bt2", [128, 4, 2, TT], BF16), sb("kt2", [128, 4, 2, TT], BF16)
        XX = [[sb("XX%d_%d" % (pc, i), [128, 512], BF16) for i in range(2)] for pc in range(4)]
        Pp = [[sb("Pp%d_%d" % (pc, i), [128, 256], BF16) for i in range(2)] for pc in range(4)]
        Pfin = sb("Pfin", [128, 4, 256], BF16)
        btok, ktok = sb("btok", [128, 4, 128], BF16), sb("ktok", [128, 4, 128], BF16)
        Zs, Us = sb("Zs", [128, 4, 128], BF16), sb("Us", [128, 4, 128], BF16)
        ytok = sb("ytok", [128, NCL, 512], F32)
        H32, Hbf, Htmp = sb("H32", [128, 4, 128], F32), sb("Hbf", [128, 4, 128], BF16), sb("Htmp", [128, 4, 128], F32)
        ysq, st_sb = sb("ysq", [128, 512], F32), sb("st_sb", [128, 40], F32)
        out_sb = [sb("out_sb%d" % i, [128, 512], F32) for i in range(2)]
        bk = [P.ps("bk%d" % i, [128, 512]) for i in range(5)] + [None] + [P.ps("bk%d" % i, [128, 512]) for i in (6, 7)]
        bk5 = P.ps("bk5", [128, 1024], BF16)
        BK = lambda i: "bk%d" % i
        P.psum_keys.update([BK(i) for i in range(8)])

        for dst, src, q in ((mu_sb, mu_d, "sp"), (lng_sb, lng_d, "sp"), (lnb_sb, lnb_d, "sp"),
                            (identf, ident_d, "sp"), (mask2_sb, mask2_d, "sp"), (maskL_sb, maskL_d, "sp"),
                            (rmask_sb, rmask_d, "sp"), (ident_bf, ident_d, "pool"), (ind_sb, ind_d, "pool"),
                            (w2_sb, w2_d, "pool"), (a2_sb, a2_d, "pool")):
            P.dma(q, dst[:, :], src[:, :], w=["const"])
        P.dma("sp", cv_sb[:, 0:20], cv_d[:, :], w=["const"])
        P.dma("pool", g2a_sb[:, :], g2_d[0:128, :], w=["const"])
        P.dma("pool", g2b_sb[:, :], g2_d[128:160, :], w=["const"])
        _blockones(P, bo_sb, 1.0, "bo")
        P.dve(lambda e: e.tensor_scalar(out=omu_sb[:, :], in0=mu_sb[:, :], scalar1=-1.0, scalar2=1.0,
                                        op0=ALU.mult, op1=ALU.add), r=["const"], w=["omu"])
        P.dve(lambda e: e.tensor_scalar(out=cv_sb[:, 20:24], in0=cv_sb[:, 12:16], scalar1=-1.0, scalar2=1.0,
                                        op0=ALU.mult, op1=ALU.add), r=["const"], w=["cv2"])
        P.pool(lambda e: e.memset(H32[:, :, :], 0.0), w=["H32"])
        for tz, kz in ((at2, "at2"), (rt2, "rt2"), (bt2, "bt2"), (kt2, "kt2")):
            P.pool(lambda e, tz=tz: e.memset(tz[:, :, :, :], 0.0), w=[kz])
        P.pool(lambda e: e.memset(Hbf[:, :, :], 0.0), w=["Hbf"])
        ns = 0
        for wi, (n, src) in enumerate((("r", wr_d), ("k", wk_d), ("v", wv_d))):
            for kc in range(KC):
                s = ns % 2
                ns += 1
                P.dma("sp", stg[s][:, :], src[kc * 128:(kc + 1) * 128, :], w=["stg%d" % s])
                P.dve(lambda e, s=s, n=n, kc=kc, wi=wi: e.tensor_scalar(
                    out=W1[n][:, kc, :], in0=stg[s][:, :], scalar1=omu_sb[:, wi * 8 + kc:wi * 8 + kc + 1],
                    scalar2=None, op0=ALU.mult), r=["stg%d" % s, "omu"], w=["W"])
                P.dve(lambda e, s=s, n=n, kc=kc, wi=wi: e.tensor_scalar(
                    out=W2[n][:, kc, :], in0=stg[s][:, :], scalar1=mu_sb[:, wi * 8 + kc:wi * 8 + kc + 1],
                    scalar2=None, op0=ALU.mult), r=["stg%d" % s, "const"], w=["W"])
        for kc in range(KC):
            s_ = ns % 2
            ns += 1
            P.dma("sp", stg[s_][:, 0:288], l1_d[kc * 128:(kc + 1) * 128, :], w=["stg%d" % s_])
            for li, (c0, c1) in enumerate(((0, 64), (64, 128), (128, 288))):
                wi = 3 + li
                P.dve(lambda e, kc=kc, wi=wi, c0=c0, c1=c1, s_=s_: e.tensor_scalar(
                    out=L1a[:, kc, c0:c1], in0=stg[s_][:, c0:c1], scalar1=omu_sb[:, wi * 8 + kc:wi * 8 + kc + 1],
                    scalar2=None, op0=ALU.mult), r=["stg%d" % s_, "omu"], w=["W"])
                P.dve(lambda e, kc=kc, wi=wi, c0=c0, c1=c1, s_=s_: e.tensor_scalar(
                    out=L1b[:, kc, c0:c1], in0=stg[s_][:, c0:c1], scalar1=mu_sb[:, wi * 8 + kc:wi * 8 + kc + 1],
                    scalar2=None, op0=ALU.mult), r=["stg%d" % s_, "const"], w=["W"])

        nb = [0]

        def nbk():
            nb[0] += 1
            return nb[0] % 2

        def proj_fm(pb, key, wa, wb, c0, c1, hpt, hk, n, hqt=None):
            m = c1 - c0
            hpt, hqt = hpt
            for kc in range(KC):
                P.pe(lambda e, kc=kc: e.matmul(pb[0:m, 0:n], lhsT=wa[:, kc, c0:c1], rhs=hpt[:, kc, 0:n],
                                               start=(kc == 0), stop=False), r=["W", hk], w=[key])
            for kc in range(KC):
                P.pe(lambda e, kc=kc: e.matmul(pb[0:m, 0:n], lhsT=wb[:, kc, c0:c1], rhs=hqt[:, kc, 0:n],
                                               start=False, stop=(kc == KC - 1)), r=["W", hk], w=[key])

        fo = 0
        for ti in range(NTI if LEVEL >= 1 else 0):
            t0 = ti * TT
            hs = ti % 2
            hk = "hp%d" % hs
            hp_, hq_ = hp[hs], hq[hs]
            hpt = (hp_, hq_)
            P.dma("pool", hp_[:, :, :], hTp_v[:, :, t0 + 1:t0 + TT + 1], w=[hk])
            P.dma("pool", hq_[:, :, :], hTp_v[:, :, t0:t0 + TT], w=[hk])
            for pc in range(4):
                for n, dst in (("r", r32), ("k", k32)):
                    b = nbk()
                    proj_fm(bk[b], BK(b), W1[n], W2[n], pc * 128, (pc + 1) * 128, hpt, hk, TT)
                    P.act(lambda e, b=b, dst=dst, pc=pc: e.copy(out=dst[:, pc, :], in_=bk[b][:, 0:TT]),
                          r=[BK(b)], w=[n + "32"])
            for cl in range(NCL):
                b = nbk()
                for kc in range(KC):
                    P.pe(lambda e, kc=kc, b=b, cl=cl, hpt=hp_: e.matmul(
                        bk[b][:, :], lhsT=hpt[:, kc, cl * 128:(cl + 1) * 128], rhs=W1["v"][:, kc, :],
                        start=(kc == 0), stop=False), r=["W", hk], w=[BK(b)])
                for kc in range(KC):
                    P.pe(lambda e, kc=kc, b=b, cl=cl, hpt=hq_: e.matmul(
                        bk[b][:, :], lhsT=hpt[:, kc, cl * 128:(cl + 1) * 128], rhs=W2["v"][:, kc, :],
                        start=False, stop=(kc == KC - 1)), r=["W", hk], w=[BK(b)])
                P.act(lambda e, b=b, cl=cl: e.copy(out=vtok[:, cl, :], in_=bk[b][:, :]), r=[BK(b)], w=["vtok"])
                P.act(lambda e, b=b, cl=cl: e.copy(out=v32[:, cl, :], in_=bk[b][:, :]), r=[BK(b)], w=["v32"])
            b = nbk()
            proj_fm(bk[b], BK(b), L1a, L1b, 0, 64, hpt, hk, TT)
            P.act(lambda e, b=b: e.activation(out=lw_sb[:, :], in_=bk[b][0:64, 0:TT], func=AF.Tanh),
                  r=[BK(b)], w=["lw"])
            b = nbk()
            proj_fm(bk[b], BK(b), L1a, L1b, 64, 128, hpt, hk, TT)
            P.act(lambda e, b=b: e.copy(out=la_sb[:, :], in_=bk[b][0:64, 0:TT]), r=[BK(b)], w=["la"])
            b = nbk()
            proj_fm(bk[b], BK(b), L1a, L1b, 128, 256, hpt, hk, TT)
            P.act(lambda e, b=b: e.activation(out=lga_sb[:, :], in_=bk[b][:, 0:TT], func=AF.Sigmoid),
                  r=[BK(b)], w=["lg"])
            b = nbk()
            proj_fm(bk[b], BK(b), L1a, L1b, 256, 288, hpt, hk, TT)
            P.act(lambda e, b=b: e.activation(out=lgb_sb[:, :], in_=bk[b][0:32, 0:TT], func=AF.Sigmoid),
                  r=[BK(b)], w=["lg"])
            for cl in range(NCL):
                b = nbk()
                P.pe(lambda e, b=b, cl=cl: e.matmul(bk[b][:, :], lhsT=lga_sb[:, cl * 128:(cl + 1) * 128],
                                                   rhs=g2a_sb[:, :], start=True, stop=False),
                     r=["lg", "const"], w=[BK(b)])
                P.pe(lambda e, b=b, cl=cl: e.matmul(bk[b][:, :], lhsT=lgb_sb[:, cl * 128:(cl + 1) * 128],
                                                   rhs=g2b_sb[:, :], start=False, stop=True),
                     r=["lg", "const"], w=[BK(b)])
                P.act(lambda e, b=b, cl=cl: e.copy(out=gtok[:, cl, :], in_=bk[b][:, :]), r=[BK(b)], w=["gtok"])
            def a5_gen(pc, sc, kk2_sb, sx):
                cvc = lambda w, pc=pc: cv_sb[:, w * 4 + pc:w * 4 + pc + 1]
                b = nbk()
                P.pe(lambda e, b=b, pc=pc: e.matmul(bk[b][:, 0:TT], lhsT=w2_sb[:, pc * 128:(pc + 1) * 128],
                                                   rhs=lw_sb[:, :], start=True, stop=True),
                     r=["lw", "const"], w=[BK(b)])
                yield
                P.act(lambda e, b=b, pc=pc: e.activation(out=sc["sg"][:, :], in_=bk[b][:, 0:TT], func=AF.Sigmoid,
                                                         bias=cv_sb[:, pc:pc + 1], scale=1.0),
                      r=[BK(b), "const"], w=["sg" + sx])
                yield
                b = nbk()
                P.pe(lambda e, b=b, pc=pc: e.matmul(bk[b][:, 0:TT], lhsT=a2_sb[:, pc * 128:(pc + 1) * 128],
                                                   rhs=la_sb[:, :], start=True, stop=True),
                     r=["la", "const"], w=[BK(b)])
                yield
                P.act(lambda e, b=b, pc=pc: e.activation(out=sc["a"][:, :], in_=bk[b][:, 0:TT], func=AF.Sigmoid,
                                                         bias=cv_sb[:, 4 + pc:5 + pc], scale=1.0),
                      r=[BK(b), "const"], w=["a" + sx])
                yield
                P.dve(lambda e, pc=pc: e.tensor_scalar(out=sc["kk"][:, :], in0=k32[:, pc, :],
                                                       scalar1=cv_sb[:, 8 + pc:9 + pc], scalar2=None, op0=ALU.mult),
                      r=["k32", "const"], w=["kk" + sx])
                yield
                P.act(lambda e: e.activation(out=kk2_sb[:, :], in_=sc["kk"][:, :], func=AF.Square),
                      r=["kk" + sx], w=["kk2" + sx])
                yield
                b = nbk()
                P.pe(lambda e, b=b: e.matmul(bk[b][:, 0:TT], lhsT=bo_sb[:, :], rhs=kk2_sb[:, :], start=True, stop=True),
                     r=["kk2" + sx, "bo"], w=[BK(b)])
                yield
                P.act(lambda e, b=b: e.activation(out=sc["rn"][:, :], in_=bk[b][:, 0:TT], func=AF.Sqrt,
                                                  bias=1e-24, scale=1.0), r=[BK(b)], w=["rn" + sx])
                yield
                P.dve(lambda e: e.reciprocal(out=sc["rn"][:, :], in_=sc["rn"][:, :]), r=["rn" + sx], w=["rn" + sx])
                yield
                P.dve(lambda e: e.tensor_tensor(out=sc["kkn"][:, :], in0=sc["kk"][:, :], in1=sc["rn"][:, :],
                                                op=ALU.mult), r=["kk" + sx, "rn" + sx], w=["kkn" + sx])
                yield
                P.dve(lambda e, pc=pc: e.tensor_scalar(out=sc["tmp"][:, :], in0=sc["a"][:, :],
                                                       scalar1=cv_sb[:, 12 + pc:13 + pc],
                                                       scalar2=cv_sb[:, 20 + pc:21 + pc], op0=ALU.mult, op1=ALU.add),
                      r=["a" + sx, "const", "cv2"], w=["tmp" + sx])
                yield
                P.dve(lambda e, pc=pc: e.tensor_tensor(out=sc["kmod"][:, :], in0=k32[:, pc, :], in1=sc["tmp"][:, :],
                                                       op=ALU.mult), r=["k32", "tmp" + sx], w=["kmod" + sx])
                yield
                P.dve(lambda e: e.tensor_tensor(out=sc["bvec"][:, :], in0=sc["kkn"][:, :], in1=sc["a"][:, :],
                                                op=ALU.mult), r=["kkn" + sx, "a" + sx], w=["bvec" + sx])
                yield
                P.dve(lambda e: e.tensor_tensor_scan(out=sc["cs"][:, :], data0=rmask_sb[:, :], data1=sc["sg"][:, :],
                                                     initial=0.0, op0=ALU.mult, op1=ALU.add),
                      r=["sg" + sx, "const"], w=["cs" + sx])
                yield
                P.dve(lambda e: e.tensor_tensor(out=sc["tmp2"][:, :], in0=sc["cs"][:, :], in1=sc["sg"][:, :],
                                                op=ALU.subtract), r=["cs" + sx, "sg" + sx], w=["tmp2" + sx])
                yield
                P.act(lambda e, pc=pc: e.activation(out=Ep[:, pc, :], in_=sc["cs"][:, :], func=AF.Exp, scale=-C0),
                      r=["cs" + sx], w=["Ep"])
                yield
                P.act(lambda e: e.activation(out=sc["Em"][:, :], in_=sc["cs"][:, :], func=AF.Exp, scale=C0),
                      r=["cs" + sx], w=["Em" + sx])
                yield
                P.act(lambda e: e.activation(out=sc["Epv"][:, :], in_=sc["tmp2"][:, :], func=AF.Exp, scale=-C0),
                      r=["tmp2" + sx], w=["Epv" + sx])
                yield
                for cl in range(NCL):
                    cs_ = slice(cl * 128, (cl + 1) * 128)
                    P.dve(lambda e, pc=pc, cl=cl, cs_=cs_: e.scalar_tensor_tensor(
                        out=AR[:, pc, cl, 0:128], in0=sc["kkn"][:, cs_], scalar=-1.0, in1=sc["Epv"][:, cs_],
                        op0=ALU.mult, op1=ALU.mult), r=["kkn" + sx, "Epv" + sx], w=["AR"])
                    P.dve(lambda e, pc=pc, cl=cl, cs_=cs_: e.tensor_tensor(
                        out=AR[:, pc, cl, 128:256], in0=r32[:, pc, cs_], in1=Ep[:, pc, cs_], op=ALU.mult),
                        r=["r32", "Ep"], w=["AR"])
                P.dve(lambda e, pc=pc: e.tensor_tensor(out=kt[:, pc, :], in0=sc["kmod"][:, :], in1=sc["Em"][:, :],
                                                       op=ALU.mult), r=["kmod" + sx, "Em" + sx], w=["kt"])
                yield
                P.dve(lambda e, pc=pc: e.tensor_tensor(out=bt[:, pc, :], in0=sc["bvec"][:, :], in1=sc["Em"][:, :],
                                                       op=ALU.mult), r=["bvec" + sx, "Em" + sx], w=["bt"])
                yield
                P.dve(lambda e, pc=pc: e.scalar_tensor_tensor(
                    out=rk[:, pc, :], in0=r32[:, pc, :], scalar=cv_sb[:, 16 + pc:17 + pc], in1=sc["kmod"][:, :],
                    op0=ALU.mult, op1=ALU.mult), r=["r32", "kmod" + sx, "const"], w=["rk"])
                yield
                for hd in range(2):
                    rows = slice(64 * hd, 64 * hd + 64)
                    for cl in range(NCL):
                        cs_ = slice(cl * 128, (cl + 1) * 128)
                        P.pool(lambda e, pc=pc, hd=hd, rows=rows, cl=cl, cs_=cs_: e.tensor_copy(
                            out=at2[rows, pc, hd, cs_], in_=AR[rows, pc, cl, 0:128]), r=["AR"], w=["at2"])
                        P.pool(lambda e, pc=pc, hd=hd, rows=rows, cl=cl, cs_=cs_: e.tensor_copy(
                            out=rt2[rows, pc, hd, cs_], in_=AR[rows, pc, cl, 128:256]), r=["AR"], w=["rt2"])
                    P.pool(lambda e, pc=pc, hd=hd, rows=rows: e.tensor_copy(
                        out=bt2[rows, pc, hd, :], in_=bt[rows, pc, :]), r=["bt"], w=["bt2"])
                    P.pool(lambda e, pc=pc, hd=hd, rows=rows: e.tensor_copy(
                        out=kt2[rows, pc, hd, :], in_=kt[rows, pc, :]), r=["kt"], w=["kt2"])
            if LEVEL >= 2:
                for pa in (0, 2):
                    gens = [a5_gen(pa, sc, kk2_sb, "_0"), a5_gen(pa + 1, sc2, kk2b_sb, "_1")]
                    alive = list(gens)
                    while alive:
                        for g in list(alive):
                            try:
                                next(g)
                            except StopIteration:
                                alive.remove(g)
            for cl in range(NCL if LEVEL >= 3 else 0):
                cs_ = slice(cl * 128, (cl + 1) * 128)
                for pc in range(4):
                    for hd in range(2):
                        P.pe(lambda e, pc=pc, cl=cl, hd=hd, cs_=cs_: e.matmul(
                            bk[2][:, hd * 256:(hd + 1) * 256], lhsT=bt2[:, pc, hd, cs_], rhs=AR[:, pc, cl, :],
                            start=True, stop=True), r=["bt2", "AR"], w=[BK(2)])
                        P.pe(lambda e, pc=pc, cl=cl, hd=hd, cs_=cs_: e.matmul(
                            bk[3][:, hd * 256:(hd + 1) * 256], lhsT=kt2[:, pc, hd, cs_], rhs=AR[:, pc, cl, :],
                            start=True, stop=True), r=["kt2", "AR"], w=[BK(3)])
                        P.pe(lambda e, pc=pc, cl=cl, hd=hd, cs_=cs_: e.matmul(
                            bk[4][:, hd * 128:(hd + 1) * 128], lhsT=at2[:, pc, hd, cs_], rhs=bt[:, pc, cs_],
                            start=True, stop=True), r=["bt", "at2"], w=[BK(4)])
                    xk0 = "XX%d_0" % pc
                    for hd in range(2):
                        P.dve(lambda e, pc=pc, hd=hd: e.tensor_tensor(
                            out=NTb_sb[:, pc, hd * 256:(hd + 1) * 256], in0=bk[2][:, hd * 256:(hd + 1) * 256],
                            in1=mask2_sb[:, :], op=ALU.mult), r=[BK(2), "const"], w=["NTb%d" % pc])
                        P.dve(lambda e, pc=pc, hd=hd: e.tensor_tensor(
                            out=NTk_sb[:, pc, hd * 256:(hd + 1) * 256], in0=bk[3][:, hd * 256:(hd + 1) * 256],
                            in1=mask2_sb[:, :], op=ALU.mult), r=[BK(3), "const"], w=["NTk%d" % pc])
                        P.dve(lambda e, pc=pc, hd=hd: e.tensor_tensor(
                            out=XX[pc][0][:, 256 + hd * 128:256 + (hd + 1) * 128], in0=bk[4][:, hd * 128:(hd + 1) * 128],
                            in1=maskL_sb[:, :], op=ALU.mult), r=[BK(4), "const"], w=[xk0])
                        P.pool(lambda e, pc=pc, hd=hd: e.tensor_copy(
                            out=XX[pc][0][:, hd * 128:(hd + 1) * 128], in_=NTb_sb[:, pc, hd * 256:hd * 256 + 128]),
                            r=["NTb%d" % pc], w=[xk0])
                        P.pool(lambda e, pc=pc, hd=hd: e.tensor_tensor(
                            out=Pp[pc][0][:, hd * 128:(hd + 1) * 128], in0=NTb_sb[:, pc, hd * 256:hd * 256 + 128],
                            in1=identf[:, :], op=ALU.add), r=["NTb%d" % pc, "const"], w=["Pp%d_0" % pc])
                    P.pe(lambda e, pc=pc, cs_=cs_: e.transpose(out=bk5[:, 0:128], in_=bt[:, pc, cs_],
                                                               identity=ident_bf[:, :]), r=["bt", "const"], w=[BK(5)])
                    P.pe(lambda e, pc=pc, cs_=cs_: e.transpose(out=bk5[:, 128:256], in_=kt[:, pc, cs_],
                                                               identity=ident_bf[:, :]), r=["kt", "const"], w=[BK(5)])
                    P.act(lambda e, pc=pc: e.copy(out=btok[:, pc, :], in_=bk5[:, 0:128]), r=[BK(5)], w=["btok%d" % pc])
                    P.act(lambda e, pc=pc: e.copy(out=ktok[:, pc, :], in_=bk5[:, 128:256]), r=[BK(5)], w=["ktok%d" % pc])
                cur = 0
                for step in range(6):
                    nxt = 1 - cur
                    last = step == 5
                    for pc in range(4):
                        bx, bxk = (bk[6], BK(6)) if pc % 2 == 0 else (bk[0], BK(0))
                        xc, xn = "XX%d_%d" % (pc, cur), "XX%d_%d" % (pc, nxt)
                        for hd in range(2):
                            hs_ = slice(hd * 128, (hd + 1) * 128)
                            hx_ = slice(256 + hd * 128, 256 + (hd + 1) * 128)
                            if not last:
                                P.pe(lambda e, pc=pc, cur=cur, hs_=hs_, hx_=hx_, bx=bx: e.matmul(
                                    bx[:, hs_], lhsT=XX[pc][cur][:, hx_], rhs=XX[pc][cur][:, hs_],
                                    start=True, stop=True), r=[xc], w=[bxk])
                            P.pe(lambda e, pc=pc, cur=cur, hs_=hs_, hx_=hx_, bx=bx: e.matmul(
                                bx[:, hx_], lhsT=XX[pc][cur][:, hs_], rhs=XX[pc][cur][:, hx_],
                                start=True, stop=True), r=[xc], w=[bxk])
                        if not last:
                            P.act(lambda e, pc=pc, nxt=nxt, bx=bx: e.copy(out=XX[pc][nxt][:, :], in_=bx[:, :]),
                                  r=[bxk], w=[xn])
                        else:
                            P.act(lambda e, pc=pc, nxt=nxt, bx=bx: e.copy(out=XX[pc][nxt][:, 256:512], in_=bx[:, 256:512]),
                                  r=[bxk], w=[xn])
                    for pc in range(4):
                        bp, bpk = (bk[7], BK(7)) if pc % 2 == 0 else (bk[1], BK(1))
                        xn = "XX%d_%d" % (pc, nxt)
                        pk_c, pk_n = "Pp%d_%d" % (pc, cur), "Pp%d_%d" % (pc, nxt)
                        for hd in range(2):
                            hs_ = slice(hd * 128, (hd + 1) * 128)
                            hx_ = slice(256 + hd * 128, 256 + (hd + 1) * 128)
                            P.pe(lambda e, pc=pc, cur=cur, nxt=nxt, hs_=hs_, hx_=hx_, bp=bp: e.matmul(
                                bp[:, hs_], lhsT=XX[pc][nxt][:, hx_], rhs=Pp[pc][cur][:, hs_], start=True, stop=True),
                                r=[xn, pk_c], w=[bpk])
                        if last:
                            P.dve(lambda e, cur=cur, pc=pc, bp=bp: e.tensor_tensor(
                                out=Pfin[:, pc, :], in0=bp[:, 0:256], in1=Pp[pc][cur][:, :], op=ALU.add),
                                r=[bpk, pk_c], w=["Pfin%d" % pc])
                        else:
                            P.dve(lambda e, cur=cur, nxt=nxt, pc=pc, bp=bp: e.tensor_tensor(
                                out=Pp[pc][nxt][:, :], in0=bp[:, 0:256], in1=Pp[pc][cur][:, :], op=ALU.add),
                                r=[bpk, pk_c], w=[pk_n])
                    cur = nxt
                if LEVEL < 4:
                    continue
                for pc in range(4):
                    for hd in range(2):
                        rows = slice(64 * hd, 64 * hd + 64)
                        oc_ = slice(pc * 128 + hd * 64, pc * 128 + hd * 64 + 64)
                        P.pe(lambda e, pc=pc, cl=cl, rows=rows, oc_=oc_, hd=hd, cs_=cs_: e.matmul(
                            bk[0][:, oc_], lhsT=at2[:, pc, hd, cs_], rhs=Hbf[:, pc, hd * 64:hd * 64 + 64],
                            start=True, stop=False), r=["at2", "Hbf%d" % pc], w=[BK(0)])
                        P.pe(lambda e, pc=pc, cl=cl, oc_=oc_, hd=hd: e.matmul(
                            bk[0][:, oc_], lhsT=NTk_sb[:, pc, hd * 256:hd * 256 + 128], rhs=vtok[:, cl, oc_],
                            start=False, stop=True), r=["NTk%d" % pc, "vtok"], w=[BK(0)])
                P.act(lambda e: e.copy(out=Zs[:, :, :], in_=bk[0][:, :]), r=[BK(0)], w=["Zs"])
                for pc in range(4):
                    for hd in range(2):
                        oc_ = slice(pc * 128 + hd * 64, pc * 128 + hd * 64 + 64)
                        P.pe(lambda e, pc=pc, oc_=oc_, hd=hd: e.matmul(
                            bk[1][:, oc_], lhsT=Pfin[:, pc, hd * 128:(hd + 1) * 128], rhs=Zs[:, pc, hd * 64:hd * 64 + 64],
                            start=True, stop=True), r=["Pfin%d" % pc, "Zs"], w=[BK(1)])
                P.dve(lambda e: e.tensor_copy(out=Us[:, :, :], in_=bk[1][:, :]), r=[BK(1)], w=["Us"])
                for pc in range(4):
                    for hd in range(2):
                        rows = slice(64 * hd, 64 * hd + 64)
                        oc_ = slice(pc * 128 + hd * 64, pc * 128 + hd * 64 + 64)
                        P.pe(lambda e, pc=pc, cl=cl, rows=rows, oc_=oc_, hd=hd, cs_=cs_: e.matmul(
                            bk[2][:, oc_], lhsT=rt2[:, pc, hd, cs_], rhs=Hbf[:, pc, hd * 64:hd * 64 + 64],
                            start=True, stop=False), r=["rt2", "Hbf%d" % pc], w=[BK(2)])
                        P.pe(lambda e, pc=pc, oc_=oc_, hd=hd: e.matmul(
                            bk[2][:, oc_], lhsT=NTb_sb[:, pc, hd * 256 + 128:hd * 256 + 256],
                            rhs=Us[:, pc, hd * 64:hd * 64 + 64], start=False, stop=False),
                            r=["NTb%d" % pc, "Us"], w=[BK(2)])
                        P.pe(lambda e, pc=pc, cl=cl, oc_=oc_, hd=hd: e.matmul(
                            bk[2][:, oc_], lhsT=NTk_sb[:, pc, hd * 256 + 128:hd * 256 + 256], rhs=vtok[:, cl, oc_],
                            start=False, stop=True), r=["NTk%d" % pc, "vtok"], w=[BK(2)])
                P.act(lambda e, cl=cl: e.copy(out=ytok[:, cl, :], in_=bk[2][:, :]), r=[BK(2)], w=["ytok"])
                for pc in range(4):
                    pcs = slice(pc * 128, (pc + 1) * 128)
                    P.pe(lambda e, pc=pc, pcs=pcs: e.matmul(bk[3][:, pcs], lhsT=btok[:, pc, :], rhs=Us[:, pc, :],
                                                            start=True, stop=False),
                         r=["btok%d" % pc, "Us"], w=[BK(3)])
                    P.pe(lambda e, pc=pc, pcs=pcs, cl=cl: e.matmul(bk[3][:, pcs], lhsT=ktok[:, pc, :], rhs=vtok[:, cl, pcs],
                                                                   start=False, stop=True),
                         r=["ktok%d" % pc, "vtok"], w=[BK(3)])
                P.dve(lambda e: e.tensor_tensor(out=Htmp[:, :, :], in0=bk[3][:, :], in1=H32[:, :, :], op=ALU.add),
                      r=[BK(3), "H32"], w=["Htmp"])
                for pc in range(4):
                    wl = Ep[:, pc, cl * 128 + 127:cl * 128 + 128]
                    P.dve(lambda e, pc=pc, wl=wl: e.tensor_scalar(out=H32[:, pc, :], in0=Htmp[:, pc, :], scalar1=wl,
                                                                  scalar2=None, op0=ALU.mult),
                          r=["Htmp", "Ep"], w=["H32"])
                    P.act(lambda e, pc=pc, wl=wl: e.activation(out=Hbf[:, pc, :], in_=Htmp[:, pc, :], func=AF.Copy,
                                                               scale=wl), r=["Htmp", "Ep"], w=["Hbf%d" % pc])
            for cl in range(NCL if LEVEL >= 5 else 0):
                cs_ = slice(cl * 128, (cl + 1) * 128)
                o = out_sb[fo % 2]
                ok = "out%d" % (fo % 2)
                fo += 1
                for pc in range(4):
                    P.pe(lambda e, pc=pc, cs_=cs_: e.matmul(bk[4][:, 2 * pc:2 * pc + 2], lhsT=rk[:, pc, cs_],
                                                            rhs=ind_sb[:, :], start=True, stop=True),
                         r=["rk", "const"], w=[BK(4)])
                P.dve(lambda e: e.tensor_copy(out=st_sb[:, 32:40], in_=bk[4][:, 0:8]), r=[BK(4)], w=["bonus"])
                y3 = ytok[:, cl, :].rearrange("p (h n) -> p h n", n=64)
                P.dve(lambda e, y3=y3: e.tensor_reduce(out=st_sb[:, 0:8], in_=y3, axis=AX.X, op=ALU.add),
                      r=["ytok"], w=["st"])
                P.act(lambda e, cl=cl: e.activation(out=ysq[:, :], in_=ytok[:, cl, :], func=AF.Square),
                      r=["ytok"], w=["ysq"])
                P.dve(lambda e: e.tensor_reduce(out=st_sb[:, 8:16], in_=ysq[:, :].rearrange("p (h n) -> p h n", n=64),
                                                axis=AX.X, op=ALU.add), r=["ysq"], w=["st"])
                P.dve(lambda e: e.tensor_scalar(out=st_sb[:, 0:8], in0=st_sb[:, 0:8], scalar1=1.0 / 64, scalar2=None,
                                                op0=ALU.mult), r=["st"], w=["st"])
                P.dve(lambda e: e.tensor_tensor(out=st_sb[:, 16:24], in0=st_sb[:, 0:8], in1=st_sb[:, 0:8], op=ALU.mult),
                      r=["st"], w=["st"])
                P.dve(lambda e: e.scalar_tensor_tensor(out=st_sb[:, 24:32], in0=st_sb[:, 8:16], scalar=1.0 / 64,
                                                       in1=st_sb[:, 16:24], op0=ALU.mult, op1=ALU.subtract),
                      r=["st"], w=["st"])
                P.act(lambda e: e.activation(out=st_sb[:, 24:32], in_=st_sb[:, 24:32], func=AF.Sqrt, bias=RW_LN_EPS,
                                             scale=1.0), r=["st"], w=["st"])
                P.dve(lambda e: e.reciprocal(out=st_sb[:, 24:32], in_=st_sb[:, 24:32]), r=["st"], w=["st"])
                for h in range(8):
                    hs_ = slice(h * 64, (h + 1) * 64)
                    P.dve(lambda e, h=h, hs_=hs_, o=o, cl=cl: e.tensor_scalar(
                        out=o[:, hs_], in0=ytok[:, cl, hs_], scalar1=st_sb[:, h:h + 1], scalar2=st_sb[:, 24 + h:25 + h],
                        op0=ALU.subtract, op1=ALU.mult), r=["ytok", "st"], w=[ok])
                P.dve(lambda e, o=o: e.tensor_tensor(out=o[:, :], in0=o[:, :], in1=lng_sb[:, :], op=ALU.mult),
                      r=[ok, "const"], w=[ok])
                P.dve(lambda e, o=o: e.tensor_tensor(out=o[:, :], in0=o[:, :], in1=lnb_sb[:, :], op=ALU.add),
                      r=[ok, "const"], w=[ok])
                for h in range(8):
                    hs_ = slice(h * 64, (h + 1) * 64)
                    P.dve(lambda e, h=h, hs_=hs_, o=o, cl=cl: e.scalar_tensor_tensor(
                        out=o[:, hs_], in0=v32[:, cl, hs_], scalar=st_sb[:, 32 + h:33 + h], in1=o[:, hs_],
                        op0=ALU.mult, op1=ALU.add), r=["v32", "bonus", ok], w=[ok])
                P.dve(lambda e, o=o, cl=cl: e.tensor_tensor(out=o[:, :], in0=o[:, :], in1=gtok[:, cl, :], op=ALU.mult),
                      r=[ok, "gtok"], w=[ok])
                P.dma("sp", yg[t0 + cl * 128:t0 + (cl + 1) * 128, :], o[:, :], r=[ok], w=["yg"])
        pass


def rw_consts(TT=256):
    ident = np.eye(128, dtype=np.float32)
    p = np.arange(128)
    up_strict = (p[:, None] < p[None, :]).astype(np.float32)
    up_incl = (p[:, None] <= p[None, :]).astype(np.float32)
    mask2 = np.concatenate([up_strict, up_incl], axis=1)
    maskL = (p[:, None] > p[None, :]).astype(np.float32)
    rmask = np.ones((128, TT), np.float32)
    rmask[:, ::CL] = 0.0
    ind = np.zeros((128, 2), np.float32)
    ind[:64, 0] = 1.0
    ind[64:, 1] = 1.0
    return dict(ident=ident, mask2=mask2, maskL=maskL, rmask=rmask, ind=ind)


def rw_inputs(hT_b, hh, p):
    cols = slice(hh * 512, (hh + 1) * 512)
    T = hT_b.shape[1]
    hTp = np.zeros((D, T + 1), np.float32)
    hTp[:, 1:] = hT_b
    vec = lambda v: np.ascontiguousarray(v[cols].reshape(4, 128).T)
    cv = np.concatenate([vec(p["rwkv_w0"][0]), vec(p["rwkv_a0"][0]), vec(p["rwkv_k_k"][0]), vec(p["rwkv_k_a"][0]),
                         vec(p["rwkv_r_k"][0].reshape(-1))], axis=1)
    mu6 = np.ascontiguousarray(p["rwkv_mu"][0].reshape(6, 8, 128).transpose(2, 0, 1).reshape(128, 48))
    m = dict(hTp=hTp,
             wr=np.ascontiguousarray(p["rwkv_w_rkv"][0, 0][:, cols]), wk=np.ascontiguousarray(p["rwkv_w_rkv"][0, 1][:, cols]),
             wv=np.ascontiguousarray(p["rwkv_w_rkv"][0, 2][:, cols]),
             l1=np.ascontiguousarray(np.concatenate([p["rwkv_w1"][0], p["rwkv_a1"][0], p["rwkv_g1"][0]], axis=1)),
             w2c=np.ascontiguousarray(p["rwkv_w2"][0][:, cols]), a2c=np.ascontiguousarray(p["rwkv_a2"][0][:, cols]),
             g2c=np.ascontiguousarray(p["rwkv_g2"][0][:, cols]), mu6=mu6, cv=np.ascontiguousarray(cv),
             lng=np.ascontiguousarray(np.broadcast_to(p["rwkv_ln_g"][0][cols], (128, 512))),
             lnb=np.ascontiguousarray(np.broadcast_to(p["rwkv_ln_b"][0][cols], (128, 512))))
    m.update(rw_consts())
    return m


def _std_io(nc):
    def io(name, shape, kind):
        return nc.dram_tensor(name, list(shape), F32,
                              kind="ExternalInput" if kind == "in" else "ExternalOutput").ap()
    return io


def build_LA(NT, has_add, TB=1024, TT=512):
    nc = bass.Bass("TRN2", target_bir_lowering=False)
    with ExitStack() as es:
        P = Prog(nc, es)
        emit_LA(nc, P, _std_io(nc), NT, has_add, TB, TT)
    return nc


def build_LP(NT, has_v, TB=1024, TT=512):
    nc = bass.Bass("TRN2", target_bir_lowering=False)
    with ExitStack() as es:
        P = Prog(nc, es)
        emit_LP(nc, P, _std_io(nc), NT, has_v, TB, TT)
    return nc


def build_ATT(T, HL=4):
    nc = bass.Bass("TRN2", target_bir_lowering=False)
    with ExitStack() as es:
        P = Prog(nc, es)
        emit_ATT(nc, P, _std_io(nc), T, HL)
    return nc


def build_RW(T, TT=256, LEVEL=9):
    nc = bass.Bass("TRN2", target_bir_lowering=False)
    with ExitStack() as es:
        P = Prog(nc, es)
        emit_RW(nc, P, _std_io(nc), T, TT, LEVEL)
    return nc


_PROGS = {}


def _prog(name, fn):
    if name not in _PROGS:
        _PROGS[name] = fn()
    return _PROGS[name]


def _tile_win(w):
    g = w[:, :FF].reshape(8, 128, 22, 128)
    u = w[:, FF:].reshape(8, 128, 22, 128)
    return np.ascontiguousarray(np.concatenate([g, u], axis=3).transpose(2, 1, 0, 3))


def _tile_wout(w):
    return np.ascontiguousarray(w.reshape(22, 128, 8, 128).transpose(2, 1, 0, 3))


def _tile_sq(w):
    return np.ascontiguousarray(w.reshape(8, 128, 8, 128).transpose(2, 1, 0, 3))


def _gains(g1, g2):
    return np.ascontiguousarray(np.concatenate([g1.reshape(8, 128).T, g2.reshape(8, 128).T], axis=1))


def _run(nc, in_maps):
    return run_bass_kernel_spmd(nc, in_maps, core_ids=list(range(NCORES))).results


def _run_LA(xT_list, w_in, w_out, g1, g2, aT_list=None, w_add=None):
    NT = xT_list[0].shape[1]
    has_add = aT_list is not None
    nc = _prog(("LA", NT, has_add), lambda: build_LA(NT, has_add))
    wi, wo, gg = _tile_win(w_in), _tile_wout(w_out), _gains(g1, g2)
    wa = _tile_sq(w_add) if has_add else None
    maps = []
    for c in range(NCORES):
        m = {"xT": xT_list[c], "gains": gg, "w_in": wi, "w_out": wo}
        if has_add:
            m["aT"] = aT_list[c]
            m["w_add"] = wa
        maps.append(m)
    res = _run(nc, maps)
    return [r["yT"] for r in res], [r["hT"] for r in res]


def _run_LP(hT_list, wn, gn, wv=None):
    NT = hT_list[0].shape[1]
    has_v = wv is not None
    nc = _prog(("LP", NT, has_v), lambda: build_LP(NT, has_v))
    wnt = _tile_sq(wn)
    g = np.ascontiguousarray(np.tile(gn, 2).reshape(128, 1))
    maps = []
    for c in range(NCORES):
        m = {"hT": hT_list[c], "wn": wnt, "gn": g}
        if has_v:
            m["wv"] = np.ascontiguousarray(wv.reshape(8, 128, D).transpose(1, 0, 2))
        maps.append(m)
    res = _run(nc, maps)
    return [r["nT"] for r in res], ([r["v_tok"] for r in res] if has_v else None)


def kernel_unfused(**inp):
    p = {k: np.asarray(v, dtype=np.float32) for k, v in inp.items()}
    x = p["x"]
    B, T, _ = x.shape
    HT = T // 2
    xT = [np.ascontiguousarray(x[c // 2, (c % 2) * HT:(c % 2 + 1) * HT].T) for c in range(NCORES)]

    def full_seq(lst, b):
        return np.concatenate([lst[2 * b], lst[2 * b + 1]], axis=1)

    x1T, h1T = _run_LA(xT, p["ffn_w_in"][0, 0], p["ffn_w_out"][0, 0], p["ffn_norm"][0, 0], p["mix_norm"][0])
    nc_rw = _prog(("RW", T), lambda: build_RW(T))
    rw_maps = [rw_inputs(full_seq(h1T, c // 2), c % 2, p) for c in range(NCORES)]
    yg = [r["yg_tok"] for r in _run(nc_rw, rw_maps)]
    ygT = [np.ascontiguousarray(np.concatenate([yg[2 * b], yg[2 * b + 1]], axis=1).T) for b in range(B)]
    aT = [np.ascontiguousarray(ygT[c // 2][:, (c % 2) * HT:(c % 2 + 1) * HT]) for c in range(NCORES)]
    x3T, hkvT = _run_LA(x1T, p["ffn_w_in"][0, 1], p["ffn_w_out"][0, 1], p["ffn_norm"][0, 1], p["kv_norm"],
                        aT_list=aT, w_add=p["rwkv_w_o"][0])
    kT, v_tok = _run_LP(hkvT, p["w_kv"][:, :D], p["k_norm"], wv=p["w_kv"][:, D:])
    x4T, h2T = _run_LA(x3T, p["ffn_w_in"][1, 0], p["ffn_w_out"][1, 0], p["ffn_norm"][1, 0], p["mix_norm"][1])
    qT, _ = _run_LP(h2T, p["diff_w_q"][0], p["diff_q_norm"][0])
    nc_att = _prog(("ATT", T), lambda: build_ATT(T, 4))
    att_maps = []
    subg = np.ascontiguousarray(np.broadcast_to(p["diff_subln"][0], (128, 128)))
    lam = np.ascontiguousarray(p["diff_lambda"][0].reshape(1, 256))
    for c in range(NCORES):
        b, hh = c // 2, c % 2
        rows = slice(hh * 512, (hh + 1) * 512)
        qaug, kaug, btab, tri = att_consts(T, [hh * 4 + i for i in range(4)])
        att_maps.append({"qT": np.ascontiguousarray(full_seq(qT, b)[rows]), "kT": np.ascontiguousarray(full_seq(kT, b)[rows]),
                         "v_tok": np.ascontiguousarray(np.concatenate([v_tok[2 * b], v_tok[2 * b + 1]], axis=0)[:, rows]),
                         "qaug": qaug, "kaug": kaug, "btab": btab, "tri": tri, "lam": lam, "subg": subg})
    ot = [r["o_tok"] for r in _run(nc_att, att_maps)]
    oT = [np.ascontiguousarray(np.concatenate([ot[2 * b], ot[2 * b + 1]], axis=1).T) for b in range(B)]
    aT = [np.ascontiguousarray(oT[c // 2][:, (c % 2) * HT:(c % 2 + 1) * HT]) for c in range(NCORES)]
    outT, _ = _run_LA(x4T, p["ffn_w_in"][1, 1], p["ffn_w_out"][1, 1], p["ffn_norm"][1, 1], p["ffn_norm"][1, 1],
                      aT_list=aT, w_add=p["diff_w_o"][0])
    out = np.empty((B, T, D), np.float32)
    for c in range(NCORES):
        out[c // 2, (c % 2) * HT:(c % 2 + 1) * HT] = outT[c].T
    return out


RG_PAIRS = [[0, 1], [2, 3], [4, 5], [6, 7]]


CC_MAX_BYTES = 2 * 1024 * 1024


class _Gathered:
    def __init__(self, nc, name, src, R, C):
        self.src, self.R, self.C = src, R, C
        self.RC = min(R, CC_MAX_BYTES // (C * 4))
        assert R % self.RC == 0
        self.nch = R // self.RC
        self.g = nc.dram_tensor(name, [self.nch * 2 * self.RC, C], F32).ap()

    def rows(self, j, r0, r1):
        ch = r0 // self.RC
        assert (r1 - 1) // self.RC == ch
        base = (ch * 2 + j) * self.RC - ch * self.RC
        return self.g[base + r0:base + r1, :]


def _add_allgather_ops(P, gs):
    if True:
        for G in gs:
            for ch in range(G.nch):
                src = G.src[ch * G.RC:(ch + 1) * G.RC, :]
                dst = G.g[ch * 2 * G.RC:(ch + 1) * 2 * G.RC, :]
                P.add("pool", lambda e, src=src, dst=dst: e.collective_compute(
                    "AllGather", ALU.bypass, replica_groups=RG_PAIRS, ins=[src.opt()], outs=[dst.opt()]),
                    dma="cc")


def _emit_allgather(nc, P, gs):
    with P.stage():
        _add_allgather_ops(P, gs)


def _emit_select(nc, P, sel, jobs, F, ident=None):
    NSB = 4 if F <= 1024 else 3
    with P.stage():
        sel_sb = P.sb("sel_sb", [128, 2], F32)
        P.dma("sp", sel_sb[:, :], sel[:, :], w=["sel"])
        a_sb = [P.sb("a_sb%d" % i, [128, F], F32) for i in range(NSB)]
        b_sb = [P.sb("b_sb%d" % i, [128, F], F32) for i in range(NSB)]
        o_sb = [P.sb("o_sb%d" % i, [128, F], F32) for i in range(NSB)]
        if ident is not None:
            id_sb = P.sb("id_sb", [128, 128], F32)
            P.dma("sp", id_sb[:, :], ident[:, :], w=["ident"])
            t_sb = [P.sb("t_sb%d" % i, [128, 512], F32) for i in range(4)]
            ps_t = [P.ps("ps_t%d" % i, [128, 512]) for i in range(4)]
            P.psum_keys.update(["ps_t0", "ps_t1", "ps_t2", "ps_t3"])
        for i, (A, B, dst) in enumerate(jobs):
            q = i % NSB
            P.dma("sp", a_sb[q][:, :], A, w=["a%d" % q])
            P.dma("sp", b_sb[q][:, :], B, w=["b%d" % q])
            P.dve(lambda e, q=q: e.tensor_scalar(out=a_sb[q][:, :], in0=a_sb[q][:, :], scalar1=sel_sb[:, 0:1],
                                                 scalar2=None, op0=ALU.mult), r=["a%d" % q, "sel"], w=["a%d" % q])
            P.dve(lambda e, q=q: e.scalar_tensor_tensor(out=o_sb[q][:, :], in0=b_sb[q][:, :], scalar=sel_sb[:, 1:2],
                                                        in1=a_sb[q][:, :], op0=ALU.mult, op1=ALU.add),
                  r=["a%d" % q, "b%d" % q, "sel"], w=["o%d" % q])
            if ident is None:
                P.dma("sp", dst, o_sb[q][:, :], r=["o%d" % q], w=["seldst"])
            else:
                for cc in range(4):
                    P.pe(lambda e, q=q, cc=cc: e.transpose(out=ps_t[q % 4][:, cc * 128:(cc + 1) * 128],
                                                           in_=o_sb[q][:, cc * 128:(cc + 1) * 128],
                                                           identity=id_sb[:, :]),
                         r=["o%d" % q, "ident"], w=["ps_t%d" % (q % 4)])
                P.act(lambda e, q=q: e.copy(out=t_sb[q % 4][:, :], in_=ps_t[q % 4][:, :]), r=["ps_t%d" % (q % 4)], w=["t%d" % (q % 4)])
                P.dma("sp", dst, t_sb[q % 4][:, :].rearrange("p (c t) -> p c t", c=4), r=["t%d" % (q % 4)], w=["seldst"])


class _Skip:
    def __init__(self, P):
        self.P = P

    def __enter__(self):
        self.n = len(self.P.ops)
        self.P.es = ExitStack()
        self.P.es.__enter__()
        return self.P

    def __exit__(self, *a):
        del self.P.ops[self.n:]
        self.P.lastw, self.P.readers = {}, {}
        self.P.es.__exit__(*a)
        self.P.es = self.P.sem_es
        return False


def build_FUSED(T=8192, UPTO=99):
    HT = T // 2
    _st = [0]

    def go():
        _st[0] += 1
        return _st[0] <= UPTO
    nc = bass.Bass("TRN2", target_bir_lowering=False)
    ext_in = lambda n, shp: nc.dram_tensor(n, list(shp), F32, kind="ExternalInput").ap()
    ext_out = lambda n, shp: nc.dram_tensor(n, list(shp), F32, kind="ExternalOutput").ap()
    internal = lambda n, shp: nc.dram_tensor(n, list(shp), F32).ap()
    with ExitStack() as es:
        P = Prog(nc, es)

        def mk_io(prefix, bind):
            def io(name, shape, kind):
                if name in bind:
                    return bind[name]
                assert kind == "in", name
                return ext_in(prefix + name, shape)
            return io

        sel = ext_in("sel", [128, 2])
        ident = ext_in("SEL_ident", [128, 128])
        xT = ext_in("xT", [D, HT])
        outT = ext_out("outT", [D, HT])
        x1T, h1T = internal("x1T", [D, HT]), internal("h1T", [D, HT])
        if go():
            emit_LA(nc, P, mk_io("A1_", {"xT": xT, "yT": x1T, "hT": h1T}), HT, False)
        h1g = _Gathered(nc, "h1g", h1T, D, HT)
        if go():
            _emit_allgather(nc, P, [h1g])
        hTp = internal("hTp", [D, T + 1])
        with (P.stage() if go() else _Skip(P)):
            z_sb = P.sb("z_sb", [128, KC, 1], F32)
            P.pool(lambda e: e.memset(z_sb[:, :, :], 0.0), w=["z"])
            P.add("sp", lambda e: e.dma_start(out=hTp.rearrange("(kc p) t -> p kc t", p=128)[:, :, 0:1],
                                              in_=z_sb[:, :, :], allow_slow_non_contiguous=True),
                  r=["z"], w=["hTp"], dma=True)
            for j in range(2):
                for kc in range(KC):
                    P.dma("sp", hTp[kc * 128:(kc + 1) * 128, 1 + j * HT:1 + (j + 1) * HT],
                          h1g.rows(j, kc * 128, (kc + 1) * 128), w=["hTp"])
        yg_tok = internal("yg_tok", [T, 512])
        if go():
            emit_RW(nc, P, mk_io("RW_", {"hTp": hTp, "yg_tok": yg_tok}), T)
        ygg = _Gathered(nc, "ygg", yg_tok, T, 512)
        if go():
            _emit_allgather(nc, P, [ygg])
        aT2 = internal("aT2", [D, HT])

        def tok2feat_jobs(g, dst):
            jobs = []
            for j in range(2):
                for tb in range(HT // 128):
                    A = g.rows(j, tb * 128, (tb + 1) * 128)
                    B = g.rows(j, HT + tb * 128, HT + (tb + 1) * 128)
                    dd = dst[j * 512:(j + 1) * 512, tb * 128:(tb + 1) * 128].rearrange("(c p) t -> p c t", p=128)
                    jobs.append((A, B, dd))
            return jobs

        if go():
            _emit_select(nc, P, sel, tok2feat_jobs(ygg, aT2), 512, ident=ident)
        x3T, hkvT = internal("x3T", [D, HT]), internal("hkvT", [D, HT])
        if go():
            emit_LA(nc, P, mk_io("A2_", {"xT": x1T, "aT": aT2, "yT": x3T, "hT": hkvT}), HT, True)
        kT, v_tok = internal("kT", [D, HT]), internal("v_tok", [HT, D])
        if go():
            emit_LP(nc, P, mk_io("P1_", {"hT": hkvT, "nT": kT, "v_tok": v_tok}), HT, True)
        x4T, h2T = internal("x4T", [D, HT]), internal("h2T", [D, HT])
        qT = internal("qT", [D, HT])
        qg, kg, vg = _Gathered(nc, "qg", qT, D, HT), _Gathered(nc, "kg", kT, D, HT), _Gathered(nc, "vg", v_tok, HT, D)
        if go():
            emit_LA(nc, P, mk_io("A3_", {"xT": x3T, "yT": x4T, "hT": h2T}), HT, False,
                    pre=lambda: _add_allgather_ops(P, [kg, vg]))
        if go():
            emit_LP(nc, P, mk_io("P2_", {"hT": h2T, "nT": qT}), HT, False)
        if go():
            _emit_allgather(nc, P, [qg])
        o_tok = internal("o_tok", [T, 512])
        if go():
            emit_ATT(nc, P, mk_io("AT_", {"o_tok": o_tok}), T, 4, gath=(qg, kg, vg, sel))
        og = _Gathered(nc, "og", o_tok, T, 512)
        if go():
            _emit_allgather(nc, P, [og])
        aT4 = internal("aT4", [D, HT])
        if go():
            _emit_select(nc, P, sel, tok2feat_jobs(og, aT4), 512, ident=ident)
        hdum = internal("hdum", [D, HT])
        if go():
            emit_LA(nc, P, mk_io("A4_", {"xT": x4T, "aT": aT4, "yT": outT, "hT": hdum}), HT, True)
    return nc


def kernel(**inp):
    p = {k: np.asarray(v, dtype=np.float32) for k, v in inp.items()}
    x = p["x"]
    B, T, _ = x.shape
    HT = T // 2
    import os
    nc = _prog(("FUSED", T), lambda: build_FUSED(T, int(os.environ.get("FUSED_UPTO", "99"))))
    shared = {"SEL_ident": np.eye(128, dtype=np.float32)}

    def la(prefix, l, i, g2, w_add=None):
        shared[prefix + "gains"] = _gains(p["ffn_norm"][l, i], g2)
        shared[prefix + "w_in"] = _tile_win(p["ffn_w_in"][l, i])
        shared[prefix + "w_out"] = _tile_wout(p["ffn_w_out"][l, i])
        if w_add is not None:
            shared[prefix + "w_add"] = _tile_sq(w_add)

    la("A1_", 0, 0, p["mix_norm"][0])
    la("A2_", 0, 1, p["kv_norm"], p["rwkv_w_o"][0])
    la("A3_", 1, 0, p["mix_norm"][1])
    la("A4_", 1, 1, p["ffn_norm"][1, 1], p["diff_w_o"][0])
    shared["P1_wn"] = _tile_sq(p["w_kv"][:, :D])
    shared["P1_gn"] = np.ascontiguousarray(np.tile(p["k_norm"], 2).reshape(128, 1))
    shared["P1_wv"] = np.ascontiguousarray(p["w_kv"][:, D:].reshape(8, 128, D).transpose(1, 0, 2))
    shared["P2_wn"] = _tile_sq(p["diff_w_q"][0])
    shared["P2_gn"] = np.ascontiguousarray(np.tile(p["diff_q_norm"][0], 2).reshape(128, 1))
    shared["AT_lam"] = np.ascontiguousarray(p["diff_lambda"][0].reshape(1, 256))
    shared["AT_subg"] = np.ascontiguousarray(np.broadcast_to(p["diff_subln"][0], (128, 128)))
    dummy_h = np.zeros((D, 1), np.float32)
    maps = []
    for c in range(NCORES):
        b, hh = c // 2, c % 2
        m = dict(shared)
        m["xT"] = np.ascontiguousarray(x[b, hh * HT:(hh + 1) * HT].T)
        s_ = np.zeros((128, 2), np.float32)
        s_[:, hh] = 1.0
        m["sel"] = s_
        rw = rw_inputs(dummy_h, hh, p)
        del rw["hTp"]
        for k_, v_ in rw.items():
            m["RW_" + k_] = v_
        qaug, kaug, btab, tri = att_consts(T, [hh * 4 + i for i in range(4)])
        m.update({"AT_qaug": qaug, "AT_kaug": kaug, "AT_btab": btab, "AT_tri": tri})
        maps.append(m)
    res = _run(nc, maps)
    out = np.empty((B, T, D), np.float32)
    for c in range(NCORES):
        out[c // 2, (c % 2) * HT:(c % 2 + 1) * HT] = res[c]["outT"].T
    return out
```
